# Optimizing a Trainium2 kernel written in Bass

```python
import jax, jax.numpy as jnp
from jax import lax
import numpy as np

D_MODEL = 2048
BATCH = 1
SEQ = 16384
DEPTH = 1
DEC_BATCH = 2
DEC_SEQ = 8192
PAST_LEN = 128

D_RNN = 2048
LRU_HEADS = 16
LRU_BLOCK = D_RNN // LRU_HEADS
LRU_C = 8.0
CONV_A = 4
CONV_A_PAD = (2, 1)
FOURIER_GROUPS = 4
D_FOURIER = 1024
FOURIER_GW = D_FOURIER // FOURIER_GROUPS
D_FF = 6144
CONV_F = 3
CONV_F_PAD = (1, 1)
EPS = 1e-6
D_IN = 2 * D_RNN + D_FOURIER + 2 * D_MODEL

kernel_name = "hawk_fnet_parallel_encoder"


def rmsnorm(x, g):
    xf = x.astype(jnp.float32)
    y = xf * lax.rsqrt(jnp.mean(xf * xf, axis=-1, keepdims=True) + EPS)
    return y * g.astype(jnp.float32)


def dwconv(x, w, b, pad):
    c = x.shape[-1]
    y = lax.conv_general_dilated(
        x, w[:, None, :].astype(x.dtype), window_strides=(1,), padding=[pad],
        dimension_numbers=("NWC", "WIO", "NWC"), feature_group_count=c)
    return y + b.astype(x.dtype)


def _lin_combine(e1, e2):
    a1, b1 = e1
    a2, b2 = e2
    return a1 * a2, a2 * b1 + b2


def linear_scan(a, bx, reverse):
    return lax.associative_scan(_lin_combine, (a, bx), axis=1, reverse=reverse)[1]


def rglru_bidir(u, w_a, b_a, w_x, b_x, lam):
    bsz, s, _ = u.shape
    ub = u.reshape(bsz, s, LRU_HEADS, LRU_BLOCK)
    r = jax.nn.sigmoid(jnp.einsum("bshi,dhij->dbshj", ub, w_a.astype(jnp.float32)).reshape(2, bsz, s, D_RNN)
                       + b_a.astype(jnp.float32)[:, None, None, :])
    i = jax.nn.sigmoid(jnp.einsum("bshi,dhij->dbshj", ub, w_x.astype(jnp.float32)).reshape(2, bsz, s, D_RNN)
                       + b_x.astype(jnp.float32)[:, None, None, :])
    log_a = LRU_C * r * jax.nn.log_sigmoid(lam.astype(jnp.float32))[:, None, None, :]
    a = jnp.exp(log_a)
    mult = jnp.sqrt(jnp.maximum(-jnp.expm1(2.0 * log_a), 0.0))
    bx = mult * i * u[None]
    h_fwd = linear_scan(a[0], bx[0], reverse=False)
    h_bwd = linear_scan(a[1], bx[1], reverse=True)
    return h_fwd + h_bwd


def fourier_mix(v):
    bsz, s, _ = v.shape
    vg = v.astype(jnp.float32).reshape(bsz, s, FOURIER_GROUPS, FOURIER_GW)
    f = jnp.fft.fftn(vg, axes=(1, 3), norm="ortho").real
    return f.reshape(bsz, s, D_FOURIER).astype(jnp.float32)


def trunk(x, g_mix, w_in, b_in, conv_a_w, conv_a_b, lru_w_a, lru_b_a, lru_w_x, lru_b_x, lru_lam,
          w_out_a, w_out_b, w_out, b_out, g_ffn, w_up, b_up, conv_f_w, conv_f_b, w_down, b_down, g_final):
    out_dtype = x.dtype
    h = x.astype(jnp.float32)
    for l in range(DEPTH):
        n = rmsnorm(h, g_mix[l])
        p = n @ w_in[l].astype(jnp.float32) + b_in[l].astype(jnp.float32)
        u_x, u_g, u_f, z_a, z_b = jnp.split(
            p, [D_RNN, 2 * D_RNN, 2 * D_RNN + D_FOURIER, 2 * D_RNN + D_FOURIER + D_MODEL], axis=-1)
        u_x = dwconv(u_x, conv_a_w[l], conv_a_b[l], CONV_A_PAD)
        hr = rglru_bidir(u_x, lru_w_a[l], lru_b_a[l], lru_w_x[l], lru_b_x[l], lru_lam[l])
        y_a = (jax.nn.gelu(u_g) * hr) @ w_out_a[l].astype(jnp.float32)
        y_b = fourier_mix(u_f) @ w_out_b[l].astype(jnp.float32)
        m = jax.nn.sigmoid(z_a) * y_a + jax.nn.sigmoid(z_b) * y_b
        h = h + m @ w_out[l].astype(jnp.float32) + b_out[l].astype(jnp.float32)
        n2 = rmsnorm(h, g_ffn[l])
        up = n2 @ w_up[l].astype(jnp.float32) + b_up[l].astype(jnp.float32)
        up = dwconv(up, conv_f_w[l], conv_f_b[l], CONV_F_PAD)
        gate, val = jnp.split(up, 2, axis=-1)
        h = h + (jax.nn.gelu(gate) * val) @ w_down[l].astype(jnp.float32) + b_down[l].astype(jnp.float32)
    return rmsnorm(h, g_final).astype(out_dtype)


def setup_inputs(seed: int = 0) -> dict:
    key = jax.random.key(seed)
    ks = jax.random.split(key, 32)
    f32 = jnp.float32
    nrm = lambda k, shape, scale: jax.random.normal(k, shape, f32) * scale
    x_prompt = jax.random.normal(ks[0], (BATCH, SEQ, D_MODEL), f32)
    x_sample = jax.random.normal(ks[1], (DEC_BATCH, DEC_SEQ, D_MODEL), f32)
    g_mix = 1.0 + nrm(ks[2], (DEPTH, D_MODEL), 0.02)
    w_in = nrm(ks[3], (DEPTH, D_MODEL, D_IN), D_MODEL ** -0.5)
    b_in = nrm(ks[4], (DEPTH, D_IN), 0.02)
    conv_a_w = nrm(ks[5], (DEPTH, CONV_A, D_RNN), CONV_A ** -0.5)
    conv_a_b = nrm(ks[6], (DEPTH, D_RNN), 0.02)
    lru_w_a = nrm(ks[7], (DEPTH, 2, LRU_HEADS, LRU_BLOCK, LRU_BLOCK), LRU_BLOCK ** -0.5)
    lru_b_a = nrm(ks[8], (DEPTH, 2, D_RNN), 0.02)
    lru_w_x = nrm(ks[9], (DEPTH, 2, LRU_HEADS, LRU_BLOCK, LRU_BLOCK), LRU_BLOCK ** -0.5)
    lru_b_x = nrm(ks[10], (DEPTH, 2, D_RNN), 0.02)
    a0 = jax.random.uniform(ks[11], (DEPTH, 2, D_RNN), f32, 0.9, 0.999)
    s = a0 ** (1.0 / LRU_C)
    lru_lam = jnp.log(s) - jnp.log1p(-s)
    w_out_a = nrm(ks[12], (DEPTH, D_RNN, D_MODEL), D_RNN ** -0.5)
    w_out_b = nrm(ks[13], (DEPTH, D_FOURIER, D_MODEL), D_FOURIER ** -0.5)
    w_out = nrm(ks[14], (DEPTH, D_MODEL, D_MODEL), D_MODEL ** -0.5)
    b_out = nrm(ks[15], (DEPTH, D_MODEL), 0.02)
    g_ffn = 1.0 + nrm(ks[16], (DEPTH, D_MODEL), 0.02)
    w_up = nrm(ks[17], (DEPTH, D_MODEL, 2 * D_FF), D_MODEL ** -0.5)
    b_up = nrm(ks[18], (DEPTH, 2 * D_FF), 0.02)
    conv_f_w = nrm(ks[19], (DEPTH, CONV_F, 2 * D_FF), CONV_F ** -0.5)
    conv_f_b = nrm(ks[20], (DEPTH, 2 * D_FF), 0.02)
    w_down = nrm(ks[21], (DEPTH, D_FF, D_MODEL), D_FF ** -0.5)
    b_down = nrm(ks[22], (DEPTH, D_MODEL), 0.02)
    g_final = 1.0 + nrm(ks[23], (D_MODEL,), 0.02)
    return {"x_prompt": x_prompt, "x_sample": x_sample, "g_mix": g_mix, "w_in": w_in, "b_in": b_in,
            "conv_a_w": conv_a_w, "conv_a_b": conv_a_b, "lru_w_a": lru_w_a, "lru_b_a": lru_b_a,
            "lru_w_x": lru_w_x, "lru_b_x": lru_b_x, "lru_lam": lru_lam, "w_out_a": w_out_a,
            "w_out_b": w_out_b, "w_out": w_out, "b_out": b_out, "g_ffn": g_ffn, "w_up": w_up,
            "b_up": b_up, "conv_f_w": conv_f_w, "conv_f_b": conv_f_b, "w_down": w_down,
            "b_down": b_down, "g_final": g_final}


def reference(x_prompt, x_sample, g_mix, w_in, b_in, conv_a_w, conv_a_b, lru_w_a, lru_b_a, lru_w_x, lru_b_x,
              lru_lam, w_out_a, w_out_b, w_out, b_out, g_ffn, w_up, b_up, conv_f_w, conv_f_b, w_down, b_down,
              g_final):
    y_prompt = trunk(x_prompt, g_mix, w_in, b_in, conv_a_w, conv_a_b, lru_w_a, lru_b_a, lru_w_x, lru_b_x,
                     lru_lam, w_out_a, w_out_b, w_out, b_out, g_ffn, w_up, b_up, conv_f_w, conv_f_b,
                     w_down, b_down, g_final)
    y_sample = trunk(x_sample, g_mix, w_in, b_in, conv_a_w, conv_a_b, lru_w_a, lru_b_a, lru_w_x, lru_b_x,
                     lru_lam, w_out_a, w_out_b, w_out, b_out, g_ffn, w_up, b_up, conv_f_w, conv_f_b,
                     w_down, b_down, g_final)
    return (y_prompt, y_sample)
```

```python
import contextlib
import numpy as np
import ml_dtypes
import concourse.bass as bass
import concourse.mybir as mybir
from concourse.bass_utils import run_bass_kernel_spmd

F32 = mybir.dt.float32
BF16 = mybir.dt.bfloat16
AF = mybir.ActivationFunctionType
ALU = mybir.AluOpType
NPBF = ml_dtypes.bfloat16

D = 2048
KC = 16
TOK = 4096
TS = 16384
NT = 512
EPS = 1e-6


class Sched:
    def __init__(self, nc):
        self.nc = nc
        self.ops = []
        self.state = {}
        self.bar_from = 0

    def barrier(self):
        last = {}
        dmas = []
        for i, o in enumerate(self.ops[self.bar_from:], self.bar_from):
            if o["dma"]:
                dmas.append(i)
            elif o["fn"] is not None:
                last[o["eng"]] = i
        deps = sorted(set(dmas) | set(last.values()))
        for eng in ("pe", "act", "dve", "pool", "sp"):
            self.ops.append(dict(eng=eng, fn=None, deps=list(deps), dma=False, semkey=None, signal=False, ndma=1, inc=16,
                                 bar=True))
        self.bar_from = len(self.ops)
        self.state = {}

    def op(self, eng, fn, reads=(), writes=(), dma=False, semkey=None, ndma=1, inc=16):
        oid = len(self.ops)
        deps = set()
        for k in reads:
            st = self.state.setdefault(k, [None, []])
            if st[0] is not None:
                deps.add(st[0])
        for k in writes:
            st = self.state.setdefault(k, [None, []])
            if st[0] is not None:
                deps.add(st[0])
            last = {}
            for r in st[1]:
                o = self.ops[r]
                if o["dma"]:
                    deps.add(r)
                else:
                    last[o["eng"]] = max(last.get(o["eng"], -1), r)
            deps.update(last.values())
        for k in reads:
            self.state[k][1].append(oid)
        for k in writes:
            self.state[k] = [oid, []]
        deps.discard(oid)
        if dma and semkey is None:
            semkey = writes[0] if len(writes) else reads[0]
        self.ops.append(dict(eng=eng, fn=fn, deps=sorted(deps), dma=dma, semkey=semkey,
                             signal=dma, ndma=ndma, inc=inc))
        return oid

    def emit(self, final_keys=()):
        nc = self.nc
        ops = self.ops
        self.op("sp", None, reads=list(final_keys))
        for o in ops:
            for d in o["deps"]:
                od = ops[d]
                if od["dma"]:
                    continue
                if od["eng"] == "pe" and o["eng"] == "pe" and not o["dma"]:
                    continue
                od["signal"] = True
        cnt = {}
        for o in ops:
            if o["dma"]:
                k = ("d", o["semkey"])
                cnt[k] = cnt.get(k, 0) + o["inc"] * o["ndma"]
                o["done"] = (k, cnt[k])
            elif o["signal"]:
                k = ("e", o["eng"])
                cnt[k] = cnt.get(k, 0) + 1
                o["done"] = (k, cnt[k])
        semkeys = sorted(cnt.keys(), key=str)
        with contextlib.ExitStack() as es:
            sems = {}
            for i, k in enumerate(semkeys):
                sems[k] = es.enter_context(nc.semaphore("s%d" % i))
            block = es.enter_context(nc.Block())

            def run(engname):
                def body(e):
                    known = {}
                    for o in ops:
                        if o["eng"] != engname:
                            continue
                        for d in o["deps"]:
                            od = ops[d]
                            if "done" not in od:
                                continue
                            if (not od["dma"]) and od["eng"] == "pe" and engname == "pe" and not o["dma"]:
                                continue
                            k, v = od["done"]
                            if known.get(k, 0) < v:
                                e.wait_ge(sems[k], v)
                                known[k] = v
                        if o["fn"] is None:
                            continue
                        ins = o["fn"](e)
                        if o["dma"]:
                            if not isinstance(ins, (list, tuple)):
                                ins = [ins]
                            assert len(ins) == o["ndma"], (len(ins), o["ndma"])
                            for i_ in ins:
                                i_.then_inc(sems[o["done"][0]], o["inc"])
                        elif o["signal"]:
                            ins.then_inc(sems[o["done"][0]], 1)
                return body

            block.tensor(run("pe"))
            block.scalar(run("act"))
            block.vector(run("dve"))
            block.gpsimd(run("pool"))
            block.sync(run("sp"))


class Prog:
    def __init__(self):
        self.nc = bass.Bass("TRN2", target_bir_lowering=False)
        self.es = contextlib.ExitStack()
        self.S = Sched(self.nc)
        self.outs = []
        self._rr = 0

    def din(self, name, shape, dt=F32):
        return self.nc.dram_tensor(name, list(shape), dt, kind="ExternalInput").ap()

    def dout(self, name, shape, dt=F32):
        self.outs.append(name)
        return self.nc.dram_tensor(name, list(shape), dt, kind="ExternalOutput").ap()

    def dscr(self, name, shape, dt=F32):
        return self.nc.dram_tensor(name, list(shape), dt).ap()

    def sb(self, name, shape, dt=F32):
        return self.es.enter_context(self.nc.sbuf_tensor(name, list(shape), dt))

    def ps(self, name, shape, dt=F32):
        return self.es.enter_context(self.nc.psum_tensor(name, list(shape), dt))

    def op(self, *a, **k):
        return self.S.op(*a, **k)

    def ew(self):
        self._rr ^= 1
        return "dve" if self._rr else "pool"

    def load(self, dst_ap, src_ap, dkey, skey="in", eng="sp"):
        self.op(eng, lambda e: e.dma_start(out=dst_ap, in_=src_ap), reads=[skey], writes=[dkey], dma=True)

    def finish(self, final_keys):
        self.S.emit(final_keys=final_keys)
        self.es.close()


def rmsnorm_fm(P, tag, xT, xkey, N, gcol, outT, okey, ones32, epsT, ss_ps, sq, rt, rstd, scratch=None, skey=None):
    if scratch is None:
        scratch, skey = outT, okey
    P.op("act", lambda e: e.activation(out=scratch[:, :, :N], in_=xT[:, :, :N], func=AF.Square), reads=[xkey], writes=[skey])
    for kc in range(KC):
        P.op("pe", (lambda kc: lambda e: e.matmul(ss_ps[:, :N], lhsT=ones32[:, :], rhs=scratch[:, kc, :N],
                                                  start=(kc == 0), stop=(kc == KC - 1)))(kc),
             reads=[skey, "ones32"], writes=["ss_ps"])
    P.op("act", lambda e: e.activation(out=rt[:, :N], in_=ss_ps[:, :N], func=AF.Sqrt, bias=epsT[:, 0:1], scale=1.0 / D),
         reads=["ss_ps", "epsT"], writes=["rt"])
    P.op("dve", lambda e: e.reciprocal(out=rstd[:, :N], in_=rt[:, :N]), reads=["rt"], writes=["rstd"])
    for kc in range(KC):
        P.op("dve", (lambda kc: lambda e: e.scalar_tensor_tensor(out=outT[:, kc, :N], in0=xT[:, kc, :N],
                                                                scalar=gcol[:, kc:kc + 1], in1=rstd[:, :N],
                                                                op0=ALU.mult, op1=ALU.mult))(kc),
             reads=[xkey, "rstd", "consts"] + ([skey] if skey != okey else []), writes=[okey])


def norm_bufs(P):
    ones32 = P.sb("ones32", [128, 128])
    epsT = P.sb("epsT", [128, 1])
    P.op("dve", lambda e: e.memset(ones32[:], 1.0), writes=["ones32"])
    P.op("dve", lambda e: e.memset(epsT[:], EPS), writes=["epsT"])
    ss_ps = P.ps("ss_ps", [128, NT])
    sq = [P.sb("sq%d" % i, [128, NT]) for i in range(2)]
    rt = P.sb("rt", [128, NT])
    rstd = P.sb("rstd", [128, NT])
    return dict(ones32=ones32, epsT=epsT, ss_ps=ss_ps, sq=sq, rt=rt, rstd=rstd)


class Arena:
    def __init__(self, P, nbytes):
        self.t = P.sb("arena", [128, nbytes // 4])
        self.nbytes = nbytes
        self.off = 0

    def reset(self, off=0):
        self.off = off

    def _take(self, nb):
        nb = (nb + 63) // 64 * 64
        o = self.off
        self.off += nb
        assert self.off <= self.nbytes, (self.off, self.nbytes)
        return self.t[:, o // 4:(o + nb) // 4]

    @staticmethod
    def _shape(ap, shape):
        if len(shape) == 2:
            return ap
        if len(shape) == 3:
            return ap.rearrange("p (a b) -> p a b", a=shape[1])
        if len(shape) == 4:
            return ap.rearrange("p (a b c) -> p a b c", a=shape[1], b=shape[2])
        raise ValueError(shape)

    def f32(self, shape):
        n = int(np.prod(shape[1:]))
        return self._shape(self._take(4 * n)[:, :n], shape)

    def bf16(self, shape):
        n = int(np.prod(shape[1:]))
        return self._shape(self._take(2 * n).bitcast(BF16)[:, :n], shape)


RG = [[0, 1, 2, 3], [4, 5, 6, 7]]
G_MIX, G_FFN, G_FIN, B_GZ, B_OUT, B_DN, B_UP, CFW, CFB, MK = 0, 16, 32, 48, 96, 112, 128, 224, 512, 608


def build_fused(stop=None):
    P = Prog()
    nc = P.nc
    NTL = TOK // NT
    NTG = TS // NT
    HALF = TS // 2
    xg = P.din("xg", [128, KC, TS])
    xo = P.din("xo", [128, KC, TOK])
    wx = P.din("wx", [128, KC, 512])
    wf = P.din("wf", [128, KC, 256])
    bfT = P.din("bfT", [128, 256])
    wg = P.din("wg", [128, 2048])
    c2 = P.din("c2", [128, 64])
    tb = P.din("tb", [128, 5, 256])
    if stop is None:
        wgz = P.din("wgz", [12, 128, KC, 512])
        woa = P.din("woa", [4, 128, KC, 512])
        wob = P.din("wob", [4, 128, 8, 512])
        wo = P.din("wo", [4, 128, KC, 512])
        wup = P.din("wup", [24, 128, KC, 512])
        wdn = P.din("wdn", [16, 128, 48, 128])
        YT = P.dout("YT", [KC, 128, TOK])
        wsrc = dict(wgz=(wgz, 12, KC * 512), woa=(woa, 4, KC * 512), wob=(wob, 4, 8 * 512), wo=(wo, 4, KC * 512),
                    wup=(wup, 24, KC * 512), wdn=(wdn, 16, 48 * 128))
        wbf = {}
        for nm, (src, nchunk, ncol) in wsrc.items():
            wbf[nm] = P.dscr(nm + "_bf", [nchunk, 128, ncol], BF16)
    else:
        DBG = P.dout("DBG", [2048, NT], BF16)
        DBF = P.dout("DBF", [128, NT])
    dft = P.din("dft", [128, 1024])
    c3 = P.din("c3", [128, 640])
    Uscr = P.dscr("Uscr", [4, 128, TS + 3])
    HF = P.dscr("HF", [4, 128, TS])
    HS = P.dscr("HS", [KC, 128, TOK])
    N2S = P.dscr("N2S", [KC, 128, TOK + 2], BF16)
    HRloc = P.dscr("HRloc", [8 * 2048, NT], BF16)
    Qloc = P.dscr("Qloc", [2 * 8 * 2048, NT], BF16)
    HRin_h = nc.dram_tensor("HRin", [16 * 512, 2 * NT], BF16)
    HRall_h = nc.dram_tensor("HRall", [16 * 2048, 2 * NT], BF16)
    Qin_h = nc.dram_tensor("Qin", [2 * 2 * 8 * 256, 4 * NT], BF16)
    Qall_h = nc.dram_tensor("Qall", [2 * 2 * 8 * 1024, 4 * NT], BF16)
    NBin_h = nc.dram_tensor("NBin", [2048, 2], BF16)
    NBall_h = nc.dram_tensor("NBall", [8192, 2], BF16)
    HRin, HRall, Qin, Qall, NBin, NBall = (h.ap() for h in (HRin_h, HRall_h, Qin_h, Qall_h, NBin_h, NBall_h))

    cs2 = P.sb("cs2", [128, 64])
    cs = P.sb("cs3", [128, 640])
    P.load(cs2[:], c2[:, :], "consts")
    P.load(cs[:], c3[:, :], "consts")
    onesb = P.sb("onesb", [128, 128], BF16)
    epsT = P.sb("epsT", [128, 1])
    P.op("dve", lambda e: e.memset(onesb[:], 1.0), writes=["ones32"])
    P.op("dve", lambda e: e.memset(epsT[:], EPS), writes=["epsT"])
    PS = [P.ps("PS%d" % i, [128, NT]) for i in range(8)]
    sq = [P.sb("sq%d" % i, [128, NT], BF16) for i in range(2)]
    rt = P.sb("rt", [128, NT])
    rstd = P.sb("rstd", [128, NT])
    nb = dict(ones32=onesb, epsT=epsT, ss_ps=PS[0], sq=sq, rt=rt, rstd=rstd)
    wgb_ = P.sb("wgb", [128, 2048], BF16)
    P.op("pool", lambda e: e.dma_start(out=wgb_[:], in_=wg[:, :]), reads=["in"], writes=["wgb"], dma=True)
    wgb = wgb_[:].rearrange("p (d a m j) -> p d a m j", d=2, a=2, m=4)
    tbb = P.sb("tbb", [128, 5, 256], BF16)
    P.op("pool", lambda e: e.dma_start(out=tbb[:], in_=tb[:, :, :]), reads=["in"], writes=["tbb"], dma=True)
    tw = P.sb("tw", [128, 256])
    P.load(tw[:], tb[:, 3, :], "tw")
    dftb_ = P.sb("dftb", [128, 1024], BF16)
    P.op("pool", lambda e: e.dma_start(out=dftb_[:], in_=dft[:, :]), reads=["in"], writes=["dftb"], dma=True)
    dftb = dftb_[:].rearrange("p (a b c) -> p a b c", a=2, b=2)
    bft = P.sb("bft", [128, 256])
    P.load(bft[:], bfT[:, :], "bft")
    kt = P.sb("kt", [128, 8]); kl = P.sb("kl", [128, 8]); kl2 = P.sb("kl2", [128, 8])
    st = P.sb("st", [128, 8])
    zt = P.sb("zt", [128, 4])
    AR = Arena(P, 186 * 1024)

    pcnt = [0]

    def ppb():
        b = pcnt[0] % 4
        pcnt[0] += 1
        return b

    pid_cache = {}

    def ids(e):
        k = id(e)
        if k not in pid_cache:
            pid = e.partition_id()
            pid_cache[k] = (pid % 4, pid // 4)
        return pid_cache[k]

    Xq = AR.bf16([128, 2, 128, 128])
    XQ_END = AR.off
    xs = [AR.f32([128, KC, NT]) for _ in range(2)]
    nT = AR.bf16([128, KC, NT])
    wxb = AR.bf16([128, KC, 512])
    wfb = AR.bf16([128, KC, 256])
    ob = [AR.f32([128, NT]) for _ in range(4)]
    P.op("pool", lambda e: e.dma_start(out=wxb, in_=wx[:, :, :]), reads=["in"], writes=["wxb"], dma=True)
    P.op("pool", lambda e: e.dma_start(out=wfb, in_=wf[:, :, :]), reads=["in"], writes=["wfb"], dma=True)
    P.op("dve", lambda e: e.memset(zt[:], 0.0), writes=["zt"])
    P.op("sp", lambda e: [e.dma_start(out=Uscr[mx, :, 0:2], in_=zt[:, 0:2], allow_slow_non_contiguous=True) for mx in range(4)] +
                         [e.dma_start(out=Uscr[mx, :, TS + 2:TS + 3], in_=zt[:, 0:1], allow_slow_non_contiguous=True) for mx in range(4)],
         reads=["zt"], writes=["Upad"], dma=True, ndma=8, semkey="zt")
    for t in range(NTG):
        x = xs[t % 2]
        xk = ("x", t % 2)
        P.load(x, xg[:, :, t * NT:(t + 1) * NT], xk)
        rmsnorm_fm(P, "a", x, xk, NT, cs[:, G_MIX:G_MIX + 16], nT, "nT", **nb)
        for mx in range(4):
            b = ppb()
            for kc in range(KC):
                P.op("pe", (lambda mx, kc, b: lambda e: e.matmul(PS[1 + b][:], lhsT=wxb[:, kc, mx * 128:(mx + 1) * 128], rhs=nT[:, kc, :],
                                                               start=(kc == 0), stop=(kc == KC - 1)))(mx, kc, b),
                     reads=["nT", "wxb"], writes=[("pp", b)])
            P.op("act", (lambda mx, b: lambda e: e.activation(out=ob[b], in_=PS[1 + b][:], func=AF.Identity,
                                                             bias=cs2[:, 48 + mx:49 + mx], scale=1.0))(mx, b),
                 reads=[("pp", b), "consts"], writes=[("ob", b)])
            P.op("sp", (lambda mx, b, t: lambda e: e.dma_start(out=Uscr[mx, :, 2 + t * NT:2 + (t + 1) * NT], in_=ob[b]))(mx, b, t),
                 reads=[("ob", b)], writes=[("U", mx, t)], dma=True, semkey=("ob", b))
        for blk in range(4):
            b = ppb()
            for kc in range(KC):
                P.op("pe", (lambda blk, kc, b: lambda e: e.matmul(PS[1 + b][:, 0:256], lhsT=nT[:, kc, blk * 128:(blk + 1) * 128], rhs=wfb[:, kc, :],
                                                                start=(kc == 0), stop=(kc == KC - 1)))(blk, kc, b),
                     reads=["nT", "wfb"], writes=[("pp", b)])
            m = 4 * t + blk
            P.op("dve", (lambda m, b: lambda e: e.tensor_tensor(out=Xq[:, :, m, :], in0=PS[1 + b][:, 0:256].rearrange("p (h c) -> p h c", h=2),
                                                                in1=bft[:, :].rearrange("p (h c) -> p h c", h=2), op=ALU.add))(m, b),
                 reads=[("pp", b), "bft"], writes=["Xq"])
    P.S.barrier()
    if stop == "A":
        P.op("sp", lambda e: [e.dma_start(out=DBF[:, :], in_=Uscr[1, :, 2 + 512:2 + 1024]),
                              e.dma_start(out=DBG[0:128, :], in_=Xq[:, 0, 5, :].rearrange("p c -> p c") if False else Xq[:, :, 5, :].rearrange("p h c -> p (h c)")[:, 0:256] if False else Xq[:, 0, 0:4, :].rearrange("p m c -> p (m c)"))],
             reads=[], writes=["dbg"], dma=True, ndma=2, semkey="dbg")
        P.finish(["dbg"])
        return P

    AR.reset(XQ_END)
    conv_list = []
    if stop is None:
        shp = "c p k m -> c p (k m)"
        for nm, (src, nchunk, ncol) in wsrc.items():
            for ci in range(nchunk):
                conv_list.append((nm, src, ci))

    def issue_conv(n):
        for _ in range(n):
            if conv_list:
                nm, src, ci = conv_list.pop(0)
                P.op("pool", (lambda nm, src, ci: lambda e: e.dma_start(out=wbf[nm][ci, :, :], in_=src.rearrange(shp)[ci, :, :]))(nm, src, ci),
                     reads=["in"], writes=[("wbf", nm, ci)], dma=True, semkey="wconv")
    P.op("act", lambda e: e.activation(out=kt[:], in_=cs2[:, 36:44], func=AF.Exp, scale=-1.0), reads=[], writes=["kt"])
    P.op("dve", lambda e: e.tensor_scalar_add(out=kt[:], in0=kt[:], scalar1=1.0), reads=["kt"], writes=["kt"])
    P.op("act", lambda e: e.activation(out=kt[:], in_=kt[:], func=AF.Ln), reads=["kt"], writes=["kt"])
    P.op("dve", lambda e: e.tensor_scalar_mul(out=kl[:], in0=kt[:], scalar1=-8.0), reads=["kt"], writes=["kl"])
    P.op("dve", lambda e: e.tensor_scalar_mul(out=kl2[:], in0=kt[:], scalar1=-16.0), reads=["kt"], writes=["kl2"])
    P.op("dve", lambda e: e.memset(st[:], 0.0), writes=[("st", c_) for c_ in range(8)])
    NB = 4
    ut = [[AR.f32([128, NT + 3]) for _ in range(2)] for _ in range(NB)]
    u = [[AR.f32([128, NT]) for _ in range(2)] for _ in range(NB)]
    ub = [[AR.bf16([128, NT]) for _ in range(2)] for _ in range(NB)]
    r_ = [AR.f32([128, NT]) for _ in range(NB)]
    i_ = [AR.f32([128, NT]) for _ in range(NB)]
    a_ = [AR.f32([128, NT]) for _ in range(NB)]
    m_ = [AR.f32([128, NT]) for _ in range(NB)]
    h_ = [AR.f32([128, NT]) for _ in range(NB)]
    hf = [AR.f32([128, NT]) for _ in range(NB)]
    hb = [AR.bf16([128, NT]) for _ in range(NB)]
    pg = [PS[1], PS[2], PS[3], PS[4]]
    onesT = P.sb("onesT", [128, 1])
    P.op("dve", lambda e: e.memset(onesT[:], 1.0), writes=["onesT"])
    hrkeys = []

    def stage1(d, t, s_):
        issue_conv(2)
        for mx in range(4):
            b = mx
            utb, uu, ubb = ut[b][s_], u[b][s_], ub[b][s_]
            kut, ku, kub = ("ut", b, s_), ("u", b, s_), ("ub", b, s_)
            P.load(utb, Uscr[mx, :, t * NT:t * NT + NT + 3], kut)
            if t == NTG // 2 - 1:
                P.op("dve", (lambda utb: lambda e: e.tensor_scalar_mul(out=utb[:, NT + 2:NT + 3], in0=utb[:, NT + 2:NT + 3], scalar1=cs2[:, 44:45]))(utb),
                     reads=[kut], writes=[kut])
            if t == NTG // 2:
                P.op("dve", (lambda utb: lambda e: e.tensor_scalar_mul(out=utb[:, 0:2], in0=utb[:, 0:2], scalar1=cs2[:, 44:45]))(utb),
                     reads=[kut], writes=[kut])
            P.op("dve", (lambda utb, uu, mx: lambda e: e.tensor_scalar(out=uu, in0=utb[:, 0:NT], scalar1=cs2[:, mx * 4:mx * 4 + 1],
                                                                       scalar2=cs2[:, 16 + mx:17 + mx], op0=ALU.mult, op1=ALU.add))(utb, uu, mx),
                 reads=[kut], writes=[ku])
            for k in range(1, 4):
                P.op("dve", (lambda utb, uu, mx, k: lambda e: e.scalar_tensor_tensor(out=uu, in0=utb[:, k:k + NT],
                                                                                     scalar=cs2[:, mx * 4 + k:mx * 4 + k + 1], in1=uu,
                                                                                     op0=ALU.mult, op1=ALU.add))(utb, uu, mx, k),
                     reads=[kut, ku], writes=[ku])
            P.op("act", (lambda uu, ubb: lambda e: e.copy(out=ubb, in_=uu))(uu, ubb), reads=[ku], writes=[kub])

    def stage23(d, t, s_):
        for mx in range(4):
            b = mx
            P.op("pe", (lambda b, d, mx, ubb: lambda e: e.matmul(pg[b][:], lhsT=wgb[:, d, 0, mx, :], rhs=ubb, start=True, stop=True))(b, d, mx, ub[b][s_]),
                 reads=[("ub", b, s_), "wgb"], writes=[("pp", b)])
        for mx in range(4):
            b = mx
            col = d * 4 + mx
            P.op("act", (lambda b, col: lambda e: e.activation(out=r_[b], in_=pg[b][:], func=AF.Sigmoid, bias=cs2[:, 20 + col:21 + col], scale=1.0))(b, col),
                 reads=[("pp", b)], writes=[("r", b)])
            P.op("pe", (lambda b, d, mx, ubb: lambda e: e.matmul(pg[b][:], lhsT=wgb[:, d, 1, mx, :], rhs=ubb, start=True, stop=True))(b, d, mx, ub[b][s_]),
                 reads=[("ub", b, s_), "wgb"], writes=[("pp", b)])
        for mx in range(4):
            b = mx
            col = d * 4 + mx
            P.op("act", (lambda b, col: lambda e: e.activation(out=i_[b], in_=pg[b][:], func=AF.Sigmoid, bias=cs2[:, 28 + col:29 + col], scale=1.0))(b, col),
                 reads=[("pp", b)], writes=[("i", b)])
        for mx in range(4):
            b = mx
            col = d * 4 + mx
            P.op("act", (lambda b, col: lambda e: e.activation(out=a_[b], in_=r_[b], func=AF.Exp, scale=kl[:, col:col + 1]))(b, col),
                 reads=[("r", b), "kl"], writes=[("a", b)])
            P.op("act", (lambda b, col: lambda e: e.activation(out=m_[b], in_=r_[b], func=AF.Exp, scale=kl2[:, col:col + 1]))(b, col),
                 reads=[("r", b), "kl2"], writes=[("m", b)])
        for mx in range(4):
            b = mx
            P.op("act", (lambda b: lambda e: e.activation(out=m_[b], in_=m_[b], func=AF.Sqrt, bias=onesT[:, 0:1], scale=-1.0))(b),
                 reads=[("m", b), "onesT"], writes=[("m", b)])

    def stage45(d, t, s_):
        for mx in range(4):
            b = mx
            uu = u[b][s_]
            P.op("pool", (lambda b, uu: lambda e: e.tensor_tensor(out=i_[b], in0=i_[b], in1=uu, op=ALU.mult))(b, uu),
                 reads=[("i", b), ("u", b, s_)], writes=[("i", b)])
            P.op("pool", (lambda b: lambda e: e.tensor_tensor(out=i_[b], in0=i_[b], in1=m_[b], op=ALU.mult))(b),
                 reads=[("i", b), ("m", b)], writes=[("i", b)])
        for mx in range(4):
            b = mx
            col = d * 4 + mx
            if d == 0:
                P.op("dve", (lambda b, col: lambda e: e.tensor_tensor_scan(out=h_[b], data0=a_[b], data1=i_[b], initial=st[:, col:col + 1],
                                                                           op0=ALU.mult, op1=ALU.add))(b, col),
                     reads=[("a", b), ("i", b), ("st", col)], writes=[("h", b)])
                P.op("dve", (lambda b, col: lambda e: e.tensor_copy(out=st[:, col:col + 1], in_=h_[b][:, NT - 1:NT]))(b, col),
                     reads=[("h", b)], writes=[("st", col)])
                P.op("sp", (lambda b, mx, t: lambda e: e.dma_start(out=HF[mx, :, t * NT:(t + 1) * NT], in_=h_[b]))(b, mx, t),
                     reads=[("h", b)], writes=[("HF", mx, t)], dma=True, semkey=("h", b))
            else:
                P.load(hf[b], HF[mx, :, t * NT:(t + 1) * NT], ("hf", b), skey=("HF", mx, t))
                P.op("dve", (lambda b, col: lambda e: e.tensor_tensor_scan(out=h_[b][:, ::-1], data0=a_[b][:, ::-1], data1=i_[b][:, ::-1],
                                                                           initial=st[:, col:col + 1], op0=ALU.mult, op1=ALU.add))(b, col),
                     reads=[("a", b), ("i", b), ("st", col)], writes=[("h", b)])
                P.op("dve", (lambda b, col: lambda e: e.tensor_copy(out=st[:, col:col + 1], in_=h_[b][:, 0:1]))(b, col),
                     reads=[("h", b)], writes=[("st", col)])
                P.op("pool", (lambda b: lambda e: e.tensor_tensor(out=hb[b], in0=h_[b], in1=hf[b], op=ALU.add))(b),
                     reads=[("h", b), ("hf", b)], writes=[("hb", b)])
                k = ("HR", mx, t)
                P.op("sp", (lambda b, mx, t: lambda e: e.dma_start(out=HRin[(t // 2) * 512 + mx * 128:(t // 2) * 512 + (mx + 1) * 128, (t % 2) * NT:(t % 2 + 1) * NT], in_=hb[b]))(b, mx, t),
                     reads=[("hb", b)], writes=[k], dma=True, semkey=("hb", b))
                if mx == 3 and t % 2 == 0:
                    tp = t // 2
                    P.op("pool", (lambda tp: lambda e: e.collective_compute("AllGather", ALU.bypass, replica_groups=RG,
                                                                           ins=[HRin[tp * 512:(tp + 1) * 512, :].opt()],
                                                                           outs=[HRall[tp * 2048:(tp + 1) * 2048, :].opt()]))(tp),
                         reads=[("HR", m2, t_) for m2 in range(4) for t_ in (t, t + 1)], writes=[("HRall", tp)], dma=True, inc=1, semkey="agHR")
                    hrkeys.append(("HRall", tp))

    for d in range(2):
        order = list(range(NTG)) if d == 0 else list(range(NTG - 1, -1, -1))
        stage1(d, order[0], 0)
        for ti, t in enumerate(order):
            if ti == NTG // 2:
                P.op("dve", (lambda d: lambda e: e.tensor_scalar_mul(out=st[:, d * 4:d * 4 + 4], in0=st[:, d * 4:d * 4 + 4],
                                                                      scalar1=cs2[:, 44:45]))(d),
                     reads=[("st", d * 4 + c_) for c_ in range(4)], writes=[("st", d * 4 + c_) for c_ in range(4)])
            stage23(d, t, ti % 2)
            if ti + 1 < NTG:
                stage1(d, order[ti + 1], (ti + 1) % 2)
            stage45(d, t, ti % 2)
    P.S.barrier()
    if stop == "B":
        P.op("sp", lambda e: [e.dma_start(out=DBG[:, :], in_=HRall[3 * 2048:4 * 2048, 0:NT]), e.dma_start(out=DBF[:, :], in_=HF[2, :, 512:1024])],
             reads=[], writes=["dbg"], dma=True, ndma=2, semkey="dbg")
        P.finish(["dbg"])
        return P

    AR.reset(XQ_END)
    FB = AR.bf16([128, 128, 128])
    Bp = AR.bf16([128, 128, 2, 128])
    s1 = [AR.f32([128, 256]) for _ in range(2)]
    t1 = [AR.f32([128, 128]) for _ in range(2)]
    t2 = [AR.f32([128, 128]) for _ in range(2)]
    t3 = [AR.f32([128, 128]) for _ in range(2)]
    t4 = [AR.f32([128, 128]) for _ in range(2)]
    p1 = [PS[5], PS[6]]
    p2 = [PS[1], PS[2]]
    ptrs = [PS[7][:, :].bitcast(BF16)[:, 0:128], PS[0][:, :].bitcast(BF16)[:, 0:128]]
    ptrk = [("ps7", 0), "ss_ps"]
    Tr = tw[:, 0:128]
    Ti = tw[:, 128:256]
    ident = tbb[:, 4, 0:128]
    qkeys = []
    for ch in range(2):
        for c in range(128):
            b = c % 2
            P.op("pe", (lambda ch, c, b: lambda e: e.transpose(out=ptrs[b], in_=Xq[:, ch, :, c], identity=ident))(ch, c, b),
                 reads=["Xq", "tbb"], writes=[ptrk[b]])
            P.op("act" if b else "dve", (lambda c, b: lambda e: (e.copy if b else e.tensor_copy)(out=FB[:, :, c], in_=ptrs[b]))(c, b),
                 reads=[ptrk[b]], writes=["FB"])
        if stop == "C0":
            P.op("sp", lambda e: e.dma_start(out=DBG[0:128, :], in_=FB[:, 0:4, :].rearrange("p a b -> p (a b)")), reads=["FB"], writes=["dbg"], dma=True, semkey="dbg")
            P.op("sp", lambda e: e.dma_start(out=DBF[:, :], in_=HF[2, :, 512:1024]), reads=[], writes=["dbg2"], dma=True, semkey="dbg2")
            P.finish(["dbg", "dbg2"])
            return P
        for c in range(128):
            b = c % 2
            P.op("pe", (lambda c, b: lambda e: e.matmul(p1[b][:, 0:256], lhsT=FB[:, :, c], rhs=tbb[:, 0, :], start=True, stop=True))(c, b),
                 reads=["FB", "tbb"], writes=[("ps", 5 + b)])
            P.op("act", (lambda b: lambda e: e.copy(out=s1[b], in_=p1[b][:, 0:256]))(b), reads=[("ps", 5 + b)], writes=[("s1", b)])
            P.op("dve", (lambda b: lambda e: e.tensor_tensor(out=t1[b], in0=s1[b][:, 0:128], in1=Tr, op=ALU.mult))(b),
                 reads=[("s1", b), "tw"], writes=[("t1", b)])
            P.op("dve", (lambda b: lambda e: e.tensor_tensor(out=t2[b], in0=s1[b][:, 128:256], in1=Ti, op=ALU.mult))(b),
                 reads=[("s1", b), "tw"], writes=[("t2", b)])
            P.op("dve", (lambda b, c: lambda e: e.tensor_tensor(out=Bp[:, c, 0, :], in0=t1[b], in1=t2[b], op=ALU.subtract))(b, c),
                 reads=[("t1", b), ("t2", b)], writes=[("Bp", c)])
            P.op("pool", (lambda b: lambda e: e.tensor_tensor(out=t3[b], in0=s1[b][:, 0:128], in1=Ti, op=ALU.mult))(b),
                 reads=[("s1", b), "tw"], writes=[("t3", b)])
            P.op("pool", (lambda b: lambda e: e.tensor_tensor(out=t4[b], in0=s1[b][:, 128:256], in1=Tr, op=ALU.mult))(b),
                 reads=[("s1", b), "tw"], writes=[("t4", b)])
            P.op("pool", (lambda b, c: lambda e: e.tensor_tensor(out=Bp[:, c, 1, :], in0=t3[b], in1=t4[b], op=ALU.add))(b, c),
                 reads=[("t3", b), ("t4", b)], writes=[("Bp", c)])
        bpk = [("Bp", c) for c in range(128)]
        if stop == "C1":
            P.op("sp", lambda e: e.dma_start(out=DBG[0:128, :], in_=Bp[:, 0:2, :, :].rearrange("p a b c -> p (a b c)")), reads=bpk, writes=["dbg"], dma=True, semkey="dbg")
            P.op("sp", lambda e: e.dma_start(out=DBF[:, :], in_=HF[2, :, 512:1024]), reads=[], writes=["dbg2"], dma=True, semkey="dbg2")
            P.finish(["dbg", "dbg2"])
            return P
        for ri in range(2):
            for j in range(128):
                b = j % 2
                P.op("pe", (lambda j, b, ri: lambda e: e.matmul(p2[b][:, 0:128], lhsT=Bp[:, :, 0, j], rhs=tbb[:, 1, ri * 128:(ri + 1) * 128], start=True, stop=False))(j, b, ri),
                     reads=bpk + ["tbb"], writes=[("pp", b)])
                P.op("pe", (lambda j, b, ri: lambda e: e.matmul(p2[b][:, 0:128], lhsT=Bp[:, :, 1, j], rhs=tbb[:, 2, ri * 128:(ri + 1) * 128], start=False, stop=True))(j, b, ri),
                     reads=bpk + ["tbb"], writes=[("pp", b)])
                P.op("act" if b else "dve", (lambda j, b: lambda e: (e.copy if b else e.tensor_copy)(out=FB[:, :, j], in_=p2[b][:, 0:128]))(j, b),
                     reads=[("pp", b)], writes=["FB"])
            k = ("Q", ch, ri)
            qq = ch * 2 + ri
            P.op("sp", (lambda ch, ri: lambda e: e.dma_start(out=Qin[ch * 2048:(ch + 1) * 2048, :].rearrange("(s i c) n -> c s i n", i=2, c=128)[:, :, ri, :],
                                                             in_=FB.rearrange("p a b -> p (a b)").rearrange("p (s n) -> p s n", n=4 * NT)))(ch, ri),
                 reads=["FB"], writes=[k], dma=True, semkey="FB")
            qkeys.append(k)
            def st2(e, ch=ch, ri=ri):
                outl = []
                fb4 = FB.rearrange("p (kh kl) (b x) -> p kh kl b x", kl=32, b=2)
                q2 = Qin[4096 + ch * 2048:4096 + (ch + 1) * 2048, :].rearrange("(s i c) (kl x) -> c s i kl x", i=2, c=128, x=64)
                for b_ in range(2):
                    for kh in range(4):
                        outl.append(e.dma_start(out=q2[:, 4 * b_ + kh, ri, :, :], in_=fb4[:, kh, :, b_, :]))
                return outl
            k2_ = ("Q2", ch, ri)
            P.op("sp", st2, reads=["FB"], writes=[k2_], dma=True, ndma=8, semkey="FB")
            qkeys.append(k2_)
        if stop != "C2":
            for cp_ in range(2):
                for sq_ in range(8):
                    idx = (cp_ * 2 + ch) * 8 + sq_
                    P.op("pool", (lambda idx: lambda e: e.collective_compute("AllGather", ALU.bypass, replica_groups=RG,
                                                                             ins=[Qin[idx * 256:(idx + 1) * 256, :].opt()],
                                                                             outs=[Qall[idx * 1024:(idx + 1) * 1024, :].opt()]))(idx),
                         reads=[("Q", ch, 0), ("Q", ch, 1), ("Q2", ch, 0), ("Q2", ch, 1)], writes=[("Qall", idx)], dma=True, inc=1, semkey="agQ")
    if stop == "C2":
        P.op("sp", lambda e: e.dma_start(out=DBG[0:512, :], in_=Qin[512:1024, :]), reads=qkeys, writes=["dbg"], dma=True, semkey="dbg")
        P.op("sp", lambda e: e.dma_start(out=DBF[:, :], in_=HF[2, :, 512:1024]), reads=[], writes=["dbg2"], dma=True, semkey="dbg2")
        P.finish(["dbg", "dbg2"])
        return P
    P.S.barrier()
    if stop == "C":
        P.op("sp", lambda e: [e.dma_start(out=DBG[:, :], in_=Qall[2048:4096, 0:NT]), e.dma_start(out=DBF[:, :], in_=HF[2, :, 512:1024])],
             reads=[], writes=["dbg"], dma=True, ndma=2, semkey="dbg")
        P.finish(["dbg"])
        return P

    AR.reset(0)
    X32 = AR.f32([128, KC, NT])
    NTb = AR.bf16([128, KC, NT + 2])
    QTb = AR.bf16([128, 16, NT])
    MT = QTb
    FT = AR.bf16([128, 8, NT])
    BIG = AR.bf16([128, 48, NT])
    WB = [AR.bf16([128, KC * 512]) for _ in range(2)]
    e1 = [AR.f32([128, NT]) for _ in range(2)]
    e2 = [AR.f32([128, NT]) for _ in range(2)]
    upS = [AR.f32([128, NT + 2]) for _ in range(2)]
    upH = [AR.f32([128, 2]) for _ in range(2)]
    cv = [AR.f32([128, NT]) for _ in range(2)]
    gg = [AR.f32([128, NT]) for _ in range(2)]
    pp = [PS[1], PS[2], PS[3], PS[4]]
    ph = [PS[5], PS[6]]
    wcnt = [0]

    def wload(nm, ci, kcn, cols):
        i = wcnt[0] % 2
        wcnt[0] += 1
        view = WB[i][:, 0:kcn * cols].rearrange("p (k m) -> p k m", k=kcn)
        P.op("sp", lambda e: e.dma_start(out=WB[i][:, 0:kcn * cols], in_=wbf[nm][ci, :, :]), reads=["in"], writes=[("WB", i)], dma=True)
        return view, ("WB", i)

    def linear(wview, wkey, mloc, kcn, rhs_fn, rkeys, N):
        b = ppb()
        for kc in range(kcn):
            P.op("pe", (lambda kc, b: lambda e: e.matmul(pp[b][:, :N], lhsT=wview[:, kc, mloc * 128:(mloc + 1) * 128], rhs=rhs_fn(kc),
                                                       start=(kc == 0), stop=(kc == kcn - 1)))(kc, b),
                 reads=list(rkeys) + [wkey], writes=[("pp", b)])
        return b

    ecnt = [0]
    nbkeys = []

    def p3a(t):
        t0 = t * NT
        N = NT

        P.load(X32, xo[:, :, t0:t0 + NT], "X32")
        def ldh(e):
            j, g = ids(e)
            return e.dma_start(out=BIG[:, 0:16, :], in_=HRall[bass.ds(j * 8192 + (t // 2) * 2048, 2048), (t % 2) * NT:(t % 2 + 1) * NT].rearrange("(k p) n -> p k n", p=128))
        P.op("sp", ldh, reads=[], writes=["BIG0"], dma=True)

        def ldq(e):
            j, g = ids(e)
            return [e.dma_start(out=QTb[:, ch_ * 8:(ch_ + 1) * 8, :],
                                in_=Qall[bass.ds(g * 16384 + j * 2048 + (ch_ * 8192 + (t // 4) * 1024), 1024), (t % 4) * NT:(t % 4 + 1) * NT].rearrange("(k c) n -> c k n", c=128))
                    for ch_ in range(2)]
        P.op("act", ldq, reads=[], writes=["QTb"], dma=True, ndma=2)
        rmsnorm_fm(P, "a", X32, "X32", N, cs[:, G_MIX:G_MIX + 16], NTb, "NTb", **nb)
        for part in range(3):
            for cq in range(4):
                wv, wk = wload("wgz", part * 4 + cq, KC, 512)
                for ml in range(4):
                    mt = cq * 4 + ml
                    b = linear(wv, wk, ml, KC, lambda kc: NTb[:, kc, :N], ["NTb"], N)
                    bias = cs[:, B_GZ + part * 16 + mt:B_GZ + part * 16 + mt + 1]
                    if part == 0:
                        i = ecnt[0] % 2
                        ecnt[0] += 1
                        P.op("act", (lambda b, i, bias: lambda e: e.activation(out=e1[i][:, :N], in_=pp[b][:, :N], func=AF.Gelu_apprx_tanh, bias=bias, scale=1.0))(b, i, bias),
                             reads=[("pp", b)], writes=[("e1", i)])
                        P.op("dve", (lambda mt, i: lambda e: e.tensor_tensor(out=BIG[:, mt, :N], in0=BIG[:, mt, :N], in1=e1[i][:, :N], op=ALU.mult))(mt, i),
                             reads=[("e1", i), "BIG0"], writes=["BIG0"])
                    else:
                        P.op("act", (lambda b, mt, bias, part: lambda e: e.activation(out=BIG[:, part * 16 + mt, :N], in_=pp[b][:, :N], func=AF.Sigmoid, bias=bias, scale=1.0))(b, mt, bias, part),
                             reads=[("pp", b)], writes=["BIG%d" % part])
        for g4 in range(4):
            for ct in range(2):
                b = ppb()
                n = 0
                for chh in range(2):
                    for ri in range(2):
                        P.op("pe", (lambda g4, ct, chh, ri, b, n: lambda e: e.matmul(pp[b][:, :N], lhsT=dftb[:, chh, ri, ct * 128:(ct + 1) * 128],
                                                                                     rhs=QTb[:, chh * 8 + g4 * 2 + ri, :N], start=(n == 0), stop=(n == 3)))(g4, ct, chh, ri, b, n),
                             reads=["QTb", "dftb"], writes=[("pp", b)])
                        n += 1
                P.op("act", (lambda g4, ct, b: lambda e: e.copy(out=FT[:, g4 * 2 + ct, :N], in_=pp[b][:, :N]))(g4, ct, b),
                     reads=[("pp", b)], writes=["FT"])
        for cq in range(4):
            wva, wka = wload("woa", cq, KC, 512)
            wvb, wkb = wload("wob", cq, 8, 512)
            for ml in range(4):
                mt = cq * 4 + ml
                ba = linear(wva, wka, ml, KC, lambda kc: BIG[:, kc, :N], ["BIG0"], N)
                bb = linear(wvb, wkb, ml, 8, lambda kc: FT[:, kc, :N], ["FT"], N)
                i = ecnt[0] % 2
                ecnt[0] += 1
                P.op("dve", (lambda ba, mt, i: lambda e: e.tensor_tensor(out=e1[i][:, :N], in0=pp[ba][:, :N], in1=BIG[:, 16 + mt, :N], op=ALU.mult))(ba, mt, i),
                     reads=[("pp", ba), "BIG1"], writes=[("e1", i)])
                P.op("dve", (lambda bb, mt, i: lambda e: e.tensor_tensor(out=e2[i][:, :N], in0=pp[bb][:, :N], in1=BIG[:, 32 + mt, :N], op=ALU.mult))(bb, mt, i),
                     reads=[("pp", bb), "BIG2"], writes=[("e2", i)])
                P.op("pool", (lambda mt, i: lambda e: e.tensor_tensor(out=MT[:, mt, :N], in0=e1[i][:, :N], in1=e2[i][:, :N], op=ALU.add))(mt, i),
                     reads=[("e1", i), ("e2", i)], writes=["QTb"])
        for cq in range(4):
            wv, wk = wload("wo", cq, KC, 512)
            for ml in range(4):
                mt = cq * 4 + ml
                b = linear(wv, wk, ml, KC, lambda kc: MT[:, kc, :N], ["QTb"], N)
                P.op("dve", (lambda b, mt: lambda e: e.scalar_tensor_tensor(out=X32[:, mt, :N], in0=pp[b][:, :N], scalar=cs[:, B_OUT + mt:B_OUT + mt + 1],
                                                                           in1=X32[:, mt, :N], op0=ALU.add, op1=ALU.add))(b, mt),
                     reads=[("pp", b), "X32"], writes=["X32"])
        P.op("sp", lambda e: e.dma_start(out=HS[:, :, t0:t0 + N].rearrange("k p n -> p k n"), in_=X32),
             reads=["X32"], writes=[("HS", t0)], dma=True, semkey="X32")
        rmsnorm_fm(P, "b", X32, "X32", N, cs[:, G_FFN:G_FFN + 16], NTb, "NTb", **nb)
        P.op("sp", lambda e: e.dma_start(out=N2S[:, :, 1 + t0:1 + t0 + N].rearrange("k p n -> p k n"), in_=NTb[:, :, :N]),
             reads=["NTb"], writes=[("N2S", t0)], dma=True, semkey="NTb")
        if t == 0:
            P.op("sp", lambda e: e.dma_start(out=NBin[:, 0:1].rearrange("(k p) n -> p k n", p=128), in_=NTb[:, :, 0:1], allow_slow_non_contiguous=True),
                 reads=["NTb"], writes=[("NB", 0)], dma=True, semkey="NTb")
            nbkeys.append(("NB", 0))
        if t == NTL - 1:
            P.op("sp", lambda e: e.dma_start(out=NBin[:, 1:2].rearrange("(k p) n -> p k n", p=128), in_=NTb[:, :, NT - 1:NT], allow_slow_non_contiguous=True),
                 reads=["NTb"], writes=[("NB", 1)], dma=True, semkey="NTb")
            nbkeys.append(("NB", 1))

    def p3b(t):
        t0 = t * NT
        rk = [("N2S", t0)]
        rk.append(("N2S", t0 - NT) if t > 0 else ("N2S", "halo"))
        rk.append(("N2S", t0 + NT) if t < NTL - 1 else ("N2S", "halo"))
        P.op("sp", lambda e: e.dma_start(out=NTb, in_=N2S[:, :, t0:t0 + NT + 2].rearrange("k p n -> p k n")),
             reads=rk, writes=["NTb"], dma=True)
        P.op("sp", lambda e: e.dma_start(out=X32, in_=HS[:, :, t0:t0 + NT].rearrange("k p n -> p k n")),
             reads=[("HS", t0)], writes=["X32"], dma=True)
        mL = cs[:, MK:MK + 1] if t == 0 else cs[:, MK + 2:MK + 3]
        mR = cs[:, MK + 1:MK + 2] if t == NTL - 1 else cs[:, MK + 2:MK + 3]
        for q in range(24):
            wv, wk = wload("wup", q, KC, 512)
            for ml in range(4):
                gv, jj = ml // 2, ml % 2
                jp = 2 * q + jj
                mt = jp + 48 * gv
                b = ppb()
                hbk = b % 2
                for kc in range(KC):
                    P.op("pe", (lambda kc, b, ml, wv: lambda e: e.matmul(pp[b][:], lhsT=wv[:, kc, ml * 128:(ml + 1) * 128], rhs=NTb[:, kc, 1:NT + 1],
                                                                       start=(kc == 0), stop=(kc == KC - 1)))(kc, b, ml, wv),
                         reads=["NTb", wk], writes=[("pp", b)])
                for kc in range(KC):
                    P.op("pe", (lambda kc, hbk, ml, wv: lambda e: e.matmul(ph[hbk][:, 0:2], lhsT=wv[:, kc, ml * 128:(ml + 1) * 128], rhs=NTb[:, kc, 0:NT + 2:NT + 1],
                                                                         start=(kc == 0), stop=(kc == KC - 1)))(kc, hbk, ml, wv),
                         reads=["NTb", wk], writes=[("ps", 5 + hbk)])
                i = ecnt[0] % 2
                ecnt[0] += 1
                bias = cs[:, B_UP + mt:B_UP + mt + 1]
                P.op("act", (lambda b, i, bias: lambda e: e.activation(out=upS[i][:, 1:NT + 1], in_=pp[b][:], func=AF.Identity, bias=bias, scale=1.0))(b, i, bias),
                     reads=[("pp", b)], writes=[("upS", i)])
                P.op("act", (lambda hbk, i, bias: lambda e: e.activation(out=upH[i], in_=ph[hbk][:, 0:2], func=AF.Identity, bias=bias, scale=1.0))(hbk, i, bias),
                     reads=[("ps", 5 + hbk)], writes=[("upH", i)])
                P.op("dve", (lambda i, mL: lambda e: e.tensor_scalar_mul(out=upS[i][:, 0:1], in0=upH[i][:, 0:1], scalar1=mL))(i, mL),
                     reads=[("upH", i)], writes=[("upS", i)])
                P.op("dve", (lambda i, mR: lambda e: e.tensor_scalar_mul(out=upS[i][:, NT + 1:NT + 2], in0=upH[i][:, 1:2], scalar1=mR))(i, mR),
                     reads=[("upH", i)], writes=[("upS", i)])
                w0 = cs[:, CFW + mt * 3:CFW + mt * 3 + 1]
                w1 = cs[:, CFW + mt * 3 + 1:CFW + mt * 3 + 2]
                w2 = cs[:, CFW + mt * 3 + 2:CFW + mt * 3 + 3]
                cb = cs[:, CFB + mt:CFB + mt + 1]
                P.op("dve", (lambda i, w0, cb: lambda e: e.tensor_scalar(out=cv[i], in0=upS[i][:, 0:NT], scalar1=w0, scalar2=cb, op0=ALU.mult, op1=ALU.add))(i, w0, cb),
                     reads=[("upS", i)], writes=[("cv", i)])
                P.op("dve", (lambda i, w1: lambda e: e.scalar_tensor_tensor(out=cv[i], in0=upS[i][:, 1:NT + 1], scalar=w1, in1=cv[i], op0=ALU.mult, op1=ALU.add))(i, w1),
                     reads=[("upS", i), ("cv", i)], writes=[("cv", i)])
                P.op("dve", (lambda i, w2: lambda e: e.scalar_tensor_tensor(out=cv[i], in0=upS[i][:, 2:NT + 2], scalar=w2, in1=cv[i], op0=ALU.mult, op1=ALU.add))(i, w2),
                     reads=[("upS", i), ("cv", i)], writes=[("cv", i)])
                if gv == 0:
                    P.op("act", (lambda i, jj: lambda e: e.activation(out=gg[jj], in_=cv[i], func=AF.Gelu_apprx_tanh))(i, jj),
                         reads=[("cv", i)], writes=[("gg", jj)])
                else:
                    P.op("pool", (lambda i, jj, jp: lambda e: e.tensor_tensor(out=BIG[:, jp, :], in0=gg[jj], in1=cv[i], op=ALU.mult))(i, jj, jp),
                         reads=[("gg", jj), ("cv", i)], writes=["BIG%d" % (jp // 16)])
        for mt in range(KC):
            wv, wk = wload("wdn", mt, 48, 128)
            b = linear(wv, wk, 0, 48, lambda kc: BIG[:, kc, :], ["BIG0", "BIG1", "BIG2"], NT)
            P.op("dve", (lambda b, mt: lambda e: e.scalar_tensor_tensor(out=X32[:, mt, :], in0=pp[b][:], scalar=cs[:, B_DN + mt:B_DN + mt + 1],
                                                                       in1=X32[:, mt, :], op0=ALU.add, op1=ALU.add))(b, mt),
                 reads=[("pp", b), "X32"], writes=["X32"])
        rmsnorm_fm(P, "c", X32, "X32", NT, cs[:, G_FIN:G_FIN + 16], X32, "X32", scratch=NTb, skey="NTb", **nb)
        k = ("YT", t)
        P.op("sp", lambda e: e.dma_start(out=YT[:, :, t0:t0 + NT].rearrange("k p n -> p k n"), in_=X32),
             reads=["X32"], writes=[k], dma=True)
        return k

    for t in range(NTL):
        p3a(t)
    P.op("pool", lambda e: e.collective_compute("AllGather", ALU.bypass, replica_groups=RG, ins=[NBin_h.ap().opt()], outs=[NBall_h.ap().opt()]),
         reads=nbkeys, writes=["NBall"], dma=True, inc=1, semkey="agNB")

    hl = P.sb("hl", [128, KC, 2], BF16)

    def ldhalo(e):
        j, g = ids(e)
        rl = ((j + 3) % 4) * 2048
        rr = ((j + 1) % 4) * 2048
        return [e.dma_start(out=hl[:, :, 0:1], in_=NBall[bass.ds(rl, 2048), 1:2].rearrange("(k p) n -> p k n", p=128), allow_slow_non_contiguous=True),
                e.dma_start(out=hl[:, :, 1:2], in_=NBall[bass.ds(rr, 2048), 0:1].rearrange("(k p) n -> p k n", p=128), allow_slow_non_contiguous=True)]
    P.op("pool", ldhalo, reads=["NBall"], writes=["hl"], dma=True, ndma=2, semkey="halo")
    P.op("sp", lambda e: [e.dma_start(out=N2S[:, :, 0:1].rearrange("k p n -> p k n"), in_=hl[:, :, 0:1], allow_slow_non_contiguous=True),
                          e.dma_start(out=N2S[:, :, TOK + 1:TOK + 2].rearrange("k p n -> p k n"), in_=hl[:, :, 1:2], allow_slow_non_contiguous=True)],
         reads=["hl"], writes=[("N2S", "halo")], dma=True, ndma=2, semkey="halo2")
    fin = [p3b(t) for t in range(NTL)]
    P.finish(fin)
    return P


def fm(a):
    a = np.asarray(a, np.float32)
    return np.ascontiguousarray(a.reshape(-1, 128).T)


def wfm(w):
    K, M = w.shape
    return np.ascontiguousarray(w.reshape(K // 128, 128, M).transpose(1, 0, 2))


def xfm(x):
    T, Fd = x.shape
    return np.ascontiguousarray(x.T.reshape(Fd // 128, 128, T).transpose(1, 0, 2))


def run(P, in_maps, names=None):
    if names is not None:
        in_maps = [{k: v for k, v in m.items() if k in names} for m in in_maps]
    res = run_bass_kernel_spmd(P.nc, in_maps, core_ids=list(range(8)))
    return res.results


def dft_tables(seq_len):
    nb = TS // seq_len
    n2n = seq_len // 128
    p = np.arange(128)
    bidx, n2 = p // n2n, p % n2n
    j = np.arange(128)
    bj, k2 = j // n2n, j % n2n
    ang = 2 * np.pi * np.outer(n2, k2) / n2n
    same = (bidx[:, None] == bj[None, :]).astype(np.float64)
    sc = 1.0 / np.sqrt(seq_len)
    T1 = np.concatenate([np.cos(ang) * same, -np.sin(ang) * same], 1) * sc
    n1 = np.arange(128)
    a2 = 2 * np.pi * np.outer(n1, n1) / 128.0
    Gr, Gi = np.cos(a2), -np.sin(a2)
    G1 = np.concatenate([Gr, Gi], 1)
    G2 = np.concatenate([-Gi, Gr], 1)
    at = 2 * np.pi * np.outer(n1, k2) / seq_len
    TW = np.concatenate([np.cos(at), -np.sin(at)], 1)
    return np.ascontiguousarray(np.stack([T1, G1, G2, TW], 1).astype(np.float32))


_PROGS = {}
_STOP = None


def kernel(x_prompt, x_sample, g_mix, w_in, b_in, conv_a_w, conv_a_b, lru_w_a, lru_b_a, lru_w_x, lru_b_x,
           lru_lam, w_out_a, w_out_b, w_out, b_out, g_ffn, w_up, b_up, conv_f_w, conv_f_b, w_down, b_down,
           g_final):
    f32 = np.float32
    xs = [np.asarray(x_prompt, f32).reshape(TS, D), np.asarray(x_sample, f32).reshape(TS, D)]
    seqlen = [16384, 8192]
    w_in = np.asarray(w_in, f32)[0]
    b_in = np.asarray(b_in, f32)[0]
    if "f" not in _PROGS:
        _PROGS["f"] = build_fused(_STOP)
    caw = np.asarray(conv_a_w, f32)[0]
    cab = np.asarray(conv_a_b, f32)[0]
    lwa = np.asarray(lru_w_a, f32)[0]
    lwx = np.asarray(lru_w_x, f32)[0]
    lba = np.asarray(lru_b_a, f32)[0]
    lbx = np.asarray(lru_b_x, f32)[0]
    lam = np.asarray(lru_lam, f32)[0]
    bgz = np.concatenate([fm(b_in[2048:4096]), fm(b_in[5120:7168]), fm(b_in[7168:9216])], 1)
    wgz_full = np.concatenate([w_in[:, 2048:4096], w_in[:, 5120:9216]], 1)
    wgz = np.stack([wfm(wgz_full[:, q * 512:(q + 1) * 512]) for q in range(12)], 0)
    woa_ = np.asarray(w_out_a, f32)[0]
    wob_ = np.asarray(w_out_b, f32)[0]
    wo_ = np.asarray(w_out, f32)[0]
    wup_ = np.asarray(w_up, f32)[0]
    wdn_ = np.asarray(w_down, f32)[0]
    woa = np.stack([wfm(woa_[:, q * 512:(q + 1) * 512]) for q in range(4)], 0)
    wob = np.stack([wfm(wob_[:, q * 512:(q + 1) * 512]) for q in range(4)], 0)
    wo = np.stack([wfm(wo_[:, q * 512:(q + 1) * 512]) for q in range(4)], 0)
    wupc = []
    for q in range(24):
        cols = np.concatenate([wup_[:, 256 * q:256 * q + 256], wup_[:, 6144 + 256 * q:6144 + 256 * q + 256]], 1)
        wupc.append(wfm(cols))
    wupc = np.stack(wupc, 0)
    wdn = np.stack([wfm(wdn_[:, m * 128:(m + 1) * 128]) for m in range(16)], 0)
    cc = np.arange(256)
    ang = 2 * np.pi * np.outer(cc, cc) / 256.0
    dm = np.stack([np.cos(ang), np.sin(ang)], 0) / 16.0
    dft = np.ascontiguousarray(dm.reshape(2, 2, 128, 256).transpose(2, 1, 0, 3).astype(f32)).reshape(128, 1024)
    cfw = np.asarray(conv_f_w, f32)[0]
    c3 = np.zeros((128, 640), f32)
    c3[:, 0:16] = fm(np.asarray(g_mix, f32)[0])
    c3[:, 16:32] = fm(np.asarray(g_ffn, f32)[0])
    c3[:, 32:48] = fm(np.asarray(g_final, f32))
    c3[:, 48:96] = bgz
    c3[:, 96:112] = fm(np.asarray(b_out, f32)[0])
    c3[:, 112:128] = fm(np.asarray(b_down, f32)[0])
    c3[:, 128:224] = fm(np.asarray(b_up, f32)[0])
    c3[:, 224:512] = cfw.reshape(3, 96, 128).transpose(2, 1, 0).reshape(128, 288)
    c3[:, 512:608] = fm(np.asarray(conv_f_b, f32)[0])
    c3[:, 610] = 1.0
    xgs = [xfm(xs[g]) for g in range(2)]
    ident = np.concatenate([np.eye(128, dtype=f32), np.zeros((128, 128), f32)], 1)
    ins = []
    for c in range(8):
        g, j = c // 4, c % 4
        L = seqlen[g]
        chs = slice(512 * j, 512 * j + 512)
        fch = slice(4096 + 256 * j, 4096 + 256 * j + 256)
        heads = slice(4 * j, 4 * j + 4)
        wg = np.stack([np.stack([lwa[d, heads], lwx[d, heads]], 0) for d in range(2)], 0)
        wg = np.ascontiguousarray(wg.transpose(3, 0, 1, 2, 4)).reshape(128, 2048)
        c2 = np.zeros((128, 64), f32)
        c2[:, 0:16] = caw[:, chs].reshape(4, 4, 128).transpose(2, 1, 0).reshape(128, 16)
        c2[:, 16:20] = fm(cab[chs])
        c2[:, 20:28] = np.concatenate([fm(lba[0, chs]), fm(lba[1, chs])], 1)
        c2[:, 28:36] = np.concatenate([fm(lbx[0, chs]), fm(lbx[1, chs])], 1)
        c2[:, 36:44] = np.concatenate([fm(lam[0, chs]), fm(lam[1, chs])], 1)
        c2[:, 44] = 1.0 if L == TS else 0.0
        c2[:, 48:52] = fm(b_in[chs])
        tb = np.concatenate([dft_tables(L), ident[:, None, :]], 1)
        lo, hi = j * TOK, (j + 1) * TOK
        cm = c3.copy()
        cm[:, 608] = 1.0 if (lo % L) != 0 else 0.0
        cm[:, 609] = 1.0 if (hi % L) != 0 else 0.0
        cm[:, 611] = 1.0 if L == TS else 0.0
        cm[:, 612] = 0.0 if L == TS else 1.0
        ins.append({"xg": xgs[g], "xo": np.ascontiguousarray(xgs[g][:, :, lo:hi]), "wx": wfm(w_in[:, chs]), "wf": wfm(w_in[:, fch]),
                    "bfT": np.ascontiguousarray(np.tile(b_in[fch][None, :], (128, 1))),
                    "wg": wg, "c2": c2, "tb": np.ascontiguousarray(tb), "wgz": wgz, "woa": woa, "wob": wob, "wo": wo,
                    "wup": wupc, "wdn": wdn, "dft": dft, "c3": cm})
    if _STOP is not None:
        return run(_PROGS["f"], ins, names={"xg", "xo", "wx", "wf", "bfT", "wg", "c2", "tb", "dft", "c3"})
    r = run(_PROGS["f"], ins)
    ys = []
    for g in range(2):
        yt = np.concatenate([r[g * 4 + j]["YT"] for j in range(4)], 2)
        ys.append(np.ascontiguousarray(yt.reshape(D, TS).T))
    return (ys[0].reshape(1, 16384, D).astype(f32), ys[1].reshape(2, 8192, D).astype(f32))
```

```python
import contextlib
import numpy as np
import ml_dtypes
import concourse.bass as bass
import concourse.mybir as mybir
from concourse.bass_utils import run_bass_kernel_spmd

F32 = mybir.dt.float32
BF16 = mybir.dt.bfloat16
AF = mybir.ActivationFunctionType
ALU = mybir.AluOpType
NPBF = ml_dtypes.bfloat16

D = 2048
KC = 16
TOK = 4096
TS = 16384
NT = 512
EPS = 1e-6


class Sched:
    def __init__(self, nc):
        self.nc = nc
        self.ops = []
        self.state = {}
        self.bar_from = 0

    def barrier(self):
        last = {}
        dmas = []
        for i, o in enumerate(self.ops[self.bar_from:], self.bar_from):
            if o["dma"]:
                dmas.append(i)
            elif o["fn"] is not None:
                last[o["eng"]] = i
        deps = sorted(set(dmas) | set(last.values()))
        for eng in ("pe", "act", "dve", "pool", "sp"):
            self.ops.append(dict(eng=eng, fn=None, deps=list(deps), dma=False, semkey=None, signal=False, ndma=1, inc=16,
                                 bar=True))
        self.bar_from = len(self.ops)
        self.state = {}

    def op(self, eng, fn, reads=(), writes=(), dma=False, semkey=None, ndma=1, inc=16):
        oid = len(self.ops)
        deps = set()
        for k in reads:
            st = self.state.setdefault(k, [None, []])
            if st[0] is not None:
                deps.add(st[0])
        for k in writes:
            st = self.state.setdefault(k, [None, []])
            if st[0] is not None:
                deps.add(st[0])
            last = {}
            for r in st[1]:
                o = self.ops[r]
                if o["dma"]:
                    deps.add(r)
                else:
                    last[o["eng"]] = max(last.get(o["eng"], -1), r)
            deps.update(last.values())
        for k in reads:
            self.state[k][1].append(oid)
        for k in writes:
            self.state[k] = [oid, []]
        deps.discard(oid)
        if dma and semkey is None:
            semkey = writes[0] if len(writes) else reads[0]
        self.ops.append(dict(eng=eng, fn=fn, deps=sorted(deps), dma=dma, semkey=semkey,
                             signal=dma, ndma=ndma, inc=inc))
        return oid

    def emit(self, final_keys=()):
        nc = self.nc
        ops = self.ops
        self.op("sp", None, reads=list(final_keys))
        for o in ops:
            for d in o["deps"]:
                od = ops[d]
                if od["dma"]:
                    continue
                if od["eng"] == "pe" and o["eng"] == "pe" and not o["dma"]:
                    continue
                od["signal"] = True
        cnt = {}
        for o in ops:
            if o["dma"]:
                k = ("d", o["semkey"])
                cnt[k] = cnt.get(k, 0) + o["inc"] * o["ndma"]
                o["done"] = (k, cnt[k])
            elif o["signal"]:
                k = ("e", o["eng"])
                cnt[k] = cnt.get(k, 0) + 1
                o["done"] = (k, cnt[k])
        semkeys = sorted(cnt.keys(), key=str)
        with contextlib.ExitStack() as es:
            sems = {}
            for i, k in enumerate(semkeys):
                sems[k] = es.enter_context(nc.semaphore("s%d" % i))
            block = es.enter_context(nc.Block())

            def run(engname):
                def body(e):
                    known = {}
                    for o in ops:
                        if o["eng"] != engname:
                            continue
                        for d in o["deps"]:
                            od = ops[d]
                            if "done" not in od:
                                continue
                            if (not od["dma"]) and od["eng"] == "pe" and engname == "pe" and not o["dma"]:
                                continue
                            k, v = od["done"]
                            if known.get(k, 0) < v:
                                e.wait_ge(sems[k], v)
                                known[k] = v
                        if o["fn"] is None:
                            continue
                        ins = o["fn"](e)
                        if o["dma"]:
                            if not isinstance(ins, (list, tuple)):
                                ins = [ins]
                            assert len(ins) == o["ndma"], (len(ins), o["ndma"])
                            for i_ in ins:
                                i_.then_inc(sems[o["done"][0]], o["inc"])
                        elif o["signal"]:
                            ins.then_inc(sems[o["done"][0]], 1)
                return body

            block.tensor(run("pe"))
            block.scalar(run("act"))
            block.vector(run("dve"))
            block.gpsimd(run("pool"))
            block.sync(run("sp"))


class Prog:
    def __init__(self):
        self.nc = bass.Bass("TRN2", target_bir_lowering=False)
        self.es = contextlib.ExitStack()
        self.S = Sched(self.nc)
        self.outs = []
        self._rr = 0

    def din(self, name, shape, dt=F32):
        return self.nc.dram_tensor(name, list(shape), dt, kind="ExternalInput").ap()

    def dout(self, name, shape, dt=F32):
        self.outs.append(name)
        return self.nc.dram_tensor(name, list(shape), dt, kind="ExternalOutput").ap()

    def dscr(self, name, shape, dt=F32):
        return self.nc.dram_tensor(name, list(shape), dt).ap()

    def sb(self, name, shape, dt=F32):
        return self.es.enter_context(self.nc.sbuf_tensor(name, list(shape), dt))

    def ps(self, name, shape, dt=F32):
        return self.es.enter_context(self.nc.psum_tensor(name, list(shape), dt))

    def op(self, *a, **k):
        return self.S.op(*a, **k)

    def ew(self):
        self._rr ^= 1
        return "dve" if self._rr else "pool"

    def load(self, dst_ap, src_ap, dkey, skey="in", eng="sp"):
        self.op(eng, lambda e: e.dma_start(out=dst_ap, in_=src_ap), reads=[skey], writes=[dkey], dma=True)

    def finish(self, final_keys):
        self.S.emit(final_keys=final_keys)
        self.es.close()


def rmsnorm_fm(P, tag, xT, xkey, N, gcol, outT, okey, ones32, epsT, ss_ps, sq, rt, rstd, scratch=None, skey=None):
    if scratch is None:
        scratch, skey = outT, okey
    P.op("act", lambda e: e.activation(out=scratch[:, :, :N], in_=xT[:, :, :N], func=AF.Square), reads=[xkey], writes=[skey])
    for kc in range(KC):
        P.op("pe", (lambda kc: lambda e: e.matmul(ss_ps[:, :N], lhsT=ones32[:, :], rhs=scratch[:, kc, :N],
                                                  start=(kc == 0), stop=(kc == KC - 1)))(kc),
             reads=[skey, "ones32"], writes=["ss_ps"])
    P.op("act", lambda e: e.activation(out=rt[:, :N], in_=ss_ps[:, :N], func=AF.Sqrt, bias=epsT[:, 0:1], scale=1.0 / D),
         reads=["ss_ps", "epsT"], writes=["rt"])
    P.op("dve", lambda e: e.reciprocal(out=rstd[:, :N], in_=rt[:, :N]), reads=["rt"], writes=["rstd"])
    for kc in range(KC):
        P.op("dve", (lambda kc: lambda e: e.scalar_tensor_tensor(out=outT[:, kc, :N], in0=xT[:, kc, :N],
                                                                scalar=gcol[:, kc:kc + 1], in1=rstd[:, :N],
                                                                op0=ALU.mult, op1=ALU.mult))(kc),
             reads=[xkey, "rstd", "consts"] + ([skey] if skey != okey else []), writes=[okey])


def norm_bufs(P):
    ones32 = P.sb("ones32", [128, 128])
    epsT = P.sb("epsT", [128, 1])
    P.op("dve", lambda e: e.memset(ones32[:], 1.0), writes=["ones32"])
    P.op("dve", lambda e: e.memset(epsT[:], EPS), writes=["epsT"])
    ss_ps = P.ps("ss_ps", [128, NT])
    sq = [P.sb("sq%d" % i, [128, NT]) for i in range(2)]
    rt = P.sb("rt", [128, NT])
    rstd = P.sb("rstd", [128, NT])
    return dict(ones32=ones32, epsT=epsT, ss_ps=ss_ps, sq=sq, rt=rt, rstd=rstd)


class Arena:
    def __init__(self, P, nbytes):
        self.t = P.sb("arena", [128, nbytes // 4])
        self.nbytes = nbytes
        self.off = 0

    def reset(self, off=0):
        self.off = off

    def _take(self, nb):
        nb = (nb + 63) // 64 * 64
        o = self.off
        self.off += nb
        assert self.off <= self.nbytes, (self.off, self.nbytes)
        return self.t[:, o // 4:(o + nb) // 4]

    @staticmethod
    def _shape(ap, shape):
        if len(shape) == 2:
            return ap
        if len(shape) == 3:
            return ap.rearrange("p (a b) -> p a b", a=shape[1])
        if len(shape) == 4:
            return ap.rearrange("p (a b c) -> p a b c", a=shape[1], b=shape[2])
        raise ValueError(shape)

    def f32(self, shape):
        n = int(np.prod(shape[1:]))
        return self._shape(self._take(4 * n)[:, :n], shape)

    def bf16(self, shape):
        n = int(np.prod(shape[1:]))
        return self._shape(self._take(2 * n).bitcast(BF16)[:, :n], shape)


RG = [[0, 1, 2, 3], [4, 5, 6, 7]]
G_MIX, G_FFN, G_FIN, B_GZ, B_OUT, B_DN, B_UP, CFW, CFB, MK = 0, 16, 32, 48, 96, 112, 128, 224, 512, 608


def build_fused(stop=None):
    P = Prog()
    nc = P.nc
    NTL = TOK // NT
    NTG = TS // NT
    HALF = TS // 2
    xg = P.din("xg", [128, KC, TS])
    xo = P.din("xo", [128, KC, TOK])
    wx = P.din("wx", [128, KC, 512])
    wf = P.din("wf", [128, KC, 256])
    bfT = P.din("bfT", [128, 256])
    wg = P.din("wg", [128, 2048])
    c2 = P.din("c2", [128, 64])
    tb = P.din("tb", [128, 5, 256])
    if stop is None:
        wgz = P.din("wgz", [12, 128, KC, 512])
        woa = P.din("woa", [4, 128, KC, 512])
        wob = P.din("wob", [4, 128, 8, 512])
        wo = P.din("wo", [4, 128, KC, 512])
        wup = P.din("wup", [24, 128, KC, 512])
        wdn = P.din("wdn", [16, 128, 48, 128])
        YT = P.dout("YT", [KC, 128, TOK])
        wsrc = dict(wgz=(wgz, 12, KC * 512), woa=(woa, 4, KC * 512), wob=(wob, 4, 8 * 512), wo=(wo, 4, KC * 512),
                    wup=(wup, 24, KC * 512), wdn=(wdn, 16, 48 * 128))
        wbf = {}
        for nm, (src, nchunk, ncol) in wsrc.items():
            wbf[nm] = P.dscr(nm + "_bf", [nchunk, 128, ncol], BF16)
    else:
        DBG = P.dout("DBG", [2048, NT], BF16)
        DBF = P.dout("DBF", [128, NT])
    dft = P.din("dft", [128, 1024])
    c3 = P.din("c3", [128, 640])
    Uscr = P.dscr("Uscr", [4, 128, TS + 3])
    HF = P.dscr("HF", [4, 128, TS])
    HS = P.dscr("HS", [KC, 128, TOK])
    N2S = P.dscr("N2S", [KC, 128, TOK + 2], BF16)
    HRloc = P.dscr("HRloc", [8 * 2048, NT], BF16)
    Qloc = P.dscr("Qloc", [2 * 8 * 2048, NT], BF16)
    HRin_h = nc.dram_tensor("HRin", [16 * 512, 2 * NT], BF16)
    HRall_h = nc.dram_tensor("HRall", [16 * 2048, 2 * NT], BF16)
    Qin_h = nc.dram_tensor("Qin", [2 * 8 * 256, 4 * NT], BF16)
    Qall_h = nc.dram_tensor("Qall", [2 * 8 * 1024, 4 * NT], BF16)
    NBin_h = nc.dram_tensor("NBin", [2048, 2], BF16)
    NBall_h = nc.dram_tensor("NBall", [8192, 2], BF16)
    HRin, HRall, Qin, Qall, NBin, NBall = (h.ap() for h in (HRin_h, HRall_h, Qin_h, Qall_h, NBin_h, NBall_h))

    cs2 = P.sb("cs2", [128, 64])
    cs = P.sb("cs3", [128, 640])
    P.load(cs2[:], c2[:, :], "consts")
    P.load(cs[:], c3[:, :], "consts")
    onesb = P.sb("onesb", [128, 128], BF16)
    epsT = P.sb("epsT", [128, 1])
    P.op("dve", lambda e: e.memset(onesb[:], 1.0), writes=["ones32"])
    P.op("dve", lambda e: e.memset(epsT[:], EPS), writes=["epsT"])
    PS = [P.ps("PS%d" % i, [128, NT]) for i in range(8)]
    sq = [P.sb("sq%d" % i, [128, NT], BF16) for i in range(2)]
    rt = P.sb("rt", [128, NT])
    rstd = P.sb("rstd", [128, NT])
    nb = dict(ones32=onesb, epsT=epsT, ss_ps=PS[0], sq=sq, rt=rt, rstd=rstd)
    wgb_ = P.sb("wgb", [128, 2048], BF16)
    P.op("pool", lambda e: e.dma_start(out=wgb_[:], in_=wg[:, :]), reads=["in"], writes=["wgb"], dma=True)
    wgb = wgb_[:].rearrange("p (d a m j) -> p d a m j", d=2, a=2, m=4)
    tbb = P.sb("tbb", [128, 5, 256], BF16)
    P.op("pool", lambda e: e.dma_start(out=tbb[:], in_=tb[:, :, :]), reads=["in"], writes=["tbb"], dma=True)
    tw = P.sb("tw", [128, 256])
    P.load(tw[:], tb[:, 3, :], "tw")
    dftb_ = P.sb("dftb", [128, 1024], BF16)
    P.op("pool", lambda e: e.dma_start(out=dftb_[:], in_=dft[:, :]), reads=["in"], writes=["dftb"], dma=True)
    dftb = dftb_[:].rearrange("p (a b c) -> p a b c", a=2, b=2)
    bft = P.sb("bft", [128, 256])
    P.load(bft[:], bfT[:, :], "bft")
    kt = P.sb("kt", [128, 8]); kl = P.sb("kl", [128, 8]); kl2 = P.sb("kl2", [128, 8])
    st = P.sb("st", [128, 8])
    zt = P.sb("zt", [128, 4])
    AR = Arena(P, 186 * 1024)

    pcnt = [0]

    def ppb():
        b = pcnt[0] % 4
        pcnt[0] += 1
        return b

    pid_cache = {}

    def ids(e):
        k = id(e)
        if k not in pid_cache:
            pid = e.partition_id()
            pid_cache[k] = (pid % 4, pid // 4)
        return pid_cache[k]

    Xq = AR.bf16([128, 2, 128, 128])
    XQ_END = AR.off
    xs = [AR.f32([128, KC, NT]) for _ in range(2)]
    nT = AR.bf16([128, KC, NT])
    wxb = AR.bf16([128, KC, 512])
    wfb = AR.bf16([128, KC, 256])
    ob = [AR.f32([128, NT]) for _ in range(4)]
    P.op("pool", lambda e: e.dma_start(out=wxb, in_=wx[:, :, :]), reads=["in"], writes=["wxb"], dma=True)
    P.op("pool", lambda e: e.dma_start(out=wfb, in_=wf[:, :, :]), reads=["in"], writes=["wfb"], dma=True)
    P.op("dve", lambda e: e.memset(zt[:], 0.0), writes=["zt"])
    P.op("sp", lambda e: [e.dma_start(out=Uscr[mx, :, 0:2], in_=zt[:, 0:2], allow_slow_non_contiguous=True) for mx in range(4)] +
                         [e.dma_start(out=Uscr[mx, :, TS + 2:TS + 3], in_=zt[:, 0:1], allow_slow_non_contiguous=True) for mx in range(4)],
         reads=["zt"], writes=["Upad"], dma=True, ndma=8, semkey="zt")
    for t in range(NTG):
        x = xs[t % 2]
        xk = ("x", t % 2)
        P.load(x, xg[:, :, t * NT:(t + 1) * NT], xk)
        rmsnorm_fm(P, "a", x, xk, NT, cs[:, G_MIX:G_MIX + 16], nT, "nT", **nb)
        for mx in range(4):
            b = ppb()
            for kc in range(KC):
                P.op("pe", (lambda mx, kc, b: lambda e: e.matmul(PS[1 + b][:], lhsT=wxb[:, kc, mx * 128:(mx + 1) * 128], rhs=nT[:, kc, :],
                                                               start=(kc == 0), stop=(kc == KC - 1)))(mx, kc, b),
                     reads=["nT", "wxb"], writes=[("pp", b)])
            P.op("act", (lambda mx, b: lambda e: e.activation(out=ob[b], in_=PS[1 + b][:], func=AF.Identity,
                                                             bias=cs2[:, 48 + mx:49 + mx], scale=1.0))(mx, b),
                 reads=[("pp", b), "consts"], writes=[("ob", b)])
            P.op("sp", (lambda mx, b, t: lambda e: e.dma_start(out=Uscr[mx, :, 2 + t * NT:2 + (t + 1) * NT], in_=ob[b]))(mx, b, t),
                 reads=[("ob", b)], writes=[("U", mx, t)], dma=True, semkey=("ob", b))
        for blk in range(4):
            b = ppb()
            for kc in range(KC):
                P.op("pe", (lambda blk, kc, b: lambda e: e.matmul(PS[1 + b][:, 0:256], lhsT=nT[:, kc, blk * 128:(blk + 1) * 128], rhs=wfb[:, kc, :],
                                                                start=(kc == 0), stop=(kc == KC - 1)))(blk, kc, b),
                     reads=["nT", "wfb"], writes=[("pp", b)])
            m = 4 * t + blk
            P.op("dve", (lambda m, b: lambda e: e.tensor_tensor(out=Xq[:, :, m, :], in0=PS[1 + b][:, 0:256].rearrange("p (h c) -> p h c", h=2),
                                                                in1=bft[:, :].rearrange("p (h c) -> p h c", h=2), op=ALU.add))(m, b),
                 reads=[("pp", b), "bft"], writes=["Xq"])
    P.S.barrier()
    if stop == "A":
        P.op("sp", lambda e: [e.dma_start(out=DBF[:, :], in_=Uscr[1, :, 2 + 512:2 + 1024]),
                              e.dma_start(out=DBG[0:128, :], in_=Xq[:, 0, 5, :].rearrange("p c -> p c") if False else Xq[:, :, 5, :].rearrange("p h c -> p (h c)")[:, 0:256] if False else Xq[:, 0, 0:4, :].rearrange("p m c -> p (m c)"))],
             reads=[], writes=["dbg"], dma=True, ndma=2, semkey="dbg")
        P.finish(["dbg"])
        return P

    AR.reset(XQ_END)
    conv_list = []
    if stop is None:
        shp = "c p k m -> c p (k m)"
        for nm, (src, nchunk, ncol) in wsrc.items():
            for ci in range(nchunk):
                conv_list.append((nm, src, ci))

    def issue_conv(n):
        for _ in range(n):
            if conv_list:
                nm, src, ci = conv_list.pop(0)
                P.op("pool", (lambda nm, src, ci: lambda e: e.dma_start(out=wbf[nm][ci, :, :], in_=src.rearrange(shp)[ci, :, :]))(nm, src, ci),
                     reads=["in"], writes=[("wbf", nm, ci)], dma=True, semkey="wconv")
    P.op("act", lambda e: e.activation(out=kt[:], in_=cs2[:, 36:44], func=AF.Exp, scale=-1.0), reads=[], writes=["kt"])
    P.op("dve", lambda e: e.tensor_scalar_add(out=kt[:], in0=kt[:], scalar1=1.0), reads=["kt"], writes=["kt"])
    P.op("act", lambda e: e.activation(out=kt[:], in_=kt[:], func=AF.Ln), reads=["kt"], writes=["kt"])
    P.op("dve", lambda e: e.tensor_scalar_mul(out=kl[:], in0=kt[:], scalar1=-8.0), reads=["kt"], writes=["kl"])
    P.op("dve", lambda e: e.tensor_scalar_mul(out=kl2[:], in0=kt[:], scalar1=-16.0), reads=["kt"], writes=["kl2"])
    P.op("dve", lambda e: e.memset(st[:], 0.0), writes=[("st", c_) for c_ in range(8)])
    NB = 4
    ut = [[AR.f32([128, NT + 3]) for _ in range(2)] for _ in range(NB)]
    u = [[AR.f32([128, NT]) for _ in range(2)] for _ in range(NB)]
    ub = [[AR.bf16([128, NT]) for _ in range(2)] for _ in range(NB)]
    r_ = [AR.f32([128, NT]) for _ in range(NB)]
    i_ = [AR.f32([128, NT]) for _ in range(NB)]
    a_ = [AR.f32([128, NT]) for _ in range(NB)]
    m_ = [AR.f32([128, NT]) for _ in range(NB)]
    h_ = [AR.f32([128, NT]) for _ in range(NB)]
    hf = [AR.f32([128, NT]) for _ in range(NB)]
    hb = [AR.bf16([128, NT]) for _ in range(NB)]
    pg = [PS[1], PS[2], PS[3], PS[4]]
    onesT = P.sb("onesT", [128, 1])
    P.op("dve", lambda e: e.memset(onesT[:], 1.0), writes=["onesT"])
    hrkeys = []

    def stage1(d, t, s_):
        issue_conv(2)
        for mx in range(4):
            b = mx
            utb, uu, ubb = ut[b][s_], u[b][s_], ub[b][s_]
            kut, ku, kub = ("ut", b, s_), ("u", b, s_), ("ub", b, s_)
            P.load(utb, Uscr[mx, :, t * NT:t * NT + NT + 3], kut)
            if t == NTG // 2 - 1:
                P.op("dve", (lambda utb: lambda e: e.tensor_scalar_mul(out=utb[:, NT + 2:NT + 3], in0=utb[:, NT + 2:NT + 3], scalar1=cs2[:, 44:45]))(utb),
                     reads=[kut], writes=[kut])
            if t == NTG // 2:
                P.op("dve", (lambda utb: lambda e: e.tensor_scalar_mul(out=utb[:, 0:2], in0=utb[:, 0:2], scalar1=cs2[:, 44:45]))(utb),
                     reads=[kut], writes=[kut])
            P.op("dve", (lambda utb, uu, mx: lambda e: e.tensor_scalar(out=uu, in0=utb[:, 0:NT], scalar1=cs2[:, mx * 4:mx * 4 + 1],
                                                                       scalar2=cs2[:, 16 + mx:17 + mx], op0=ALU.mult, op1=ALU.add))(utb, uu, mx),
                 reads=[kut], writes=[ku])
            for k in range(1, 4):
                P.op("dve", (lambda utb, uu, mx, k: lambda e: e.scalar_tensor_tensor(out=uu, in0=utb[:, k:k + NT],
                                                                                     scalar=cs2[:, mx * 4 + k:mx * 4 + k + 1], in1=uu,
                                                                                     op0=ALU.mult, op1=ALU.add))(utb, uu, mx, k),
                     reads=[kut, ku], writes=[ku])
            P.op("act", (lambda uu, ubb: lambda e: e.copy(out=ubb, in_=uu))(uu, ubb), reads=[ku], writes=[kub])

    def stage23(d, t, s_):
        for mx in range(4):
            b = mx
            P.op("pe", (lambda b, d, mx, ubb: lambda e: e.matmul(pg[b][:], lhsT=wgb[:, d, 0, mx, :], rhs=ubb, start=True, stop=True))(b, d, mx, ub[b][s_]),
                 reads=[("ub", b, s_), "wgb"], writes=[("pp", b)])
        for mx in range(4):
            b = mx
            col = d * 4 + mx
            P.op("act", (lambda b, col: lambda e: e.activation(out=r_[b], in_=pg[b][:], func=AF.Sigmoid, bias=cs2[:, 20 + col:21 + col], scale=1.0))(b, col),
                 reads=[("pp", b)], writes=[("r", b)])
            P.op("pe", (lambda b, d, mx, ubb: lambda e: e.matmul(pg[b][:], lhsT=wgb[:, d, 1, mx, :], rhs=ubb, start=True, stop=True))(b, d, mx, ub[b][s_]),
                 reads=[("ub", b, s_), "wgb"], writes=[("pp", b)])
        for mx in range(4):
            b = mx
            col = d * 4 + mx
            P.op("act", (lambda b, col: lambda e: e.activation(out=i_[b], in_=pg[b][:], func=AF.Sigmoid, bias=cs2[:, 28 + col:29 + col], scale=1.0))(b, col),
                 reads=[("pp", b)], writes=[("i", b)])
        for mx in range(4):
            b = mx
            col = d * 4 + mx
            P.op("act", (lambda b, col: lambda e: e.activation(out=a_[b], in_=r_[b], func=AF.Exp, scale=kl[:, col:col + 1]))(b, col),
                 reads=[("r", b), "kl"], writes=[("a", b)])
            P.op("act", (lambda b, col: lambda e: e.activation(out=m_[b], in_=r_[b], func=AF.Exp, scale=kl2[:, col:col + 1]))(b, col),
                 reads=[("r", b), "kl2"], writes=[("m", b)])
        for mx in range(4):
            b = mx
            P.op("act", (lambda b: lambda e: e.activation(out=m_[b], in_=m_[b], func=AF.Sqrt, bias=onesT[:, 0:1], scale=-1.0))(b),
                 reads=[("m", b), "onesT"], writes=[("m", b)])

    def stage45(d, t, s_):
        for mx in range(4):
            b = mx
            uu = u[b][s_]
            P.op("pool", (lambda b, uu: lambda e: e.tensor_tensor(out=i_[b], in0=i_[b], in1=uu, op=ALU.mult))(b, uu),
                 reads=[("i", b), ("u", b, s_)], writes=[("i", b)])
            P.op("pool", (lambda b: lambda e: e.tensor_tensor(out=i_[b], in0=i_[b], in1=m_[b], op=ALU.mult))(b),
                 reads=[("i", b), ("m", b)], writes=[("i", b)])
        for mx in range(4):
            b = mx
            col = d * 4 + mx
            if d == 0:
                P.op("dve", (lambda b, col: lambda e: e.tensor_tensor_scan(out=h_[b], data0=a_[b], data1=i_[b], initial=st[:, col:col + 1],
                                                                           op0=ALU.mult, op1=ALU.add))(b, col),
                     reads=[("a", b), ("i", b), ("st", col)], writes=[("h", b)])
                P.op("dve", (lambda b, col: lambda e: e.tensor_copy(out=st[:, col:col + 1], in_=h_[b][:, NT - 1:NT]))(b, col),
                     reads=[("h", b)], writes=[("st", col)])
                P.op("sp", (lambda b, mx, t: lambda e: e.dma_start(out=HF[mx, :, t * NT:(t + 1) * NT], in_=h_[b]))(b, mx, t),
                     reads=[("h", b)], writes=[("HF", mx, t)], dma=True, semkey=("h", b))
            else:
                P.load(hf[b], HF[mx, :, t * NT:(t + 1) * NT], ("hf", b), skey=("HF", mx, t))
                P.op("dve", (lambda b, col: lambda e: e.tensor_tensor_scan(out=h_[b][:, ::-1], data0=a_[b][:, ::-1], data1=i_[b][:, ::-1],
                                                                           initial=st[:, col:col + 1], op0=ALU.mult, op1=ALU.add))(b, col),
                     reads=[("a", b), ("i", b), ("st", col)], writes=[("h", b)])
                P.op("dve", (lambda b, col: lambda e: e.tensor_copy(out=st[:, col:col + 1], in_=h_[b][:, 0:1]))(b, col),
                     reads=[("h", b)], writes=[("st", col)])
                P.op("pool", (lambda b: lambda e: e.tensor_tensor(out=hb[b], in0=h_[b], in1=hf[b], op=ALU.add))(b),
                     reads=[("h", b), ("hf", b)], writes=[("hb", b)])
                k = ("HR", mx, t)
                P.op("sp", (lambda b, mx, t: lambda e: e.dma_start(out=HRin[(t // 2) * 512 + mx * 128:(t // 2) * 512 + (mx + 1) * 128, (t % 2) * NT:(t % 2 + 1) * NT], in_=hb[b]))(b, mx, t),
                     reads=[("hb", b)], writes=[k], dma=True, semkey=("hb", b))
                if mx == 3 and t % 2 == 0:
                    tp = t // 2
                    P.op("pool", (lambda tp: lambda e: e.collective_compute("AllGather", ALU.bypass, replica_groups=RG,
                                                                           ins=[HRin[tp * 512:(tp + 1) * 512, :].opt()],
                                                                           outs=[HRall[tp * 2048:(tp + 1) * 2048, :].opt()]))(tp),
                         reads=[("HR", m2, t_) for m2 in range(4) for t_ in (t, t + 1)], writes=[("HRall", tp)], dma=True, inc=1, semkey="agHR")
                    hrkeys.append(("HRall", tp))

    for d in range(2):
        order = list(range(NTG)) if d == 0 else list(range(NTG - 1, -1, -1))
        stage1(d, order[0], 0)
        for ti, t in enumerate(order):
            if ti == NTG // 2:
                P.op("dve", (lambda d: lambda e: e.tensor_scalar_mul(out=st[:, d * 4:d * 4 + 4], in0=st[:, d * 4:d * 4 + 4],
                                                                      scalar1=cs2[:, 44:45]))(d),
                     reads=[("st", d * 4 + c_) for c_ in range(4)], writes=[("st", d * 4 + c_) for c_ in range(4)])
            stage23(d, t, ti % 2)
            if ti + 1 < NTG:
                stage1(d, order[ti + 1], (ti + 1) % 2)
            stage45(d, t, ti % 2)
    P.S.barrier()
    if stop == "B":
        P.op("sp", lambda e: [e.dma_start(out=DBG[:, :], in_=HRall[3 * 2048:4 * 2048, 0:NT]), e.dma_start(out=DBF[:, :], in_=HF[2, :, 512:1024])],
             reads=[], writes=["dbg"], dma=True, ndma=2, semkey="dbg")
        P.finish(["dbg"])
        return P

    AR.reset(XQ_END)
    FB = AR.bf16([128, 128, 128])
    Bp = AR.bf16([128, 128, 2, 128])
    s1 = [AR.f32([128, 256]) for _ in range(2)]
    t1 = [AR.f32([128, 128]) for _ in range(2)]
    t2 = [AR.f32([128, 128]) for _ in range(2)]
    t3 = [AR.f32([128, 128]) for _ in range(2)]
    t4 = [AR.f32([128, 128]) for _ in range(2)]
    p1 = [PS[5], PS[6]]
    p2 = [PS[1], PS[2]]
    ptrs = [PS[7][:, :].bitcast(BF16)[:, 0:128], PS[0][:, :].bitcast(BF16)[:, 0:128]]
    ptrk = [("ps7", 0), "ss_ps"]
    Tr = tw[:, 0:128]
    Ti = tw[:, 128:256]
    ident = tbb[:, 4, 0:128]
    qkeys = []
    for ch in range(2):
        for c in range(128):
            b = c % 2
            P.op("pe", (lambda ch, c, b: lambda e: e.transpose(out=ptrs[b], in_=Xq[:, ch, :, c], identity=ident))(ch, c, b),
                 reads=[("SEL", ch), "tbb"], writes=[ptrk[b]])
            P.op("act" if b else "dve", (lambda c, b: lambda e: (e.copy if b else e.tensor_copy)(out=FB[:, :, c], in_=ptrs[b]))(c, b),
                 reads=[ptrk[b]], writes=["FB"])
        if stop == "C0":
            P.op("sp", lambda e: e.dma_start(out=DBG[0:128, :], in_=FB[:, 0:4, :].rearrange("p a b -> p (a b)")), reads=["FB"], writes=["dbg"], dma=True, semkey="dbg")
            P.op("sp", lambda e: e.dma_start(out=DBF[:, :], in_=HF[2, :, 512:1024]), reads=[], writes=["dbg2"], dma=True, semkey="dbg2")
            P.finish(["dbg", "dbg2"])
            return P
        for c in range(128):
            b = c % 2
            P.op("pe", (lambda c, b: lambda e: e.matmul(p1[b][:, 0:256], lhsT=FB[:, :, c], rhs=tbb[:, 0, :], start=True, stop=True))(c, b),
                 reads=["FB", "tbb"], writes=[("ps", 5 + b)])
            P.op("act", (lambda b: lambda e: e.copy(out=s1[b], in_=p1[b][:, 0:256]))(b), reads=[("ps", 5 + b)], writes=[("s1", b)])
            P.op("dve", (lambda b: lambda e: e.tensor_tensor(out=t1[b], in0=s1[b][:, 0:128], in1=Tr, op=ALU.mult))(b),
                 reads=[("s1", b), "tw"], writes=[("t1", b)])
            P.op("dve", (lambda b: lambda e: e.tensor_tensor(out=t2[b], in0=s1[b][:, 128:256], in1=Ti, op=ALU.mult))(b),
                 reads=[("s1", b), "tw"], writes=[("t2", b)])
            P.op("dve", (lambda b, c: lambda e: e.tensor_tensor(out=Bp[:, c, 0, :], in0=t1[b], in1=t2[b], op=ALU.subtract))(b, c),
                 reads=[("t1", b), ("t2", b)], writes=[("Bp", c)])
            P.op("pool", (lambda b: lambda e: e.tensor_tensor(out=t3[b], in0=s1[b][:, 0:128], in1=Ti, op=ALU.mult))(b),
                 reads=[("s1", b), "tw"], writes=[("t3", b)])
            P.op("pool", (lambda b: lambda e: e.tensor_tensor(out=t4[b], in0=s1[b][:, 128:256], in1=Tr, op=ALU.mult))(b),
                 reads=[("s1", b), "tw"], writes=[("t4", b)])
            P.op("pool", (lambda b, c: lambda e: e.tensor_tensor(out=Bp[:, c, 1, :], in0=t3[b], in1=t4[b], op=ALU.add))(b, c),
                 reads=[("t3", b), ("t4", b)], writes=[("Bp", c)])
        bpk = [("Bp", c) for c in range(128)]
        if stop == "C1":
            P.op("sp", lambda e: e.dma_start(out=DBG[0:128, :], in_=Bp[:, 0:2, :, :].rearrange("p a b c -> p (a b c)")), reads=bpk, writes=["dbg"], dma=True, semkey="dbg")
            P.op("sp", lambda e: e.dma_start(out=DBF[:, :], in_=HF[2, :, 512:1024]), reads=[], writes=["dbg2"], dma=True, semkey="dbg2")
            P.finish(["dbg", "dbg2"])
            return P
        for ri in range(2):
            for j in range(128):
                b = j % 2
                P.op("pe", (lambda j, b, ri: lambda e: e.matmul(p2[b][:, 0:128], lhsT=Bp[:, :, 0, j], rhs=tbb[:, 1, ri * 128:(ri + 1) * 128], start=True, stop=False))(j, b, ri),
                     reads=bpk + ["tbb"], writes=[("pp", b)])
                P.op("pe", (lambda j, b, ri: lambda e: e.matmul(p2[b][:, 0:128], lhsT=Bp[:, :, 1, j], rhs=tbb[:, 2, ri * 128:(ri + 1) * 128], start=False, stop=True))(j, b, ri),
                     reads=bpk + ["tbb"], writes=[("pp", b)])
                P.op("act" if b else "dve", (lambda j, b: lambda e: (e.copy if b else e.tensor_copy)(out=FB[:, :, j], in_=p2[b][:, 0:128]))(j, b),
                     reads=[("pp", b)], writes=["FB"])
            k = ("Q", ch, ri)
            SEL = Xq[:, ch, :, :].rearrange("p a b -> p (a b)")
            P.op("dve", (lambda SEL: lambda e: e.tensor_scalar_mul(out=SEL.rearrange("p (b k x) -> p b k x", b=2, x=64),
                                                                  in0=FB.rearrange("p k (b x) -> p b k x", b=2),
                                                                  scalar1=cs[:, MK + 4:MK + 5]))(SEL),
                 reads=["FB"], writes=[("SEL", ch)])
            P.op("dve", (lambda SEL: lambda e: e.scalar_tensor_tensor(out=SEL, in0=FB.rearrange("p a b -> p (a b)"), scalar=cs[:, MK + 3:MK + 4],
                                                                     in1=SEL, op0=ALU.mult, op1=ALU.add))(SEL),
                 reads=["FB", ("SEL", ch)], writes=[("SEL", ch)])
            P.op("sp", (lambda ch, ri, SEL: lambda e: e.dma_start(out=Qin[ch * 2048:(ch + 1) * 2048, :].rearrange("(s i c) n -> c s i n", i=2, c=128)[:, :, ri, :],
                                                                  in_=SEL.rearrange("p (s n) -> p s n", n=4 * NT)))(ch, ri, SEL),
                 reads=[("SEL", ch)], writes=[k], dma=True, semkey=("SEL", ch))
            qkeys.append(k)
        if stop != "C2":
            for sq_ in range(8):
                idx = ch * 8 + sq_
                P.op("pool", (lambda idx: lambda e: e.collective_compute("AllGather", ALU.bypass, replica_groups=RG,
                                                                         ins=[Qin[idx * 256:(idx + 1) * 256, :].opt()],
                                                                         outs=[Qall[idx * 1024:(idx + 1) * 1024, :].opt()]))(idx),
                     reads=[("Q", ch, 0), ("Q", ch, 1)], writes=[("Qall", idx)], dma=True, inc=1, semkey="agQ")
    if stop == "C2":
        P.op("sp", lambda e: e.dma_start(out=DBG[0:512, :], in_=Qin[512:1024, :]), reads=qkeys, writes=["dbg"], dma=True, semkey="dbg")
        P.op("sp", lambda e: e.dma_start(out=DBF[:, :], in_=HF[2, :, 512:1024]), reads=[], writes=["dbg2"], dma=True, semkey="dbg2")
        P.finish(["dbg", "dbg2"])
        return P
    P.S.barrier()
    if stop == "C":
        P.op("sp", lambda e: [e.dma_start(out=DBG[:, :], in_=Qall[2048:4096, 0:NT]), e.dma_start(out=DBF[:, :], in_=HF[2, :, 512:1024])],
             reads=[], writes=["dbg"], dma=True, ndma=2, semkey="dbg")
        P.finish(["dbg"])
        return P

    AR.reset(0)
    X32 = AR.f32([128, KC, NT])
    NTb = AR.bf16([128, KC, NT + 2])
    QTb = AR.bf16([128, 16, NT])
    MT = QTb
    FT = AR.bf16([128, 8, NT])
    BIG = AR.bf16([128, 48, NT])
    WB = [AR.bf16([128, KC * 512]) for _ in range(2)]
    e1 = [AR.f32([128, NT]) for _ in range(2)]
    e2 = [AR.f32([128, NT]) for _ in range(2)]
    upS = [AR.f32([128, NT + 2]) for _ in range(2)]
    upH = [AR.f32([128, 2]) for _ in range(2)]
    cv = [AR.f32([128, NT]) for _ in range(2)]
    gg = [AR.f32([128, NT]) for _ in range(2)]
    pp = [PS[1], PS[2], PS[3], PS[4]]
    ph = [PS[5], PS[6]]
    wcnt = [0]

    def wload(nm, ci, kcn, cols):
        i = wcnt[0] % 2
        wcnt[0] += 1
        view = WB[i][:, 0:kcn * cols].rearrange("p (k m) -> p k m", k=kcn)
        P.op("sp", lambda e: e.dma_start(out=WB[i][:, 0:kcn * cols], in_=wbf[nm][ci, :, :]), reads=["in"], writes=[("WB", i)], dma=True)
        return view, ("WB", i)

    def linear(wview, wkey, mloc, kcn, rhs_fn, rkeys, N):
        b = ppb()
        for kc in range(kcn):
            P.op("pe", (lambda kc, b: lambda e: e.matmul(pp[b][:, :N], lhsT=wview[:, kc, mloc * 128:(mloc + 1) * 128], rhs=rhs_fn(kc),
                                                       start=(kc == 0), stop=(kc == kcn - 1)))(kc, b),
                 reads=list(rkeys) + [wkey], writes=[("pp", b)])
        return b

    ecnt = [0]
    nbkeys = []

    def p3a(t):
        t0 = t * NT
        N = NT

        P.load(X32, xo[:, :, t0:t0 + NT], "X32")
        def ldh(e):
            j, g = ids(e)
            return e.dma_start(out=BIG[:, 0:16, :], in_=HRall[bass.ds(j * 8192 + (t // 2) * 2048, 2048), (t % 2) * NT:(t % 2 + 1) * NT].rearrange("(k p) n -> p k n", p=128))
        P.op("sp", ldh, reads=[], writes=["BIG0"], dma=True)

        def ldq(e):
            j, g = ids(e)
            return [e.dma_start(out=QTb[:, ch_ * 8:(ch_ + 1) * 8, :],
                                in_=Qall[bass.ds(j * 2048 + (ch_ * 8192 + (t // 4) * 1024), 1024), (t % 4) * NT:(t % 4 + 1) * NT].rearrange("(k c) n -> c k n", c=128))
                    for ch_ in range(2)]
        P.op("act", ldq, reads=[], writes=["QTb"], dma=True, ndma=2)
        rmsnorm_fm(P, "a", X32, "X32", N, cs[:, G_MIX:G_MIX + 16], NTb, "NTb", **nb)
        for part in range(3):
            for cq in range(4):
                wv, wk = wload("wgz", part * 4 + cq, KC, 512)
                for ml in range(4):
                    mt = cq * 4 + ml
                    b = linear(wv, wk, ml, KC, lambda kc: NTb[:, kc, :N], ["NTb"], N)
                    bias = cs[:, B_GZ + part * 16 + mt:B_GZ + part * 16 + mt + 1]
                    if part == 0:
                        i = ecnt[0] % 2
                        ecnt[0] += 1
                        P.op("act", (lambda b, i, bias: lambda e: e.activation(out=e1[i][:, :N], in_=pp[b][:, :N], func=AF.Gelu_apprx_tanh, bias=bias, scale=1.0))(b, i, bias),
                             reads=[("pp", b)], writes=[("e1", i)])
                        P.op("dve", (lambda mt, i: lambda e: e.tensor_tensor(out=BIG[:, mt, :N], in0=BIG[:, mt, :N], in1=e1[i][:, :N], op=ALU.mult))(mt, i),
                             reads=[("e1", i), "BIG0"], writes=["BIG0"])
                    else:
                        P.op("act", (lambda b, mt, bias, part: lambda e: e.activation(out=BIG[:, part * 16 + mt, :N], in_=pp[b][:, :N], func=AF.Sigmoid, bias=bias, scale=1.0))(b, mt, bias, part),
                             reads=[("pp", b)], writes=["BIG%d" % part])
        for g4 in range(4):
            for ct in range(2):
                b = ppb()
                n = 0
                for chh in range(2):
                    for ri in range(2):
                        P.op("pe", (lambda g4, ct, chh, ri, b, n: lambda e: e.matmul(pp[b][:, :N], lhsT=dftb[:, chh, ri, ct * 128:(ct + 1) * 128],
                                                                                     rhs=QTb[:, chh * 8 + g4 * 2 + ri, :N], start=(n == 0), stop=(n == 3)))(g4, ct, chh, ri, b, n),
                             reads=["QTb", "dftb"], writes=[("pp", b)])
                        n += 1
                P.op("act", (lambda g4, ct, b: lambda e: e.copy(out=FT[:, g4 * 2 + ct, :N], in_=pp[b][:, :N]))(g4, ct, b),
                     reads=[("pp", b)], writes=["FT"])
        for cq in range(4):
            wva, wka = wload("woa", cq, KC, 512)
            wvb, wkb = wload("wob", cq, 8, 512)
            for ml in range(4):
                mt = cq * 4 + ml
                ba = linear(wva, wka, ml, KC, lambda kc: BIG[:, kc, :N], ["BIG0"], N)
                bb = linear(wvb, wkb, ml, 8, lambda kc: FT[:, kc, :N], ["FT"], N)
                i = ecnt[0] % 2
                ecnt[0] += 1
                P.op("dve", (lambda ba, mt, i: lambda e: e.tensor_tensor(out=e1[i][:, :N], in0=pp[ba][:, :N], in1=BIG[:, 16 + mt, :N], op=ALU.mult))(ba, mt, i),
                     reads=[("pp", ba), "BIG1"], writes=[("e1", i)])
                P.op("dve", (lambda bb, mt, i: lambda e: e.tensor_tensor(out=e2[i][:, :N], in0=pp[bb][:, :N], in1=BIG[:, 32 + mt, :N], op=ALU.mult))(bb, mt, i),
                     reads=[("pp", bb), "BIG2"], writes=[("e2", i)])
                P.op("pool", (lambda mt, i: lambda e: e.tensor_tensor(out=MT[:, mt, :N], in0=e1[i][:, :N], in1=e2[i][:, :N], op=ALU.add))(mt, i),
                     reads=[("e1", i), ("e2", i)], writes=["QTb"])
        for cq in range(4):
            wv, wk = wload("wo", cq, KC, 512)
            for ml in range(4):
                mt = cq * 4 + ml
                b = linear(wv, wk, ml, KC, lambda kc: MT[:, kc, :N], ["QTb"], N)
                P.op("dve", (lambda b, mt: lambda e: e.scalar_tensor_tensor(out=X32[:, mt, :N], in0=pp[b][:, :N], scalar=cs[:, B_OUT + mt:B_OUT + mt + 1],
                                                                           in1=X32[:, mt, :N], op0=ALU.add, op1=ALU.add))(b, mt),
                     reads=[("pp", b), "X32"], writes=["X32"])
        P.op("sp", lambda e: e.dma_start(out=HS[:, :, t0:t0 + N].rearrange("k p n -> p k n"), in_=X32),
             reads=["X32"], writes=[("HS", t0)], dma=True, semkey="X32")
        rmsnorm_fm(P, "b", X32, "X32", N, cs[:, G_FFN:G_FFN + 16], NTb, "NTb", **nb)
        P.op("sp", lambda e: e.dma_start(out=N2S[:, :, 1 + t0:1 + t0 + N].rearrange("k p n -> p k n"), in_=NTb[:, :, :N]),
             reads=["NTb"], writes=[("N2S", t0)], dma=True, semkey="NTb")
        if t == 0:
            P.op("sp", lambda e: e.dma_start(out=NBin[:, 0:1].rearrange("(k p) n -> p k n", p=128), in_=NTb[:, :, 0:1], allow_slow_non_contiguous=True),
                 reads=["NTb"], writes=[("NB", 0)], dma=True, semkey="NTb")
            nbkeys.append(("NB", 0))
        if t == NTL - 1:
            P.op("sp", lambda e: e.dma_start(out=NBin[:, 1:2].rearrange("(k p) n -> p k n", p=128), in_=NTb[:, :, NT - 1:NT], allow_slow_non_contiguous=True),
                 reads=["NTb"], writes=[("NB", 1)], dma=True, semkey="NTb")
            nbkeys.append(("NB", 1))

    def p3b(t):
        t0 = t * NT
        rk = [("N2S", t0)]
        rk.append(("N2S", t0 - NT) if t > 0 else ("N2S", "halo"))
        rk.append(("N2S", t0 + NT) if t < NTL - 1 else ("N2S", "halo"))
        P.op("sp", lambda e: e.dma_start(out=NTb, in_=N2S[:, :, t0:t0 + NT + 2].rearrange("k p n -> p k n")),
             reads=rk, writes=["NTb"], dma=True)
        P.op("sp", lambda e: e.dma_start(out=X32, in_=HS[:, :, t0:t0 + NT].rearrange("k p n -> p k n")),
             reads=[("HS", t0)], writes=["X32"], dma=True)
        mL = cs[:, MK:MK + 1] if t == 0 else cs[:, MK + 2:MK + 3]
        mR = cs[:, MK + 1:MK + 2] if t == NTL - 1 else cs[:, MK + 2:MK + 3]
        for q in range(24):
            wv, wk = wload("wup", q, KC, 512)
            for ml in range(4):
                gv, jj = ml // 2, ml % 2
                jp = 2 * q + jj
                mt = jp + 48 * gv
                b = ppb()
                hbk = b % 2
                for kc in range(KC):
                    P.op("pe", (lambda kc, b, ml, wv: lambda e: e.matmul(pp[b][:], lhsT=wv[:, kc, ml * 128:(ml + 1) * 128], rhs=NTb[:, kc, 1:NT + 1],
                                                                       start=(kc == 0), stop=(kc == KC - 1)))(kc, b, ml, wv),
                         reads=["NTb", wk], writes=[("pp", b)])
                for kc in range(KC):
                    P.op("pe", (lambda kc, hbk, ml, wv: lambda e: e.matmul(ph[hbk][:, 0:2], lhsT=wv[:, kc, ml * 128:(ml + 1) * 128], rhs=NTb[:, kc, 0:NT + 2:NT + 1],
                                                                         start=(kc == 0), stop=(kc == KC - 1)))(kc, hbk, ml, wv),
                         reads=["NTb", wk], writes=[("ps", 5 + hbk)])
                i = ecnt[0] % 2
                ecnt[0] += 1
                bias = cs[:, B_UP + mt:B_UP + mt + 1]
                P.op("act", (lambda b, i, bias: lambda e: e.activation(out=upS[i][:, 1:NT + 1], in_=pp[b][:], func=AF.Identity, bias=bias, scale=1.0))(b, i, bias),
                     reads=[("pp", b)], writes=[("upS", i)])
                P.op("act", (lambda hbk, i, bias: lambda e: e.activation(out=upH[i], in_=ph[hbk][:, 0:2], func=AF.Identity, bias=bias, scale=1.0))(hbk, i, bias),
                     reads=[("ps", 5 + hbk)], writes=[("upH", i)])
                P.op("dve", (lambda i, mL: lambda e: e.tensor_scalar_mul(out=upS[i][:, 0:1], in0=upH[i][:, 0:1], scalar1=mL))(i, mL),
                     reads=[("upH", i)], writes=[("upS", i)])
                P.op("dve", (lambda i, mR: lambda e: e.tensor_scalar_mul(out=upS[i][:, NT + 1:NT + 2], in0=upH[i][:, 1:2], scalar1=mR))(i, mR),
                     reads=[("upH", i)], writes=[("upS", i)])
                w0 = cs[:, CFW + mt * 3:CFW + mt * 3 + 1]
                w1 = cs[:, CFW + mt * 3 + 1:CFW + mt * 3 + 2]
                w2 = cs[:, CFW + mt * 3 + 2:CFW + mt * 3 + 3]
                cb = cs[:, CFB + mt:CFB + mt + 1]
                P.op("dve", (lambda i, w0, cb: lambda e: e.tensor_scalar(out=cv[i], in0=upS[i][:, 0:NT], scalar1=w0, scalar2=cb, op0=ALU.mult, op1=ALU.add))(i, w0, cb),
                     reads=[("upS", i)], writes=[("cv", i)])
                P.op("dve", (lambda i, w1: lambda e: e.scalar_tensor_tensor(out=cv[i], in0=upS[i][:, 1:NT + 1], scalar=w1, in1=cv[i], op0=ALU.mult, op1=ALU.add))(i, w1),
                     reads=[("upS", i), ("cv", i)], writes=[("cv", i)])
                P.op("dve", (lambda i, w2: lambda e: e.scalar_tensor_tensor(out=cv[i], in0=upS[i][:, 2:NT + 2], scalar=w2, in1=cv[i], op0=ALU.mult, op1=ALU.add))(i, w2),
                     reads=[("upS", i), ("cv", i)], writes=[("cv", i)])
                if gv == 0:
                    P.op("act", (lambda i, jj: lambda e: e.activation(out=gg[jj], in_=cv[i], func=AF.Gelu_apprx_tanh))(i, jj),
                         reads=[("cv", i)], writes=[("gg", jj)])
                else:
                    P.op("pool", (lambda i, jj, jp: lambda e: e.tensor_tensor(out=BIG[:, jp, :], in0=gg[jj], in1=cv[i], op=ALU.mult))(i, jj, jp),
                         reads=[("gg", jj), ("cv", i)], writes=["BIG%d" % (jp // 16)])
        for mt in range(KC):
            wv, wk = wload("wdn", mt, 48, 128)
            b = linear(wv, wk, 0, 48, lambda kc: BIG[:, kc, :], ["BIG0", "BIG1", "BIG2"], NT)
            P.op("dve", (lambda b, mt: lambda e: e.scalar_tensor_tensor(out=X32[:, mt, :], in0=pp[b][:], scalar=cs[:, B_DN + mt:B_DN + mt + 1],
                                                                       in1=X32[:, mt, :], op0=ALU.add, op1=ALU.add))(b, mt),
                 reads=[("pp", b), "X32"], writes=["X32"])
        rmsnorm_fm(P, "c", X32, "X32", NT, cs[:, G_FIN:G_FIN + 16], X32, "X32", scratch=NTb, skey="NTb", **nb)
        k = ("YT", t)
        P.op("sp", lambda e: e.dma_start(out=YT[:, :, t0:t0 + NT].rearrange("k p n -> p k n"), in_=X32),
             reads=["X32"], writes=[k], dma=True)
        return k

    for t in range(NTL):
        p3a(t)
    P.op("pool", lambda e: e.collective_compute("AllGather", ALU.bypass, replica_groups=RG, ins=[NBin_h.ap().opt()], outs=[NBall_h.ap().opt()]),
         reads=nbkeys, writes=["NBall"], dma=True, inc=1, semkey="agNB")

    hl = P.sb("hl", [128, KC, 2], BF16)

    def ldhalo(e):
        j, g = ids(e)
        rl = ((j + 3) % 4) * 2048
        rr = ((j + 1) % 4) * 2048
        return [e.dma_start(out=hl[:, :, 0:1], in_=NBall[bass.ds(rl, 2048), 1:2].rearrange("(k p) n -> p k n", p=128), allow_slow_non_contiguous=True),
                e.dma_start(out=hl[:, :, 1:2], in_=NBall[bass.ds(rr, 2048), 0:1].rearrange("(k p) n -> p k n", p=128), allow_slow_non_contiguous=True)]
    P.op("pool", ldhalo, reads=["NBall"], writes=["hl"], dma=True, ndma=2, semkey="halo")
    P.op("sp", lambda e: [e.dma_start(out=N2S[:, :, 0:1].rearrange("k p n -> p k n"), in_=hl[:, :, 0:1], allow_slow_non_contiguous=True),
                          e.dma_start(out=N2S[:, :, TOK + 1:TOK + 2].rearrange("k p n -> p k n"), in_=hl[:, :, 1:2], allow_slow_non_contiguous=True)],
         reads=["hl"], writes=[("N2S", "halo")], dma=True, ndma=2, semkey="halo2")
    fin = [p3b(t) for t in range(NTL)]
    P.finish(fin)
    return P


def fm(a):
    a = np.asarray(a, np.float32)
    return np.ascontiguousarray(a.reshape(-1, 128).T)


def wfm(w):
    K, M = w.shape
    return np.ascontiguousarray(w.reshape(K // 128, 128, M).transpose(1, 0, 2))


def xfm(x):
    T, Fd = x.shape
    return np.ascontiguousarray(x.T.reshape(Fd // 128, 128, T).transpose(1, 0, 2))


def run(P, in_maps, names=None):
    if names is not None:
        in_maps = [{k: v for k, v in m.items() if k in names} for m in in_maps]
    res = run_bass_kernel_spmd(P.nc, in_maps, core_ids=list(range(8)))
    return res.results


def dft_tables(seq_len):
    nb = TS // seq_len
    n2n = seq_len // 128
    p = np.arange(128)
    bidx, n2 = p // n2n, p % n2n
    j = np.arange(128)
    bj, k2 = j // n2n, j % n2n
    ang = 2 * np.pi * np.outer(n2, k2) / n2n
    same = (bidx[:, None] == bj[None, :]).astype(np.float64)
    sc = 1.0 / np.sqrt(seq_len)
    T1 = np.concatenate([np.cos(ang) * same, -np.sin(ang) * same], 1) * sc
    n1 = np.arange(128)
    a2 = 2 * np.pi * np.outer(n1, n1) / 128.0
    Gr, Gi = np.cos(a2), -np.sin(a2)
    G1 = np.concatenate([Gr, Gi], 1)
    G2 = np.concatenate([-Gi, Gr], 1)
    at = 2 * np.pi * np.outer(n1, k2) / seq_len
    TW = np.concatenate([np.cos(at), -np.sin(at)], 1)
    return np.ascontiguousarray(np.stack([T1, G1, G2, TW], 1).astype(np.float32))


_PROGS = {}
_STOP = None


def kernel(x_prompt, x_sample, g_mix, w_in, b_in, conv_a_w, conv_a_b, lru_w_a, lru_b_a, lru_w_x, lru_b_x,
           lru_lam, w_out_a, w_out_b, w_out, b_out, g_ffn, w_up, b_up, conv_f_w, conv_f_b, w_down, b_down,
           g_final):
    f32 = np.float32
    xs = [np.asarray(x_prompt, f32).reshape(TS, D), np.asarray(x_sample, f32).reshape(TS, D)]
    seqlen = [16384, 8192]
    w_in = np.asarray(w_in, f32)[0]
    b_in = np.asarray(b_in, f32)[0]
    if "f" not in _PROGS:
        _PROGS["f"] = build_fused(_STOP)
    caw = np.asarray(conv_a_w, f32)[0]
    cab = np.asarray(conv_a_b, f32)[0]
    lwa = np.asarray(lru_w_a, f32)[0]
    lwx = np.asarray(lru_w_x, f32)[0]
    lba = np.asarray(lru_b_a, f32)[0]
    lbx = np.asarray(lru_b_x, f32)[0]
    lam = np.asarray(lru_lam, f32)[0]
    bgz = np.concatenate([fm(b_in[2048:4096]), fm(b_in[5120:7168]), fm(b_in[7168:9216])], 1)
    wgz_full = np.concatenate([w_in[:, 2048:4096], w_in[:, 5120:9216]], 1)
    wgz = np.stack([wfm(wgz_full[:, q * 512:(q + 1) * 512]) for q in range(12)], 0)
    woa_ = np.asarray(w_out_a, f32)[0]
    wob_ = np.asarray(w_out_b, f32)[0]
    wo_ = np.asarray(w_out, f32)[0]
    wup_ = np.asarray(w_up, f32)[0]
    wdn_ = np.asarray(w_down, f32)[0]
    woa = np.stack([wfm(woa_[:, q * 512:(q + 1) * 512]) for q in range(4)], 0)
    wob = np.stack([wfm(wob_[:, q * 512:(q + 1) * 512]) for q in range(4)], 0)
    wo = np.stack([wfm(wo_[:, q * 512:(q + 1) * 512]) for q in range(4)], 0)
    wupc = []
    for q in range(24):
        cols = np.concatenate([wup_[:, 256 * q:256 * q + 256], wup_[:, 6144 + 256 * q:6144 + 256 * q + 256]], 1)
        wupc.append(wfm(cols))
    wupc = np.stack(wupc, 0)
    wdn = np.stack([wfm(wdn_[:, m * 128:(m + 1) * 128]) for m in range(16)], 0)
    cc = np.arange(256)
    ang = 2 * np.pi * np.outer(cc, cc) / 256.0
    dm = np.stack([np.cos(ang), np.sin(ang)], 0) / 16.0
    dft = np.ascontiguousarray(dm.reshape(2, 2, 128, 256).transpose(2, 1, 0, 3).astype(f32)).reshape(128, 1024)
    cfw = np.asarray(conv_f_w, f32)[0]
    c3 = np.zeros((128, 640), f32)
    c3[:, 0:16] = fm(np.asarray(g_mix, f32)[0])
    c3[:, 16:32] = fm(np.asarray(g_ffn, f32)[0])
    c3[:, 32:48] = fm(np.asarray(g_final, f32))
    c3[:, 48:96] = bgz
    c3[:, 96:112] = fm(np.asarray(b_out, f32)[0])
    c3[:, 112:128] = fm(np.asarray(b_down, f32)[0])
    c3[:, 128:224] = fm(np.asarray(b_up, f32)[0])
    c3[:, 224:512] = cfw.reshape(3, 96, 128).transpose(2, 1, 0).reshape(128, 288)
    c3[:, 512:608] = fm(np.asarray(conv_f_b, f32)[0])
    c3[:, 610] = 1.0
    xgs = [xfm(xs[g]) for g in range(2)]
    ident = np.concatenate([np.eye(128, dtype=f32), np.zeros((128, 128), f32)], 1)
    ins = []
    for c in range(8):
        g, j = c // 4, c % 4
        L = seqlen[g]
        chs = slice(512 * j, 512 * j + 512)
        fch = slice(4096 + 256 * j, 4096 + 256 * j + 256)
        heads = slice(4 * j, 4 * j + 4)
        wg = np.stack([np.stack([lwa[d, heads], lwx[d, heads]], 0) for d in range(2)], 0)
        wg = np.ascontiguousarray(wg.transpose(3, 0, 1, 2, 4)).reshape(128, 2048)
        c2 = np.zeros((128, 64), f32)
        c2[:, 0:16] = caw[:, chs].reshape(4, 4, 128).transpose(2, 1, 0).reshape(128, 16)
        c2[:, 16:20] = fm(cab[chs])
        c2[:, 20:28] = np.concatenate([fm(lba[0, chs]), fm(lba[1, chs])], 1)
        c2[:, 28:36] = np.concatenate([fm(lbx[0, chs]), fm(lbx[1, chs])], 1)
        c2[:, 36:44] = np.concatenate([fm(lam[0, chs]), fm(lam[1, chs])], 1)
        c2[:, 44] = 1.0 if L == TS else 0.0
        c2[:, 48:52] = fm(b_in[chs])
        tb = np.concatenate([dft_tables(L), ident[:, None, :]], 1)
        lo, hi = j * TOK, (j + 1) * TOK
        cm = c3.copy()
        cm[:, 608] = 1.0 if (lo % L) != 0 else 0.0
        cm[:, 609] = 1.0 if (hi % L) != 0 else 0.0
        cm[:, 611] = 1.0 if L == TS else 0.0
        cm[:, 612] = 0.0 if L == TS else 1.0
        ins.append({"xg": xgs[g], "xo": np.ascontiguousarray(xgs[g][:, :, lo:hi]), "wx": wfm(w_in[:, chs]), "wf": wfm(w_in[:, fch]),
                    "bfT": np.ascontiguousarray(np.tile(b_in[fch][None, :], (128, 1))),
                    "wg": wg, "c2": c2, "tb": np.ascontiguousarray(tb), "wgz": wgz, "woa": woa, "wob": wob, "wo": wo,
                    "wup": wupc, "wdn": wdn, "dft": dft, "c3": cm})
    if _STOP is not None:
        return run(_PROGS["f"], ins, names={"xg", "xo", "wx", "wf", "bfT", "wg", "c2", "tb", "dft", "c3"})
    r = run(_PROGS["f"], ins)
    ys = []
    for g in range(2):
        yt = np.concatenate([r[g * 4 + j]["YT"] for j in range(4)], 2)
        ys.append(np.ascontiguousarray(yt.reshape(D, TS).T))
    return (ys[0].reshape(1, 16384, D).astype(f32), ys[1].reshape(2, 8192, D).astype(f32))
```

```python
import contextlib
import numpy as np
import ml_dtypes
import concourse.bass as bass
import concourse.mybir as mybir
from concourse.bass_utils import run_bass_kernel_spmd

F32 = mybir.dt.float32
BF16 = mybir.dt.bfloat16
AF = mybir.ActivationFunctionType
ALU = mybir.AluOpType
NPBF = ml_dtypes.bfloat16

D = 2048
KC = 16
TOK = 4096
TS = 16384
NT = 512
EPS = 1e-6


class Sched:
    def __init__(self, nc):
        self.nc = nc
        self.ops = []
        self.state = {}
        self.bar_from = 0

    def barrier(self):
        last = {}
        dmas = []
        for i, o in enumerate(self.ops[self.bar_from:], self.bar_from):
            if o["dma"]:
                dmas.append(i)
            elif o["fn"] is not None:
                last[o["eng"]] = i
        deps = sorted(set(dmas) | set(last.values()))
        for eng in ("pe", "act", "dve", "pool", "sp"):
            self.ops.append(dict(eng=eng, fn=None, deps=list(deps), dma=False, semkey=None, signal=False, ndma=1, inc=16,
                                 bar=True))
        self.bar_from = len(self.ops)
        self.state = {}

    def op(self, eng, fn, reads=(), writes=(), dma=False, semkey=None, ndma=1, inc=16):
        oid = len(self.ops)
        deps = set()
        for k in reads:
            st = self.state.setdefault(k, [None, []])
            if st[0] is not None:
                deps.add(st[0])
        for k in writes:
            st = self.state.setdefault(k, [None, []])
            if st[0] is not None:
                deps.add(st[0])
            last = {}
            for r in st[1]:
                o = self.ops[r]
                if o["dma"]:
                    deps.add(r)
                else:
                    last[o["eng"]] = max(last.get(o["eng"], -1), r)
            deps.update(last.values())
        for k in reads:
            self.state[k][1].append(oid)
        for k in writes:
            self.state[k] = [oid, []]
        deps.discard(oid)
        if dma and semkey is None:
            semkey = writes[0] if len(writes) else reads[0]
        self.ops.append(dict(eng=eng, fn=fn, deps=sorted(deps), dma=dma, semkey=semkey,
                             signal=dma, ndma=ndma, inc=inc))
        return oid

    def emit(self, final_keys=()):
        nc = self.nc
        ops = self.ops
        self.op("sp", None, reads=list(final_keys))
        for o in ops:
            for d in o["deps"]:
                od = ops[d]
                if od["dma"]:
                    continue
                if od["eng"] == "pe" and o["eng"] == "pe" and not o["dma"]:
                    continue
                od["signal"] = True
        cnt = {}
        for o in ops:
            if o["dma"]:
                k = ("d", o["semkey"])
                cnt[k] = cnt.get(k, 0) + o["inc"] * o["ndma"]
                o["done"] = (k, cnt[k])
            elif o["signal"]:
                k = ("e", o["eng"])
                cnt[k] = cnt.get(k, 0) + 1
                o["done"] = (k, cnt[k])
        semkeys = sorted(cnt.keys(), key=str)
        with contextlib.ExitStack() as es:
            sems = {}
            for i, k in enumerate(semkeys):
                sems[k] = es.enter_context(nc.semaphore("s%d" % i))
            block = es.enter_context(nc.Block())

            def run(engname):
                def body(e):
                    known = {}
                    for o in ops:
                        if o["eng"] != engname:
                            continue
                        for d in o["deps"]:
                            od = ops[d]
                            if "done" not in od:
                                continue
                            if (not od["dma"]) and od["eng"] == "pe" and engname == "pe" and not o["dma"]:
                                continue
                            k, v = od["done"]
                            if known.get(k, 0) < v:
                                e.wait_ge(sems[k], v)
                                known[k] = v
                        if o["fn"] is None:
                            continue
                        ins = o["fn"](e)
                        if o["dma"]:
                            if not isinstance(ins, (list, tuple)):
                                ins = [ins]
                            assert len(ins) == o["ndma"], (len(ins), o["ndma"])
                            for i_ in ins:
                                i_.then_inc(sems[o["done"][0]], o["inc"])
                        elif o["signal"]:
                            ins.then_inc(sems[o["done"][0]], 1)
                return body

            block.tensor(run("pe"))
            block.scalar(run("act"))
            block.vector(run("dve"))
            block.gpsimd(run("pool"))
            block.sync(run("sp"))


class Prog:
    def __init__(self):
        self.nc = bass.Bass("TRN2", target_bir_lowering=False)
        self.es = contextlib.ExitStack()
        self.S = Sched(self.nc)
        self.outs = []
        self._rr = 0

    def din(self, name, shape, dt=F32):
        return self.nc.dram_tensor(name, list(shape), dt, kind="ExternalInput").ap()

    def dout(self, name, shape, dt=F32):
        self.outs.append(name)
        return self.nc.dram_tensor(name, list(shape), dt, kind="ExternalOutput").ap()

    def dscr(self, name, shape, dt=F32):
        return self.nc.dram_tensor(name, list(shape), dt).ap()

    def sb(self, name, shape, dt=F32):
        return self.es.enter_context(self.nc.sbuf_tensor(name, list(shape), dt))

    def ps(self, name, shape, dt=F32):
        return self.es.enter_context(self.nc.psum_tensor(name, list(shape), dt))

    def op(self, *a, **k):
        return self.S.op(*a, **k)

    def ew(self):
        self._rr ^= 1
        return "dve" if self._rr else "pool"

    def load(self, dst_ap, src_ap, dkey, skey="in", eng="sp"):
        self.op(eng, lambda e: e.dma_start(out=dst_ap, in_=src_ap), reads=[skey], writes=[dkey], dma=True)

    def finish(self, final_keys):
        self.S.emit(final_keys=final_keys)
        self.es.close()


def rmsnorm_fm(P, tag, xT, xkey, N, gcol, outT, okey, ones32, epsT, ss_ps, sq, rt, rstd, scratch=None, skey=None):
    if scratch is None:
        scratch, skey = outT, okey
    P.op("act", lambda e: e.activation(out=scratch[:, :, :N], in_=xT[:, :, :N], func=AF.Square), reads=[xkey], writes=[skey])
    for kc in range(KC):
        P.op("pe", (lambda kc: lambda e: e.matmul(ss_ps[:, :N], lhsT=ones32[:, :], rhs=scratch[:, kc, :N],
                                                  start=(kc == 0), stop=(kc == KC - 1)))(kc),
             reads=[skey, "ones32"], writes=["ss_ps"])
    P.op("act", lambda e: e.activation(out=rt[:, :N], in_=ss_ps[:, :N], func=AF.Sqrt, bias=epsT[:, 0:1], scale=1.0 / D),
         reads=["ss_ps", "epsT"], writes=["rt"])
    P.op("dve", lambda e: e.reciprocal(out=rstd[:, :N], in_=rt[:, :N]), reads=["rt"], writes=["rstd"])
    for kc in range(KC):
        P.op("dve", (lambda kc: lambda e: e.scalar_tensor_tensor(out=outT[:, kc, :N], in0=xT[:, kc, :N],
                                                                scalar=gcol[:, kc:kc + 1], in1=rstd[:, :N],
                                                                op0=ALU.mult, op1=ALU.mult))(kc),
             reads=[xkey, "rstd", "consts"] + ([skey] if skey != okey else []), writes=[okey])


def norm_bufs(P):
    ones32 = P.sb("ones32", [128, 128])
    epsT = P.sb("epsT", [128, 1])
    P.op("dve", lambda e: e.memset(ones32[:], 1.0), writes=["ones32"])
    P.op("dve", lambda e: e.memset(epsT[:], EPS), writes=["epsT"])
    ss_ps = P.ps("ss_ps", [128, NT])
    sq = [P.sb("sq%d" % i, [128, NT]) for i in range(2)]
    rt = P.sb("rt", [128, NT])
    rstd = P.sb("rstd", [128, NT])
    return dict(ones32=ones32, epsT=epsT, ss_ps=ss_ps, sq=sq, rt=rt, rstd=rstd)


class Arena:
    def __init__(self, P, nbytes):
        self.t = P.sb("arena", [128, nbytes // 4])
        self.nbytes = nbytes
        self.off = 0

    def reset(self, off=0):
        self.off = off

    def _take(self, nb):
        nb = (nb + 63) // 64 * 64
        o = self.off
        self.off += nb
        assert self.off <= self.nbytes, (self.off, self.nbytes)
        return self.t[:, o // 4:(o + nb) // 4]

    @staticmethod
    def _shape(ap, shape):
        if len(shape) == 2:
            return ap
        if len(shape) == 3:
            return ap.rearrange("p (a b) -> p a b", a=shape[1])
        if len(shape) == 4:
            return ap.rearrange("p (a b c) -> p a b c", a=shape[1], b=shape[2])
        raise ValueError(shape)

    def f32(self, shape):
        n = int(np.prod(shape[1:]))
        return self._shape(self._take(4 * n)[:, :n], shape)

    def bf16(self, shape):
        n = int(np.prod(shape[1:]))
        return self._shape(self._take(2 * n).bitcast(BF16)[:, :n], shape)


RG = [[0, 1, 2, 3], [4, 5, 6, 7]]
G_MIX, G_FFN, G_FIN, B_GZ, B_OUT, B_DN, B_UP, CFW, CFB, MK = 0, 16, 32, 48, 96, 112, 128, 224, 512, 608


def build_fused(stop=None):
    P = Prog()
    nc = P.nc
    NTL = TOK // NT
    NTG = TS // NT
    HALF = TS // 2
    xg = P.din("xg", [128, KC, TS])
    xo = P.din("xo", [128, KC, TOK])
    wx = P.din("wx", [128, KC, 512])
    wf = P.din("wf", [128, KC, 256])
    bfT = P.din("bfT", [128, 256])
    wg = P.din("wg", [128, 2048])
    c2 = P.din("c2", [128, 64])
    tb = P.din("tb", [128, 5, 256])
    if stop is None:
        wgz = P.din("wgz", [12, 128, KC, 512])
        woa = P.din("woa", [4, 128, KC, 512])
        wob = P.din("wob", [4, 128, 8, 512])
        wo = P.din("wo", [4, 128, KC, 512])
        wup = P.din("wup", [24, 128, KC, 512])
        wdn = P.din("wdn", [16, 128, 48, 128])
        YT = P.dout("YT", [KC, 128, TOK])
        wsrc = dict(wgz=(wgz, 12, KC * 512), woa=(woa, 4, KC * 512), wob=(wob, 4, 8 * 512), wo=(wo, 4, KC * 512),
                    wup=(wup, 24, KC * 512), wdn=(wdn, 16, 48 * 128))
        wbf = {}
        for nm, (src, nchunk, ncol) in wsrc.items():
            wbf[nm] = P.dscr(nm + "_bf", [nchunk, 128, ncol], BF16)
    else:
        DBG = P.dout("DBG", [2048, NT], BF16)
        DBF = P.dout("DBF", [128, NT])
    dft = P.din("dft", [128, 1024])
    c3 = P.din("c3", [128, 640])
    Uscr = P.dscr("Uscr", [4, 128, TS + 3])
    HF = P.dscr("HF", [4, 128, TS])
    HS = P.dscr("HS", [KC, 128, TOK])
    N2S = P.dscr("N2S", [KC, 128, TOK + 2], BF16)
    HRloc = P.dscr("HRloc", [8 * 2048, NT], BF16)
    Qloc = P.dscr("Qloc", [2 * 8 * 2048, NT], BF16)
    HRin_h = nc.dram_tensor("HRin", [16 * 512, 2 * NT], BF16)
    HRall_h = nc.dram_tensor("HRall", [16 * 2048, 2 * NT], BF16)
    Qin_h = nc.dram_tensor("Qin", [2 * 8 * 256, 4 * NT], BF16)
    Qall_h = nc.dram_tensor("Qall", [2 * 8 * 1024, 4 * NT], BF16)
    NBin_h = nc.dram_tensor("NBin", [2048, 2], BF16)
    NBall_h = nc.dram_tensor("NBall", [8192, 2], BF16)
    HRin, HRall, Qin, Qall, NBin, NBall = (h.ap() for h in (HRin_h, HRall_h, Qin_h, Qall_h, NBin_h, NBall_h))

    cs2 = P.sb("cs2", [128, 64])
    cs = P.sb("cs3", [128, 640])
    P.load(cs2[:], c2[:, :], "consts")
    P.load(cs[:], c3[:, :], "consts")
    onesb = P.sb("onesb", [128, 128], BF16)
    epsT = P.sb("epsT", [128, 1])
    P.op("dve", lambda e: e.memset(onesb[:], 1.0), writes=["ones32"])
    P.op("dve", lambda e: e.memset(epsT[:], EPS), writes=["epsT"])
    PS = [P.ps("PS%d" % i, [128, NT]) for i in range(8)]
    sq = None
    rt = P.sb("rt", [128, NT])
    rstd = P.sb("rstd", [128, NT])
    nb = dict(ones32=onesb, epsT=epsT, ss_ps=PS[0], sq=sq, rt=rt, rstd=rstd)
    wgb_ = P.sb("wgb", [128, 2048], BF16)
    P.op("pool", lambda e: e.dma_start(out=wgb_[:], in_=wg[:, :]), reads=["in"], writes=["wgb"], dma=True)
    wgb = wgb_[:].rearrange("p (d a m j) -> p d a m j", d=2, a=2, m=4)
    tbb = P.sb("tbb", [128, 5, 256], BF16)
    P.op("pool", lambda e: e.dma_start(out=tbb[:], in_=tb[:, :, :]), reads=["in"], writes=["tbb"], dma=True)
    tw = P.sb("tw", [128, 256])
    P.load(tw[:], tb[:, 3, :], "tw")
    dftb_ = P.sb("dftb", [128, 1024], BF16)
    P.op("pool", lambda e: e.dma_start(out=dftb_[:], in_=dft[:, :]), reads=["in"], writes=["dftb"], dma=True)
    dftb = dftb_[:].rearrange("p (a b c) -> p a b c", a=2, b=2)
    bft = P.sb("bft", [128, 256])
    P.load(bft[:], bfT[:, :], "bft")
    kt = P.sb("kt", [128, 8]); kl = P.sb("kl", [128, 8]); kl2 = P.sb("kl2", [128, 8])
    st = P.sb("st", [128, 8])
    zt = P.sb("zt", [128, 4])
    AR = Arena(P, 188 * 1024)

    pcnt = [0]

    def ppb():
        b = pcnt[0] % 4
        pcnt[0] += 1
        return b

    pid_cache = {}

    def ids(e):
        k = id(e)
        if k not in pid_cache:
            pid = e.partition_id()
            pid_cache[k] = (pid % 4, pid // 4)
        return pid_cache[k]

    Xq = AR.bf16([128, 2, 128, 128])
    XQ_END = AR.off
    xs = [AR.f32([128, KC, NT]) for _ in range(2)]
    nTs = [AR.bf16([128, KC, NT]) for _ in range(2)]
    wxb = AR.bf16([128, KC, 512])
    wfb = AR.bf16([128, KC, 256])
    ob = [AR.f32([128, NT]) for _ in range(2)]
    P.op("pool", lambda e: e.dma_start(out=wxb, in_=wx[:, :, :]), reads=["in"], writes=["wxb"], dma=True)
    P.op("pool", lambda e: e.dma_start(out=wfb, in_=wf[:, :, :]), reads=["in"], writes=["wfb"], dma=True)
    P.op("dve", lambda e: e.memset(zt[:], 0.0), writes=["zt"])
    P.op("sp", lambda e: [e.dma_start(out=Uscr[mx, :, 0:2], in_=zt[:, 0:2], allow_slow_non_contiguous=True) for mx in range(4)] +
                         [e.dma_start(out=Uscr[mx, :, TS + 2:TS + 3], in_=zt[:, 0:1], allow_slow_non_contiguous=True) for mx in range(4)],
         reads=["zt"], writes=["Upad"], dma=True, ndma=8, semkey="zt")
    ocnt = [0]

    def normA(t):
        x = xs[t % 2]
        xk = ("x", t % 2)
        P.load(x, xg[:, :, t * NT:(t + 1) * NT], xk)
        rmsnorm_fm(P, "a", x, xk, NT, cs[:, G_MIX:G_MIX + 16], nTs[t % 2], ("nT", t % 2), **nb)

    def mmA(t):
        nT = nTs[t % 2]
        nk = ("nT", t % 2)
        for mx in range(4):
            b = ppb()
            o_ = ocnt[0] % 2
            ocnt[0] += 1
            for kc in range(KC):
                P.op("pe", (lambda mx, kc, b, nT: lambda e: e.matmul(PS[1 + b][:], lhsT=wxb[:, kc, mx * 128:(mx + 1) * 128], rhs=nT[:, kc, :],
                                                                   start=(kc == 0), stop=(kc == KC - 1)))(mx, kc, b, nT),
                     reads=[nk, "wxb"], writes=[("pp", b)])
            P.op("act", (lambda mx, b, o_: lambda e: e.activation(out=ob[o_], in_=PS[1 + b][:], func=AF.Identity,
                                                                 bias=cs2[:, 48 + mx:49 + mx], scale=1.0))(mx, b, o_),
                 reads=[("pp", b), "consts"], writes=[("ob", o_)])
            P.op("sp", (lambda mx, o_, t: lambda e: e.dma_start(out=Uscr[mx, :, 2 + t * NT:2 + (t + 1) * NT], in_=ob[o_]))(mx, o_, t),
                 reads=[("ob", o_)], writes=[("U", mx, t)], dma=True, semkey=("ob", o_))
        for blk in range(4):
            b = ppb()
            for kc in range(KC):
                P.op("pe", (lambda blk, kc, b, nT: lambda e: e.matmul(PS[1 + b][:, 0:256], lhsT=nT[:, kc, blk * 128:(blk + 1) * 128], rhs=wfb[:, kc, :],
                                                                    start=(kc == 0), stop=(kc == KC - 1)))(blk, kc, b, nT),
                     reads=[nk, "wfb"], writes=[("pp", b)])
            m = 4 * t + blk
            P.op("dve", (lambda m, b: lambda e: e.tensor_tensor(out=Xq[:, :, m, :], in0=PS[1 + b][:, 0:256].rearrange("p (h c) -> p h c", h=2),
                                                                in1=bft[:, :].rearrange("p (h c) -> p h c", h=2), op=ALU.add))(m, b),
                 reads=[("pp", b), "bft"], writes=["Xq"])

    normA(0)
    for t in range(NTG):
        if t + 1 < NTG:
            normA(t + 1)
        mmA(t)
    P.S.barrier()
    if stop == "A":
        P.op("sp", lambda e: [e.dma_start(out=DBF[:, :], in_=Uscr[1, :, 2 + 512:2 + 1024]),
                              e.dma_start(out=DBG[0:128, :], in_=Xq[:, 0, 5, :].rearrange("p c -> p c") if False else Xq[:, :, 5, :].rearrange("p h c -> p (h c)")[:, 0:256] if False else Xq[:, 0, 0:4, :].rearrange("p m c -> p (m c)"))],
             reads=[], writes=["dbg"], dma=True, ndma=2, semkey="dbg")
        P.finish(["dbg"])
        return P

    AR.reset(XQ_END)
    conv_list = []
    if stop is None:
        shp = "c p k m -> c p (k m)"
        for nm, (src, nchunk, ncol) in wsrc.items():
            for ci in range(nchunk):
                conv_list.append((nm, src, ci))

    def issue_conv(n):
        for _ in range(n):
            if conv_list:
                nm, src, ci = conv_list.pop(0)
                P.op("pool", (lambda nm, src, ci: lambda e: e.dma_start(out=wbf[nm][ci, :, :], in_=src.rearrange(shp)[ci, :, :]))(nm, src, ci),
                     reads=["in"], writes=[("wbf", nm, ci)], dma=True, semkey="wconv")
    P.op("act", lambda e: e.activation(out=kt[:], in_=cs2[:, 36:44], func=AF.Exp, scale=-1.0), reads=[], writes=["kt"])
    P.op("dve", lambda e: e.tensor_scalar_add(out=kt[:], in0=kt[:], scalar1=1.0), reads=["kt"], writes=["kt"])
    P.op("act", lambda e: e.activation(out=kt[:], in_=kt[:], func=AF.Ln), reads=["kt"], writes=["kt"])
    P.op("dve", lambda e: e.tensor_scalar_mul(out=kl[:], in0=kt[:], scalar1=-8.0), reads=["kt"], writes=["kl"])
    P.op("dve", lambda e: e.tensor_scalar_mul(out=kl2[:], in0=kt[:], scalar1=-16.0), reads=["kt"], writes=["kl2"])
    P.op("dve", lambda e: e.memset(st[:], 0.0), writes=[("st", c_) for c_ in range(8)])
    NB = 4
    ut = [[AR.f32([128, NT + 3]) for _ in range(2)] for _ in range(NB)]
    u = [[AR.f32([128, NT]) for _ in range(2)] for _ in range(NB)]
    ub = [[AR.bf16([128, NT]) for _ in range(2)] for _ in range(NB)]
    r_ = [AR.f32([128, NT]) for _ in range(NB)]
    i_ = [AR.f32([128, NT]) for _ in range(NB)]
    a_ = [AR.f32([128, NT]) for _ in range(NB)]
    m_ = [AR.f32([128, NT]) for _ in range(NB)]
    h_ = [AR.f32([128, NT]) for _ in range(NB)]
    hf = [AR.f32([128, NT]) for _ in range(NB)]
    hb = [AR.bf16([128, NT]) for _ in range(NB)]
    pg = [PS[1], PS[2], PS[3], PS[4]]
    onesT = P.sb("onesT", [128, 1])
    P.op("dve", lambda e: e.memset(onesT[:], 1.0), writes=["onesT"])
    hrkeys = []

    def stage1(d, t, s_):
        issue_conv(2)
        for mx in range(4):
            b = mx
            utb, uu, ubb = ut[b][s_], u[b][s_], ub[b][s_]
            kut, ku, kub = ("ut", b, s_), ("u", b, s_), ("ub", b, s_)
            P.load(utb, Uscr[mx, :, t * NT:t * NT + NT + 3], kut)
            if t == NTG // 2 - 1:
                P.op("dve", (lambda utb: lambda e: e.tensor_scalar_mul(out=utb[:, NT + 2:NT + 3], in0=utb[:, NT + 2:NT + 3], scalar1=cs2[:, 44:45]))(utb),
                     reads=[kut], writes=[kut])
            if t == NTG // 2:
                P.op("dve", (lambda utb: lambda e: e.tensor_scalar_mul(out=utb[:, 0:2], in0=utb[:, 0:2], scalar1=cs2[:, 44:45]))(utb),
                     reads=[kut], writes=[kut])
            P.op("dve", (lambda utb, uu, mx: lambda e: e.tensor_scalar(out=uu, in0=utb[:, 0:NT], scalar1=cs2[:, mx * 4:mx * 4 + 1],
                                                                       scalar2=cs2[:, 16 + mx:17 + mx], op0=ALU.mult, op1=ALU.add))(utb, uu, mx),
                 reads=[kut], writes=[ku])
            for k in range(1, 4):
                P.op("dve", (lambda utb, uu, mx, k: lambda e: e.scalar_tensor_tensor(out=uu, in0=utb[:, k:k + NT],
                                                                                     scalar=cs2[:, mx * 4 + k:mx * 4 + k + 1], in1=uu,
                                                                                     op0=ALU.mult, op1=ALU.add))(utb, uu, mx, k),
                     reads=[kut, ku], writes=[ku])
            P.op("act", (lambda uu, ubb: lambda e: e.copy(out=ubb, in_=uu))(uu, ubb), reads=[ku], writes=[kub])

    def stage23(d, t, s_):
        for mx in range(4):
            b = mx
            P.op("pe", (lambda b, d, mx, ubb: lambda e: e.matmul(pg[b][:], lhsT=wgb[:, d, 0, mx, :], rhs=ubb, start=True, stop=True))(b, d, mx, ub[b][s_]),
                 reads=[("ub", b, s_), "wgb"], writes=[("pp", b)])
        for mx in range(4):
            b = mx
            col = d * 4 + mx
            P.op("act", (lambda b, col: lambda e: e.activation(out=r_[b], in_=pg[b][:], func=AF.Sigmoid, bias=cs2[:, 20 + col:21 + col], scale=1.0))(b, col),
                 reads=[("pp", b)], writes=[("r", b)])
            P.op("pe", (lambda b, d, mx, ubb: lambda e: e.matmul(pg[b][:], lhsT=wgb[:, d, 1, mx, :], rhs=ubb, start=True, stop=True))(b, d, mx, ub[b][s_]),
                 reads=[("ub", b, s_), "wgb"], writes=[("pp", b)])
        for mx in range(4):
            b = mx
            col = d * 4 + mx
            P.op("act", (lambda b, col: lambda e: e.activation(out=i_[b], in_=pg[b][:], func=AF.Sigmoid, bias=cs2[:, 28 + col:29 + col], scale=1.0))(b, col),
                 reads=[("pp", b)], writes=[("i", b)])
        for mx in range(4):
            b = mx
            col = d * 4 + mx
            P.op("act", (lambda b, col: lambda e: e.activation(out=a_[b], in_=r_[b], func=AF.Exp, scale=kl[:, col:col + 1]))(b, col),
                 reads=[("r", b), "kl"], writes=[("a", b)])
            P.op("act", (lambda b, col: lambda e: e.activation(out=m_[b], in_=r_[b], func=AF.Exp, scale=kl2[:, col:col + 1]))(b, col),
                 reads=[("r", b), "kl2"], writes=[("m", b)])
        for mx in range(4):
            b = mx
            P.op("act", (lambda b: lambda e: e.activation(out=m_[b], in_=m_[b], func=AF.Sqrt, bias=onesT[:, 0:1], scale=-1.0))(b),
                 reads=[("m", b), "onesT"], writes=[("m", b)])

    def stage45(d, t, s_):
        for mx in range(4):
            b = mx
            uu = u[b][s_]
            P.op("pool", (lambda b, uu: lambda e: e.tensor_tensor(out=i_[b], in0=i_[b], in1=uu, op=ALU.mult))(b, uu),
                 reads=[("i", b), ("u", b, s_)], writes=[("i", b)])
            P.op("pool", (lambda b: lambda e: e.tensor_tensor(out=i_[b], in0=i_[b], in1=m_[b], op=ALU.mult))(b),
                 reads=[("i", b), ("m", b)], writes=[("i", b)])
        for mx in range(4):
            b = mx
            col = d * 4 + mx
            if d == 0:
                P.op("dve", (lambda b, col: lambda e: e.tensor_tensor_scan(out=h_[b], data0=a_[b], data1=i_[b], initial=st[:, col:col + 1],
                                                                           op0=ALU.mult, op1=ALU.add))(b, col),
                     reads=[("a", b), ("i", b), ("st", col)], writes=[("h", b)])
                P.op("dve", (lambda b, col: lambda e: e.tensor_copy(out=st[:, col:col + 1], in_=h_[b][:, NT - 1:NT]))(b, col),
                     reads=[("h", b)], writes=[("st", col)])
                P.op("sp", (lambda b, mx, t: lambda e: e.dma_start(out=HF[mx, :, t * NT:(t + 1) * NT], in_=h_[b]))(b, mx, t),
                     reads=[("h", b)], writes=[("HF", mx, t)], dma=True, semkey=("h", b))
            else:
                P.load(hf[b], HF[mx, :, t * NT:(t + 1) * NT], ("hf", b), skey=("HF", mx, t))
                P.op("dve", (lambda b, col: lambda e: e.tensor_tensor_scan(out=h_[b][:, ::-1], data0=a_[b][:, ::-1], data1=i_[b][:, ::-1],
                                                                           initial=st[:, col:col + 1], op0=ALU.mult, op1=ALU.add))(b, col),
                     reads=[("a", b), ("i", b), ("st", col)], writes=[("h", b)])
                P.op("dve", (lambda b, col: lambda e: e.tensor_copy(out=st[:, col:col + 1], in_=h_[b][:, 0:1]))(b, col),
                     reads=[("h", b)], writes=[("st", col)])
                P.op("pool", (lambda b: lambda e: e.tensor_tensor(out=hb[b], in0=h_[b], in1=hf[b], op=ALU.add))(b),
                     reads=[("h", b), ("hf", b)], writes=[("hb", b)])
                k = ("HR", mx, t)
                P.op("sp", (lambda b, mx, t: lambda e: e.dma_start(out=HRin[(t // 2) * 512 + mx * 128:(t // 2) * 512 + (mx + 1) * 128, (t % 2) * NT:(t % 2 + 1) * NT], in_=hb[b]))(b, mx, t),
                     reads=[("hb", b)], writes=[k], dma=True, semkey=("hb", b))
                if mx == 3 and t % 2 == 0:
                    tp = t // 2
                    P.op("pool", (lambda tp: lambda e: e.collective_compute("AllGather", ALU.bypass, replica_groups=RG,
                                                                           ins=[HRin[tp * 512:(tp + 1) * 512, :].opt()],
                                                                           outs=[HRall[tp * 2048:(tp + 1) * 2048, :].opt()]))(tp),
                         reads=[("HR", m2, t_) for m2 in range(4) for t_ in (t, t + 1)], writes=[("HRall", tp)], dma=True, inc=1, semkey="agHR")
                    hrkeys.append(("HRall", tp))

    for d in range(2):
        order = list(range(NTG)) if d == 0 else list(range(NTG - 1, -1, -1))
        stage1(d, order[0], 0)
        for ti, t in enumerate(order):
            if ti == NTG // 2:
                P.op("dve", (lambda d: lambda e: e.tensor_scalar_mul(out=st[:, d * 4:d * 4 + 4], in0=st[:, d * 4:d * 4 + 4],
                                                                      scalar1=cs2[:, 44:45]))(d),
                     reads=[("st", d * 4 + c_) for c_ in range(4)], writes=[("st", d * 4 + c_) for c_ in range(4)])
            stage23(d, t, ti % 2)
            if ti + 1 < NTG:
                stage1(d, order[ti + 1], (ti + 1) % 2)
            stage45(d, t, ti % 2)
    P.S.barrier()
    if stop == "B":
        P.op("sp", lambda e: [e.dma_start(out=DBG[:, :], in_=HRall[3 * 2048:4 * 2048, 0:NT]), e.dma_start(out=DBF[:, :], in_=HF[2, :, 512:1024])],
             reads=[], writes=["dbg"], dma=True, ndma=2, semkey="dbg")
        P.finish(["dbg"])
        return P

    AR.reset(XQ_END)
    FB = AR.bf16([128, 128, 128])
    Bp = AR.bf16([128, 128, 2, 128])
    s1 = [AR.f32([128, 256]) for _ in range(2)]
    t1 = [AR.f32([128, 128]) for _ in range(2)]
    t2 = [AR.f32([128, 128]) for _ in range(2)]
    t3 = [AR.f32([128, 128]) for _ in range(2)]
    t4 = [AR.f32([128, 128]) for _ in range(2)]
    p1 = [PS[5], PS[6]]
    p2 = [PS[1], PS[2]]
    ptrs = [PS[7][:, :].bitcast(BF16)[:, 0:128], PS[0][:, :].bitcast(BF16)[:, 0:128]]
    ptrk = [("ps7", 0), "ss_ps"]
    Tr = tw[:, 0:128]
    Ti = tw[:, 128:256]
    ident = tbb[:, 4, 0:128]
    qkeys = []
    for ch in range(2):
        for c in range(128):
            b = c % 2
            P.op("pe", (lambda ch, c, b: lambda e: e.transpose(out=ptrs[b], in_=Xq[:, ch, :, c], identity=ident))(ch, c, b),
                 reads=[("SEL", ch), "tbb"], writes=[ptrk[b]])
            P.op("act" if b else "dve", (lambda c, b: lambda e: (e.copy if b else e.tensor_copy)(out=FB[:, :, c], in_=ptrs[b]))(c, b),
                 reads=[ptrk[b]], writes=["FB"])
        if stop == "C0":
            P.op("sp", lambda e: e.dma_start(out=DBG[0:128, :], in_=FB[:, 0:4, :].rearrange("p a b -> p (a b)")), reads=["FB"], writes=["dbg"], dma=True, semkey="dbg")
            P.op("sp", lambda e: e.dma_start(out=DBF[:, :], in_=HF[2, :, 512:1024]), reads=[], writes=["dbg2"], dma=True, semkey="dbg2")
            P.finish(["dbg", "dbg2"])
            return P
        for c in range(128):
            b = c % 2
            P.op("pe", (lambda c, b: lambda e: e.matmul(p1[b][:, 0:256], lhsT=FB[:, :, c], rhs=tbb[:, 0, :], start=True, stop=True))(c, b),
                 reads=["FB", "tbb"], writes=[("ps", 5 + b)])
            P.op("act", (lambda b: lambda e: e.copy(out=s1[b], in_=p1[b][:, 0:256]))(b), reads=[("ps", 5 + b)], writes=[("s1", b)])
            P.op("dve", (lambda b: lambda e: e.tensor_tensor(out=t1[b], in0=s1[b][:, 0:128], in1=Tr, op=ALU.mult))(b),
                 reads=[("s1", b), "tw"], writes=[("t1", b)])
            P.op("dve", (lambda b: lambda e: e.tensor_tensor(out=t2[b], in0=s1[b][:, 128:256], in1=Ti, op=ALU.mult))(b),
                 reads=[("s1", b), "tw"], writes=[("t2", b)])
            P.op("dve", (lambda b, c: lambda e: e.tensor_tensor(out=Bp[:, c, 0, :], in0=t1[b], in1=t2[b], op=ALU.subtract))(b, c),
                 reads=[("t1", b), ("t2", b)], writes=[("Bp", c)])
            P.op("pool", (lambda b: lambda e: e.tensor_tensor(out=t3[b], in0=s1[b][:, 0:128], in1=Ti, op=ALU.mult))(b),
                 reads=[("s1", b), "tw"], writes=[("t3", b)])
            P.op("pool", (lambda b: lambda e: e.tensor_tensor(out=t4[b], in0=s1[b][:, 128:256], in1=Tr, op=ALU.mult))(b),
                 reads=[("s1", b), "tw"], writes=[("t4", b)])
            P.op("pool", (lambda b, c: lambda e: e.tensor_tensor(out=Bp[:, c, 1, :], in0=t3[b], in1=t4[b], op=ALU.add))(b, c),
                 reads=[("t3", b), ("t4", b)], writes=[("Bp", c)])
        bpk = [("Bp", c) for c in range(128)]
        if stop == "C1":
            P.op("sp", lambda e: e.dma_start(out=DBG[0:128, :], in_=Bp[:, 0:2, :, :].rearrange("p a b c -> p (a b c)")), reads=bpk, writes=["dbg"], dma=True, semkey="dbg")
            P.op("sp", lambda e: e.dma_start(out=DBF[:, :], in_=HF[2, :, 512:1024]), reads=[], writes=["dbg2"], dma=True, semkey="dbg2")
            P.finish(["dbg", "dbg2"])
            return P
        for ri in range(2):
            for j in range(128):
                b = j % 2
                P.op("pe", (lambda j, b, ri: lambda e: e.matmul(p2[b][:, 0:128], lhsT=Bp[:, :, 0, j], rhs=tbb[:, 1, ri * 128:(ri + 1) * 128], start=True, stop=False))(j, b, ri),
                     reads=bpk + ["tbb"], writes=[("pp", b)])
                P.op("pe", (lambda j, b, ri: lambda e: e.matmul(p2[b][:, 0:128], lhsT=Bp[:, :, 1, j], rhs=tbb[:, 2, ri * 128:(ri + 1) * 128], start=False, stop=True))(j, b, ri),
                     reads=bpk + ["tbb"], writes=[("pp", b)])
                P.op("act" if b else "dve", (lambda j, b: lambda e: (e.copy if b else e.tensor_copy)(out=FB[:, :, j], in_=p2[b][:, 0:128]))(j, b),
                     reads=[("pp", b)], writes=["FB"])
            k = ("Q", ch, ri)
            SEL = Xq[:, ch, :, :].rearrange("p a b -> p (a b)")
            P.op("dve", (lambda SEL: lambda e: e.tensor_scalar_mul(out=SEL.rearrange("p (b k x) -> p b k x", b=2, x=64),
                                                                  in0=FB.rearrange("p k (b x) -> p b k x", b=2),
                                                                  scalar1=cs[:, MK + 4:MK + 5]))(SEL),
                 reads=["FB"], writes=[("SEL", ch)])
            P.op("dve", (lambda SEL: lambda e: e.scalar_tensor_tensor(out=SEL, in0=FB.rearrange("p a b -> p (a b)"), scalar=cs[:, MK + 3:MK + 4],
                                                                     in1=SEL, op0=ALU.mult, op1=ALU.add))(SEL),
                 reads=["FB", ("SEL", ch)], writes=[("SEL", ch)])
            P.op("sp", (lambda ch, ri, SEL: lambda e: e.dma_start(out=Qin[ch * 2048:(ch + 1) * 2048, :].rearrange("(s i c) n -> c s i n", i=2, c=128)[:, :, ri, :],
                                                                  in_=SEL.rearrange("p (s n) -> p s n", n=4 * NT)))(ch, ri, SEL),
                 reads=[("SEL", ch)], writes=[k], dma=True, semkey=("SEL", ch))
            qkeys.append(k)
        if stop != "C2":
            for sq_ in range(8):
                idx = ch * 8 + sq_
                P.op("pool", (lambda idx: lambda e: e.collective_compute("AllGather", ALU.bypass, replica_groups=RG,
                                                                         ins=[Qin[idx * 256:(idx + 1) * 256, :].opt()],
                                                                         outs=[Qall[idx * 1024:(idx + 1) * 1024, :].opt()]))(idx),
                     reads=[("Q", ch, 0), ("Q", ch, 1)], writes=[("Qall", idx)], dma=True, inc=1, semkey="agQ")
    if stop == "C2":
        P.op("sp", lambda e: e.dma_start(out=DBG[0:512, :], in_=Qin[512:1024, :]), reads=qkeys, writes=["dbg"], dma=True, semkey="dbg")
        P.op("sp", lambda e: e.dma_start(out=DBF[:, :], in_=HF[2, :, 512:1024]), reads=[], writes=["dbg2"], dma=True, semkey="dbg2")
        P.finish(["dbg", "dbg2"])
        return P
    P.S.barrier()
    if stop == "C":
        P.op("sp", lambda e: [e.dma_start(out=DBG[:, :], in_=Qall[2048:4096, 0:NT]), e.dma_start(out=DBF[:, :], in_=HF[2, :, 512:1024])],
             reads=[], writes=["dbg"], dma=True, ndma=2, semkey="dbg")
        P.finish(["dbg"])
        return P

    AR.reset(0)
    X32 = AR.f32([128, KC, NT])
    NTb = AR.bf16([128, KC, NT + 2])
    QTb = AR.bf16([128, 16, NT])
    MT = QTb
    FT = AR.bf16([128, 8, NT])
    BIG = AR.bf16([128, 48, NT])
    WB = [AR.bf16([128, KC * 512]) for _ in range(2)]
    e1 = [AR.f32([128, NT]) for _ in range(2)]
    e2 = [AR.f32([128, NT]) for _ in range(2)]
    upS = [AR.f32([128, NT + 2]) for _ in range(2)]
    upH = [AR.f32([128, 2]) for _ in range(2)]
    cv = [AR.f32([128, NT]) for _ in range(2)]
    gg = [AR.f32([128, NT]) for _ in range(2)]
    pp = [PS[1], PS[2], PS[3], PS[4]]
    ph = [PS[5], PS[6]]
    wcnt = [0]

    def wload(nm, ci, kcn, cols):
        i = wcnt[0] % 2
        wcnt[0] += 1
        view = WB[i][:, 0:kcn * cols].rearrange("p (k m) -> p k m", k=kcn)
        P.op("sp", lambda e: e.dma_start(out=WB[i][:, 0:kcn * cols], in_=wbf[nm][ci, :, :]), reads=["in"], writes=[("WB", i)], dma=True)
        return view, ("WB", i)

    def linear(wview, wkey, mloc, kcn, rhs_fn, rkeys, N):
        b = ppb()
        for kc in range(kcn):
            P.op("pe", (lambda kc, b: lambda e: e.matmul(pp[b][:, :N], lhsT=wview[:, kc, mloc * 128:(mloc + 1) * 128], rhs=rhs_fn(kc),
                                                       start=(kc == 0), stop=(kc == kcn - 1)))(kc, b),
                 reads=list(rkeys) + [wkey], writes=[("pp", b)])
        return b

    ecnt = [0]
    nbkeys = []

    def p3a(t):
        t0 = t * NT
        N = NT

        P.load(X32, xo[:, :, t0:t0 + NT], "X32")
        def ldh(e):
            j, g = ids(e)
            return e.dma_start(out=BIG[:, 0:16, :], in_=HRall[bass.ds(j * 8192 + (t // 2) * 2048, 2048), (t % 2) * NT:(t % 2 + 1) * NT].rearrange("(k p) n -> p k n", p=128))
        P.op("sp", ldh, reads=[], writes=["BIG0"], dma=True)

        def ldq(e):
            j, g = ids(e)
            return [e.dma_start(out=QTb[:, ch_ * 8:(ch_ + 1) * 8, :],
                                in_=Qall[bass.ds(j * 2048 + (ch_ * 8192 + (t // 4) * 1024), 1024), (t % 4) * NT:(t % 4 + 1) * NT].rearrange("(k c) n -> c k n", c=128))
                    for ch_ in range(2)]
        P.op("act", ldq, reads=[], writes=["QTb"], dma=True, ndma=2)
        rmsnorm_fm(P, "a", X32, "X32", N, cs[:, G_MIX:G_MIX + 16], NTb, "NTb", **nb)
        for part in range(3):
            for cq in range(4):
                wv, wk = wload("wgz", part * 4 + cq, KC, 512)
                for ml in range(4):
                    mt = cq * 4 + ml
                    b = linear(wv, wk, ml, KC, lambda kc: NTb[:, kc, :N], ["NTb"], N)
                    bias = cs[:, B_GZ + part * 16 + mt:B_GZ + part * 16 + mt + 1]
                    if part == 0:
                        i = ecnt[0] % 2
                        ecnt[0] += 1
                        P.op("act", (lambda b, i, bias: lambda e: e.activation(out=e1[i][:, :N], in_=pp[b][:, :N], func=AF.Gelu_apprx_tanh, bias=bias, scale=1.0))(b, i, bias),
                             reads=[("pp", b)], writes=[("e1", i)])
                        P.op("dve", (lambda mt, i: lambda e: e.tensor_tensor(out=BIG[:, mt, :N], in0=BIG[:, mt, :N], in1=e1[i][:, :N], op=ALU.mult))(mt, i),
                             reads=[("e1", i), "BIG0"], writes=["BIG0"])
                    else:
                        P.op("act", (lambda b, mt, bias, part: lambda e: e.activation(out=BIG[:, part * 16 + mt, :N], in_=pp[b][:, :N], func=AF.Sigmoid, bias=bias, scale=1.0))(b, mt, bias, part),
                             reads=[("pp", b)], writes=["BIG%d" % part])
        for g4 in range(4):
            for ct in range(2):
                b = ppb()
                n = 0
                for chh in range(2):
                    for ri in range(2):
                        P.op("pe", (lambda g4, ct, chh, ri, b, n: lambda e: e.matmul(pp[b][:, :N], lhsT=dftb[:, chh, ri, ct * 128:(ct + 1) * 128],
                                                                                     rhs=QTb[:, chh * 8 + g4 * 2 + ri, :N], start=(n == 0), stop=(n == 3)))(g4, ct, chh, ri, b, n),
                             reads=["QTb", "dftb"], writes=[("pp", b)])
                        n += 1
                P.op("act", (lambda g4, ct, b: lambda e: e.copy(out=FT[:, g4 * 2 + ct, :N], in_=pp[b][:, :N]))(g4, ct, b),
                     reads=[("pp", b)], writes=["FT"])
        for cq in range(4):
            wva, wka = wload("woa", cq, KC, 512)
            wvb, wkb = wload("wob", cq, 8, 512)
            for ml in range(4):
                mt = cq * 4 + ml
                ba = linear(wva, wka, ml, KC, lambda kc: BIG[:, kc, :N], ["BIG0"], N)
                bb = linear(wvb, wkb, ml, 8, lambda kc: FT[:, kc, :N], ["FT"], N)
                i = ecnt[0] % 2
                ecnt[0] += 1
                P.op("dve", (lambda ba, mt, i: lambda e: e.tensor_tensor(out=e1[i][:, :N], in0=pp[ba][:, :N], in1=BIG[:, 16 + mt, :N], op=ALU.mult))(ba, mt, i),
                     reads=[("pp", ba), "BIG1"], writes=[("e1", i)])
                P.op("dve", (lambda bb, mt, i: lambda e: e.tensor_tensor(out=e2[i][:, :N], in0=pp[bb][:, :N], in1=BIG[:, 32 + mt, :N], op=ALU.mult))(bb, mt, i),
                     reads=[("pp", bb), "BIG2"], writes=[("e2", i)])
                P.op("pool", (lambda mt, i: lambda e: e.tensor_tensor(out=MT[:, mt, :N], in0=e1[i][:, :N], in1=e2[i][:, :N], op=ALU.add))(mt, i),
                     reads=[("e1", i), ("e2", i)], writes=["QTb"])
        for cq in range(4):
            wv, wk = wload("wo", cq, KC, 512)
            for ml in range(4):
                mt = cq * 4 + ml
                b = linear(wv, wk, ml, KC, lambda kc: MT[:, kc, :N], ["QTb"], N)
                P.op("dve", (lambda b, mt: lambda e: e.scalar_tensor_tensor(out=X32[:, mt, :N], in0=pp[b][:, :N], scalar=cs[:, B_OUT + mt:B_OUT + mt + 1],
                                                                           in1=X32[:, mt, :N], op0=ALU.add, op1=ALU.add))(b, mt),
                     reads=[("pp", b), "X32"], writes=["X32"])
        P.op("sp", lambda e: e.dma_start(out=HS[:, :, t0:t0 + N].rearrange("k p n -> p k n"), in_=X32),
             reads=["X32"], writes=[("HS", t0)], dma=True, semkey="X32")
        rmsnorm_fm(P, "b", X32, "X32", N, cs[:, G_FFN:G_FFN + 16], NTb, "NTb", **nb)
        P.op("sp", lambda e: e.dma_start(out=N2S[:, :, 1 + t0:1 + t0 + N].rearrange("k p n -> p k n"), in_=NTb[:, :, :N]),
             reads=["NTb"], writes=[("N2S", t0)], dma=True, semkey="NTb")
        if t == 0:
            P.op("sp", lambda e: e.dma_start(out=NBin[:, 0:1].rearrange("(k p) n -> p k n", p=128), in_=NTb[:, :, 0:1], allow_slow_non_contiguous=True),
                 reads=["NTb"], writes=[("NB", 0)], dma=True, semkey="NTb")
            nbkeys.append(("NB", 0))
        if t == NTL - 1:
            P.op("sp", lambda e: e.dma_start(out=NBin[:, 1:2].rearrange("(k p) n -> p k n", p=128), in_=NTb[:, :, NT - 1:NT], allow_slow_non_contiguous=True),
                 reads=["NTb"], writes=[("NB", 1)], dma=True, semkey="NTb")
            nbkeys.append(("NB", 1))

    def p3b(t):
        t0 = t * NT
        rk = [("N2S", t0)]
        rk.append(("N2S", t0 - NT) if t > 0 else ("N2S", "halo"))
        rk.append(("N2S", t0 + NT) if t < NTL - 1 else ("N2S", "halo"))
        P.op("sp", lambda e: e.dma_start(out=NTb, in_=N2S[:, :, t0:t0 + NT + 2].rearrange("k p n -> p k n")),
             reads=rk, writes=["NTb"], dma=True)
        P.op("sp", lambda e: e.dma_start(out=X32, in_=HS[:, :, t0:t0 + NT].rearrange("k p n -> p k n")),
             reads=[("HS", t0)], writes=["X32"], dma=True)
        mL = cs[:, MK:MK + 1] if t == 0 else cs[:, MK + 2:MK + 3]
        mR = cs[:, MK + 1:MK + 2] if t == NTL - 1 else cs[:, MK + 2:MK + 3]
        for q in range(24):
            wv, wk = wload("wup", q, KC, 512)
            for ml in range(4):
                gv, jj = ml // 2, ml % 2
                jp = 2 * q + jj
                mt = jp + 48 * gv
                b = ppb()
                hbk = b % 2
                for kc in range(KC):
                    P.op("pe", (lambda kc, b, ml, wv: lambda e: e.matmul(pp[b][:], lhsT=wv[:, kc, ml * 128:(ml + 1) * 128], rhs=NTb[:, kc, 1:NT + 1],
                                                                       start=(kc == 0), stop=(kc == KC - 1)))(kc, b, ml, wv),
                         reads=["NTb", wk], writes=[("pp", b)])
                for kc in range(KC):
                    P.op("pe", (lambda kc, hbk, ml, wv: lambda e: e.matmul(ph[hbk][:, 0:2], lhsT=wv[:, kc, ml * 128:(ml + 1) * 128], rhs=NTb[:, kc, 0:NT + 2:NT + 1],
                                                                         start=(kc == 0), stop=(kc == KC - 1)))(kc, hbk, ml, wv),
                         reads=["NTb", wk], writes=[("ps", 5 + hbk)])
                i = ecnt[0] % 2
                ecnt[0] += 1
                bias = cs[:, B_UP + mt:B_UP + mt + 1]
                P.op("act", (lambda b, i, bias: lambda e: e.activation(out=upS[i][:, 1:NT + 1], in_=pp[b][:], func=AF.Identity, bias=bias, scale=1.0))(b, i, bias),
                     reads=[("pp", b)], writes=[("upS", i)])
                P.op("act", (lambda hbk, i, bias: lambda e: e.activation(out=upH[i], in_=ph[hbk][:, 0:2], func=AF.Identity, bias=bias, scale=1.0))(hbk, i, bias),
                     reads=[("ps", 5 + hbk)], writes=[("upH", i)])
                P.op("dve", (lambda i, mL: lambda e: e.tensor_scalar_mul(out=upS[i][:, 0:1], in0=upH[i][:, 0:1], scalar1=mL))(i, mL),
                     reads=[("upH", i)], writes=[("upS", i)])
                P.op("dve", (lambda i, mR: lambda e: e.tensor_scalar_mul(out=upS[i][:, NT + 1:NT + 2], in0=upH[i][:, 1:2], scalar1=mR))(i, mR),
                     reads=[("upH", i)], writes=[("upS", i)])
                w0 = cs[:, CFW + mt * 3:CFW + mt * 3 + 1]
                w1 = cs[:, CFW + mt * 3 + 1:CFW + mt * 3 + 2]
                w2 = cs[:, CFW + mt * 3 + 2:CFW + mt * 3 + 3]
                cb = cs[:, CFB + mt:CFB + mt + 1]
                P.op("dve", (lambda i, w0, cb: lambda e: e.tensor_scalar(out=cv[i], in0=upS[i][:, 0:NT], scalar1=w0, scalar2=cb, op0=ALU.mult, op1=ALU.add))(i, w0, cb),
                     reads=[("upS", i)], writes=[("cv", i)])
                P.op("dve", (lambda i, w1: lambda e: e.scalar_tensor_tensor(out=cv[i], in0=upS[i][:, 1:NT + 1], scalar=w1, in1=cv[i], op0=ALU.mult, op1=ALU.add))(i, w1),
                     reads=[("upS", i), ("cv", i)], writes=[("cv", i)])
                P.op("dve", (lambda i, w2: lambda e: e.scalar_tensor_tensor(out=cv[i], in0=upS[i][:, 2:NT + 2], scalar=w2, in1=cv[i], op0=ALU.mult, op1=ALU.add))(i, w2),
                     reads=[("upS", i), ("cv", i)], writes=[("cv", i)])
                if gv == 0:
                    P.op("act", (lambda i, jj: lambda e: e.activation(out=gg[jj], in_=cv[i], func=AF.Gelu_apprx_tanh))(i, jj),
                         reads=[("cv", i)], writes=[("gg", jj)])
                else:
                    P.op("pool", (lambda i, jj, jp: lambda e: e.tensor_tensor(out=BIG[:, jp, :], in0=gg[jj], in1=cv[i], op=ALU.mult))(i, jj, jp),
                         reads=[("gg", jj), ("cv", i)], writes=["BIG%d" % (jp // 16)])
        for mt in range(KC):
            wv, wk = wload("wdn", mt, 48, 128)
            b = linear(wv, wk, 0, 48, lambda kc: BIG[:, kc, :], ["BIG0", "BIG1", "BIG2"], NT)
            P.op("dve", (lambda b, mt: lambda e: e.scalar_tensor_tensor(out=X32[:, mt, :], in0=pp[b][:], scalar=cs[:, B_DN + mt:B_DN + mt + 1],
                                                                       in1=X32[:, mt, :], op0=ALU.add, op1=ALU.add))(b, mt),
                 reads=[("pp", b), "X32"], writes=["X32"])
        rmsnorm_fm(P, "c", X32, "X32", NT, cs[:, G_FIN:G_FIN + 16], X32, "X32", scratch=NTb, skey="NTb", **nb)
        k = ("YT", t)
        P.op("sp", lambda e: e.dma_start(out=YT[:, :, t0:t0 + NT].rearrange("k p n -> p k n"), in_=X32),
             reads=["X32"], writes=[k], dma=True)
        return k

    for t in range(NTL):
        p3a(t)
    P.op("pool", lambda e: e.collective_compute("AllGather", ALU.bypass, replica_groups=RG, ins=[NBin_h.ap().opt()], outs=[NBall_h.ap().opt()]),
         reads=nbkeys, writes=["NBall"], dma=True, inc=1, semkey="agNB")

    hl = P.sb("hl", [128, KC, 2], BF16)

    def ldhalo(e):
        j, g = ids(e)
        rl = ((j + 3) % 4) * 2048
        rr = ((j + 1) % 4) * 2048
        return [e.dma_start(out=hl[:, :, 0:1], in_=NBall[bass.ds(rl, 2048), 1:2].rearrange("(k p) n -> p k n", p=128), allow_slow_non_contiguous=True),
                e.dma_start(out=hl[:, :, 1:2], in_=NBall[bass.ds(rr, 2048), 0:1].rearrange("(k p) n -> p k n", p=128), allow_slow_non_contiguous=True)]
    P.op("pool", ldhalo, reads=["NBall"], writes=["hl"], dma=True, ndma=2, semkey="halo")
    P.op("sp", lambda e: [e.dma_start(out=N2S[:, :, 0:1].rearrange("k p n -> p k n"), in_=hl[:, :, 0:1], allow_slow_non_contiguous=True),
                          e.dma_start(out=N2S[:, :, TOK + 1:TOK + 2].rearrange("k p n -> p k n"), in_=hl[:, :, 1:2], allow_slow_non_contiguous=True)],
         reads=["hl"], writes=[("N2S", "halo")], dma=True, ndma=2, semkey="halo2")
    fin = [p3b(t) for t in range(NTL)]
    P.finish(fin)
    return P


def fm(a):
    a = np.asarray(a, np.float32)
    return np.ascontiguousarray(a.reshape(-1, 128).T)


def wfm(w):
    K, M = w.shape
    return np.ascontiguousarray(w.reshape(K // 128, 128, M).transpose(1, 0, 2))


def xfm(x):
    T, Fd = x.shape
    return np.ascontiguousarray(x.T.reshape(Fd // 128, 128, T).transpose(1, 0, 2))


def run(P, in_maps, names=None):
    if names is not None:
        in_maps = [{k: v for k, v in m.items() if k in names} for m in in_maps]
    res = run_bass_kernel_spmd(P.nc, in_maps, core_ids=list(range(8)))
    return res.results


def dft_tables(seq_len):
    nb = TS // seq_len
    n2n = seq_len // 128
    p = np.arange(128)
    bidx, n2 = p // n2n, p % n2n
    j = np.arange(128)
    bj, k2 = j // n2n, j % n2n
    ang = 2 * np.pi * np.outer(n2, k2) / n2n
    same = (bidx[:, None] == bj[None, :]).astype(np.float64)
    sc = 1.0 / np.sqrt(seq_len)
    T1 = np.concatenate([np.cos(ang) * same, -np.sin(ang) * same], 1) * sc
    n1 = np.arange(128)
    a2 = 2 * np.pi * np.outer(n1, n1) / 128.0
    Gr, Gi = np.cos(a2), -np.sin(a2)
    G1 = np.concatenate([Gr, Gi], 1)
    G2 = np.concatenate([-Gi, Gr], 1)
    at = 2 * np.pi * np.outer(n1, k2) / seq_len
    TW = np.concatenate([np.cos(at), -np.sin(at)], 1)
    return np.ascontiguousarray(np.stack([T1, G1, G2, TW], 1).astype(np.float32))


_PROGS = {}
_STOP = None


def kernel(x_prompt, x_sample, g_mix, w_in, b_in, conv_a_w, conv_a_b, lru_w_a, lru_b_a, lru_w_x, lru_b_x,
           lru_lam, w_out_a, w_out_b, w_out, b_out, g_ffn, w_up, b_up, conv_f_w, conv_f_b, w_down, b_down,
           g_final):
    f32 = np.float32
    xs = [np.asarray(x_prompt, f32).reshape(TS, D), np.asarray(x_sample, f32).reshape(TS, D)]
    seqlen = [16384, 8192]
    w_in = np.asarray(w_in, f32)[0]
    b_in = np.asarray(b_in, f32)[0]
    if "f" not in _PROGS:
        _PROGS["f"] = build_fused(_STOP)
    caw = np.asarray(conv_a_w, f32)[0]
    cab = np.asarray(conv_a_b, f32)[0]
    lwa = np.asarray(lru_w_a, f32)[0]
    lwx = np.asarray(lru_w_x, f32)[0]
    lba = np.asarray(lru_b_a, f32)[0]
    lbx = np.asarray(lru_b_x, f32)[0]
    lam = np.asarray(lru_lam, f32)[0]
    bgz = np.concatenate([fm(b_in[2048:4096]), fm(b_in[5120:7168]), fm(b_in[7168:9216])], 1)
    wgz_full = np.concatenate([w_in[:, 2048:4096], w_in[:, 5120:9216]], 1)
    wgz = np.stack([wfm(wgz_full[:, q * 512:(q + 1) * 512]) for q in range(12)], 0)
    woa_ = np.asarray(w_out_a, f32)[0]
    wob_ = np.asarray(w_out_b, f32)[0]
    wo_ = np.asarray(w_out, f32)[0]
    wup_ = np.asarray(w_up, f32)[0]
    wdn_ = np.asarray(w_down, f32)[0]
    woa = np.stack([wfm(woa_[:, q * 512:(q + 1) * 512]) for q in range(4)], 0)
    wob = np.stack([wfm(wob_[:, q * 512:(q + 1) * 512]) for q in range(4)], 0)
    wo = np.stack([wfm(wo_[:, q * 512:(q + 1) * 512]) for q in range(4)], 0)
    wupc = []
    for q in range(24):
        cols = np.concatenate([wup_[:, 256 * q:256 * q + 256], wup_[:, 6144 + 256 * q:6144 + 256 * q + 256]], 1)
        wupc.append(wfm(cols))
    wupc = np.stack(wupc, 0)
    wdn = np.stack([wfm(wdn_[:, m * 128:(m + 1) * 128]) for m in range(16)], 0)
    cc = np.arange(256)
    ang = 2 * np.pi * np.outer(cc, cc) / 256.0
    dm = np.stack([np.cos(ang), np.sin(ang)], 0) / 16.0
    dft = np.ascontiguousarray(dm.reshape(2, 2, 128, 256).transpose(2, 1, 0, 3).astype(f32)).reshape(128, 1024)
    cfw = np.asarray(conv_f_w, f32)[0]
    c3 = np.zeros((128, 640), f32)
    c3[:, 0:16] = fm(np.asarray(g_mix, f32)[0])
    c3[:, 16:32] = fm(np.asarray(g_ffn, f32)[0])
    c3[:, 32:48] = fm(np.asarray(g_final, f32))
    c3[:, 48:96] = bgz
    c3[:, 96:112] = fm(np.asarray(b_out, f32)[0])
    c3[:, 112:128] = fm(np.asarray(b_down, f32)[0])
    c3[:, 128:224] = fm(np.asarray(b_up, f32)[0])
    c3[:, 224:512] = cfw.reshape(3, 96, 128).transpose(2, 1, 0).reshape(128, 288)
    c3[:, 512:608] = fm(np.asarray(conv_f_b, f32)[0])
    c3[:, 610] = 1.0
    xgs = [xfm(xs[g]) for g in range(2)]
    ident = np.concatenate([np.eye(128, dtype=f32), np.zeros((128, 128), f32)], 1)
    ins = []
    for c in range(8):
        g, j = c // 4, c % 4
        L = seqlen[g]
        chs = slice(512 * j, 512 * j + 512)
        fch = slice(4096 + 256 * j, 4096 + 256 * j + 256)
        heads = slice(4 * j, 4 * j + 4)
        wg = np.stack([np.stack([lwa[d, heads], lwx[d, heads]], 0) for d in range(2)], 0)
        wg = np.ascontiguousarray(wg.transpose(3, 0, 1, 2, 4)).reshape(128, 2048)
        c2 = np.zeros((128, 64), f32)
        c2[:, 0:16] = caw[:, chs].reshape(4, 4, 128).transpose(2, 1, 0).reshape(128, 16)
        c2[:, 16:20] = fm(cab[chs])
        c2[:, 20:28] = np.concatenate([fm(lba[0, chs]), fm(lba[1, chs])], 1)
        c2[:, 28:36] = np.concatenate([fm(lbx[0, chs]), fm(lbx[1, chs])], 1)
        c2[:, 36:44] = np.concatenate([fm(lam[0, chs]), fm(lam[1, chs])], 1)
        c2[:, 44] = 1.0 if L == TS else 0.0
        c2[:, 48:52] = fm(b_in[chs])
        tb = np.concatenate([dft_tables(L), ident[:, None, :]], 1)
        lo, hi = j * TOK, (j + 1) * TOK
        cm = c3.copy()
        cm[:, 608] = 1.0 if (lo % L) != 0 else 0.0
        cm[:, 609] = 1.0 if (hi % L) != 0 else 0.0
        cm[:, 611] = 1.0 if L == TS else 0.0
        cm[:, 612] = 0.0 if L == TS else 1.0
        ins.append({"xg": xgs[g], "xo": np.ascontiguousarray(xgs[g][:, :, lo:hi]), "wx": wfm(w_in[:, chs]), "wf": wfm(w_in[:, fch]),
                    "bfT": np.ascontiguousarray(np.tile(b_in[fch][None, :], (128, 1))),
                    "wg": wg, "c2": c2, "tb": np.ascontiguousarray(tb), "wgz": wgz, "woa": woa, "wob": wob, "wo": wo,
                    "wup": wupc, "wdn": wdn, "dft": dft, "c3": cm})
    if _STOP is not None:
        return run(_PROGS["f"], ins, names={"xg", "xo", "wx", "wf", "bfT", "wg", "c2", "tb", "dft", "c3"})
    r = run(_PROGS["f"], ins)
    ys = []
    for g in range(2):
        yt = np.concatenate([r[g * 4 + j]["YT"] for j in range(4)], 2)
        ys.append(np.ascontiguousarray(yt.reshape(D, TS).T))
    return (ys[0].reshape(1, 16384, D).astype(f32), ys[1].reshape(2, 8192, D).astype(f32))
```

```python
import contextlib
import numpy as np
import ml_dtypes
import concourse.bass as bass
import concourse.mybir as mybir
from concourse.bass_utils import run_bass_kernel_spmd

F32 = mybir.dt.float32
BF16 = mybir.dt.bfloat16
AF = mybir.ActivationFunctionType
ALU = mybir.AluOpType
NPBF = ml_dtypes.bfloat16

D = 2048
KC = 16
TOK = 4096
TS = 16384
NT = 512
EPS = 1e-6


class Sched:
    def __init__(self, nc):
        self.nc = nc
        self.ops = []
        self.state = {}
        self.bar_from = 0

    def barrier(self):
        last = {}
        dmas = []
        for i, o in enumerate(self.ops[self.bar_from:], self.bar_from):
            if o["dma"]:
                dmas.append(i)
            elif o["fn"] is not None:
                last[o["eng"]] = i
        deps = sorted(set(dmas) | set(last.values()))
        for eng in ("pe", "act", "dve", "pool", "sp"):
            self.ops.append(dict(eng=eng, fn=None, deps=list(deps), dma=False, semkey=None, signal=False, ndma=1, inc=16,
                                 bar=True))
        self.bar_from = len(self.ops)
        self.state = {}

    def op(self, eng, fn, reads=(), writes=(), dma=False, semkey=None, ndma=1, inc=16):
        oid = len(self.ops)
        deps = set()
        for k in reads:
            st = self.state.setdefault(k, [None, []])
            if st[0] is not None:
                deps.add(st[0])
        for k in writes:
            st = self.state.setdefault(k, [None, []])
            if st[0] is not None:
                deps.add(st[0])
            last = {}
            for r in st[1]:
                o = self.ops[r]
                if o["dma"]:
                    deps.add(r)
                else:
                    last[o["eng"]] = max(last.get(o["eng"], -1), r)
            deps.update(last.values())
        for k in reads:
            self.state[k][1].append(oid)
        for k in writes:
            self.state[k] = [oid, []]
        deps.discard(oid)
        if dma and semkey is None:
            semkey = writes[0] if len(writes) else reads[0]
        self.ops.append(dict(eng=eng, fn=fn, deps=sorted(deps), dma=dma, semkey=semkey,
                             signal=dma, ndma=ndma, inc=inc))
        return oid

    def emit(self, final_keys=()):
        nc = self.nc
        ops = self.ops
        self.op("sp", None, reads=list(final_keys))
        for o in ops:
            for d in o["deps"]:
                od = ops[d]
                if od["dma"]:
                    continue
                if od["eng"] == "pe" and o["eng"] == "pe" and not o["dma"]:
                    continue
                od["signal"] = True
        cnt = {}
        for o in ops:
            if o["dma"]:
                k = ("d", o["semkey"])
                cnt[k] = cnt.get(k, 0) + o["inc"] * o["ndma"]
                o["done"] = (k, cnt[k])
            elif o["signal"]:
                k = ("e", o["eng"])
                cnt[k] = cnt.get(k, 0) + 1
                o["done"] = (k, cnt[k])
        semkeys = sorted(cnt.keys(), key=str)
        with contextlib.ExitStack() as es:
            sems = {}
            for i, k in enumerate(semkeys):
                sems[k] = es.enter_context(nc.semaphore("s%d" % i))
            block = es.enter_context(nc.Block())

            def run(engname):
                def body(e):
                    known = {}
                    for o in ops:
                        if o["eng"] != engname:
                            continue
                        for d in o["deps"]:
                            od = ops[d]
                            if "done" not in od:
                                continue
                            if (not od["dma"]) and od["eng"] == "pe" and engname == "pe" and not o["dma"]:
                                continue
                            k, v = od["done"]
                            if known.get(k, 0) < v:
                                e.wait_ge(sems[k], v)
                                known[k] = v
                        if o["fn"] is None:
                            continue
                        ins = o["fn"](e)
                        if o["dma"]:
                            if not isinstance(ins, (list, tuple)):
                                ins = [ins]
                            assert len(ins) == o["ndma"], (len(ins), o["ndma"])
                            for i_ in ins:
                                i_.then_inc(sems[o["done"][0]], o["inc"])
                        elif o["signal"]:
                            ins.then_inc(sems[o["done"][0]], 1)
                return body

            block.tensor(run("pe"))
            block.scalar(run("act"))
            block.vector(run("dve"))
            block.gpsimd(run("pool"))
            block.sync(run("sp"))


class Prog:
    def __init__(self):
        self.nc = bass.Bass("TRN2", target_bir_lowering=False)
        self.es = contextlib.ExitStack()
        self.S = Sched(self.nc)
        self.outs = []
        self._rr = 0

    def din(self, name, shape, dt=F32):
        return self.nc.dram_tensor(name, list(shape), dt, kind="ExternalInput").ap()

    def dout(self, name, shape, dt=F32):
        self.outs.append(name)
        return self.nc.dram_tensor(name, list(shape), dt, kind="ExternalOutput").ap()

    def dscr(self, name, shape, dt=F32):
        return self.nc.dram_tensor(name, list(shape), dt).ap()

    def sb(self, name, shape, dt=F32):
        return self.es.enter_context(self.nc.sbuf_tensor(name, list(shape), dt))

    def ps(self, name, shape, dt=F32):
        return self.es.enter_context(self.nc.psum_tensor(name, list(shape), dt))

    def op(self, *a, **k):
        return self.S.op(*a, **k)

    def ew(self):
        self._rr ^= 1
        return "dve" if self._rr else "pool"

    def load(self, dst_ap, src_ap, dkey, skey="in", eng="sp"):
        self.op(eng, lambda e: e.dma_start(out=dst_ap, in_=src_ap), reads=[skey], writes=[dkey], dma=True)

    def finish(self, final_keys):
        self.S.emit(final_keys=final_keys)
        self.es.close()


def rmsnorm_fm(P, tag, xT, xkey, N, gcol, outT, okey, ones32, epsT, ss_ps, sq, rt, rstd, scratch=None, skey=None):
    if scratch is None:
        scratch, skey = outT, okey
    P.op("act", lambda e: e.activation(out=scratch[:, :, :N], in_=xT[:, :, :N], func=AF.Square), reads=[xkey], writes=[skey])
    for kc in range(KC):
        P.op("pe", (lambda kc: lambda e: e.matmul(ss_ps[:, :N], lhsT=ones32[:, :], rhs=scratch[:, kc, :N],
                                                  start=(kc == 0), stop=(kc == KC - 1)))(kc),
             reads=[skey, "ones32"], writes=["ss_ps"])
    P.op("act", lambda e: e.activation(out=rt[:, :N], in_=ss_ps[:, :N], func=AF.Sqrt, bias=epsT[:, 0:1], scale=1.0 / D),
         reads=["ss_ps", "epsT"], writes=["rt"])
    P.op("dve", lambda e: e.reciprocal(out=rstd[:, :N], in_=rt[:, :N]), reads=["rt"], writes=["rstd"])
    for kc in range(KC):
        P.op("dve", (lambda kc: lambda e: e.scalar_tensor_tensor(out=outT[:, kc, :N], in0=xT[:, kc, :N],
                                                                scalar=gcol[:, kc:kc + 1], in1=rstd[:, :N],
                                                                op0=ALU.mult, op1=ALU.mult))(kc),
             reads=[xkey, "rstd", "consts"] + ([skey] if skey != okey else []), writes=[okey])


def norm_bufs(P):
    ones32 = P.sb("ones32", [128, 128])
    epsT = P.sb("epsT", [128, 1])
    P.op("dve", lambda e: e.memset(ones32[:], 1.0), writes=["ones32"])
    P.op("dve", lambda e: e.memset(epsT[:], EPS), writes=["epsT"])
    ss_ps = P.ps("ss_ps", [128, NT])
    sq = [P.sb("sq%d" % i, [128, NT]) for i in range(2)]
    rt = P.sb("rt", [128, NT])
    rstd = P.sb("rstd", [128, NT])
    return dict(ones32=ones32, epsT=epsT, ss_ps=ss_ps, sq=sq, rt=rt, rstd=rstd)


class Arena:
    def __init__(self, P, nbytes):
        self.t = P.sb("arena", [128, nbytes // 4])
        self.nbytes = nbytes
        self.off = 0

    def reset(self, off=0):
        self.off = off

    def _take(self, nb):
        nb = (nb + 63) // 64 * 64
        o = self.off
        self.off += nb
        assert self.off <= self.nbytes, (self.off, self.nbytes)
        return self.t[:, o // 4:(o + nb) // 4]

    @staticmethod
    def _shape(ap, shape):
        if len(shape) == 2:
            return ap
        if len(shape) == 3:
            return ap.rearrange("p (a b) -> p a b", a=shape[1])
        if len(shape) == 4:
            return ap.rearrange("p (a b c) -> p a b c", a=shape[1], b=shape[2])
        raise ValueError(shape)

    def f32(self, shape):
        n = int(np.prod(shape[1:]))
        return self._shape(self._take(4 * n)[:, :n], shape)

    def bf16(self, shape):
        n = int(np.prod(shape[1:]))
        return self._shape(self._take(2 * n).bitcast(BF16)[:, :n], shape)


RG = [[0, 1, 2, 3], [4, 5, 6, 7]]
G_MIX, G_FFN, G_FIN, B_GZ, B_OUT, B_DN, B_UP, CFW, CFB, MK = 0, 16, 32, 48, 96, 112, 128, 224, 512, 608


def build_fused(stop=None):
    P = Prog()
    nc = P.nc
    NTL = TOK // NT
    NTG = TS // NT
    HALF = TS // 2
    xg = P.din("xg", [128, KC, TS])
    xo = P.din("xo", [128, KC, TOK])
    wx = P.din("wx", [128, KC, 512])
    wf = P.din("wf", [128, KC, 256])
    bfT = P.din("bfT", [128, 256])
    wg = P.din("wg", [128, 2048])
    c2 = P.din("c2", [128, 64])
    tb = P.din("tb", [128, 5, 256])
    if stop is None:
        wgz = P.din("wgz", [12, 128, KC, 512])
        woa = P.din("woa", [4, 128, KC, 512])
        wob = P.din("wob", [4, 128, 8, 512])
        wo = P.din("wo", [4, 128, KC, 512])
        wup = P.din("wup", [24, 128, KC, 512])
        wdn = P.din("wdn", [16, 128, 48, 128])
        YT = P.dout("YT", [KC, 128, TOK])
        wsrc = dict(wgz=(wgz, 12, KC * 512), woa=(woa, 4, KC * 512), wob=(wob, 4, 8 * 512), wo=(wo, 4, KC * 512),
                    wup=(wup, 24, KC * 512), wdn=(wdn, 16, 48 * 128))
        wbf = {}
        for nm, (src, nchunk, ncol) in wsrc.items():
            wbf[nm] = P.dscr(nm + "_bf", [nchunk, 128, ncol], BF16)
    else:
        DBG = P.dout("DBG", [2048, NT], BF16)
        DBF = P.dout("DBF", [128, NT])
    dft = P.din("dft", [128, 1024])
    c3 = P.din("c3", [128, 640])
    Uscr = P.dscr("Uscr", [4, 128, TS + 3])
    HF = P.dscr("HF", [4, 128, TS])
    HS = P.dscr("HS", [KC, 128, TOK])
    N2S = P.dscr("N2S", [KC, 128, TOK + 2], BF16)
    HRloc = P.dscr("HRloc", [8 * 2048, NT], BF16)
    Qloc = P.dscr("Qloc", [2 * 8 * 2048, NT], BF16)
    HRin_h = nc.dram_tensor("HRin", [16 * 512, 2 * NT], BF16)
    HRall_h = nc.dram_tensor("HRall", [16 * 2048, 2 * NT], BF16)
    Qin_h = nc.dram_tensor("Qin", [2 * 8 * 256, 4 * NT], BF16)
    Qall_h = nc.dram_tensor("Qall", [2 * 8 * 1024, 4 * NT], BF16)
    NBin_h = nc.dram_tensor("NBin", [2048, 2], BF16)
    NBall_h = nc.dram_tensor("NBall", [8192, 2], BF16)
    HRin, HRall, Qin, Qall, NBin, NBall = (h.ap() for h in (HRin_h, HRall_h, Qin_h, Qall_h, NBin_h, NBall_h))

    cs2 = P.sb("cs2", [128, 64])
    cs = P.sb("cs3", [128, 640])
    P.load(cs2[:], c2[:, :], "consts")
    P.load(cs[:], c3[:, :], "consts")
    onesb = P.sb("onesb", [128, 128], BF16)
    epsT = P.sb("epsT", [128, 1])
    P.op("dve", lambda e: e.memset(onesb[:], 1.0), writes=["ones32"])
    P.op("dve", lambda e: e.memset(epsT[:], EPS), writes=["epsT"])
    PS = [P.ps("PS%d" % i, [128, NT]) for i in range(8)]
    sq = None
    rt = P.sb("rt", [128, NT])
    rstd = P.sb("rstd", [128, NT])
    nb = dict(ones32=onesb, epsT=epsT, ss_ps=PS[0], sq=sq, rt=rt, rstd=rstd)
    wgb_ = P.sb("wgb", [128, 2048], BF16)
    P.op("pool", lambda e: e.dma_start(out=wgb_[:], in_=wg[:, :]), reads=["in"], writes=["wgb"], dma=True)
    wgb = wgb_[:].rearrange("p (d a m j) -> p d a m j", d=2, a=2, m=4)
    tbb = P.sb("tbb", [128, 5, 256], BF16)
    P.op("pool", lambda e: e.dma_start(out=tbb[:], in_=tb[:, :, :]), reads=["in"], writes=["tbb"], dma=True)
    tw = P.sb("tw", [128, 256])
    P.load(tw[:], tb[:, 3, :], "tw")
    dftb_ = P.sb("dftb", [128, 1024], BF16)
    P.op("pool", lambda e: e.dma_start(out=dftb_[:], in_=dft[:, :]), reads=["in"], writes=["dftb"], dma=True)
    dftb = dftb_[:].rearrange("p (a b c) -> p a b c", a=2, b=2)
    bft = P.sb("bft", [128, 256])
    P.load(bft[:], bfT[:, :], "bft")
    kt = P.sb("kt", [128, 8]); kl = P.sb("kl", [128, 8]); kl2 = P.sb("kl2", [128, 8])
    st = P.sb("st", [128, 8])
    zt = P.sb("zt", [128, 4])
    AR = Arena(P, 188 * 1024)

    pcnt = [0]

    def ppb():
        b = pcnt[0] % 4
        pcnt[0] += 1
        return b

    pid_cache = {}

    def ids(e):
        k = id(e)
        if k not in pid_cache:
            pid = e.partition_id()
            pid_cache[k] = (pid % 4, pid // 4)
        return pid_cache[k]

    Xq = AR.bf16([128, 2, 128, 128])
    XQ_END = AR.off
    xs = [AR.f32([128, KC, NT]) for _ in range(2)]
    nTs = [AR.bf16([128, KC, NT]) for _ in range(2)]
    wxb = AR.bf16([128, KC, 512])
    wfb = AR.bf16([128, KC, 256])
    ob = [AR.f32([128, NT]) for _ in range(2)]
    P.op("pool", lambda e: e.dma_start(out=wxb, in_=wx[:, :, :]), reads=["in"], writes=["wxb"], dma=True)
    P.op("pool", lambda e: e.dma_start(out=wfb, in_=wf[:, :, :]), reads=["in"], writes=["wfb"], dma=True)
    P.op("dve", lambda e: e.memset(zt[:], 0.0), writes=["zt"])
    P.op("sp", lambda e: [e.dma_start(out=Uscr[mx, :, 0:2], in_=zt[:, 0:2], allow_slow_non_contiguous=True) for mx in range(4)] +
                         [e.dma_start(out=Uscr[mx, :, TS + 2:TS + 3], in_=zt[:, 0:1], allow_slow_non_contiguous=True) for mx in range(4)],
         reads=["zt"], writes=["Upad"], dma=True, ndma=8, semkey="zt")
    ocnt = [0]

    def normA(t):
        x = xs[t % 2]
        xk = ("x", t % 2)
        P.load(x, xg[:, :, t * NT:(t + 1) * NT], xk)
        rmsnorm_fm(P, "a", x, xk, NT, cs[:, G_MIX:G_MIX + 16], nTs[t % 2], ("nT", t % 2), **nb)

    def mmA(t):
        nT = nTs[t % 2]
        nk = ("nT", t % 2)
        for mx in range(4):
            b = ppb()
            o_ = ocnt[0] % 2
            ocnt[0] += 1
            for kc in range(KC):
                P.op("pe", (lambda mx, kc, b, nT: lambda e: e.matmul(PS[1 + b][:], lhsT=wxb[:, kc, mx * 128:(mx + 1) * 128], rhs=nT[:, kc, :],
                                                                   start=(kc == 0), stop=(kc == KC - 1)))(mx, kc, b, nT),
                     reads=[nk, "wxb"], writes=[("pp", b)])
            P.op("act", (lambda mx, b, o_: lambda e: e.activation(out=ob[o_], in_=PS[1 + b][:], func=AF.Identity,
                                                                 bias=cs2[:, 48 + mx:49 + mx], scale=1.0))(mx, b, o_),
                 reads=[("pp", b), "consts"], writes=[("ob", o_)])
            P.op("sp", (lambda mx, o_, t: lambda e: e.dma_start(out=Uscr[mx, :, 2 + t * NT:2 + (t + 1) * NT], in_=ob[o_]))(mx, o_, t),
                 reads=[("ob", o_)], writes=[("U", mx, t)], dma=True, semkey=("ob", o_))
        for blk in range(4):
            b = ppb()
            for kc in range(KC):
                P.op("pe", (lambda blk, kc, b, nT: lambda e: e.matmul(PS[1 + b][:, 0:256], lhsT=nT[:, kc, blk * 128:(blk + 1) * 128], rhs=wfb[:, kc, :],
                                                                    start=(kc == 0), stop=(kc == KC - 1)))(blk, kc, b, nT),
                     reads=[nk, "wfb"], writes=[("pp", b)])
            m = 4 * t + blk
            P.op("dve", (lambda m, b: lambda e: e.tensor_tensor(out=Xq[:, :, m, :], in0=PS[1 + b][:, 0:256].rearrange("p (h c) -> p h c", h=2),
                                                                in1=bft[:, :].rearrange("p (h c) -> p h c", h=2), op=ALU.add))(m, b),
                 reads=[("pp", b), "bft"], writes=["Xq"])

    normA(0)
    for t in range(NTG):
        if t + 1 < NTG:
            normA(t + 1)
        mmA(t)
    P.S.barrier()
    if stop == "A":
        P.op("sp", lambda e: [e.dma_start(out=DBF[:, :], in_=Uscr[1, :, 2 + 512:2 + 1024]),
                              e.dma_start(out=DBG[0:128, :], in_=Xq[:, 0, 5, :].rearrange("p c -> p c") if False else Xq[:, :, 5, :].rearrange("p h c -> p (h c)")[:, 0:256] if False else Xq[:, 0, 0:4, :].rearrange("p m c -> p (m c)"))],
             reads=[], writes=["dbg"], dma=True, ndma=2, semkey="dbg")
        P.finish(["dbg"])
        return P

    AR.reset(XQ_END)
    conv_list = []
    if stop is None:
        shp = "c p k m -> c p (k m)"
        for nm, (src, nchunk, ncol) in wsrc.items():
            for ci in range(nchunk):
                conv_list.append((nm, src, ci))

    def issue_conv(n):
        for _ in range(n):
            if conv_list:
                nm, src, ci = conv_list.pop(0)
                P.op("pool", (lambda nm, src, ci: lambda e: e.dma_start(out=wbf[nm][ci, :, :], in_=src.rearrange(shp)[ci, :, :]))(nm, src, ci),
                     reads=["in"], writes=[("wbf", nm, ci)], dma=True, semkey="wconv")
    P.op("act", lambda e: e.activation(out=kt[:], in_=cs2[:, 36:44], func=AF.Exp, scale=-1.0), reads=[], writes=["kt"])
    P.op("dve", lambda e: e.tensor_scalar_add(out=kt[:], in0=kt[:], scalar1=1.0), reads=["kt"], writes=["kt"])
    P.op("act", lambda e: e.activation(out=kt[:], in_=kt[:], func=AF.Ln), reads=["kt"], writes=["kt"])
    P.op("dve", lambda e: e.tensor_scalar_mul(out=kl[:], in0=kt[:], scalar1=-8.0), reads=["kt"], writes=["kl"])
    P.op("dve", lambda e: e.tensor_scalar_mul(out=kl2[:], in0=kt[:], scalar1=-16.0), reads=["kt"], writes=["kl2"])
    P.op("dve", lambda e: e.memset(st[:], 0.0), writes=[("st", c_) for c_ in range(8)])
    NB = 4
    ut = [[AR.f32([128, NT + 3]) for _ in range(2)] for _ in range(NB)]
    u = [[AR.f32([128, NT]) for _ in range(2)] for _ in range(NB)]
    ub = [[AR.bf16([128, NT]) for _ in range(2)] for _ in range(NB)]
    r_ = [AR.f32([128, NT]) for _ in range(NB)]
    i_ = [AR.f32([128, NT]) for _ in range(NB)]
    a_ = [AR.f32([128, NT]) for _ in range(NB)]
    m_ = [AR.f32([128, NT]) for _ in range(NB)]
    h_ = [AR.f32([128, NT]) for _ in range(NB)]
    hf = [AR.f32([128, NT]) for _ in range(NB)]
    hb = [AR.bf16([128, NT]) for _ in range(NB)]
    pg = [PS[1], PS[2], PS[3], PS[4]]
    onesT = P.sb("onesT", [128, 1])
    P.op("dve", lambda e: e.memset(onesT[:], 1.0), writes=["onesT"])
    hrkeys = []

    def stage1(d, t, s_):
        issue_conv(2)
        for mx in range(4):
            b = mx
            utb, uu, ubb = ut[b][s_], u[b][s_], ub[b][s_]
            kut, ku, kub = ("ut", b, s_), ("u", b, s_), ("ub", b, s_)
            P.load(utb, Uscr[mx, :, t * NT:t * NT + NT + 3], kut)
            if t == NTG // 2 - 1:
                P.op("dve", (lambda utb: lambda e: e.tensor_scalar_mul(out=utb[:, NT + 2:NT + 3], in0=utb[:, NT + 2:NT + 3], scalar1=cs2[:, 44:45]))(utb),
                     reads=[kut], writes=[kut])
            if t == NTG // 2:
                P.op("dve", (lambda utb: lambda e: e.tensor_scalar_mul(out=utb[:, 0:2], in0=utb[:, 0:2], scalar1=cs2[:, 44:45]))(utb),
                     reads=[kut], writes=[kut])
            P.op("dve", (lambda utb, uu, mx: lambda e: e.tensor_scalar(out=uu, in0=utb[:, 0:NT], scalar1=cs2[:, mx * 4:mx * 4 + 1],
                                                                       scalar2=cs2[:, 16 + mx:17 + mx], op0=ALU.mult, op1=ALU.add))(utb, uu, mx),
                 reads=[kut], writes=[ku])
            for k in range(1, 4):
                P.op("dve", (lambda utb, uu, mx, k: lambda e: e.scalar_tensor_tensor(out=uu, in0=utb[:, k:k + NT],
                                                                                     scalar=cs2[:, mx * 4 + k:mx * 4 + k + 1], in1=uu,
                                                                                     op0=ALU.mult, op1=ALU.add))(utb, uu, mx, k),
                     reads=[kut, ku], writes=[ku])
            P.op("act", (lambda uu, ubb: lambda e: e.copy(out=ubb, in_=uu))(uu, ubb), reads=[ku], writes=[kub])

    def stage23(d, t, s_):
        for mx in range(4):
            b = mx
            P.op("pe", (lambda b, d, mx, ubb: lambda e: e.matmul(pg[b][:], lhsT=wgb[:, d, 0, mx, :], rhs=ubb, start=True, stop=True))(b, d, mx, ub[b][s_]),
                 reads=[("ub", b, s_), "wgb"], writes=[("pp", b)])
        for mx in range(4):
            b = mx
            col = d * 4 + mx
            P.op("act", (lambda b, col: lambda e: e.activation(out=r_[b], in_=pg[b][:], func=AF.Sigmoid, bias=cs2[:, 20 + col:21 + col], scale=1.0))(b, col),
                 reads=[("pp", b)], writes=[("r", b)])
            P.op("pe", (lambda b, d, mx, ubb: lambda e: e.matmul(pg[b][:], lhsT=wgb[:, d, 1, mx, :], rhs=ubb, start=True, stop=True))(b, d, mx, ub[b][s_]),
                 reads=[("ub", b, s_), "wgb"], writes=[("pp", b)])
        for mx in range(4):
            b = mx
            col = d * 4 + mx
            P.op("act", (lambda b, col: lambda e: e.activation(out=i_[b], in_=pg[b][:], func=AF.Sigmoid, bias=cs2[:, 28 + col:29 + col], scale=1.0))(b, col),
                 reads=[("pp", b)], writes=[("i", b)])
        for mx in range(4):
            b = mx
            col = d * 4 + mx
            P.op("act", (lambda b, col: lambda e: e.activation(out=a_[b], in_=r_[b], func=AF.Exp, scale=kl[:, col:col + 1]))(b, col),
                 reads=[("r", b), "kl"], writes=[("a", b)])
            P.op("act", (lambda b, col: lambda e: e.activation(out=m_[b], in_=r_[b], func=AF.Exp, scale=kl2[:, col:col + 1]))(b, col),
                 reads=[("r", b), "kl2"], writes=[("m", b)])
        for mx in range(4):
            b = mx
            P.op("act", (lambda b: lambda e: e.activation(out=m_[b], in_=m_[b], func=AF.Sqrt, bias=onesT[:, 0:1], scale=-1.0))(b),
                 reads=[("m", b), "onesT"], writes=[("m", b)])

    def stage45(d, t, s_):
        for mx in range(4):
            b = mx
            uu = u[b][s_]
            P.op("pool", (lambda b, uu: lambda e: e.tensor_tensor(out=i_[b], in0=i_[b], in1=uu, op=ALU.mult))(b, uu),
                 reads=[("i", b), ("u", b, s_)], writes=[("i", b)])
            P.op("pool", (lambda b: lambda e: e.tensor_tensor(out=i_[b], in0=i_[b], in1=m_[b], op=ALU.mult))(b),
                 reads=[("i", b), ("m", b)], writes=[("i", b)])
        for mx in range(4):
            b = mx
            col = d * 4 + mx
            if d == 0:
                P.op("dve", (lambda b, col: lambda e: e.tensor_tensor_scan(out=h_[b], data0=a_[b], data1=i_[b], initial=st[:, col:col + 1],
                                                                           op0=ALU.mult, op1=ALU.add))(b, col),
                     reads=[("a", b), ("i", b), ("st", col)], writes=[("h", b)])
                P.op("dve", (lambda b, col: lambda e: e.tensor_copy(out=st[:, col:col + 1], in_=h_[b][:, NT - 1:NT]))(b, col),
                     reads=[("h", b)], writes=[("st", col)])
                P.op("sp", (lambda b, mx, t: lambda e: e.dma_start(out=HF[mx, :, t * NT:(t + 1) * NT], in_=h_[b]))(b, mx, t),
                     reads=[("h", b)], writes=[("HF", mx, t)], dma=True, semkey=("h", b))
            else:
                P.load(hf[b], HF[mx, :, t * NT:(t + 1) * NT], ("hf", b), skey=("HF", mx, t))
                P.op("dve", (lambda b, col: lambda e: e.tensor_tensor_scan(out=h_[b][:, ::-1], data0=a_[b][:, ::-1], data1=i_[b][:, ::-1],
                                                                           initial=st[:, col:col + 1], op0=ALU.mult, op1=ALU.add))(b, col),
                     reads=[("a", b), ("i", b), ("st", col)], writes=[("h", b)])
                P.op("dve", (lambda b, col: lambda e: e.tensor_copy(out=st[:, col:col + 1], in_=h_[b][:, 0:1]))(b, col),
                     reads=[("h", b)], writes=[("st", col)])
                P.op("pool", (lambda b: lambda e: e.tensor_tensor(out=hb[b], in0=h_[b], in1=hf[b], op=ALU.add))(b),
                     reads=[("h", b), ("hf", b)], writes=[("hb", b)])
                k = ("HR", mx, t)
                P.op("sp", (lambda b, mx, t: lambda e: e.dma_start(out=HRin[(t // 2) * 512 + mx * 128:(t // 2) * 512 + (mx + 1) * 128, (t % 2) * NT:(t % 2 + 1) * NT], in_=hb[b]))(b, mx, t),
                     reads=[("hb", b)], writes=[k], dma=True, semkey=("hb", b))
                if mx == 3 and t % 2 == 0:
                    tp = t // 2
                    P.op("pool", (lambda tp: lambda e: e.collective_compute("AllGather", ALU.bypass, replica_groups=RG,
                                                                           ins=[HRin[tp * 512:(tp + 1) * 512, :].opt()],
                                                                           outs=[HRall[tp * 2048:(tp + 1) * 2048, :].opt()]))(tp),
                         reads=[("HR", m2, t_) for m2 in range(4) for t_ in (t, t + 1)], writes=[("HRall", tp)], dma=True, inc=1, semkey="agHR")
                    hrkeys.append(("HRall", tp))

    for d in range(2):
        order = list(range(NTG)) if d == 0 else list(range(NTG - 1, -1, -1))
        stage1(d, order[0], 0)
        for ti, t in enumerate(order):
            if ti == NTG // 2:
                P.op("dve", (lambda d: lambda e: e.tensor_scalar_mul(out=st[:, d * 4:d * 4 + 4], in0=st[:, d * 4:d * 4 + 4],
                                                                      scalar1=cs2[:, 44:45]))(d),
                     reads=[("st", d * 4 + c_) for c_ in range(4)], writes=[("st", d * 4 + c_) for c_ in range(4)])
            stage23(d, t, ti % 2)
            if ti + 1 < NTG:
                stage1(d, order[ti + 1], (ti + 1) % 2)
            stage45(d, t, ti % 2)
    P.S.barrier()
    if stop == "B":
        P.op("sp", lambda e: [e.dma_start(out=DBG[:, :], in_=HRall[3 * 2048:4 * 2048, 0:NT]), e.dma_start(out=DBF[:, :], in_=HF[2, :, 512:1024])],
             reads=[], writes=["dbg"], dma=True, ndma=2, semkey="dbg")
        P.finish(["dbg"])
        return P

    AR.reset(XQ_END)
    FB = AR.bf16([128, 128, 128])
    Bp = AR.bf16([128, 128, 2, 128])
    s1 = [AR.f32([128, 256]) for _ in range(2)]
    t1 = [AR.f32([128, 128]) for _ in range(2)]
    t2 = [AR.f32([128, 128]) for _ in range(2)]
    t3 = [AR.f32([128, 128]) for _ in range(2)]
    t4 = [AR.f32([128, 128]) for _ in range(2)]
    p1 = [PS[5], PS[6]]
    p2 = [PS[1], PS[2]]
    ptrs = [PS[7][:, :].bitcast(BF16)[:, 0:128], PS[0][:, :].bitcast(BF16)[:, 0:128]]
    ptrk = [("ps7", 0), "ss_ps"]
    Tr = tw[:, 0:128]
    Ti = tw[:, 128:256]
    ident = tbb[:, 4, 0:128]
    qkeys = []
    for ch in range(2):
        for c in range(128):
            b = c % 2
            P.op("pe", (lambda ch, c, b: lambda e: e.transpose(out=ptrs[b], in_=Xq[:, ch, :, c], identity=ident))(ch, c, b),
                 reads=[("SEL", ch), "tbb"], writes=[ptrk[b]])
            P.op("act" if b else "dve", (lambda c, b: lambda e: (e.copy if b else e.tensor_copy)(out=FB[:, :, c], in_=ptrs[b]))(c, b),
                 reads=[ptrk[b]], writes=["FB"])
        if stop == "C0":
            P.op("sp", lambda e: e.dma_start(out=DBG[0:128, :], in_=FB[:, 0:4, :].rearrange("p a b -> p (a b)")), reads=["FB"], writes=["dbg"], dma=True, semkey="dbg")
            P.op("sp", lambda e: e.dma_start(out=DBF[:, :], in_=HF[2, :, 512:1024]), reads=[], writes=["dbg2"], dma=True, semkey="dbg2")
            P.finish(["dbg", "dbg2"])
            return P
        for c in range(128):
            b = c % 2
            P.op("pe", (lambda c, b: lambda e: e.matmul(p1[b][:, 0:256], lhsT=FB[:, :, c], rhs=tbb[:, 0, :], start=True, stop=True))(c, b),
                 reads=["FB", "tbb"], writes=[("ps", 5 + b)])
            P.op("act", (lambda b: lambda e: e.copy(out=s1[b], in_=p1[b][:, 0:256]))(b), reads=[("ps", 5 + b)], writes=[("s1", b)])
            P.op("dve", (lambda b: lambda e: e.tensor_tensor(out=t1[b], in0=s1[b][:, 0:128], in1=Tr, op=ALU.mult))(b),
                 reads=[("s1", b), "tw"], writes=[("t1", b)])
            P.op("dve", (lambda b: lambda e: e.tensor_tensor(out=t2[b], in0=s1[b][:, 128:256], in1=Ti, op=ALU.mult))(b),
                 reads=[("s1", b), "tw"], writes=[("t2", b)])
            P.op("dve", (lambda b, c: lambda e: e.tensor_tensor(out=Bp[:, c, 0, :], in0=t1[b], in1=t2[b], op=ALU.subtract))(b, c),
                 reads=[("t1", b), ("t2", b)], writes=[("Bp", c)])
            P.op("pool", (lambda b: lambda e: e.tensor_tensor(out=t3[b], in0=s1[b][:, 0:128], in1=Ti, op=ALU.mult))(b),
                 reads=[("s1", b), "tw"], writes=[("t3", b)])
            P.op("pool", (lambda b: lambda e: e.tensor_tensor(out=t4[b], in0=s1[b][:, 128:256], in1=Tr, op=ALU.mult))(b),
                 reads=[("s1", b), "tw"], writes=[("t4", b)])
            P.op("pool", (lambda b, c: lambda e: e.tensor_tensor(out=Bp[:, c, 1, :], in0=t3[b], in1=t4[b], op=ALU.add))(b, c),
                 reads=[("t3", b), ("t4", b)], writes=[("Bp", c)])
        bpk = [("Bp", c) for c in range(128)]
        if stop == "C1":
            P.op("sp", lambda e: e.dma_start(out=DBG[0:128, :], in_=Bp[:, 0:2, :, :].rearrange("p a b c -> p (a b c)")), reads=bpk, writes=["dbg"], dma=True, semkey="dbg")
            P.op("sp", lambda e: e.dma_start(out=DBF[:, :], in_=HF[2, :, 512:1024]), reads=[], writes=["dbg2"], dma=True, semkey="dbg2")
            P.finish(["dbg", "dbg2"])
            return P
        for ri in range(2):
            for j in range(128):
                b = j % 2
                P.op("pe", (lambda j, b, ri: lambda e: e.matmul(p2[b][:, 0:128], lhsT=Bp[:, :, 0, j], rhs=tbb[:, 1, ri * 128:(ri + 1) * 128], start=True, stop=False))(j, b, ri),
                     reads=bpk + ["tbb"], writes=[("pp", b)])
                P.op("pe", (lambda j, b, ri: lambda e: e.matmul(p2[b][:, 0:128], lhsT=Bp[:, :, 1, j], rhs=tbb[:, 2, ri * 128:(ri + 1) * 128], start=False, stop=True))(j, b, ri),
                     reads=bpk + ["tbb"], writes=[("pp", b)])
                P.op("act" if b else "dve", (lambda j, b: lambda e: (e.copy if b else e.tensor_copy)(out=FB[:, :, j], in_=p2[b][:, 0:128]))(j, b),
                     reads=[("pp", b)], writes=["FB"])
            k = ("Q", ch, ri)
            SEL = Xq[:, ch, :, :].rearrange("p a b -> p (a b)")
            P.op("dve", (lambda SEL: lambda e: e.tensor_scalar_mul(out=SEL.rearrange("p (b k x) -> p b k x", b=2, x=64),
                                                                  in0=FB.rearrange("p k (b x) -> p b k x", b=2),
                                                                  scalar1=cs[:, MK + 4:MK + 5]))(SEL),
                 reads=["FB"], writes=[("SEL", ch)])
            P.op("dve", (lambda SEL: lambda e: e.scalar_tensor_tensor(out=SEL, in0=FB.rearrange("p a b -> p (a b)"), scalar=cs[:, MK + 3:MK + 4],
                                                                     in1=SEL, op0=ALU.mult, op1=ALU.add))(SEL),
                 reads=["FB", ("SEL", ch)], writes=[("SEL", ch)])
            P.op("sp", (lambda ch, ri, SEL: lambda e: e.dma_start(out=Qin[ch * 2048:(ch + 1) * 2048, :].rearrange("(s i c) n -> c s i n", i=2, c=128)[:, :, ri, :],
                                                                  in_=SEL.rearrange("p (s n) -> p s n", n=4 * NT)))(ch, ri, SEL),
                 reads=[("SEL", ch)], writes=[k], dma=True, semkey=("SEL", ch))
            qkeys.append(k)
        if stop != "C2":
            for sq_ in range(8):
                idx = ch * 8 + sq_
                P.op("pool", (lambda idx: lambda e: e.collective_compute("AllGather", ALU.bypass, replica_groups=RG,
                                                                         ins=[Qin[idx * 256:(idx + 1) * 256, :].opt()],
                                                                         outs=[Qall[idx * 1024:(idx + 1) * 1024, :].opt()]))(idx),
                     reads=[("Q", ch, 0), ("Q", ch, 1)], writes=[("Qall", idx)], dma=True, inc=1, semkey="agQ")
    if stop == "C2":
        P.op("sp", lambda e: e.dma_start(out=DBG[0:512, :], in_=Qin[512:1024, :]), reads=qkeys, writes=["dbg"], dma=True, semkey="dbg")
        P.op("sp", lambda e: e.dma_start(out=DBF[:, :], in_=HF[2, :, 512:1024]), reads=[], writes=["dbg2"], dma=True, semkey="dbg2")
        P.finish(["dbg", "dbg2"])
        return P
    P.S.barrier()
    if stop == "C":
        P.op("sp", lambda e: [e.dma_start(out=DBG[:, :], in_=Qall[2048:4096, 0:NT]), e.dma_start(out=DBF[:, :], in_=HF[2, :, 512:1024])],
             reads=[], writes=["dbg"], dma=True, ndma=2, semkey="dbg")
        P.finish(["dbg"])
        return P

    AR.reset(0)
    X32 = AR.f32([128, KC, NT])
    NTb = AR.bf16([128, KC, NT + 2])
    QTb = AR.bf16([128, 16, NT])
    MT = QTb
    FT = AR.bf16([128, 8, NT])
    BIG = AR.bf16([128, 48, NT])
    WB = [AR.bf16([128, KC * 512]) for _ in range(3)]
    e1 = [AR.f32([128, NT]) for _ in range(2)]
    e2 = [AR.f32([128, NT]) for _ in range(2)]
    upS = [AR.f32([128, NT + 2]) for _ in range(2)]
    upH = [AR.f32([128, 2]) for _ in range(2)]
    cv = e2
    gg = [AR.f32([128, NT]) for _ in range(2)]
    pp = [PS[1], PS[2], PS[3], PS[4]]
    ph = [PS[5], PS[6]]
    wcnt = [0]

    def wload(nm, ci, kcn, cols):
        i = wcnt[0] % 3
        wcnt[0] += 1
        view = WB[i][:, 0:kcn * cols].rearrange("p (k m) -> p k m", k=kcn)
        P.op("sp", lambda e: e.dma_start(out=WB[i][:, 0:kcn * cols], in_=wbf[nm][ci, :, :]), reads=["in"], writes=[("WB", i)], dma=True)
        return view, ("WB", i)

    def linear(wview, wkey, mloc, kcn, rhs_fn, rkeys, N):
        b = ppb()
        for kc in range(kcn):
            P.op("pe", (lambda kc, b: lambda e: e.matmul(pp[b][:, :N], lhsT=wview[:, kc, mloc * 128:(mloc + 1) * 128], rhs=rhs_fn(kc),
                                                       start=(kc == 0), stop=(kc == kcn - 1)))(kc, b),
                 reads=list(rkeys) + [wkey], writes=[("pp", b)])
        return b

    ecnt = [0]
    nbkeys = []

    def p3a(t):
        t0 = t * NT
        N = NT

        P.load(X32, xo[:, :, t0:t0 + NT], "X32")
        def ldh(e):
            j, g = ids(e)
            return e.dma_start(out=BIG[:, 0:16, :], in_=HRall[bass.ds(j * 8192 + (t // 2) * 2048, 2048), (t % 2) * NT:(t % 2 + 1) * NT].rearrange("(k p) n -> p k n", p=128))
        P.op("sp", ldh, reads=[], writes=["BIG0"], dma=True)

        def ldq(e):
            j, g = ids(e)
            return [e.dma_start(out=QTb[:, ch_ * 8:(ch_ + 1) * 8, :],
                                in_=Qall[bass.ds(j * 2048 + (ch_ * 8192 + (t // 4) * 1024), 1024), (t % 4) * NT:(t % 4 + 1) * NT].rearrange("(k c) n -> c k n", c=128))
                    for ch_ in range(2)]
        P.op("act", ldq, reads=[], writes=["QTb"], dma=True, ndma=2)
        rmsnorm_fm(P, "a", X32, "X32", N, cs[:, G_MIX:G_MIX + 16], NTb, "NTb", **nb)
        for part in range(3):
            for cq in range(4):
                wv, wk = wload("wgz", part * 4 + cq, KC, 512)
                for ml in range(4):
                    mt = cq * 4 + ml
                    b = linear(wv, wk, ml, KC, lambda kc: NTb[:, kc, :N], ["NTb"], N)
                    bias = cs[:, B_GZ + part * 16 + mt:B_GZ + part * 16 + mt + 1]
                    if part == 0:
                        i = ecnt[0] % 2
                        ecnt[0] += 1
                        P.op("act", (lambda b, i, bias: lambda e: e.activation(out=e1[i][:, :N], in_=pp[b][:, :N], func=AF.Gelu_apprx_tanh, bias=bias, scale=1.0))(b, i, bias),
                             reads=[("pp", b)], writes=[("e1", i)])
                        P.op("dve", (lambda mt, i: lambda e: e.tensor_tensor(out=BIG[:, mt, :N], in0=BIG[:, mt, :N], in1=e1[i][:, :N], op=ALU.mult))(mt, i),
                             reads=[("e1", i), "BIG0"], writes=["BIG0"])
                    else:
                        P.op("act", (lambda b, mt, bias, part: lambda e: e.activation(out=BIG[:, part * 16 + mt, :N], in_=pp[b][:, :N], func=AF.Sigmoid, bias=bias, scale=1.0))(b, mt, bias, part),
                             reads=[("pp", b)], writes=["BIG%d" % part])
        for g4 in range(4):
            for ct in range(2):
                b = ppb()
                n = 0
                for chh in range(2):
                    for ri in range(2):
                        P.op("pe", (lambda g4, ct, chh, ri, b, n: lambda e: e.matmul(pp[b][:, :N], lhsT=dftb[:, chh, ri, ct * 128:(ct + 1) * 128],
                                                                                     rhs=QTb[:, chh * 8 + g4 * 2 + ri, :N], start=(n == 0), stop=(n == 3)))(g4, ct, chh, ri, b, n),
                             reads=["QTb", "dftb"], writes=[("pp", b)])
                        n += 1
                P.op("act", (lambda g4, ct, b: lambda e: e.copy(out=FT[:, g4 * 2 + ct, :N], in_=pp[b][:, :N]))(g4, ct, b),
                     reads=[("pp", b)], writes=["FT"])
        for cq in range(4):
            wva, wka = wload("woa", cq, KC, 512)
            wvb, wkb = wload("wob", cq, 8, 512)
            for ml in range(4):
                mt = cq * 4 + ml
                ba = linear(wva, wka, ml, KC, lambda kc: BIG[:, kc, :N], ["BIG0"], N)
                bb = linear(wvb, wkb, ml, 8, lambda kc: FT[:, kc, :N], ["FT"], N)
                i = ecnt[0] % 2
                ecnt[0] += 1
                P.op("dve", (lambda ba, mt, i: lambda e: e.tensor_tensor(out=e1[i][:, :N], in0=pp[ba][:, :N], in1=BIG[:, 16 + mt, :N], op=ALU.mult))(ba, mt, i),
                     reads=[("pp", ba), "BIG1"], writes=[("e1", i)])
                P.op("dve", (lambda bb, mt, i: lambda e: e.tensor_tensor(out=e2[i][:, :N], in0=pp[bb][:, :N], in1=BIG[:, 32 + mt, :N], op=ALU.mult))(bb, mt, i),
                     reads=[("pp", bb), "BIG2"], writes=[("e2", i)])
                P.op("pool", (lambda mt, i: lambda e: e.tensor_tensor(out=MT[:, mt, :N], in0=e1[i][:, :N], in1=e2[i][:, :N], op=ALU.add))(mt, i),
                     reads=[("e1", i), ("e2", i)], writes=["QTb"])
        for cq in range(4):
            wv, wk = wload("wo", cq, KC, 512)
            for ml in range(4):
                mt = cq * 4 + ml
                b = linear(wv, wk, ml, KC, lambda kc: MT[:, kc, :N], ["QTb"], N)
                P.op("dve", (lambda b, mt: lambda e: e.scalar_tensor_tensor(out=X32[:, mt, :N], in0=pp[b][:, :N], scalar=cs[:, B_OUT + mt:B_OUT + mt + 1],
                                                                           in1=X32[:, mt, :N], op0=ALU.add, op1=ALU.add))(b, mt),
                     reads=[("pp", b), "X32"], writes=["X32"])
        P.op("sp", lambda e: e.dma_start(out=HS[:, :, t0:t0 + N].rearrange("k p n -> p k n"), in_=X32),
             reads=["X32"], writes=[("HS", t0)], dma=True, semkey="X32")
        rmsnorm_fm(P, "b", X32, "X32", N, cs[:, G_FFN:G_FFN + 16], NTb, "NTb", **nb)
        P.op("sp", lambda e: e.dma_start(out=N2S[:, :, 1 + t0:1 + t0 + N].rearrange("k p n -> p k n"), in_=NTb[:, :, :N]),
             reads=["NTb"], writes=[("N2S", t0)], dma=True, semkey="NTb")
        if t == 0:
            P.op("sp", lambda e: e.dma_start(out=NBin[:, 0:1].rearrange("(k p) n -> p k n", p=128), in_=NTb[:, :, 0:1], allow_slow_non_contiguous=True),
                 reads=["NTb"], writes=[("NB", 0)], dma=True, semkey="NTb")
            nbkeys.append(("NB", 0))
        if t == NTL - 1:
            P.op("sp", lambda e: e.dma_start(out=NBin[:, 1:2].rearrange("(k p) n -> p k n", p=128), in_=NTb[:, :, NT - 1:NT], allow_slow_non_contiguous=True),
                 reads=["NTb"], writes=[("NB", 1)], dma=True, semkey="NTb")
            nbkeys.append(("NB", 1))

    def p3b(t):
        t0 = t * NT
        rk = [("N2S", t0)]
        rk.append(("N2S", t0 - NT) if t > 0 else ("N2S", "halo"))
        rk.append(("N2S", t0 + NT) if t < NTL - 1 else ("N2S", "halo"))
        P.op("sp", lambda e: e.dma_start(out=NTb, in_=N2S[:, :, t0:t0 + NT + 2].rearrange("k p n -> p k n")),
             reads=rk, writes=["NTb"], dma=True)
        P.op("sp", lambda e: e.dma_start(out=X32, in_=HS[:, :, t0:t0 + NT].rearrange("k p n -> p k n")),
             reads=[("HS", t0)], writes=["X32"], dma=True)
        mL = cs[:, MK:MK + 1] if t == 0 else cs[:, MK + 2:MK + 3]
        mR = cs[:, MK + 1:MK + 2] if t == NTL - 1 else cs[:, MK + 2:MK + 3]
        for q in range(24):
            wv, wk = wload("wup", q, KC, 512)
            for ml in range(4):
                gv, jj = ml // 2, ml % 2
                jp = 2 * q + jj
                mt = jp + 48 * gv
                b = ppb()
                hbk = b % 2
                for kc in range(KC):
                    P.op("pe", (lambda kc, b, ml, wv: lambda e: e.matmul(pp[b][:], lhsT=wv[:, kc, ml * 128:(ml + 1) * 128], rhs=NTb[:, kc, 1:NT + 1],
                                                                       start=(kc == 0), stop=(kc == KC - 1)))(kc, b, ml, wv),
                         reads=["NTb", wk], writes=[("pp", b)])
                for kc in range(KC):
                    P.op("pe", (lambda kc, hbk, ml, wv: lambda e: e.matmul(ph[hbk][:, 0:2], lhsT=wv[:, kc, ml * 128:(ml + 1) * 128], rhs=NTb[:, kc, 0:NT + 2:NT + 1],
                                                                         start=(kc == 0), stop=(kc == KC - 1)))(kc, hbk, ml, wv),
                         reads=["NTb", wk], writes=[("ps", 5 + hbk)])
                i = ecnt[0] % 2
                ecnt[0] += 1
                bias = cs[:, B_UP + mt:B_UP + mt + 1]
                P.op("act", (lambda b, i, bias: lambda e: e.activation(out=upS[i][:, 1:NT + 1], in_=pp[b][:], func=AF.Identity, bias=bias, scale=1.0))(b, i, bias),
                     reads=[("pp", b)], writes=[("upS", i)])
                P.op("act", (lambda hbk, i, bias: lambda e: e.activation(out=upH[i], in_=ph[hbk][:, 0:2], func=AF.Identity, bias=bias, scale=1.0))(hbk, i, bias),
                     reads=[("ps", 5 + hbk)], writes=[("upH", i)])
                P.op("dve", (lambda i, mL: lambda e: e.tensor_scalar_mul(out=upS[i][:, 0:1], in0=upH[i][:, 0:1], scalar1=mL))(i, mL),
                     reads=[("upH", i)], writes=[("upS", i)])
                P.op("dve", (lambda i, mR: lambda e: e.tensor_scalar_mul(out=upS[i][:, NT + 1:NT + 2], in0=upH[i][:, 1:2], scalar1=mR))(i, mR),
                     reads=[("upH", i)], writes=[("upS", i)])
                w0 = cs[:, CFW + mt * 3:CFW + mt * 3 + 1]
                w1 = cs[:, CFW + mt * 3 + 1:CFW + mt * 3 + 2]
                w2 = cs[:, CFW + mt * 3 + 2:CFW + mt * 3 + 3]
                cb = cs[:, CFB + mt:CFB + mt + 1]
                P.op("dve", (lambda i, w0, cb: lambda e: e.tensor_scalar(out=cv[i], in0=upS[i][:, 0:NT], scalar1=w0, scalar2=cb, op0=ALU.mult, op1=ALU.add))(i, w0, cb),
                     reads=[("upS", i)], writes=[("e2", i)])
                P.op("dve", (lambda i, w1: lambda e: e.scalar_tensor_tensor(out=cv[i], in0=upS[i][:, 1:NT + 1], scalar=w1, in1=cv[i], op0=ALU.mult, op1=ALU.add))(i, w1),
                     reads=[("upS", i), ("e2", i)], writes=[("e2", i)])
                P.op("dve", (lambda i, w2: lambda e: e.scalar_tensor_tensor(out=cv[i], in0=upS[i][:, 2:NT + 2], scalar=w2, in1=cv[i], op0=ALU.mult, op1=ALU.add))(i, w2),
                     reads=[("upS", i), ("e2", i)], writes=[("e2", i)])
                if gv == 0:
                    P.op("act", (lambda i, jj: lambda e: e.activation(out=gg[jj], in_=cv[i], func=AF.Gelu_apprx_tanh))(i, jj),
                         reads=[("e2", i)], writes=[("gg", jj)])
                else:
                    P.op("pool", (lambda i, jj, jp: lambda e: e.tensor_tensor(out=BIG[:, jp, :], in0=gg[jj], in1=cv[i], op=ALU.mult))(i, jj, jp),
                         reads=[("gg", jj), ("e2", i)], writes=["BIG%d" % (jp // 16)])
        for mt in range(KC):
            wv, wk = wload("wdn", mt, 48, 128)
            b = linear(wv, wk, 0, 48, lambda kc: BIG[:, kc, :], ["BIG0", "BIG1", "BIG2"], NT)
            P.op("dve", (lambda b, mt: lambda e: e.scalar_tensor_tensor(out=X32[:, mt, :], in0=pp[b][:], scalar=cs[:, B_DN + mt:B_DN + mt + 1],
                                                                       in1=X32[:, mt, :], op0=ALU.add, op1=ALU.add))(b, mt),
                 reads=[("pp", b), "X32"], writes=["X32"])
        rmsnorm_fm(P, "c", X32, "X32", NT, cs[:, G_FIN:G_FIN + 16], X32, "X32", scratch=NTb, skey="NTb", **nb)
        k = ("YT", t)
        P.op("sp", lambda e: e.dma_start(out=YT[:, :, t0:t0 + NT].rearrange("k p n -> p k n"), in_=X32),
             reads=["X32"], writes=[k], dma=True)
        return k

    for t in range(NTL):
        p3a(t)
    P.op("pool", lambda e: e.collective_compute("AllGather", ALU.bypass, replica_groups=RG, ins=[NBin_h.ap().opt()], outs=[NBall_h.ap().opt()]),
         reads=nbkeys, writes=["NBall"], dma=True, inc=1, semkey="agNB")

    hl = P.sb("hl", [128, KC, 2], BF16)

    def ldhalo(e):
        j, g = ids(e)
        rl = ((j + 3) % 4) * 2048
        rr = ((j + 1) % 4) * 2048
        return [e.dma_start(out=hl[:, :, 0:1], in_=NBall[bass.ds(rl, 2048), 1:2].rearrange("(k p) n -> p k n", p=128), allow_slow_non_contiguous=True),
                e.dma_start(out=hl[:, :, 1:2], in_=NBall[bass.ds(rr, 2048), 0:1].rearrange("(k p) n -> p k n", p=128), allow_slow_non_contiguous=True)]
    P.op("pool", ldhalo, reads=["NBall"], writes=["hl"], dma=True, ndma=2, semkey="halo")
    P.op("sp", lambda e: [e.dma_start(out=N2S[:, :, 0:1].rearrange("k p n -> p k n"), in_=hl[:, :, 0:1], allow_slow_non_contiguous=True),
                          e.dma_start(out=N2S[:, :, TOK + 1:TOK + 2].rearrange("k p n -> p k n"), in_=hl[:, :, 1:2], allow_slow_non_contiguous=True)],
         reads=["hl"], writes=[("N2S", "halo")], dma=True, ndma=2, semkey="halo2")
    fin = [p3b(t) for t in range(NTL)]
    P.finish(fin)
    return P


def fm(a):
    a = np.asarray(a, np.float32)
    return np.ascontiguousarray(a.reshape(-1, 128).T)


def wfm(w):
    K, M = w.shape
    return np.ascontiguousarray(w.reshape(K // 128, 128, M).transpose(1, 0, 2))


def xfm(x):
    T, Fd = x.shape
    return np.ascontiguousarray(x.T.reshape(Fd // 128, 128, T).transpose(1, 0, 2))


def run(P, in_maps, names=None):
    if names is not None:
        in_maps = [{k: v for k, v in m.items() if k in names} for m in in_maps]
    res = run_bass_kernel_spmd(P.nc, in_maps, core_ids=list(range(8)))
    return res.results


def dft_tables(seq_len):
    nb = TS // seq_len
    n2n = seq_len // 128
    p = np.arange(128)
    bidx, n2 = p // n2n, p % n2n
    j = np.arange(128)
    bj, k2 = j // n2n, j % n2n
    ang = 2 * np.pi * np.outer(n2, k2) / n2n
    same = (bidx[:, None] == bj[None, :]).astype(np.float64)
    sc = 1.0 / np.sqrt(seq_len)
    T1 = np.concatenate([np.cos(ang) * same, -np.sin(ang) * same], 1) * sc
    n1 = np.arange(128)
    a2 = 2 * np.pi * np.outer(n1, n1) / 128.0
    Gr, Gi = np.cos(a2), -np.sin(a2)
    G1 = np.concatenate([Gr, Gi], 1)
    G2 = np.concatenate([-Gi, Gr], 1)
    at = 2 * np.pi * np.outer(n1, k2) / seq_len
    TW = np.concatenate([np.cos(at), -np.sin(at)], 1)
    return np.ascontiguousarray(np.stack([T1, G1, G2, TW], 1).astype(np.float32))


_PROGS = {}
_STOP = None


def kernel(x_prompt, x_sample, g_mix, w_in, b_in, conv_a_w, conv_a_b, lru_w_a, lru_b_a, lru_w_x, lru_b_x,
           lru_lam, w_out_a, w_out_b, w_out, b_out, g_ffn, w_up, b_up, conv_f_w, conv_f_b, w_down, b_down,
           g_final):
    f32 = np.float32
    xs = [np.asarray(x_prompt, f32).reshape(TS, D), np.asarray(x_sample, f32).reshape(TS, D)]
    seqlen = [16384, 8192]
    w_in = np.asarray(w_in, f32)[0]
    b_in = np.asarray(b_in, f32)[0]
    if "f" not in _PROGS:
        _PROGS["f"] = build_fused(_STOP)
    caw = np.asarray(conv_a_w, f32)[0]
    cab = np.asarray(conv_a_b, f32)[0]
    lwa = np.asarray(lru_w_a, f32)[0]
    lwx = np.asarray(lru_w_x, f32)[0]
    lba = np.asarray(lru_b_a, f32)[0]
    lbx = np.asarray(lru_b_x, f32)[0]
    lam = np.asarray(lru_lam, f32)[0]
    bgz = np.concatenate([fm(b_in[2048:4096]), fm(b_in[5120:7168]), fm(b_in[7168:9216])], 1)
    wgz_full = np.concatenate([w_in[:, 2048:4096], w_in[:, 5120:9216]], 1)
    wgz = np.stack([wfm(wgz_full[:, q * 512:(q + 1) * 512]) for q in range(12)], 0)
    woa_ = np.asarray(w_out_a, f32)[0]
    wob_ = np.asarray(w_out_b, f32)[0]
    wo_ = np.asarray(w_out, f32)[0]
    wup_ = np.asarray(w_up, f32)[0]
    wdn_ = np.asarray(w_down, f32)[0]
    woa = np.stack([wfm(woa_[:, q * 512:(q + 1) * 512]) for q in range(4)], 0)
    wob = np.stack([wfm(wob_[:, q * 512:(q + 1) * 512]) for q in range(4)], 0)
    wo = np.stack([wfm(wo_[:, q * 512:(q + 1) * 512]) for q in range(4)], 0)
    wupc = []
    for q in range(24):
        cols = np.concatenate([wup_[:, 256 * q:256 * q + 256], wup_[:, 6144 + 256 * q:6144 + 256 * q + 256]], 1)
        wupc.append(wfm(cols))
    wupc = np.stack(wupc, 0)
    wdn = np.stack([wfm(wdn_[:, m * 128:(m + 1) * 128]) for m in range(16)], 0)
    cc = np.arange(256)
    ang = 2 * np.pi * np.outer(cc, cc) / 256.0
    dm = np.stack([np.cos(ang), np.sin(ang)], 0) / 16.0
    dft = np.ascontiguousarray(dm.reshape(2, 2, 128, 256).transpose(2, 1, 0, 3).astype(f32)).reshape(128, 1024)
    cfw = np.asarray(conv_f_w, f32)[0]
    c3 = np.zeros((128, 640), f32)
    c3[:, 0:16] = fm(np.asarray(g_mix, f32)[0])
    c3[:, 16:32] = fm(np.asarray(g_ffn, f32)[0])
    c3[:, 32:48] = fm(np.asarray(g_final, f32))
    c3[:, 48:96] = bgz
    c3[:, 96:112] = fm(np.asarray(b_out, f32)[0])
    c3[:, 112:128] = fm(np.asarray(b_down, f32)[0])
    c3[:, 128:224] = fm(np.asarray(b_up, f32)[0])
    c3[:, 224:512] = cfw.reshape(3, 96, 128).transpose(2, 1, 0).reshape(128, 288)
    c3[:, 512:608] = fm(np.asarray(conv_f_b, f32)[0])
    c3[:, 610] = 1.0
    xgs = [xfm(xs[g]) for g in range(2)]
    ident = np.concatenate([np.eye(128, dtype=f32), np.zeros((128, 128), f32)], 1)
    ins = []
    for c in range(8):
        g, j = c // 4, c % 4
        L = seqlen[g]
        chs = slice(512 * j, 512 * j + 512)
        fch = slice(4096 + 256 * j, 4096 + 256 * j + 256)
        heads = slice(4 * j, 4 * j + 4)
        wg = np.stack([np.stack([lwa[d, heads], lwx[d, heads]], 0) for d in range(2)], 0)
        wg = np.ascontiguousarray(wg.transpose(3, 0, 1, 2, 4)).reshape(128, 2048)
        c2 = np.zeros((128, 64), f32)
        c2[:, 0:16] = caw[:, chs].reshape(4, 4, 128).transpose(2, 1, 0).reshape(128, 16)
        c2[:, 16:20] = fm(cab[chs])
        c2[:, 20:28] = np.concatenate([fm(lba[0, chs]), fm(lba[1, chs])], 1)
        c2[:, 28:36] = np.concatenate([fm(lbx[0, chs]), fm(lbx[1, chs])], 1)
        c2[:, 36:44] = np.concatenate([fm(lam[0, chs]), fm(lam[1, chs])], 1)
        c2[:, 44] = 1.0 if L == TS else 0.0
        c2[:, 48:52] = fm(b_in[chs])
        tb = np.concatenate([dft_tables(L), ident[:, None, :]], 1)
        lo, hi = j * TOK, (j + 1) * TOK
        cm = c3.copy()
        cm[:, 608] = 1.0 if (lo % L) != 0 else 0.0
        cm[:, 609] = 1.0 if (hi % L) != 0 else 0.0
        cm[:, 611] = 1.0 if L == TS else 0.0
        cm[:, 612] = 0.0 if L == TS else 1.0
        ins.append({"xg": xgs[g], "xo": np.ascontiguousarray(xgs[g][:, :, lo:hi]), "wx": wfm(w_in[:, chs]), "wf": wfm(w_in[:, fch]),
                    "bfT": np.ascontiguousarray(np.tile(b_in[fch][None, :], (128, 1))),
                    "wg": wg, "c2": c2, "tb": np.ascontiguousarray(tb), "wgz": wgz, "woa": woa, "wob": wob, "wo": wo,
                    "wup": wupc, "wdn": wdn, "dft": dft, "c3": cm})
    if _STOP is not None:
        return run(_PROGS["f"], ins, names={"xg", "xo", "wx", "wf", "bfT", "wg", "c2", "tb", "dft", "c3"})
    r = run(_PROGS["f"], ins)
    ys = []
    for g in range(2):
        yt = np.concatenate([r[g * 4 + j]["YT"] for j in range(4)], 2)
        ys.append(np.ascontiguousarray(yt.reshape(D, TS).T))
    return (ys[0].reshape(1, 16384, D).astype(f32), ys[1].reshape(2, 8192, D).astype(f32))
```

```python
import contextlib
import numpy as np
import ml_dtypes
import concourse.bass as bass
import concourse.mybir as mybir
from concourse.bass_utils import run_bass_kernel_spmd

F32 = mybir.dt.float32
BF16 = mybir.dt.bfloat16
AF = mybir.ActivationFunctionType
ALU = mybir.AluOpType
NPBF = ml_dtypes.bfloat16

D = 2048
KC = 16
TOK = 4096
TS = 16384
NT = 512
EPS = 1e-6


class Sched:
    def __init__(self, nc):
        self.nc = nc
        self.ops = []
        self.state = {}
        self.bar_from = 0

    def barrier(self):
        last = {}
        dmas = []
        for i, o in enumerate(self.ops[self.bar_from:], self.bar_from):
            if o["dma"]:
                dmas.append(i)
            elif o["fn"] is not None:
                last[o["eng"]] = i
        deps = sorted(set(dmas) | set(last.values()))
        for eng in ("pe", "act", "dve", "pool", "sp"):
            self.ops.append(dict(eng=eng, fn=None, deps=list(deps), dma=False, semkey=None, signal=False, ndma=1, inc=16,
                                 bar=True))
        self.bar_from = len(self.ops)
        self.state = {}

    def op(self, eng, fn, reads=(), writes=(), dma=False, semkey=None, ndma=1, inc=16):
        oid = len(self.ops)
        deps = set()
        for k in reads:
            st = self.state.setdefault(k, [None, []])
            if st[0] is not None:
                deps.add(st[0])
        for k in writes:
            st = self.state.setdefault(k, [None, []])
            if st[0] is not None:
                deps.add(st[0])
            last = {}
            for r in st[1]:
                o = self.ops[r]
                if o["dma"]:
                    deps.add(r)
                else:
                    last[o["eng"]] = max(last.get(o["eng"], -1), r)
            deps.update(last.values())
        for k in reads:
            self.state[k][1].append(oid)
        for k in writes:
            self.state[k] = [oid, []]
        deps.discard(oid)
        if dma and semkey is None:
            semkey = writes[0] if len(writes) else reads[0]
        self.ops.append(dict(eng=eng, fn=fn, deps=sorted(deps), dma=dma, semkey=semkey,
                             signal=dma, ndma=ndma, inc=inc))
        return oid

    def emit(self, final_keys=()):
        nc = self.nc
        ops = self.ops
        self.op("sp", None, reads=list(final_keys))
        for o in ops:
            for d in o["deps"]:
                od = ops[d]
                if od["dma"]:
                    continue
                if od["eng"] == "pe" and o["eng"] == "pe" and not o["dma"]:
                    continue
                od["signal"] = True
        cnt = {}
        for o in ops:
            if o["dma"]:
                k = ("d", o["semkey"])
                cnt[k] = cnt.get(k, 0) + o["inc"] * o["ndma"]
                o["done"] = (k, cnt[k])
            elif o["signal"]:
                k = ("e", o["eng"])
                cnt[k] = cnt.get(k, 0) + 1
                o["done"] = (k, cnt[k])
        semkeys = sorted(cnt.keys(), key=str)
        with contextlib.ExitStack() as es:
            sems = {}
            for i, k in enumerate(semkeys):
                sems[k] = es.enter_context(nc.semaphore("s%d" % i))
            block = es.enter_context(nc.Block())

            def run(engname):
                def body(e):
                    known = {}
                    for o in ops:
                        if o["eng"] != engname:
                            continue
                        for d in o["deps"]:
                            od = ops[d]
                            if "done" not in od:
                                continue
                            if (not od["dma"]) and od["eng"] == "pe" and engname == "pe" and not o["dma"]:
                                continue
                            k, v = od["done"]
                            if known.get(k, 0) < v:
                                e.wait_ge(sems[k], v)
                                known[k] = v
                        if o["fn"] is None:
                            continue
                        ins = o["fn"](e)
                        if o["dma"]:
                            if not isinstance(ins, (list, tuple)):
                                ins = [ins]
                            assert len(ins) == o["ndma"], (len(ins), o["ndma"])
                            for i_ in ins:
                                i_.then_inc(sems[o["done"][0]], o["inc"])
                        elif o["signal"]:
                            ins.then_inc(sems[o["done"][0]], 1)
                return body

            block.tensor(run("pe"))
            block.scalar(run("act"))
            block.vector(run("dve"))
            block.gpsimd(run("pool"))
            block.sync(run("sp"))


class Prog:
    def __init__(self):
        self.nc = bass.Bass("TRN2", target_bir_lowering=False)
        self.es = contextlib.ExitStack()
        self.S = Sched(self.nc)
        self.outs = []
        self._rr = 0

    def din(self, name, shape, dt=F32):
        return self.nc.dram_tensor(name, list(shape), dt, kind="ExternalInput").ap()

    def dout(self, name, shape, dt=F32):
        self.outs.append(name)
        return self.nc.dram_tensor(name, list(shape), dt, kind="ExternalOutput").ap()

    def dscr(self, name, shape, dt=F32):
        return self.nc.dram_tensor(name, list(shape), dt).ap()

    def sb(self, name, shape, dt=F32):
        return self.es.enter_context(self.nc.sbuf_tensor(name, list(shape), dt))

    def ps(self, name, shape, dt=F32):
        return self.es.enter_context(self.nc.psum_tensor(name, list(shape), dt))

    def op(self, *a, **k):
        return self.S.op(*a, **k)

    def ew(self):
        self._rr ^= 1
        return "dve" if self._rr else "pool"

    def load(self, dst_ap, src_ap, dkey, skey="in", eng="sp"):
        self.op(eng, lambda e: e.dma_start(out=dst_ap, in_=src_ap), reads=[skey], writes=[dkey], dma=True)

    def finish(self, final_keys):
        self.S.emit(final_keys=final_keys)
        self.es.close()


def rmsnorm_fm(P, tag, xT, xkey, N, gcol, outT, okey, ones32, epsT, ss_ps, sq, rt, rstd, scratch=None, skey=None):
    if scratch is None:
        scratch, skey = outT, okey
    P.op("act", lambda e: e.activation(out=scratch[:, :, :N], in_=xT[:, :, :N], func=AF.Square), reads=[xkey], writes=[skey])
    for kc in range(KC):
        P.op("pe", (lambda kc: lambda e: e.matmul(ss_ps[:, :N], lhsT=ones32[:, :], rhs=scratch[:, kc, :N],
                                                  start=(kc == 0), stop=(kc == KC - 1)))(kc),
             reads=[skey, "ones32"], writes=["ss_ps"])
    P.op("act", lambda e: e.activation(out=rt[:, :N], in_=ss_ps[:, :N], func=AF.Sqrt, bias=epsT[:, 0:1], scale=1.0 / D),
         reads=["ss_ps", "epsT"], writes=["rt"])
    P.op("dve", lambda e: e.reciprocal(out=rstd[:, :N], in_=rt[:, :N]), reads=["rt"], writes=["rstd"])
    for kc in range(KC):
        P.op("dve", (lambda kc: lambda e: e.scalar_tensor_tensor(out=outT[:, kc, :N], in0=xT[:, kc, :N],
                                                                scalar=gcol[:, kc:kc + 1], in1=rstd[:, :N],
                                                                op0=ALU.mult, op1=ALU.mult))(kc),
             reads=[xkey, "rstd", "consts"] + ([skey] if skey != okey else []), writes=[okey])


def norm_bufs(P):
    ones32 = P.sb("ones32", [128, 128])
    epsT = P.sb("epsT", [128, 1])
    P.op("dve", lambda e: e.memset(ones32[:], 1.0), writes=["ones32"])
    P.op("dve", lambda e: e.memset(epsT[:], EPS), writes=["epsT"])
    ss_ps = P.ps("ss_ps", [128, NT])
    sq = [P.sb("sq%d" % i, [128, NT]) for i in range(2)]
    rt = P.sb("rt", [128, NT])
    rstd = P.sb("rstd", [128, NT])
    return dict(ones32=ones32, epsT=epsT, ss_ps=ss_ps, sq=sq, rt=rt, rstd=rstd)


class Arena:
    def __init__(self, P, nbytes):
        self.t = P.sb("arena", [128, nbytes // 4])
        self.nbytes = nbytes
        self.off = 0

    def reset(self, off=0):
        self.off = off

    def _take(self, nb):
        nb = (nb + 63) // 64 * 64
        o = self.off
        self.off += nb
        assert self.off <= self.nbytes, (self.off, self.nbytes)
        return self.t[:, o // 4:(o + nb) // 4]

    @staticmethod
    def _shape(ap, shape):
        if len(shape) == 2:
            return ap
        if len(shape) == 3:
            return ap.rearrange("p (a b) -> p a b", a=shape[1])
        if len(shape) == 4:
            return ap.rearrange("p (a b c) -> p a b c", a=shape[1], b=shape[2])
        raise ValueError(shape)

    def f32(self, shape):
        n = int(np.prod(shape[1:]))
        return self._shape(self._take(4 * n)[:, :n], shape)

    def bf16(self, shape):
        n = int(np.prod(shape[1:]))
        return self._shape(self._take(2 * n).bitcast(BF16)[:, :n], shape)


RG = [[0, 1, 2, 3], [4, 5, 6, 7]]
G_MIX, G_FFN, G_FIN, B_GZ, B_OUT, B_DN, B_UP, CFW, CFB, MK = 0, 16, 32, 48, 96, 112, 128, 224, 512, 608


def build_fused(stop=None):
    P = Prog()
    nc = P.nc
    NTL = TOK // NT
    NTG = TS // NT
    HALF = TS // 2
    xg = P.din("xg", [128, KC, TS])
    xo = P.din("xo", [128, KC, TOK])
    wx = P.din("wx", [128, KC, 512])
    wf = P.din("wf", [128, KC, 256])
    bfT = P.din("bfT", [128, 256])
    wg = P.din("wg", [128, 2048])
    c2 = P.din("c2", [128, 64])
    tb = P.din("tb", [128, 5, 256])
    if stop is None:
        wgz = P.din("wgz", [12, 128, KC, 512])
        woa = P.din("woa", [4, 128, KC, 512])
        wob = P.din("wob", [4, 128, 8, 512])
        wo = P.din("wo", [4, 128, KC, 512])
        wup = P.din("wup", [24, 128, KC, 512])
        wdn = P.din("wdn", [16, 128, 48, 128])
        YT = P.dout("YT", [KC, 128, TOK])
        wsrc = dict(wgz=(wgz, 12, KC * 512), woa=(woa, 4, KC * 512), wob=(wob, 4, 8 * 512), wo=(wo, 4, KC * 512),
                    wup=(wup, 24, KC * 512), wdn=(wdn, 16, 48 * 128))
        wbf = {}
        for nm, (src, nchunk, ncol) in wsrc.items():
            wbf[nm] = P.dscr(nm + "_bf", [nchunk, 128, ncol], BF16)
    else:
        DBG = P.dout("DBG", [2048, NT], BF16)
        DBF = P.dout("DBF", [128, NT])
    dft = P.din("dft", [128, 1024])
    c3 = P.din("c3", [128, 640])
    Uscr = P.dscr("Uscr", [4, 128, TS + 3])
    HF = P.dscr("HF", [4, 128, TS])
    HS = P.dscr("HS", [KC, 128, TOK])
    N2S = P.dscr("N2S", [KC, 128, TOK + 2], BF16)
    HRloc = P.dscr("HRloc", [8 * 2048, NT], BF16)
    Qloc = P.dscr("Qloc", [2 * 8 * 2048, NT], BF16)
    HRin_h = nc.dram_tensor("HRin", [16 * 512, 2 * NT], BF16)
    HRall_h = nc.dram_tensor("HRall", [16 * 2048, 2 * NT], BF16)
    Qin_h = nc.dram_tensor("Qin", [2 * 8 * 256, 4 * NT], BF16)
    Qall_h = nc.dram_tensor("Qall", [2 * 8 * 1024, 4 * NT], BF16)
    NBin_h = nc.dram_tensor("NBin", [2048, 2], BF16)
    NBall_h = nc.dram_tensor("NBall", [8192, 2], BF16)
    HRin, HRall, Qin, Qall, NBin, NBall = (h.ap() for h in (HRin_h, HRall_h, Qin_h, Qall_h, NBin_h, NBall_h))

    cs2 = P.sb("cs2", [128, 64])
    cs = P.sb("cs3", [128, 640])
    P.load(cs2[:], c2[:, :], "consts")
    P.load(cs[:], c3[:, :], "consts")
    onesb = P.sb("onesb", [128, 128], BF16)
    epsT = P.sb("epsT", [128, 1])
    P.op("dve", lambda e: e.memset(onesb[:], 1.0), writes=["ones32"])
    P.op("dve", lambda e: e.memset(epsT[:], EPS), writes=["epsT"])
    PS = [P.ps("PS%d" % i, [128, NT]) for i in range(8)]
    sq = None
    rt = P.sb("rt", [128, NT])
    rstd = P.sb("rstd", [128, NT])
    nb = dict(ones32=onesb, epsT=epsT, ss_ps=PS[0], sq=sq, rt=rt, rstd=rstd)
    wgb_ = P.sb("wgb", [128, 2048], BF16)
    P.op("pool", lambda e: e.dma_start(out=wgb_[:], in_=wg[:, :]), reads=["in"], writes=["wgb"], dma=True)
    wgb = wgb_[:].rearrange("p (d a m j) -> p d a m j", d=2, a=2, m=4)
    tbb = P.sb("tbb", [128, 5, 256], BF16)
    P.op("pool", lambda e: e.dma_start(out=tbb[:], in_=tb[:, :, :]), reads=["in"], writes=["tbb"], dma=True)
    tw = P.sb("tw", [128, 256])
    P.load(tw[:], tb[:, 3, :], "tw")
    dftb_ = P.sb("dftb", [128, 1024], BF16)
    P.op("pool", lambda e: e.dma_start(out=dftb_[:], in_=dft[:, :]), reads=["in"], writes=["dftb"], dma=True)
    dftb = dftb_[:].rearrange("p (a b c) -> p a b c", a=2, b=2)
    bft = P.sb("bft", [128, 256])
    P.load(bft[:], bfT[:, :], "bft")
    kt = P.sb("kt", [128, 8]); kl = P.sb("kl", [128, 8]); kl2 = P.sb("kl2", [128, 8])
    st = P.sb("st", [128, 8])
    zt = P.sb("zt", [128, 4])
    AR = Arena(P, 188 * 1024)

    pcnt = [0]

    def ppb():
        b = pcnt[0] % 4
        pcnt[0] += 1
        return b

    pid_cache = {}

    def ids(e):
        k = id(e)
        if k not in pid_cache:
            pid = e.partition_id()
            pid_cache[k] = (pid % 4, pid // 4)
        return pid_cache[k]

    Xq = AR.bf16([128, 2, 128, 128])
    XQ_END = AR.off
    xs = [AR.f32([128, KC, NT]) for _ in range(2)]
    nTs = [AR.bf16([128, KC, NT]) for _ in range(2)]
    wxb = AR.bf16([128, KC, 512])
    wfb = AR.bf16([128, KC, 256])
    ob = [AR.f32([128, NT]) for _ in range(2)]
    P.op("pool", lambda e: e.dma_start(out=wxb, in_=wx[:, :, :]), reads=["in"], writes=["wxb"], dma=True)
    P.op("pool", lambda e: e.dma_start(out=wfb, in_=wf[:, :, :]), reads=["in"], writes=["wfb"], dma=True)
    P.op("dve", lambda e: e.memset(zt[:], 0.0), writes=["zt"])
    P.op("sp", lambda e: [e.dma_start(out=Uscr[mx, :, 0:2], in_=zt[:, 0:2], allow_slow_non_contiguous=True) for mx in range(4)] +
                         [e.dma_start(out=Uscr[mx, :, TS + 2:TS + 3], in_=zt[:, 0:1], allow_slow_non_contiguous=True) for mx in range(4)],
         reads=["zt"], writes=["Upad"], dma=True, ndma=8, semkey="zt")
    ocnt = [0]

    def normA(t):
        x = xs[t % 2]
        xk = ("x", t % 2)
        P.load(x, xg[:, :, t * NT:(t + 1) * NT], xk)
        rmsnorm_fm(P, "a", x, xk, NT, cs[:, G_MIX:G_MIX + 16], nTs[t % 2], ("nT", t % 2), **nb)

    def mmA(t):
        nT = nTs[t % 2]
        nk = ("nT", t % 2)
        for mx in range(4):
            b = ppb()
            o_ = ocnt[0] % 2
            ocnt[0] += 1
            for kc in range(KC):
                P.op("pe", (lambda mx, kc, b, nT: lambda e: e.matmul(PS[1 + b][:], lhsT=wxb[:, kc, mx * 128:(mx + 1) * 128], rhs=nT[:, kc, :],
                                                                   start=(kc == 0), stop=(kc == KC - 1)))(mx, kc, b, nT),
                     reads=[nk, "wxb"], writes=[("pp", b)])
            P.op("act", (lambda mx, b, o_: lambda e: e.activation(out=ob[o_], in_=PS[1 + b][:], func=AF.Identity,
                                                                 bias=cs2[:, 48 + mx:49 + mx], scale=1.0))(mx, b, o_),
                 reads=[("pp", b), "consts"], writes=[("ob", o_)])
            P.op("sp", (lambda mx, o_, t: lambda e: e.dma_start(out=Uscr[mx, :, 2 + t * NT:2 + (t + 1) * NT], in_=ob[o_]))(mx, o_, t),
                 reads=[("ob", o_)], writes=[("U", mx, t)], dma=True, semkey=("ob", o_))
        for blk in range(4):
            b = ppb()
            for kc in range(KC):
                P.op("pe", (lambda blk, kc, b, nT: lambda e: e.matmul(PS[1 + b][:, 0:256], lhsT=nT[:, kc, blk * 128:(blk + 1) * 128], rhs=wfb[:, kc, :],
                                                                    start=(kc == 0), stop=(kc == KC - 1)))(blk, kc, b, nT),
                     reads=[nk, "wfb"], writes=[("pp", b)])
            m = 4 * t + blk
            P.op("dve", (lambda m, b: lambda e: e.tensor_tensor(out=Xq[:, :, m, :], in0=PS[1 + b][:, 0:256].rearrange("p (h c) -> p h c", h=2),
                                                                in1=bft[:, :].rearrange("p (h c) -> p h c", h=2), op=ALU.add))(m, b),
                 reads=[("pp", b), "bft"], writes=["Xq"])

    normA(0)
    for t in range(NTG):
        if t + 1 < NTG:
            normA(t + 1)
        mmA(t)
    P.S.barrier()
    if stop == "A":
        P.op("sp", lambda e: [e.dma_start(out=DBF[:, :], in_=Uscr[1, :, 2 + 512:2 + 1024]),
                              e.dma_start(out=DBG[0:128, :], in_=Xq[:, 0, 5, :].rearrange("p c -> p c") if False else Xq[:, :, 5, :].rearrange("p h c -> p (h c)")[:, 0:256] if False else Xq[:, 0, 0:4, :].rearrange("p m c -> p (m c)"))],
             reads=[], writes=["dbg"], dma=True, ndma=2, semkey="dbg")
        P.finish(["dbg"])
        return P

    AR.reset(XQ_END)
    conv_list = []
    if stop is None:
        shp = "c p k m -> c p (k m)"
        for nm, (src, nchunk, ncol) in wsrc.items():
            for ci in range(nchunk):
                conv_list.append((nm, src, ci))

    def issue_conv(n):
        for _ in range(n):
            if conv_list:
                nm, src, ci = conv_list.pop(0)
                P.op("pool", (lambda nm, src, ci: lambda e: e.dma_start(out=wbf[nm][ci, :, :], in_=src.rearrange(shp)[ci, :, :]))(nm, src, ci),
                     reads=["in"], writes=[("wbf", nm, ci)], dma=True, semkey="wconv")
    P.op("act", lambda e: e.activation(out=kt[:], in_=cs2[:, 36:44], func=AF.Exp, scale=-1.0), reads=[], writes=["kt"])
    P.op("dve", lambda e: e.tensor_scalar_add(out=kt[:], in0=kt[:], scalar1=1.0), reads=["kt"], writes=["kt"])
    P.op("act", lambda e: e.activation(out=kt[:], in_=kt[:], func=AF.Ln), reads=["kt"], writes=["kt"])
    P.op("dve", lambda e: e.tensor_scalar_mul(out=kl[:], in0=kt[:], scalar1=-8.0), reads=["kt"], writes=["kl"])
    P.op("dve", lambda e: e.tensor_scalar_mul(out=kl2[:], in0=kt[:], scalar1=-16.0), reads=["kt"], writes=["kl2"])
    P.op("dve", lambda e: e.memset(st[:], 0.0), writes=[("st", c_) for c_ in range(8)])
    NB = 4
    ut = [[AR.f32([128, NT + 3]) for _ in range(2)] for _ in range(NB)]
    u = [[AR.f32([128, NT]) for _ in range(2)] for _ in range(NB)]
    ub = [[AR.bf16([128, NT])] * 2 for _ in range(NB)]
    r2 = [[AR.f32([128, NT]) for _ in range(2)] for _ in range(NB)]
    i2 = [[AR.f32([128, NT]) for _ in range(2)] for _ in range(NB)]
    a2 = [[AR.f32([128, NT]) for _ in range(2)] for _ in range(NB)]
    m2 = [[AR.f32([128, NT]) for _ in range(2)] for _ in range(NB)]
    h_ = [AR.f32([128, NT]) for _ in range(NB)]
    hf = [AR.f32([128, NT]) for _ in range(NB)]
    hb = [AR.bf16([128, NT]) for _ in range(NB)]
    pg = [PS[1], PS[2], PS[3], PS[4]]
    onesT = P.sb("onesT", [128, 1])
    P.op("dve", lambda e: e.memset(onesT[:], 1.0), writes=["onesT"])
    hrkeys = []

    def stage1(d, t, s_):
        issue_conv(2)
        for mx in range(4):
            b = mx
            utb, uu, ubb = ut[b][s_], u[b][s_], ub[b][s_]
            kut, ku, kub = ("ut", b, s_), ("u", b, s_), ("ub", b)
            P.load(utb, Uscr[mx, :, t * NT:t * NT + NT + 3], kut)
            if t == NTG // 2 - 1:
                P.op("dve", (lambda utb: lambda e: e.tensor_scalar_mul(out=utb[:, NT + 2:NT + 3], in0=utb[:, NT + 2:NT + 3], scalar1=cs2[:, 44:45]))(utb),
                     reads=[kut], writes=[kut])
            if t == NTG // 2:
                P.op("dve", (lambda utb: lambda e: e.tensor_scalar_mul(out=utb[:, 0:2], in0=utb[:, 0:2], scalar1=cs2[:, 44:45]))(utb),
                     reads=[kut], writes=[kut])
            P.op("dve", (lambda utb, uu, mx: lambda e: e.tensor_scalar(out=uu, in0=utb[:, 0:NT], scalar1=cs2[:, mx * 4:mx * 4 + 1],
                                                                       scalar2=cs2[:, 16 + mx:17 + mx], op0=ALU.mult, op1=ALU.add))(utb, uu, mx),
                 reads=[kut], writes=[ku])
            for k in range(1, 4):
                P.op("dve", (lambda utb, uu, mx, k: lambda e: e.scalar_tensor_tensor(out=uu, in0=utb[:, k:k + NT],
                                                                                     scalar=cs2[:, mx * 4 + k:mx * 4 + k + 1], in1=uu,
                                                                                     op0=ALU.mult, op1=ALU.add))(utb, uu, mx, k),
                     reads=[kut, ku], writes=[ku])
            P.op("act", (lambda uu, ubb: lambda e: e.copy(out=ubb, in_=uu))(uu, ubb), reads=[ku], writes=[kub])

    def stage23(d, t, s_):
        r_ = [r2[x][s_] for x in range(NB)]
        i_ = [i2[x][s_] for x in range(NB)]
        a_ = [a2[x][s_] for x in range(NB)]
        m_ = [m2[x][s_] for x in range(NB)]
        for mx in range(4):
            b = mx
            P.op("pe", (lambda b, d, mx, ubb: lambda e: e.matmul(pg[b][:], lhsT=wgb[:, d, 0, mx, :], rhs=ubb, start=True, stop=True))(b, d, mx, ub[b][s_]),
                 reads=[("ub", b), "wgb"], writes=[("pp", b)])
        for mx in range(4):
            b = mx
            col = d * 4 + mx
            P.op("act", (lambda b, col: lambda e, r_=r_: e.activation(out=r_[b], in_=pg[b][:], func=AF.Sigmoid, bias=cs2[:, 20 + col:21 + col], scale=1.0))(b, col),
                 reads=[("pp", b)], writes=[("r", b, s_)])
            P.op("pe", (lambda b, d, mx, ubb: lambda e: e.matmul(pg[b][:], lhsT=wgb[:, d, 1, mx, :], rhs=ubb, start=True, stop=True))(b, d, mx, ub[b][s_]),
                 reads=[("ub", b), "wgb"], writes=[("pp", b)])
        for mx in range(4):
            b = mx
            col = d * 4 + mx
            P.op("act", (lambda b, col: lambda e, i_=i_: e.activation(out=i_[b], in_=pg[b][:], func=AF.Sigmoid, bias=cs2[:, 28 + col:29 + col], scale=1.0))(b, col),
                 reads=[("pp", b)], writes=[("i", b, s_)])
        for mx in range(4):
            b = mx
            col = d * 4 + mx
            P.op("act", (lambda b, col: lambda e, a_=a_, r_=r_: e.activation(out=a_[b], in_=r_[b], func=AF.Exp, scale=kl[:, col:col + 1]))(b, col),
                 reads=[("r", b, s_), "kl"], writes=[("a", b, s_)])
            P.op("act", (lambda b, col: lambda e, m_=m_, r_=r_: e.activation(out=m_[b], in_=r_[b], func=AF.Exp, scale=kl2[:, col:col + 1]))(b, col),
                 reads=[("r", b, s_), "kl2"], writes=[("m", b, s_)])
        for mx in range(4):
            b = mx
            P.op("act", (lambda b: lambda e, m_=m_: e.activation(out=m_[b], in_=m_[b], func=AF.Sqrt, bias=onesT[:, 0:1], scale=-1.0))(b),
                 reads=[("m", b, s_), "onesT"], writes=[("m", b, s_)])

    def stage45(d, t, s_):
        r_ = [r2[x][s_] for x in range(NB)]
        i_ = [i2[x][s_] for x in range(NB)]
        a_ = [a2[x][s_] for x in range(NB)]
        m_ = [m2[x][s_] for x in range(NB)]
        for mx in range(4):
            b = mx
            uu = u[b][s_]
            P.op("pool", (lambda b, uu: lambda e, i_=i_: e.tensor_tensor(out=i_[b], in0=i_[b], in1=uu, op=ALU.mult))(b, uu),
                 reads=[("i", b, s_), ("u", b, s_)], writes=[("i", b, s_)])
            P.op("pool", (lambda b: lambda e, i_=i_, m_=m_: e.tensor_tensor(out=i_[b], in0=i_[b], in1=m_[b], op=ALU.mult))(b),
                 reads=[("i", b, s_), ("m", b, s_)], writes=[("i", b, s_)])
        for mx in range(4):
            b = mx
            col = d * 4 + mx
            if d == 0:
                P.op("dve", (lambda b, col: lambda e, a_=a_, i_=i_: e.tensor_tensor_scan(out=h_[b], data0=a_[b], data1=i_[b], initial=st[:, col:col + 1],
                                                                           op0=ALU.mult, op1=ALU.add))(b, col),
                     reads=[("a", b, s_), ("i", b, s_), ("st", col)], writes=[("h", b)])
                P.op("dve", (lambda b, col: lambda e: e.tensor_copy(out=st[:, col:col + 1], in_=h_[b][:, NT - 1:NT]))(b, col),
                     reads=[("h", b)], writes=[("st", col)])
                P.op("sp", (lambda b, mx, t: lambda e: e.dma_start(out=HF[mx, :, t * NT:(t + 1) * NT], in_=h_[b]))(b, mx, t),
                     reads=[("h", b)], writes=[("HF", mx, t)], dma=True, semkey=("h", b))
            else:
                P.load(hf[b], HF[mx, :, t * NT:(t + 1) * NT], ("hf", b), skey=("HF", mx, t))
                P.op("dve", (lambda b, col: lambda e, a_=a_, i_=i_: e.tensor_tensor_scan(out=h_[b][:, ::-1], data0=a_[b][:, ::-1], data1=i_[b][:, ::-1],
                                                                           initial=st[:, col:col + 1], op0=ALU.mult, op1=ALU.add))(b, col),
                     reads=[("a", b, s_), ("i", b, s_), ("st", col)], writes=[("h", b)])
                P.op("dve", (lambda b, col: lambda e: e.tensor_copy(out=st[:, col:col + 1], in_=h_[b][:, 0:1]))(b, col),
                     reads=[("h", b)], writes=[("st", col)])
                P.op("pool", (lambda b: lambda e: e.tensor_tensor(out=hb[b], in0=h_[b], in1=hf[b], op=ALU.add))(b),
                     reads=[("h", b), ("hf", b)], writes=[("hb", b)])
                k = ("HR", mx, t)
                P.op("sp", (lambda b, mx, t: lambda e: e.dma_start(out=HRin[(t // 2) * 512 + mx * 128:(t // 2) * 512 + (mx + 1) * 128, (t % 2) * NT:(t % 2 + 1) * NT], in_=hb[b]))(b, mx, t),
                     reads=[("hb", b)], writes=[k], dma=True, semkey=("hb", b))
                if mx == 3 and t % 2 == 0:
                    tp = t // 2
                    P.op("pool", (lambda tp: lambda e: e.collective_compute("AllGather", ALU.bypass, replica_groups=RG,
                                                                           ins=[HRin[tp * 512:(tp + 1) * 512, :].opt()],
                                                                           outs=[HRall[tp * 2048:(tp + 1) * 2048, :].opt()]))(tp),
                         reads=[("HR", m2, t_) for m2 in range(4) for t_ in (t, t + 1)], writes=[("HRall", tp)], dma=True, inc=1, semkey="agHR")
                    hrkeys.append(("HRall", tp))

    for d in range(2):
        order = list(range(NTG)) if d == 0 else list(range(NTG - 1, -1, -1))
        stage1(d, order[0], 0)
        for ti, t in enumerate(order):
            if ti == NTG // 2:
                P.op("dve", (lambda d: lambda e: e.tensor_scalar_mul(out=st[:, d * 4:d * 4 + 4], in0=st[:, d * 4:d * 4 + 4],
                                                                      scalar1=cs2[:, 44:45]))(d),
                     reads=[("st", d * 4 + c_) for c_ in range(4)], writes=[("st", d * 4 + c_) for c_ in range(4)])
            stage23(d, t, ti % 2)
            if ti + 1 < NTG:
                stage1(d, order[ti + 1], (ti + 1) % 2)
            stage45(d, t, ti % 2)
    P.S.barrier()
    if stop == "B":
        P.op("sp", lambda e: [e.dma_start(out=DBG[:, :], in_=HRall[3 * 2048:4 * 2048, 0:NT]), e.dma_start(out=DBF[:, :], in_=HF[2, :, 512:1024])],
             reads=[], writes=["dbg"], dma=True, ndma=2, semkey="dbg")
        P.finish(["dbg"])
        return P

    AR.reset(XQ_END)
    FB = AR.bf16([128, 128, 128])
    Bp = AR.bf16([128, 128, 2, 128])
    s1 = [AR.f32([128, 256]) for _ in range(2)]
    t1 = [AR.f32([128, 128]) for _ in range(2)]
    t2 = [AR.f32([128, 128]) for _ in range(2)]
    t3 = [AR.f32([128, 128]) for _ in range(2)]
    t4 = [AR.f32([128, 128]) for _ in range(2)]
    p1 = [PS[5], PS[6]]
    p2 = [PS[1], PS[2]]
    ptrs = [PS[7][:, :].bitcast(BF16)[:, 0:128], PS[0][:, :].bitcast(BF16)[:, 0:128]]
    ptrk = [("ps7", 0), "ss_ps"]
    Tr = tw[:, 0:128]
    Ti = tw[:, 128:256]
    ident = tbb[:, 4, 0:128]
    qkeys = []
    for ch in range(2):
        for c in range(128):
            b = c % 2
            P.op("pe", (lambda ch, c, b: lambda e: e.transpose(out=ptrs[b], in_=Xq[:, ch, :, c], identity=ident))(ch, c, b),
                 reads=[("SEL", ch), "tbb"], writes=[ptrk[b]])
            P.op("act" if b else "dve", (lambda c, b: lambda e: (e.copy if b else e.tensor_copy)(out=FB[:, :, c], in_=ptrs[b]))(c, b),
                 reads=[ptrk[b]], writes=["FB"])
        if stop == "C0":
            P.op("sp", lambda e: e.dma_start(out=DBG[0:128, :], in_=FB[:, 0:4, :].rearrange("p a b -> p (a b)")), reads=["FB"], writes=["dbg"], dma=True, semkey="dbg")
            P.op("sp", lambda e: e.dma_start(out=DBF[:, :], in_=HF[2, :, 512:1024]), reads=[], writes=["dbg2"], dma=True, semkey="dbg2")
            P.finish(["dbg", "dbg2"])
            return P
        for c in range(128):
            b = c % 2
            P.op("pe", (lambda c, b: lambda e: e.matmul(p1[b][:, 0:256], lhsT=FB[:, :, c], rhs=tbb[:, 0, :], start=True, stop=True))(c, b),
                 reads=["FB", "tbb"], writes=[("ps", 5 + b)])
            P.op("act", (lambda b: lambda e: e.copy(out=s1[b], in_=p1[b][:, 0:256]))(b), reads=[("ps", 5 + b)], writes=[("s1", b)])
            P.op("dve", (lambda b: lambda e: e.tensor_tensor(out=t1[b], in0=s1[b][:, 0:128], in1=Tr, op=ALU.mult))(b),
                 reads=[("s1", b), "tw"], writes=[("t1", b)])
            P.op("dve", (lambda b: lambda e: e.tensor_tensor(out=t2[b], in0=s1[b][:, 128:256], in1=Ti, op=ALU.mult))(b),
                 reads=[("s1", b), "tw"], writes=[("t2", b)])
            P.op("dve", (lambda b, c: lambda e: e.tensor_tensor(out=Bp[:, c, 0, :], in0=t1[b], in1=t2[b], op=ALU.subtract))(b, c),
                 reads=[("t1", b), ("t2", b)], writes=[("Bp", c)])
            P.op("pool", (lambda b: lambda e: e.tensor_tensor(out=t3[b], in0=s1[b][:, 0:128], in1=Ti, op=ALU.mult))(b),
                 reads=[("s1", b), "tw"], writes=[("t3", b)])
            P.op("pool", (lambda b: lambda e: e.tensor_tensor(out=t4[b], in0=s1[b][:, 128:256], in1=Tr, op=ALU.mult))(b),
                 reads=[("s1", b), "tw"], writes=[("t4", b)])
            P.op("pool", (lambda b, c: lambda e: e.tensor_tensor(out=Bp[:, c, 1, :], in0=t3[b], in1=t4[b], op=ALU.add))(b, c),
                 reads=[("t3", b), ("t4", b)], writes=[("Bp", c)])
        bpk = [("Bp", c) for c in range(128)]
        if stop == "C1":
            P.op("sp", lambda e: e.dma_start(out=DBG[0:128, :], in_=Bp[:, 0:2, :, :].rearrange("p a b c -> p (a b c)")), reads=bpk, writes=["dbg"], dma=True, semkey="dbg")
            P.op("sp", lambda e: e.dma_start(out=DBF[:, :], in_=HF[2, :, 512:1024]), reads=[], writes=["dbg2"], dma=True, semkey="dbg2")
            P.finish(["dbg", "dbg2"])
            return P
        for ri in range(2):
            for j in range(128):
                b = j % 2
                P.op("pe", (lambda j, b, ri: lambda e: e.matmul(p2[b][:, 0:128], lhsT=Bp[:, :, 0, j], rhs=tbb[:, 1, ri * 128:(ri + 1) * 128], start=True, stop=False))(j, b, ri),
                     reads=bpk + ["tbb"], writes=[("pp", b)])
                P.op("pe", (lambda j, b, ri: lambda e: e.matmul(p2[b][:, 0:128], lhsT=Bp[:, :, 1, j], rhs=tbb[:, 2, ri * 128:(ri + 1) * 128], start=False, stop=True))(j, b, ri),
                     reads=bpk + ["tbb"], writes=[("pp", b)])
                P.op("act" if b else "dve", (lambda j, b: lambda e: (e.copy if b else e.tensor_copy)(out=FB[:, :, j], in_=p2[b][:, 0:128]))(j, b),
                     reads=[("pp", b)], writes=["FB"])
            k = ("Q", ch, ri)
            SEL = Xq[:, ch, :, :].rearrange("p a b -> p (a b)")
            P.op("dve", (lambda SEL: lambda e: e.tensor_scalar_mul(out=SEL.rearrange("p (b k x) -> p b k x", b=2, x=64),
                                                                  in0=FB.rearrange("p k (b x) -> p b k x", b=2),
                                                                  scalar1=cs[:, MK + 4:MK + 5]))(SEL),
                 reads=["FB"], writes=[("SEL", ch)])
            P.op("dve", (lambda SEL: lambda e: e.scalar_tensor_tensor(out=SEL, in0=FB.rearrange("p a b -> p (a b)"), scalar=cs[:, MK + 3:MK + 4],
                                                                     in1=SEL, op0=ALU.mult, op1=ALU.add))(SEL),
                 reads=["FB", ("SEL", ch)], writes=[("SEL", ch)])
            P.op("sp", (lambda ch, ri, SEL: lambda e: e.dma_start(out=Qin[ch * 2048:(ch + 1) * 2048, :].rearrange("(s i c) n -> c s i n", i=2, c=128)[:, :, ri, :],
                                                                  in_=SEL.rearrange("p (s n) -> p s n", n=4 * NT)))(ch, ri, SEL),
                 reads=[("SEL", ch)], writes=[k], dma=True, semkey=("SEL", ch))
            qkeys.append(k)
        if stop != "C2":
            for sq_ in range(8):
                idx = ch * 8 + sq_
                P.op("pool", (lambda idx: lambda e: e.collective_compute("AllGather", ALU.bypass, replica_groups=RG,
                                                                         ins=[Qin[idx * 256:(idx + 1) * 256, :].opt()],
                                                                         outs=[Qall[idx * 1024:(idx + 1) * 1024, :].opt()]))(idx),
                     reads=[("Q", ch, 0), ("Q", ch, 1)], writes=[("Qall", idx)], dma=True, inc=1, semkey="agQ")
    if stop == "C2":
        P.op("sp", lambda e: e.dma_start(out=DBG[0:512, :], in_=Qin[512:1024, :]), reads=qkeys, writes=["dbg"], dma=True, semkey="dbg")
        P.op("sp", lambda e: e.dma_start(out=DBF[:, :], in_=HF[2, :, 512:1024]), reads=[], writes=["dbg2"], dma=True, semkey="dbg2")
        P.finish(["dbg", "dbg2"])
        return P
    P.S.barrier()
    if stop == "C":
        P.op("sp", lambda e: [e.dma_start(out=DBG[:, :], in_=Qall[2048:4096, 0:NT]), e.dma_start(out=DBF[:, :], in_=HF[2, :, 512:1024])],
             reads=[], writes=["dbg"], dma=True, ndma=2, semkey="dbg")
        P.finish(["dbg"])
        return P

    AR.reset(0)
    X32 = AR.f32([128, KC, NT])
    NTb = AR.bf16([128, KC, NT + 2])
    QTb = AR.bf16([128, 16, NT])
    MT = QTb
    FT = AR.bf16([128, 8, NT])
    BIG = AR.bf16([128, 48, NT])
    WB = [AR.bf16([128, KC * 512]) for _ in range(3)]
    e1 = [AR.f32([128, NT]) for _ in range(2)]
    e2 = [AR.f32([128, NT]) for _ in range(2)]
    upS = [AR.f32([128, NT + 2]) for _ in range(2)]
    upH = [AR.f32([128, 2]) for _ in range(2)]
    cv = e2
    gg = [AR.f32([128, NT]) for _ in range(2)]
    pp = [PS[1], PS[2], PS[3], PS[4]]
    ph = [PS[5], PS[6]]
    wcnt = [0]

    def wload(nm, ci, kcn, cols):
        i = wcnt[0] % 3
        wcnt[0] += 1
        view = WB[i][:, 0:kcn * cols].rearrange("p (k m) -> p k m", k=kcn)
        P.op("sp", lambda e: e.dma_start(out=WB[i][:, 0:kcn * cols], in_=wbf[nm][ci, :, :]), reads=["in"], writes=[("WB", i)], dma=True)
        return view, ("WB", i)

    def linear(wview, wkey, mloc, kcn, rhs_fn, rkeys, N):
        b = ppb()
        for kc in range(kcn):
            P.op("pe", (lambda kc, b: lambda e: e.matmul(pp[b][:, :N], lhsT=wview[:, kc, mloc * 128:(mloc + 1) * 128], rhs=rhs_fn(kc),
                                                       start=(kc == 0), stop=(kc == kcn - 1)))(kc, b),
                 reads=list(rkeys) + [wkey], writes=[("pp", b)])
        return b

    ecnt = [0]
    nbkeys = []

    def p3a(t):
        t0 = t * NT
        N = NT

        P.load(X32, xo[:, :, t0:t0 + NT], "X32")
        def ldh(e):
            j, g = ids(e)
            return e.dma_start(out=BIG[:, 0:16, :], in_=HRall[bass.ds(j * 8192 + (t // 2) * 2048, 2048), (t % 2) * NT:(t % 2 + 1) * NT].rearrange("(k p) n -> p k n", p=128))
        P.op("sp", ldh, reads=[], writes=["BIG0"], dma=True)

        def ldq(e):
            j, g = ids(e)
            return [e.dma_start(out=QTb[:, ch_ * 8:(ch_ + 1) * 8, :],
                                in_=Qall[bass.ds(j * 2048 + (ch_ * 8192 + (t // 4) * 1024), 1024), (t % 4) * NT:(t % 4 + 1) * NT].rearrange("(k c) n -> c k n", c=128))
                    for ch_ in range(2)]
        P.op("act", ldq, reads=[], writes=["QTb"], dma=True, ndma=2)
        rmsnorm_fm(P, "a", X32, "X32", N, cs[:, G_MIX:G_MIX + 16], NTb, "NTb", **nb)
        for part in range(3):
            for cq in range(4):
                wv, wk = wload("wgz", part * 4 + cq, KC, 512)
                for ml in range(4):
                    mt = cq * 4 + ml
                    b = linear(wv, wk, ml, KC, lambda kc: NTb[:, kc, :N], ["NTb"], N)
                    bias = cs[:, B_GZ + part * 16 + mt:B_GZ + part * 16 + mt + 1]
                    if part == 0:
                        i = ecnt[0] % 2
                        ecnt[0] += 1
                        P.op("act", (lambda b, i, bias: lambda e: e.activation(out=e1[i][:, :N], in_=pp[b][:, :N], func=AF.Gelu_apprx_tanh, bias=bias, scale=1.0))(b, i, bias),
                             reads=[("pp", b)], writes=[("e1", i)])
                        P.op("dve", (lambda mt, i: lambda e: e.tensor_tensor(out=BIG[:, mt, :N], in0=BIG[:, mt, :N], in1=e1[i][:, :N], op=ALU.mult))(mt, i),
                             reads=[("e1", i), "BIG0"], writes=["BIG0"])
                    else:
                        P.op("act", (lambda b, mt, bias, part: lambda e: e.activation(out=BIG[:, part * 16 + mt, :N], in_=pp[b][:, :N], func=AF.Sigmoid, bias=bias, scale=1.0))(b, mt, bias, part),
                             reads=[("pp", b)], writes=["BIG%d" % part])
        for g4 in range(4):
            for ct in range(2):
                b = ppb()
                n = 0
                for chh in range(2):
                    for ri in range(2):
                        P.op("pe", (lambda g4, ct, chh, ri, b, n: lambda e: e.matmul(pp[b][:, :N], lhsT=dftb[:, chh, ri, ct * 128:(ct + 1) * 128],
                                                                                     rhs=QTb[:, chh * 8 + g4 * 2 + ri, :N], start=(n == 0), stop=(n == 3)))(g4, ct, chh, ri, b, n),
                             reads=["QTb", "dftb"], writes=[("pp", b)])
                        n += 1
                P.op("act", (lambda g4, ct, b: lambda e: e.copy(out=FT[:, g4 * 2 + ct, :N], in_=pp[b][:, :N]))(g4, ct, b),
                     reads=[("pp", b)], writes=["FT"])
        for cq in range(4):
            wva, wka = wload("woa", cq, KC, 512)
            wvb, wkb = wload("wob", cq, 8, 512)
            for ml in range(4):
                mt = cq * 4 + ml
                ba = linear(wva, wka, ml, KC, lambda kc: BIG[:, kc, :N], ["BIG0"], N)
                bb = linear(wvb, wkb, ml, 8, lambda kc: FT[:, kc, :N], ["FT"], N)
                i = ecnt[0] % 2
                ecnt[0] += 1
                P.op("dve", (lambda ba, mt, i: lambda e: e.tensor_tensor(out=e1[i][:, :N], in0=pp[ba][:, :N], in1=BIG[:, 16 + mt, :N], op=ALU.mult))(ba, mt, i),
                     reads=[("pp", ba), "BIG1"], writes=[("e1", i)])
                P.op("dve", (lambda bb, mt, i: lambda e: e.tensor_tensor(out=e2[i][:, :N], in0=pp[bb][:, :N], in1=BIG[:, 32 + mt, :N], op=ALU.mult))(bb, mt, i),
                     reads=[("pp", bb), "BIG2"], writes=[("e2", i)])
                P.op("pool", (lambda mt, i: lambda e: e.tensor_tensor(out=MT[:, mt, :N], in0=e1[i][:, :N], in1=e2[i][:, :N], op=ALU.add))(mt, i),
                     reads=[("e1", i), ("e2", i)], writes=["QTb"])
        for cq in range(4):
            wv, wk = wload("wo", cq, KC, 512)
            for ml in range(4):
                mt = cq * 4 + ml
                b = linear(wv, wk, ml, KC, lambda kc: MT[:, kc, :N], ["QTb"], N)
                P.op("dve", (lambda b, mt: lambda e: e.scalar_tensor_tensor(out=X32[:, mt, :N], in0=pp[b][:, :N], scalar=cs[:, B_OUT + mt:B_OUT + mt + 1],
                                                                           in1=X32[:, mt, :N], op0=ALU.add, op1=ALU.add))(b, mt),
                     reads=[("pp", b), "X32"], writes=["X32"])
        P.op("sp", lambda e: e.dma_start(out=HS[:, :, t0:t0 + N].rearrange("k p n -> p k n"), in_=X32),
             reads=["X32"], writes=[("HS", t0)], dma=True, semkey="X32")
        rmsnorm_fm(P, "b", X32, "X32", N, cs[:, G_FFN:G_FFN + 16], NTb, "NTb", **nb)
        P.op("sp", lambda e: e.dma_start(out=N2S[:, :, 1 + t0:1 + t0 + N].rearrange("k p n -> p k n"), in_=NTb[:, :, :N]),
             reads=["NTb"], writes=[("N2S", t0)], dma=True, semkey="NTb")
        if t == 0:
            P.op("sp", lambda e: e.dma_start(out=NBin[:, 0:1].rearrange("(k p) n -> p k n", p=128), in_=NTb[:, :, 0:1], allow_slow_non_contiguous=True),
                 reads=["NTb"], writes=[("NB", 0)], dma=True, semkey="NTb")
            nbkeys.append(("NB", 0))
        if t == NTL - 1:
            P.op("sp", lambda e: e.dma_start(out=NBin[:, 1:2].rearrange("(k p) n -> p k n", p=128), in_=NTb[:, :, NT - 1:NT], allow_slow_non_contiguous=True),
                 reads=["NTb"], writes=[("NB", 1)], dma=True, semkey="NTb")
            nbkeys.append(("NB", 1))

    def p3b(t):
        t0 = t * NT
        rk = [("N2S", t0)]
        rk.append(("N2S", t0 - NT) if t > 0 else ("N2S", "halo"))
        rk.append(("N2S", t0 + NT) if t < NTL - 1 else ("N2S", "halo"))
        P.op("sp", lambda e: e.dma_start(out=NTb, in_=N2S[:, :, t0:t0 + NT + 2].rearrange("k p n -> p k n")),
             reads=rk, writes=["NTb"], dma=True)
        P.op("sp", lambda e: e.dma_start(out=X32, in_=HS[:, :, t0:t0 + NT].rearrange("k p n -> p k n")),
             reads=[("HS", t0)], writes=["X32"], dma=True)
        mL = cs[:, MK:MK + 1] if t == 0 else cs[:, MK + 2:MK + 3]
        mR = cs[:, MK + 1:MK + 2] if t == NTL - 1 else cs[:, MK + 2:MK + 3]
        for q in range(24):
            wv, wk = wload("wup", q, KC, 512)
            for ml in range(4):
                gv, jj = ml // 2, ml % 2
                jp = 2 * q + jj
                mt = jp + 48 * gv
                b = ppb()
                hbk = b % 2
                for kc in range(KC):
                    P.op("pe", (lambda kc, b, ml, wv: lambda e: e.matmul(pp[b][:], lhsT=wv[:, kc, ml * 128:(ml + 1) * 128], rhs=NTb[:, kc, 1:NT + 1],
                                                                       start=(kc == 0), stop=(kc == KC - 1)))(kc, b, ml, wv),
                         reads=["NTb", wk], writes=[("pp", b)])
                for kc in range(KC):
                    P.op("pe", (lambda kc, hbk, ml, wv: lambda e: e.matmul(ph[hbk][:, 0:2], lhsT=wv[:, kc, ml * 128:(ml + 1) * 128], rhs=NTb[:, kc, 0:NT + 2:NT + 1],
                                                                         start=(kc == 0), stop=(kc == KC - 1)))(kc, hbk, ml, wv),
                         reads=["NTb", wk], writes=[("ps", 5 + hbk)])
                i = ecnt[0] % 2
                ecnt[0] += 1
                bias = cs[:, B_UP + mt:B_UP + mt + 1]
                P.op("act", (lambda b, i, bias: lambda e: e.activation(out=upS[i][:, 1:NT + 1], in_=pp[b][:], func=AF.Identity, bias=bias, scale=1.0))(b, i, bias),
                     reads=[("pp", b)], writes=[("upS", i)])
                P.op("act", (lambda hbk, i, bias: lambda e: e.activation(out=upH[i], in_=ph[hbk][:, 0:2], func=AF.Identity, bias=bias, scale=1.0))(hbk, i, bias),
                     reads=[("ps", 5 + hbk)], writes=[("upH", i)])
                P.op("dve", (lambda i, mL: lambda e: e.tensor_scalar_mul(out=upS[i][:, 0:1], in0=upH[i][:, 0:1], scalar1=mL))(i, mL),
                     reads=[("upH", i)], writes=[("upS", i)])
                P.op("dve", (lambda i, mR: lambda e: e.tensor_scalar_mul(out=upS[i][:, NT + 1:NT + 2], in0=upH[i][:, 1:2], scalar1=mR))(i, mR),
                     reads=[("upH", i)], writes=[("upS", i)])
                w0 = cs[:, CFW + mt * 3:CFW + mt * 3 + 1]
                w1 = cs[:, CFW + mt * 3 + 1:CFW + mt * 3 + 2]
                w2 = cs[:, CFW + mt * 3 + 2:CFW + mt * 3 + 3]
                cb = cs[:, CFB + mt:CFB + mt + 1]
                P.op("dve", (lambda i, w0, cb: lambda e: e.tensor_scalar(out=cv[i], in0=upS[i][:, 0:NT], scalar1=w0, scalar2=cb, op0=ALU.mult, op1=ALU.add))(i, w0, cb),
                     reads=[("upS", i)], writes=[("e2", i)])
                P.op("dve", (lambda i, w1: lambda e: e.scalar_tensor_tensor(out=cv[i], in0=upS[i][:, 1:NT + 1], scalar=w1, in1=cv[i], op0=ALU.mult, op1=ALU.add))(i, w1),
                     reads=[("upS", i), ("e2", i)], writes=[("e2", i)])
                P.op("dve", (lambda i, w2: lambda e: e.scalar_tensor_tensor(out=cv[i], in0=upS[i][:, 2:NT + 2], scalar=w2, in1=cv[i], op0=ALU.mult, op1=ALU.add))(i, w2),
                     reads=[("upS", i), ("e2", i)], writes=[("e2", i)])
                if gv == 0:
                    P.op("act", (lambda i, jj: lambda e: e.activation(out=gg[jj], in_=cv[i], func=AF.Gelu_apprx_tanh))(i, jj),
                         reads=[("e2", i)], writes=[("gg", jj)])
                else:
                    P.op("pool", (lambda i, jj, jp: lambda e: e.tensor_tensor(out=BIG[:, jp, :], in0=gg[jj], in1=cv[i], op=ALU.mult))(i, jj, jp),
                         reads=[("gg", jj), ("e2", i)], writes=["BIG%d" % (jp // 16)])
        for mt in range(KC):
            wv, wk = wload("wdn", mt, 48, 128)
            b = linear(wv, wk, 0, 48, lambda kc: BIG[:, kc, :], ["BIG0", "BIG1", "BIG2"], NT)
            P.op("dve", (lambda b, mt: lambda e: e.scalar_tensor_tensor(out=X32[:, mt, :], in0=pp[b][:], scalar=cs[:, B_DN + mt:B_DN + mt + 1],
                                                                       in1=X32[:, mt, :], op0=ALU.add, op1=ALU.add))(b, mt),
                 reads=[("pp", b), "X32"], writes=["X32"])
        rmsnorm_fm(P, "c", X32, "X32", NT, cs[:, G_FIN:G_FIN + 16], X32, "X32", scratch=NTb, skey="NTb", **nb)
        k = ("YT", t)
        P.op("sp", lambda e: e.dma_start(out=YT[:, :, t0:t0 + NT].rearrange("k p n -> p k n"), in_=X32),
             reads=["X32"], writes=[k], dma=True)
        return k

    for t in range(NTL):
        p3a(t)
    P.op("pool", lambda e: e.collective_compute("AllGather", ALU.bypass, replica_groups=RG, ins=[NBin_h.ap().opt()], outs=[NBall_h.ap().opt()]),
         reads=nbkeys, writes=["NBall"], dma=True, inc=1, semkey="agNB")

    hl = P.sb("hl", [128, KC, 2], BF16)

    def ldhalo(e):
        j, g = ids(e)
        rl = ((j + 3) % 4) * 2048
        rr = ((j + 1) % 4) * 2048
        return [e.dma_start(out=hl[:, :, 0:1], in_=NBall[bass.ds(rl, 2048), 1:2].rearrange("(k p) n -> p k n", p=128), allow_slow_non_contiguous=True),
                e.dma_start(out=hl[:, :, 1:2], in_=NBall[bass.ds(rr, 2048), 0:1].rearrange("(k p) n -> p k n", p=128), allow_slow_non_contiguous=True)]
    P.op("pool", ldhalo, reads=["NBall"], writes=["hl"], dma=True, ndma=2, semkey="halo")
    P.op("sp", lambda e: [e.dma_start(out=N2S[:, :, 0:1].rearrange("k p n -> p k n"), in_=hl[:, :, 0:1], allow_slow_non_contiguous=True),
                          e.dma_start(out=N2S[:, :, TOK + 1:TOK + 2].rearrange("k p n -> p k n"), in_=hl[:, :, 1:2], allow_slow_non_contiguous=True)],
         reads=["hl"], writes=[("N2S", "halo")], dma=True, ndma=2, semkey="halo2")
    fin = [p3b(t) for t in range(NTL)]
    P.finish(fin)
    return P


def fm(a):
    a = np.asarray(a, np.float32)
    return np.ascontiguousarray(a.reshape(-1, 128).T)


def wfm(w):
    K, M = w.shape
    return np.ascontiguousarray(w.reshape(K // 128, 128, M).transpose(1, 0, 2))


def xfm(x):
    T, Fd = x.shape
    return np.ascontiguousarray(x.T.reshape(Fd // 128, 128, T).transpose(1, 0, 2))


def run(P, in_maps, names=None):
    if names is not None:
        in_maps = [{k: v for k, v in m.items() if k in names} for m in in_maps]
    res = run_bass_kernel_spmd(P.nc, in_maps, core_ids=list(range(8)))
    return res.results


def dft_tables(seq_len):
    nb = TS // seq_len
    n2n = seq_len // 128
    p = np.arange(128)
    bidx, n2 = p // n2n, p % n2n
    j = np.arange(128)
    bj, k2 = j // n2n, j % n2n
    ang = 2 * np.pi * np.outer(n2, k2) / n2n
    same = (bidx[:, None] == bj[None, :]).astype(np.float64)
    sc = 1.0 / np.sqrt(seq_len)
    T1 = np.concatenate([np.cos(ang) * same, -np.sin(ang) * same], 1) * sc
    n1 = np.arange(128)
    a2 = 2 * np.pi * np.outer(n1, n1) / 128.0
    Gr, Gi = np.cos(a2), -np.sin(a2)
    G1 = np.concatenate([Gr, Gi], 1)
    G2 = np.concatenate([-Gi, Gr], 1)
    at = 2 * np.pi * np.outer(n1, k2) / seq_len
    TW = np.concatenate([np.cos(at), -np.sin(at)], 1)
    return np.ascontiguousarray(np.stack([T1, G1, G2, TW], 1).astype(np.float32))


_PROGS = {}
_STOP = None


def kernel(x_prompt, x_sample, g_mix, w_in, b_in, conv_a_w, conv_a_b, lru_w_a, lru_b_a, lru_w_x, lru_b_x,
           lru_lam, w_out_a, w_out_b, w_out, b_out, g_ffn, w_up, b_up, conv_f_w, conv_f_b, w_down, b_down,
           g_final):
    f32 = np.float32
    xs = [np.asarray(x_prompt, f32).reshape(TS, D), np.asarray(x_sample, f32).reshape(TS, D)]
    seqlen = [16384, 8192]
    w_in = np.asarray(w_in, f32)[0]
    b_in = np.asarray(b_in, f32)[0]
    if "f" not in _PROGS:
        _PROGS["f"] = build_fused(_STOP)
    caw = np.asarray(conv_a_w, f32)[0]
    cab = np.asarray(conv_a_b, f32)[0]
    lwa = np.asarray(lru_w_a, f32)[0]
    lwx = np.asarray(lru_w_x, f32)[0]
    lba = np.asarray(lru_b_a, f32)[0]
    lbx = np.asarray(lru_b_x, f32)[0]
    lam = np.asarray(lru_lam, f32)[0]
    bgz = np.concatenate([fm(b_in[2048:4096]), fm(b_in[5120:7168]), fm(b_in[7168:9216])], 1)
    wgz_full = np.concatenate([w_in[:, 2048:4096], w_in[:, 5120:9216]], 1)
    wgz = np.stack([wfm(wgz_full[:, q * 512:(q + 1) * 512]) for q in range(12)], 0)
    woa_ = np.asarray(w_out_a, f32)[0]
    wob_ = np.asarray(w_out_b, f32)[0]
    wo_ = np.asarray(w_out, f32)[0]
    wup_ = np.asarray(w_up, f32)[0]
    wdn_ = np.asarray(w_down, f32)[0]
    woa = np.stack([wfm(woa_[:, q * 512:(q + 1) * 512]) for q in range(4)], 0)
    wob = np.stack([wfm(wob_[:, q * 512:(q + 1) * 512]) for q in range(4)], 0)
    wo = np.stack([wfm(wo_[:, q * 512:(q + 1) * 512]) for q in range(4)], 0)
    wupc = []
    for q in range(24):
        cols = np.concatenate([wup_[:, 256 * q:256 * q + 256], wup_[:, 6144 + 256 * q:6144 + 256 * q + 256]], 1)
        wupc.append(wfm(cols))
    wupc = np.stack(wupc, 0)
    wdn = np.stack([wfm(wdn_[:, m * 128:(m + 1) * 128]) for m in range(16)], 0)
    cc = np.arange(256)
    ang = 2 * np.pi * np.outer(cc, cc) / 256.0
    dm = np.stack([np.cos(ang), np.sin(ang)], 0) / 16.0
    dft = np.ascontiguousarray(dm.reshape(2, 2, 128, 256).transpose(2, 1, 0, 3).astype(f32)).reshape(128, 1024)
    cfw = np.asarray(conv_f_w, f32)[0]
    c3 = np.zeros((128, 640), f32)
    c3[:, 0:16] = fm(np.asarray(g_mix, f32)[0])
    c3[:, 16:32] = fm(np.asarray(g_ffn, f32)[0])
    c3[:, 32:48] = fm(np.asarray(g_final, f32))
    c3[:, 48:96] = bgz
    c3[:, 96:112] = fm(np.asarray(b_out, f32)[0])
    c3[:, 112:128] = fm(np.asarray(b_down, f32)[0])
    c3[:, 128:224] = fm(np.asarray(b_up, f32)[0])
    c3[:, 224:512] = cfw.reshape(3, 96, 128).transpose(2, 1, 0).reshape(128, 288)
    c3[:, 512:608] = fm(np.asarray(conv_f_b, f32)[0])
    c3[:, 610] = 1.0
    xgs = [xfm(xs[g]) for g in range(2)]
    ident = np.concatenate([np.eye(128, dtype=f32), np.zeros((128, 128), f32)], 1)
    ins = []
    for c in range(8):
        g, j = c // 4, c % 4
        L = seqlen[g]
        chs = slice(512 * j, 512 * j + 512)
        fch = slice(4096 + 256 * j, 4096 + 256 * j + 256)
        heads = slice(4 * j, 4 * j + 4)
        wg = np.stack([np.stack([lwa[d, heads], lwx[d, heads]], 0) for d in range(2)], 0)
        wg = np.ascontiguousarray(wg.transpose(3, 0, 1, 2, 4)).reshape(128, 2048)
        c2 = np.zeros((128, 64), f32)
        c2[:, 0:16] = caw[:, chs].reshape(4, 4, 128).transpose(2, 1, 0).reshape(128, 16)
        c2[:, 16:20] = fm(cab[chs])
        c2[:, 20:28] = np.concatenate([fm(lba[0, chs]), fm(lba[1, chs])], 1)
        c2[:, 28:36] = np.concatenate([fm(lbx[0, chs]), fm(lbx[1, chs])], 1)
        c2[:, 36:44] = np.concatenate([fm(lam[0, chs]), fm(lam[1, chs])], 1)
        c2[:, 44] = 1.0 if L == TS else 0.0
        c2[:, 48:52] = fm(b_in[chs])
        tb = np.concatenate([dft_tables(L), ident[:, None, :]], 1)
        lo, hi = j * TOK, (j + 1) * TOK
        cm = c3.copy()
        cm[:, 608] = 1.0 if (lo % L) != 0 else 0.0
        cm[:, 609] = 1.0 if (hi % L) != 0 else 0.0
        cm[:, 611] = 1.0 if L == TS else 0.0
        cm[:, 612] = 0.0 if L == TS else 1.0
        ins.append({"xg": xgs[g], "xo": np.ascontiguousarray(xgs[g][:, :, lo:hi]), "wx": wfm(w_in[:, chs]), "wf": wfm(w_in[:, fch]),
                    "bfT": np.ascontiguousarray(np.tile(b_in[fch][None, :], (128, 1))),
                    "wg": wg, "c2": c2, "tb": np.ascontiguousarray(tb), "wgz": wgz, "woa": woa, "wob": wob, "wo": wo,
                    "wup": wupc, "wdn": wdn, "dft": dft, "c3": cm})
    if _STOP is not None:
        return run(_PROGS["f"], ins, names={"xg", "xo", "wx", "wf", "bfT", "wg", "c2", "tb", "dft", "c3"})
    r = run(_PROGS["f"], ins)
    ys = []
    for g in range(2):
        yt = np.concatenate([r[g * 4 + j]["YT"] for j in range(4)], 2)
        ys.append(np.ascontiguousarray(yt.reshape(D, TS).T))
    return (ys[0].reshape(1, 16384, D).astype(f32), ys[1].reshape(2, 8192, D).astype(f32))
```

```python
import contextlib
import numpy as np
import ml_dtypes
import concourse.bass as bass
import concourse.mybir as mybir
from concourse.bass_utils import run_bass_kernel_spmd

F32 = mybir.dt.float32
BF16 = mybir.dt.bfloat16
AF = mybir.ActivationFunctionType
ALU = mybir.AluOpType
NPBF = ml_dtypes.bfloat16

D = 2048
KC = 16
TOK = 4096
TS = 16384
NT = 512
EPS = 1e-6


class Sched:
    def __init__(self, nc):
        self.nc = nc
        self.ops = []
        self.state = {}
        self.bar_from = 0

    def barrier(self):
        last = {}
        dmas = []
        for i, o in enumerate(self.ops[self.bar_from:], self.bar_from):
            if o["dma"]:
                dmas.append(i)
            elif o["fn"] is not None:
                last[o["eng"]] = i
        deps = sorted(set(dmas) | set(last.values()))
        for eng in ("pe", "act", "dve", "pool", "sp"):
            self.ops.append(dict(eng=eng, fn=None, deps=list(deps), dma=False, semkey=None, signal=False, ndma=1, inc=16,
                                 bar=True))
        self.bar_from = len(self.ops)
        self.state = {}

    def op(self, eng, fn, reads=(), writes=(), dma=False, semkey=None, ndma=1, inc=16):
        oid = len(self.ops)
        deps = set()
        for k in reads:
            st = self.state.setdefault(k, [None, []])
            if st[0] is not None:
                deps.add(st[0])
        for k in writes:
            st = self.state.setdefault(k, [None, []])
            if st[0] is not None:
                deps.add(st[0])
            last = {}
            for r in st[1]:
                o = self.ops[r]
                if o["dma"]:
                    deps.add(r)
                else:
                    last[o["eng"]] = max(last.get(o["eng"], -1), r)
            deps.update(last.values())
        for k in reads:
            self.state[k][1].append(oid)
        for k in writes:
            self.state[k] = [oid, []]
        deps.discard(oid)
        if dma and semkey is None:
            semkey = writes[0] if len(writes) else reads[0]
        self.ops.append(dict(eng=eng, fn=fn, deps=sorted(deps), dma=dma, semkey=semkey,
                             signal=dma, ndma=ndma, inc=inc))
        return oid

    def emit(self, final_keys=()):
        nc = self.nc
        ops = self.ops
        self.op("sp", None, reads=list(final_keys))
        for o in ops:
            for d in o["deps"]:
                od = ops[d]
                if od["dma"]:
                    continue
                if od["eng"] == "pe" and o["eng"] == "pe" and not o["dma"]:
                    continue
                od["signal"] = True
        cnt = {}
        for o in ops:
            if o["dma"]:
                k = ("d", o["semkey"])
                cnt[k] = cnt.get(k, 0) + o["inc"] * o["ndma"]
                o["done"] = (k, cnt[k])
            elif o["signal"]:
                k = ("e", o["eng"])
                cnt[k] = cnt.get(k, 0) + 1
                o["done"] = (k, cnt[k])
        semkeys = sorted(cnt.keys(), key=str)
        with contextlib.ExitStack() as es:
            sems = {}
            for i, k in enumerate(semkeys):
                sems[k] = es.enter_context(nc.semaphore("s%d" % i))
            block = es.enter_context(nc.Block())

            def run(engname):
                def body(e):
                    known = {}
                    for o in ops:
                        if o["eng"] != engname:
                            continue
                        for d in o["deps"]:
                            od = ops[d]
                            if "done" not in od:
                                continue
                            if (not od["dma"]) and od["eng"] == "pe" and engname == "pe" and not o["dma"]:
                                continue
                            k, v = od["done"]
                            if known.get(k, 0) < v:
                                e.wait_ge(sems[k], v)
                                known[k] = v
                        if o["fn"] is None:
                            continue
                        ins = o["fn"](e)
                        if o["dma"]:
                            if not isinstance(ins, (list, tuple)):
                                ins = [ins]
                            assert len(ins) == o["ndma"], (len(ins), o["ndma"])
                            for i_ in ins:
                                i_.then_inc(sems[o["done"][0]], o["inc"])
                        elif o["signal"]:
                            ins.then_inc(sems[o["done"][0]], 1)
                return body

            block.tensor(run("pe"))
            block.scalar(run("act"))
            block.vector(run("dve"))
            block.gpsimd(run("pool"))
            block.sync(run("sp"))


class Prog:
    def __init__(self):
        self.nc = bass.Bass("TRN2", target_bir_lowering=False)
        self.es = contextlib.ExitStack()
        self.S = Sched(self.nc)
        self.outs = []
        self._rr = 0

    def din(self, name, shape, dt=F32):
        return self.nc.dram_tensor(name, list(shape), dt, kind="ExternalInput").ap()

    def dout(self, name, shape, dt=F32):
        self.outs.append(name)
        return self.nc.dram_tensor(name, list(shape), dt, kind="ExternalOutput").ap()

    def dscr(self, name, shape, dt=F32):
        return self.nc.dram_tensor(name, list(shape), dt).ap()

    def sb(self, name, shape, dt=F32):
        return self.es.enter_context(self.nc.sbuf_tensor(name, list(shape), dt))

    def ps(self, name, shape, dt=F32):
        return self.es.enter_context(self.nc.psum_tensor(name, list(shape), dt))

    def op(self, *a, **k):
        return self.S.op(*a, **k)

    def ew(self):
        self._rr ^= 1
        return "dve" if self._rr else "pool"

    def load(self, dst_ap, src_ap, dkey, skey="in", eng="sp"):
        self.op(eng, lambda e: e.dma_start(out=dst_ap, in_=src_ap), reads=[skey], writes=[dkey], dma=True)

    def finish(self, final_keys):
        self.S.emit(final_keys=final_keys)
        self.es.close()


def rmsnorm_fm(P, tag, xT, xkey, N, gcol, outT, okey, ones32, epsT, ss_ps, sq, rt, rstd, scratch=None, skey=None):
    if scratch is None:
        scratch, skey = outT, okey
    P.op("act", lambda e: e.activation(out=scratch[:, :, :N], in_=xT[:, :, :N], func=AF.Square), reads=[xkey], writes=[skey])
    for kc in range(KC):
        P.op("pe", (lambda kc: lambda e: e.matmul(ss_ps[:, :N], lhsT=ones32[:, :], rhs=scratch[:, kc, :N],
                                                  start=(kc == 0), stop=(kc == KC - 1)))(kc),
             reads=[skey, "ones32"], writes=["ss_ps"])
    P.op("act", lambda e: e.activation(out=rt[:, :N], in_=ss_ps[:, :N], func=AF.Sqrt, bias=epsT[:, 0:1], scale=1.0 / D),
         reads=["ss_ps", "epsT"], writes=["rt"])
    P.op("dve", lambda e: e.reciprocal(out=rstd[:, :N], in_=rt[:, :N]), reads=["rt"], writes=["rstd"])
    for kc in range(KC):
        P.op("dve", (lambda kc: lambda e: e.scalar_tensor_tensor(out=outT[:, kc, :N], in0=xT[:, kc, :N],
                                                                scalar=gcol[:, kc:kc + 1], in1=rstd[:, :N],
                                                                op0=ALU.mult, op1=ALU.mult))(kc),
             reads=[xkey, "rstd", "consts"] + ([skey] if skey != okey else []), writes=[okey])


def norm_bufs(P):
    ones32 = P.sb("ones32", [128, 128])
    epsT = P.sb("epsT", [128, 1])
    P.op("dve", lambda e: e.memset(ones32[:], 1.0), writes=["ones32"])
    P.op("dve", lambda e: e.memset(epsT[:], EPS), writes=["epsT"])
    ss_ps = P.ps("ss_ps", [128, NT])
    sq = [P.sb("sq%d" % i, [128, NT]) for i in range(2)]
    rt = P.sb("rt", [128, NT])
    rstd = P.sb("rstd", [128, NT])
    return dict(ones32=ones32, epsT=epsT, ss_ps=ss_ps, sq=sq, rt=rt, rstd=rstd)


class Arena:
    def __init__(self, P, nbytes):
        self.t = P.sb("arena", [128, nbytes // 4])
        self.nbytes = nbytes
        self.off = 0

    def reset(self, off=0):
        self.off = off

    def _take(self, nb):
        nb = (nb + 63) // 64 * 64
        o = self.off
        self.off += nb
        assert self.off <= self.nbytes, (self.off, self.nbytes)
        return self.t[:, o // 4:(o + nb) // 4]

    @staticmethod
    def _shape(ap, shape):
        if len(shape) == 2:
            return ap
        if len(shape) == 3:
            return ap.rearrange("p (a b) -> p a b", a=shape[1])
        if len(shape) == 4:
            return ap.rearrange("p (a b c) -> p a b c", a=shape[1], b=shape[2])
        raise ValueError(shape)

    def f32(self, shape):
        n = int(np.prod(shape[1:]))
        return self._shape(self._take(4 * n)[:, :n], shape)

    def bf16(self, shape):
        n = int(np.prod(shape[1:]))
        return self._shape(self._take(2 * n).bitcast(BF16)[:, :n], shape)


RG = [[0, 1, 2, 3], [4, 5, 6, 7]]
G_MIX, G_FFN, G_FIN, B_GZ, B_OUT, B_DN, B_UP, CFW, CFB, MK = 0, 16, 32, 48, 96, 112, 128, 224, 512, 608


def build_fused(stop=None):
    P = Prog()
    nc = P.nc
    NTL = TOK // NT
    NTG = TS // NT
    HALF = TS // 2
    xg = P.din("xg", [128, KC, TS])
    xo = P.din("xo", [128, KC, TOK])
    wx = P.din("wx", [128, KC, 512])
    wf = P.din("wf", [128, KC, 256])
    bfT = P.din("bfT", [128, 256])
    wg = P.din("wg", [128, 2048])
    c2 = P.din("c2", [128, 64])
    tb = P.din("tb", [128, 5, 256])
    if stop is None:
        wgz = P.din("wgz", [12, 128, KC, 512])
        woa = P.din("woa", [4, 128, KC, 512])
        wob = P.din("wob", [4, 128, 8, 512])
        wo = P.din("wo", [4, 128, KC, 512])
        wup = P.din("wup", [24, 128, KC, 512])
        wdn = P.din("wdn", [16, 128, 48, 128])
        YT = P.dout("YT", [KC, 128, TOK])
        wsrc = dict(wgz=(wgz, 12, KC * 512), woa=(woa, 4, KC * 512), wob=(wob, 4, 8 * 512), wo=(wo, 4, KC * 512),
                    wup=(wup, 24, KC * 512), wdn=(wdn, 16, 48 * 128))
        wbf = {}
        for nm, (src, nchunk, ncol) in wsrc.items():
            wbf[nm] = P.dscr(nm + "_bf", [nchunk, 128, ncol], BF16)
    else:
        DBG = P.dout("DBG", [2048, NT], BF16)
        DBF = P.dout("DBF", [128, NT])
    dft = P.din("dft", [128, 1024])
    c3 = P.din("c3", [128, 640])
    Uscr = P.dscr("Uscr", [4, 128, TS + 3])
    HF = P.dscr("HF", [4, 128, TS])
    HS = P.dscr("HS", [KC, 128, TOK])
    N2S = P.dscr("N2S", [KC, 128, TOK + 2], BF16)
    HRloc = P.dscr("HRloc", [8 * 2048, NT], BF16)
    Qloc = P.dscr("Qloc", [2 * 8 * 2048, NT], BF16)
    HRin_h = nc.dram_tensor("HRin", [16 * 512, 2 * NT], BF16)
    HRall_h = nc.dram_tensor("HRall", [16 * 2048, 2 * NT], BF16)
    Qin_h = nc.dram_tensor("Qin", [2 * 8 * 256, 4 * NT], BF16)
    Qall_h = nc.dram_tensor("Qall", [2 * 8 * 1024, 4 * NT], BF16)
    NBin_h = nc.dram_tensor("NBin", [2048, 2], BF16)
    NBall_h = nc.dram_tensor("NBall", [8192, 2], BF16)
    HRin, HRall, Qin, Qall, NBin, NBall = (h.ap() for h in (HRin_h, HRall_h, Qin_h, Qall_h, NBin_h, NBall_h))

    cs2 = P.sb("cs2", [128, 64])
    cs = P.sb("cs3", [128, 640])
    P.load(cs2[:], c2[:, :], "consts")
    P.load(cs[:], c3[:, :], "consts")
    onesb = P.sb("onesb", [128, 128], BF16)
    epsT = P.sb("epsT", [128, 1])
    P.op("dve", lambda e: e.memset(onesb[:], 1.0), writes=["ones32"])
    P.op("dve", lambda e: e.memset(epsT[:], EPS), writes=["epsT"])
    PS = [P.ps("PS%d" % i, [128, NT]) for i in range(8)]
    sq = None
    rt = P.sb("rt", [128, NT])
    rstd = P.sb("rstd", [128, NT])
    nb = dict(ones32=onesb, epsT=epsT, ss_ps=PS[0], sq=sq, rt=rt, rstd=rstd)
    wgb_ = P.sb("wgb", [128, 2048], BF16)
    P.op("pool", lambda e: e.dma_start(out=wgb_[:], in_=wg[:, :]), reads=["in"], writes=["wgb"], dma=True)
    wgb = wgb_[:].rearrange("p (d a m j) -> p d a m j", d=2, a=2, m=4)
    tbb = P.sb("tbb", [128, 5, 256], BF16)
    P.op("pool", lambda e: e.dma_start(out=tbb[:], in_=tb[:, :, :]), reads=["in"], writes=["tbb"], dma=True)
    tw = P.sb("tw", [128, 256])
    P.load(tw[:], tb[:, 3, :], "tw")
    dftb_ = P.sb("dftb", [128, 1024], BF16)
    P.op("pool", lambda e: e.dma_start(out=dftb_[:], in_=dft[:, :]), reads=["in"], writes=["dftb"], dma=True)
    dftb = dftb_[:].rearrange("p (a b c) -> p a b c", a=2, b=2)
    bft = P.sb("bft", [128, 256])
    P.load(bft[:], bfT[:, :], "bft")
    kt = P.sb("kt", [128, 8]); kl = P.sb("kl", [128, 8]); kl2 = P.sb("kl2", [128, 8])
    st = P.sb("st", [128, 8])
    zt = P.sb("zt", [128, 4])
    AR = Arena(P, 188 * 1024)

    pcnt = [0]

    def ppb():
        b = pcnt[0] % 4
        pcnt[0] += 1
        return b

    pid_cache = {}

    def ids(e):
        k = id(e)
        if k not in pid_cache:
            pid = e.partition_id()
            pid_cache[k] = (pid % 4, pid // 4)
        return pid_cache[k]

    Xq = AR.bf16([128, 2, 128, 128])
    XQ_END = AR.off
    xs = [AR.f32([128, KC, NT]) for _ in range(2)]
    nTs = [AR.bf16([128, KC, NT]) for _ in range(2)]
    wxb = AR.bf16([128, KC, 512])
    wfb = AR.bf16([128, KC, 256])
    ob = [AR.f32([128, NT]) for _ in range(2)]
    P.op("pool", lambda e: e.dma_start(out=wxb, in_=wx[:, :, :]), reads=["in"], writes=["wxb"], dma=True)
    P.op("pool", lambda e: e.dma_start(out=wfb, in_=wf[:, :, :]), reads=["in"], writes=["wfb"], dma=True)
    P.op("dve", lambda e: e.memset(zt[:], 0.0), writes=["zt"])
    P.op("sp", lambda e: [e.dma_start(out=Uscr[mx, :, 0:2], in_=zt[:, 0:2], allow_slow_non_contiguous=True) for mx in range(4)] +
                         [e.dma_start(out=Uscr[mx, :, TS + 2:TS + 3], in_=zt[:, 0:1], allow_slow_non_contiguous=True) for mx in range(4)],
         reads=["zt"], writes=["Upad"], dma=True, ndma=8, semkey="zt")
    ocnt = [0]

    def normA(t):
        x = xs[t % 2]
        xk = ("x", t % 2)
        P.load(x, xg[:, :, t * NT:(t + 1) * NT], xk)
        rmsnorm_fm(P, "a", x, xk, NT, cs[:, G_MIX:G_MIX + 16], nTs[t % 2], ("nT", t % 2), **nb)

    def mmA(t):
        nT = nTs[t % 2]
        nk = ("nT", t % 2)
        for mx in range(4):
            b = ppb()
            o_ = ocnt[0] % 2
            ocnt[0] += 1
            for kc in range(KC):
                P.op("pe", (lambda mx, kc, b, nT: lambda e: e.matmul(PS[1 + b][:], lhsT=wxb[:, kc, mx * 128:(mx + 1) * 128], rhs=nT[:, kc, :],
                                                                   start=(kc == 0), stop=(kc == KC - 1)))(mx, kc, b, nT),
                     reads=[nk, "wxb"], writes=[("pp", b)])
            P.op("act", (lambda mx, b, o_: lambda e: e.activation(out=ob[o_], in_=PS[1 + b][:], func=AF.Identity,
                                                                 bias=cs2[:, 48 + mx:49 + mx], scale=1.0))(mx, b, o_),
                 reads=[("pp", b), "consts"], writes=[("ob", o_)])
            P.op("sp", (lambda mx, o_, t: lambda e: e.dma_start(out=Uscr[mx, :, 2 + t * NT:2 + (t + 1) * NT], in_=ob[o_]))(mx, o_, t),
                 reads=[("ob", o_)], writes=[("U", mx, t)], dma=True, semkey=("ob", o_))
        for blk in range(4):
            b = ppb()
            for kc in range(KC):
                P.op("pe", (lambda blk, kc, b, nT: lambda e: e.matmul(PS[1 + b][:, 0:256], lhsT=nT[:, kc, blk * 128:(blk + 1) * 128], rhs=wfb[:, kc, :],
                                                                    start=(kc == 0), stop=(kc == KC - 1)))(blk, kc, b, nT),
                     reads=[nk, "wfb"], writes=[("pp", b)])
            m = 4 * t + blk
            P.op("dve", (lambda m, b: lambda e: e.tensor_tensor(out=Xq[:, :, m, :], in0=PS[1 + b][:, 0:256].rearrange("p (h c) -> p h c", h=2),
                                                                in1=bft[:, :].rearrange("p (h c) -> p h c", h=2), op=ALU.add))(m, b),
                 reads=[("pp", b), "bft"], writes=["Xq"])

    normA(0)
    for t in range(NTG):
        if t + 1 < NTG:
            normA(t + 1)
        mmA(t)
    P.S.barrier()
    if stop == "A":
        P.op("sp", lambda e: [e.dma_start(out=DBF[:, :], in_=Uscr[1, :, 2 + 512:2 + 1024]),
                              e.dma_start(out=DBG[0:128, :], in_=Xq[:, 0, 5, :].rearrange("p c -> p c") if False else Xq[:, :, 5, :].rearrange("p h c -> p (h c)")[:, 0:256] if False else Xq[:, 0, 0:4, :].rearrange("p m c -> p (m c)"))],
             reads=[], writes=["dbg"], dma=True, ndma=2, semkey="dbg")
        P.finish(["dbg"])
        return P

    AR.reset(XQ_END)
    conv_list = []
    if stop is None:
        shp = "c p k m -> c p (k m)"
        for nm, (src, nchunk, ncol) in wsrc.items():
            for ci in range(nchunk):
                conv_list.append((nm, src, ci))

    def issue_conv(n):
        for _ in range(n):
            if conv_list:
                nm, src, ci = conv_list.pop(0)
                P.op("pool", (lambda nm, src, ci: lambda e: e.dma_start(out=wbf[nm][ci, :, :], in_=src.rearrange(shp)[ci, :, :]))(nm, src, ci),
                     reads=["in"], writes=[("wbf", nm, ci)], dma=True, semkey="wconv")
    P.op("act", lambda e: e.activation(out=kt[:], in_=cs2[:, 36:44], func=AF.Exp, scale=-1.0), reads=[], writes=["kt"])
    P.op("dve", lambda e: e.tensor_scalar_add(out=kt[:], in0=kt[:], scalar1=1.0), reads=["kt"], writes=["kt"])
    P.op("act", lambda e: e.activation(out=kt[:], in_=kt[:], func=AF.Ln), reads=["kt"], writes=["kt"])
    P.op("dve", lambda e: e.tensor_scalar_mul(out=kl[:], in0=kt[:], scalar1=-8.0), reads=["kt"], writes=["kl"])
    P.op("dve", lambda e: e.tensor_scalar_mul(out=kl2[:], in0=kt[:], scalar1=-16.0), reads=["kt"], writes=["kl2"])
    P.op("dve", lambda e: e.memset(st[:], 0.0), writes=[("st", c_) for c_ in range(8)])
    NB = 4
    ut = [[AR.f32([128, NT + 3]) for _ in range(2)] for _ in range(NB)]
    u = [[AR.f32([128, NT]) for _ in range(2)] for _ in range(NB)]
    ub = [[AR.bf16([128, NT])] * 2 for _ in range(NB)]
    r2 = [[AR.f32([128, NT]) for _ in range(2)] for _ in range(NB)]
    i2 = [[AR.f32([128, NT]) for _ in range(2)] for _ in range(NB)]
    a2 = [[AR.f32([128, NT]) for _ in range(2)] for _ in range(NB)]
    m2 = [[AR.f32([128, NT]) for _ in range(2)] for _ in range(NB)]
    h_ = [AR.f32([128, NT]) for _ in range(NB)]
    hf = [AR.f32([128, NT]) for _ in range(NB)]
    hb = [AR.bf16([128, NT]) for _ in range(NB)]
    pg = [PS[1], PS[2], PS[3], PS[4]]
    onesT = P.sb("onesT", [128, 1])
    P.op("dve", lambda e: e.memset(onesT[:], 1.0), writes=["onesT"])
    hrkeys = []

    def stage1(d, t, s_):
        issue_conv(2)
        for mx in range(4):
            b = mx
            utb, uu, ubb = ut[b][s_], u[b][s_], ub[b][s_]
            kut, ku, kub = ("ut", b, s_), ("u", b, s_), ("ub", b)
            P.load(utb, Uscr[mx, :, t * NT:t * NT + NT + 3], kut)
            if t == NTG // 2 - 1:
                P.op("dve", (lambda utb: lambda e: e.tensor_scalar_mul(out=utb[:, NT + 2:NT + 3], in0=utb[:, NT + 2:NT + 3], scalar1=cs2[:, 44:45]))(utb),
                     reads=[kut], writes=[kut])
            if t == NTG // 2:
                P.op("dve", (lambda utb: lambda e: e.tensor_scalar_mul(out=utb[:, 0:2], in0=utb[:, 0:2], scalar1=cs2[:, 44:45]))(utb),
                     reads=[kut], writes=[kut])
            P.op("dve", (lambda utb, uu, mx: lambda e: e.tensor_scalar(out=uu, in0=utb[:, 0:NT], scalar1=cs2[:, mx * 4:mx * 4 + 1],
                                                                       scalar2=cs2[:, 16 + mx:17 + mx], op0=ALU.mult, op1=ALU.add))(utb, uu, mx),
                 reads=[kut], writes=[ku])
            for k in range(1, 4):
                P.op("dve", (lambda utb, uu, mx, k: lambda e: e.scalar_tensor_tensor(out=uu, in0=utb[:, k:k + NT],
                                                                                     scalar=cs2[:, mx * 4 + k:mx * 4 + k + 1], in1=uu,
                                                                                     op0=ALU.mult, op1=ALU.add))(utb, uu, mx, k),
                     reads=[kut, ku], writes=[ku])
            P.op("act", (lambda uu, ubb: lambda e: e.copy(out=ubb, in_=uu))(uu, ubb), reads=[ku], writes=[kub])

    def stage23(d, t, s_):
        r_ = [r2[x][s_] for x in range(NB)]
        i_ = [i2[x][s_] for x in range(NB)]
        a_ = [a2[x][s_] for x in range(NB)]
        m_ = [m2[x][s_] for x in range(NB)]
        for mx in range(4):
            b = mx
            P.op("pe", (lambda b, d, mx, ubb: lambda e: e.matmul(pg[b][:], lhsT=wgb[:, d, 0, mx, :], rhs=ubb, start=True, stop=True))(b, d, mx, ub[b][s_]),
                 reads=[("ub", b), "wgb"], writes=[("pp", b)])
        for mx in range(4):
            b = mx
            col = d * 4 + mx
            P.op("act", (lambda b, col: lambda e, r_=r_: e.activation(out=r_[b], in_=pg[b][:], func=AF.Sigmoid, bias=cs2[:, 20 + col:21 + col], scale=1.0))(b, col),
                 reads=[("pp", b)], writes=[("r", b, s_)])
            P.op("pe", (lambda b, d, mx, ubb: lambda e: e.matmul(pg[b][:], lhsT=wgb[:, d, 1, mx, :], rhs=ubb, start=True, stop=True))(b, d, mx, ub[b][s_]),
                 reads=[("ub", b), "wgb"], writes=[("pp", b)])
        for mx in range(4):
            b = mx
            col = d * 4 + mx
            P.op("act", (lambda b, col: lambda e, i_=i_: e.activation(out=i_[b], in_=pg[b][:], func=AF.Sigmoid, bias=cs2[:, 28 + col:29 + col], scale=1.0))(b, col),
                 reads=[("pp", b)], writes=[("i", b, s_)])
        for mx in range(4):
            b = mx
            col = d * 4 + mx
            P.op("act", (lambda b, col: lambda e, a_=a_, r_=r_: e.activation(out=a_[b], in_=r_[b], func=AF.Exp, scale=kl[:, col:col + 1]))(b, col),
                 reads=[("r", b, s_), "kl"], writes=[("a", b, s_)])
            P.op("act", (lambda b, col: lambda e, m_=m_, r_=r_: e.activation(out=m_[b], in_=r_[b], func=AF.Exp, scale=kl2[:, col:col + 1]))(b, col),
                 reads=[("r", b, s_), "kl2"], writes=[("m", b, s_)])
        for mx in range(4):
            b = mx
            P.op("act", (lambda b: lambda e, m_=m_: e.activation(out=m_[b], in_=m_[b], func=AF.Sqrt, bias=onesT[:, 0:1], scale=-1.0))(b),
                 reads=[("m", b, s_), "onesT"], writes=[("m", b, s_)])

    def stage45(d, t, s_):
        r_ = [r2[x][s_] for x in range(NB)]
        i_ = [i2[x][s_] for x in range(NB)]
        a_ = [a2[x][s_] for x in range(NB)]
        m_ = [m2[x][s_] for x in range(NB)]
        for mx in range(4):
            b = mx
            uu = u[b][s_]
            P.op("pool", (lambda b, uu: lambda e, i_=i_: e.tensor_tensor(out=i_[b], in0=i_[b], in1=uu, op=ALU.mult))(b, uu),
                 reads=[("i", b, s_), ("u", b, s_)], writes=[("i", b, s_)])
            P.op("pool", (lambda b: lambda e, i_=i_, m_=m_: e.tensor_tensor(out=i_[b], in0=i_[b], in1=m_[b], op=ALU.mult))(b),
                 reads=[("i", b, s_), ("m", b, s_)], writes=[("i", b, s_)])
        for mx in range(4):
            b = mx
            col = d * 4 + mx
            if d == 0:
                P.op("dve", (lambda b, col: lambda e, a_=a_, i_=i_: e.tensor_tensor_scan(out=h_[b], data0=a_[b], data1=i_[b], initial=st[:, col:col + 1],
                                                                           op0=ALU.mult, op1=ALU.add))(b, col),
                     reads=[("a", b, s_), ("i", b, s_), ("st", col)], writes=[("h", b)])
                P.op("dve", (lambda b, col: lambda e: e.tensor_copy(out=st[:, col:col + 1], in_=h_[b][:, NT - 1:NT]))(b, col),
                     reads=[("h", b)], writes=[("st", col)])
                P.op("pool", (lambda b, mx, t: lambda e: e.dma_start(out=HF[mx, :, t * NT:(t + 1) * NT], in_=h_[b]))(b, mx, t),
                     reads=[("h", b)], writes=[("HF", mx, t)], dma=True, semkey=("h", b))
            else:
                P.load(hf[b], HF[mx, :, t * NT:(t + 1) * NT], ("hf", b), skey=("HF", mx, t))
                P.op("dve", (lambda b, col: lambda e, a_=a_, i_=i_: e.tensor_tensor_scan(out=h_[b][:, ::-1], data0=a_[b][:, ::-1], data1=i_[b][:, ::-1],
                                                                           initial=st[:, col:col + 1], op0=ALU.mult, op1=ALU.add))(b, col),
                     reads=[("a", b, s_), ("i", b, s_), ("st", col)], writes=[("h", b)])
                P.op("dve", (lambda b, col: lambda e: e.tensor_copy(out=st[:, col:col + 1], in_=h_[b][:, 0:1]))(b, col),
                     reads=[("h", b)], writes=[("st", col)])
                P.op("pool", (lambda b: lambda e: e.tensor_tensor(out=hb[b], in0=h_[b], in1=hf[b], op=ALU.add))(b),
                     reads=[("h", b), ("hf", b)], writes=[("hb", b)])
                k = ("HR", mx, t)
                P.op("pool", (lambda b, mx, t: lambda e: e.dma_start(out=HRin[(t // 2) * 512 + mx * 128:(t // 2) * 512 + (mx + 1) * 128, (t % 2) * NT:(t % 2 + 1) * NT], in_=hb[b]))(b, mx, t),
                     reads=[("hb", b)], writes=[k], dma=True, semkey=("hb", b))
                if mx == 3 and t % 2 == 0:
                    tp = t // 2
                    P.op("pool", (lambda tp: lambda e: e.collective_compute("AllGather", ALU.bypass, replica_groups=RG,
                                                                           ins=[HRin[tp * 512:(tp + 1) * 512, :].opt()],
                                                                           outs=[HRall[tp * 2048:(tp + 1) * 2048, :].opt()]))(tp),
                         reads=[("HR", m2, t_) for m2 in range(4) for t_ in (t, t + 1)], writes=[("HRall", tp)], dma=True, inc=1, semkey="agHR")
                    hrkeys.append(("HRall", tp))

    for d in range(2):
        order = list(range(NTG)) if d == 0 else list(range(NTG - 1, -1, -1))
        stage1(d, order[0], 0)
        for ti, t in enumerate(order):
            if ti == NTG // 2:
                P.op("dve", (lambda d: lambda e: e.tensor_scalar_mul(out=st[:, d * 4:d * 4 + 4], in0=st[:, d * 4:d * 4 + 4],
                                                                      scalar1=cs2[:, 44:45]))(d),
                     reads=[("st", d * 4 + c_) for c_ in range(4)], writes=[("st", d * 4 + c_) for c_ in range(4)])
            stage23(d, t, ti % 2)
            if ti + 1 < NTG:
                stage1(d, order[ti + 1], (ti + 1) % 2)
            stage45(d, t, ti % 2)
    P.S.barrier()
    if stop == "B":
        P.op("sp", lambda e: [e.dma_start(out=DBG[:, :], in_=HRall[3 * 2048:4 * 2048, 0:NT]), e.dma_start(out=DBF[:, :], in_=HF[2, :, 512:1024])],
             reads=[], writes=["dbg"], dma=True, ndma=2, semkey="dbg")
        P.finish(["dbg"])
        return P

    AR.reset(XQ_END)
    FB = AR.bf16([128, 128, 128])
    Bp = AR.bf16([128, 128, 2, 128])
    s1 = [AR.f32([128, 256]) for _ in range(2)]
    t1 = [AR.f32([128, 128]) for _ in range(2)]
    t2 = [AR.f32([128, 128]) for _ in range(2)]
    t3 = [AR.f32([128, 128]) for _ in range(2)]
    t4 = [AR.f32([128, 128]) for _ in range(2)]
    p1 = [PS[5], PS[6]]
    p2 = [PS[1], PS[2]]
    ptrs = [PS[7][:, :].bitcast(BF16)[:, 0:128], PS[0][:, :].bitcast(BF16)[:, 0:128]]
    ptrk = [("ps7", 0), "ss_ps"]
    Tr = tw[:, 0:128]
    Ti = tw[:, 128:256]
    ident = tbb[:, 4, 0:128]
    qkeys = []
    for ch in range(2):
        for c in range(128):
            b = c % 2
            P.op("pe", (lambda ch, c, b: lambda e: e.transpose(out=ptrs[b], in_=Xq[:, ch, :, c], identity=ident))(ch, c, b),
                 reads=[("SEL", ch), "tbb"], writes=[ptrk[b]])
            P.op("act" if b else "dve", (lambda c, b: lambda e: (e.copy if b else e.tensor_copy)(out=FB[:, :, c], in_=ptrs[b]))(c, b),
                 reads=[ptrk[b]], writes=["FB"])
        if stop == "C0":
            P.op("sp", lambda e: e.dma_start(out=DBG[0:128, :], in_=FB[:, 0:4, :].rearrange("p a b -> p (a b)")), reads=["FB"], writes=["dbg"], dma=True, semkey="dbg")
            P.op("sp", lambda e: e.dma_start(out=DBF[:, :], in_=HF[2, :, 512:1024]), reads=[], writes=["dbg2"], dma=True, semkey="dbg2")
            P.finish(["dbg", "dbg2"])
            return P
        for c in range(128):
            b = c % 2
            P.op("pe", (lambda c, b: lambda e: e.matmul(p1[b][:, 0:256], lhsT=FB[:, :, c], rhs=tbb[:, 0, :], start=True, stop=True))(c, b),
                 reads=["FB", "tbb"], writes=[("ps", 5 + b)])
            P.op("act", (lambda b: lambda e: e.copy(out=s1[b], in_=p1[b][:, 0:256]))(b), reads=[("ps", 5 + b)], writes=[("s1", b)])
            P.op("dve", (lambda b: lambda e: e.tensor_tensor(out=t1[b], in0=s1[b][:, 0:128], in1=Tr, op=ALU.mult))(b),
                 reads=[("s1", b), "tw"], writes=[("t1", b)])
            P.op("dve", (lambda b: lambda e: e.tensor_tensor(out=t2[b], in0=s1[b][:, 128:256], in1=Ti, op=ALU.mult))(b),
                 reads=[("s1", b), "tw"], writes=[("t2", b)])
            P.op("dve", (lambda b, c: lambda e: e.tensor_tensor(out=Bp[:, c, 0, :], in0=t1[b], in1=t2[b], op=ALU.subtract))(b, c),
                 reads=[("t1", b), ("t2", b)], writes=[("Bp", c)])
            P.op("pool", (lambda b: lambda e: e.tensor_tensor(out=t3[b], in0=s1[b][:, 0:128], in1=Ti, op=ALU.mult))(b),
                 reads=[("s1", b), "tw"], writes=[("t3", b)])
            P.op("pool", (lambda b: lambda e: e.tensor_tensor(out=t4[b], in0=s1[b][:, 128:256], in1=Tr, op=ALU.mult))(b),
                 reads=[("s1", b), "tw"], writes=[("t4", b)])
            P.op("pool", (lambda b, c: lambda e: e.tensor_tensor(out=Bp[:, c, 1, :], in0=t3[b], in1=t4[b], op=ALU.add))(b, c),
                 reads=[("t3", b), ("t4", b)], writes=[("Bp", c)])
        bpk = [("Bp", c) for c in range(128)]
        if stop == "C1":
            P.op("sp", lambda e: e.dma_start(out=DBG[0:128, :], in_=Bp[:, 0:2, :, :].rearrange("p a b c -> p (a b c)")), reads=bpk, writes=["dbg"], dma=True, semkey="dbg")
            P.op("sp", lambda e: e.dma_start(out=DBF[:, :], in_=HF[2, :, 512:1024]), reads=[], writes=["dbg2"], dma=True, semkey="dbg2")
            P.finish(["dbg", "dbg2"])
            return P
        for ri in range(2):
            for j in range(128):
                b = j % 2
                P.op("pe", (lambda j, b, ri: lambda e: e.matmul(p2[b][:, 0:128], lhsT=Bp[:, :, 0, j], rhs=tbb[:, 1, ri * 128:(ri + 1) * 128], start=True, stop=False))(j, b, ri),
                     reads=bpk + ["tbb"], writes=[("pp", b)])
                P.op("pe", (lambda j, b, ri: lambda e: e.matmul(p2[b][:, 0:128], lhsT=Bp[:, :, 1, j], rhs=tbb[:, 2, ri * 128:(ri + 1) * 128], start=False, stop=True))(j, b, ri),
                     reads=bpk + ["tbb"], writes=[("pp", b)])
                P.op("act" if b else "dve", (lambda j, b: lambda e: (e.copy if b else e.tensor_copy)(out=FB[:, :, j], in_=p2[b][:, 0:128]))(j, b),
                     reads=[("pp", b)], writes=["FB"])
            k = ("Q", ch, ri)
            SEL = Xq[:, ch, :, :].rearrange("p a b -> p (a b)")
            P.op("dve", (lambda SEL: lambda e: e.tensor_scalar_mul(out=SEL.rearrange("p (b k x) -> p b k x", b=2, x=64),
                                                                  in0=FB.rearrange("p k (b x) -> p b k x", b=2),
                                                                  scalar1=cs[:, MK + 4:MK + 5]))(SEL),
                 reads=["FB"], writes=[("SEL", ch)])
            P.op("dve", (lambda SEL: lambda e: e.scalar_tensor_tensor(out=SEL, in0=FB.rearrange("p a b -> p (a b)"), scalar=cs[:, MK + 3:MK + 4],
                                                                     in1=SEL, op0=ALU.mult, op1=ALU.add))(SEL),
                 reads=["FB", ("SEL", ch)], writes=[("SEL", ch)])
            P.op("sp", (lambda ch, ri, SEL: lambda e: e.dma_start(out=Qin[ch * 2048:(ch + 1) * 2048, :].rearrange("(s i c) n -> c s i n", i=2, c=128)[:, :, ri, :],
                                                                  in_=SEL.rearrange("p (s n) -> p s n", n=4 * NT)))(ch, ri, SEL),
                 reads=[("SEL", ch)], writes=[k], dma=True, semkey=("SEL", ch))
            qkeys.append(k)
        if stop != "C2":
            for sq_ in range(8):
                idx = ch * 8 + sq_
                P.op("pool", (lambda idx: lambda e: e.collective_compute("AllGather", ALU.bypass, replica_groups=RG,
                                                                         ins=[Qin[idx * 256:(idx + 1) * 256, :].opt()],
                                                                         outs=[Qall[idx * 1024:(idx + 1) * 1024, :].opt()]))(idx),
                     reads=[("Q", ch, 0), ("Q", ch, 1)], writes=[("Qall", idx)], dma=True, inc=1, semkey="agQ")
    if stop == "C2":
        P.op("sp", lambda e: e.dma_start(out=DBG[0:512, :], in_=Qin[512:1024, :]), reads=qkeys, writes=["dbg"], dma=True, semkey="dbg")
        P.op("sp", lambda e: e.dma_start(out=DBF[:, :], in_=HF[2, :, 512:1024]), reads=[], writes=["dbg2"], dma=True, semkey="dbg2")
        P.finish(["dbg", "dbg2"])
        return P
    P.S.barrier()
    if stop == "C":
        P.op("sp", lambda e: [e.dma_start(out=DBG[:, :], in_=Qall[2048:4096, 0:NT]), e.dma_start(out=DBF[:, :], in_=HF[2, :, 512:1024])],
             reads=[], writes=["dbg"], dma=True, ndma=2, semkey="dbg")
        P.finish(["dbg"])
        return P

    AR.reset(0)
    X32 = AR.f32([128, KC, NT])
    NTb = AR.bf16([128, KC, NT + 2])
    QTb = AR.bf16([128, 16, NT])
    MT = QTb
    FT = AR.bf16([128, 8, NT])
    BIG = AR.bf16([128, 48, NT])
    WB = [AR.bf16([128, KC * 512]) for _ in range(3)]
    e1 = [AR.f32([128, NT]) for _ in range(2)]
    e2 = [AR.f32([128, NT]) for _ in range(2)]
    upS = [AR.f32([128, NT + 2]) for _ in range(2)]
    upH = [AR.f32([128, 2]) for _ in range(2)]
    cv = e2
    gg = [AR.f32([128, NT]) for _ in range(2)]
    pp = [PS[1], PS[2], PS[3], PS[4]]
    ph = [PS[5], PS[6]]
    wcnt = [0]

    def wload(nm, ci, kcn, cols):
        i = wcnt[0] % 3
        wcnt[0] += 1
        view = WB[i][:, 0:kcn * cols].rearrange("p (k m) -> p k m", k=kcn)
        P.op("sp", lambda e: e.dma_start(out=WB[i][:, 0:kcn * cols], in_=wbf[nm][ci, :, :]), reads=["in"], writes=[("WB", i)], dma=True)
        return view, ("WB", i)

    def linear(wview, wkey, mloc, kcn, rhs_fn, rkeys, N):
        b = ppb()
        for kc in range(kcn):
            P.op("pe", (lambda kc, b: lambda e: e.matmul(pp[b][:, :N], lhsT=wview[:, kc, mloc * 128:(mloc + 1) * 128], rhs=rhs_fn(kc),
                                                       start=(kc == 0), stop=(kc == kcn - 1)))(kc, b),
                 reads=list(rkeys) + [wkey], writes=[("pp", b)])
        return b

    ecnt = [0]
    nbkeys = []

    def p3a(t):
        t0 = t * NT
        N = NT

        P.load(X32, xo[:, :, t0:t0 + NT], "X32")
        def ldh(e):
            j, g = ids(e)
            return e.dma_start(out=BIG[:, 0:16, :], in_=HRall[bass.ds(j * 8192 + (t // 2) * 2048, 2048), (t % 2) * NT:(t % 2 + 1) * NT].rearrange("(k p) n -> p k n", p=128))
        P.op("sp", ldh, reads=[], writes=["BIG0"], dma=True)

        def ldq(e):
            j, g = ids(e)
            return [e.dma_start(out=QTb[:, ch_ * 8:(ch_ + 1) * 8, :],
                                in_=Qall[bass.ds(j * 2048 + (ch_ * 8192 + (t // 4) * 1024), 1024), (t % 4) * NT:(t % 4 + 1) * NT].rearrange("(k c) n -> c k n", c=128))
                    for ch_ in range(2)]
        P.op("act", ldq, reads=[], writes=["QTb"], dma=True, ndma=2)
        rmsnorm_fm(P, "a", X32, "X32", N, cs[:, G_MIX:G_MIX + 16], NTb, "NTb", **nb)
        for part in range(3):
            for cq in range(4):
                wv, wk = wload("wgz", part * 4 + cq, KC, 512)
                for ml in range(4):
                    mt = cq * 4 + ml
                    b = linear(wv, wk, ml, KC, lambda kc: NTb[:, kc, :N], ["NTb"], N)
                    bias = cs[:, B_GZ + part * 16 + mt:B_GZ + part * 16 + mt + 1]
                    if part == 0:
                        i = ecnt[0] % 2
                        ecnt[0] += 1
                        P.op("act", (lambda b, i, bias: lambda e: e.activation(out=e1[i][:, :N], in_=pp[b][:, :N], func=AF.Gelu_apprx_tanh, bias=bias, scale=1.0))(b, i, bias),
                             reads=[("pp", b)], writes=[("e1", i)])
                        P.op("dve", (lambda mt, i: lambda e: e.tensor_tensor(out=BIG[:, mt, :N], in0=BIG[:, mt, :N], in1=e1[i][:, :N], op=ALU.mult))(mt, i),
                             reads=[("e1", i), "BIG0"], writes=["BIG0"])
                    else:
                        P.op("act", (lambda b, mt, bias, part: lambda e: e.activation(out=BIG[:, part * 16 + mt, :N], in_=pp[b][:, :N], func=AF.Sigmoid, bias=bias, scale=1.0))(b, mt, bias, part),
                             reads=[("pp", b)], writes=["BIG%d" % part])
        for g4 in range(4):
            for ct in range(2):
                b = ppb()
                n = 0
                for chh in range(2):
                    for ri in range(2):
                        P.op("pe", (lambda g4, ct, chh, ri, b, n: lambda e: e.matmul(pp[b][:, :N], lhsT=dftb[:, chh, ri, ct * 128:(ct + 1) * 128],
                                                                                     rhs=QTb[:, chh * 8 + g4 * 2 + ri, :N], start=(n == 0), stop=(n == 3)))(g4, ct, chh, ri, b, n),
                             reads=["QTb", "dftb"], writes=[("pp", b)])
                        n += 1
                P.op("act", (lambda g4, ct, b: lambda e: e.copy(out=FT[:, g4 * 2 + ct, :N], in_=pp[b][:, :N]))(g4, ct, b),
                     reads=[("pp", b)], writes=["FT"])
        for cq in range(4):
            wva, wka = wload("woa", cq, KC, 512)
            wvb, wkb = wload("wob", cq, 8, 512)
            for ml in range(4):
                mt = cq * 4 + ml
                ba = linear(wva, wka, ml, KC, lambda kc: BIG[:, kc, :N], ["BIG0"], N)
                bb = linear(wvb, wkb, ml, 8, lambda kc: FT[:, kc, :N], ["FT"], N)
                i = ecnt[0] % 2
                ecnt[0] += 1
                P.op("dve", (lambda ba, mt, i: lambda e: e.tensor_tensor(out=e1[i][:, :N], in0=pp[ba][:, :N], in1=BIG[:, 16 + mt, :N], op=ALU.mult))(ba, mt, i),
                     reads=[("pp", ba), "BIG1"], writes=[("e1", i)])
                P.op("dve", (lambda bb, mt, i: lambda e: e.tensor_tensor(out=e2[i][:, :N], in0=pp[bb][:, :N], in1=BIG[:, 32 + mt, :N], op=ALU.mult))(bb, mt, i),
                     reads=[("pp", bb), "BIG2"], writes=[("e2", i)])
                P.op("pool", (lambda mt, i: lambda e: e.tensor_tensor(out=MT[:, mt, :N], in0=e1[i][:, :N], in1=e2[i][:, :N], op=ALU.add))(mt, i),
                     reads=[("e1", i), ("e2", i)], writes=["QTb"])
        for cq in range(4):
            wv, wk = wload("wo", cq, KC, 512)
            for ml in range(4):
                mt = cq * 4 + ml
                b = linear(wv, wk, ml, KC, lambda kc: MT[:, kc, :N], ["QTb"], N)
                P.op("dve", (lambda b, mt: lambda e: e.scalar_tensor_tensor(out=X32[:, mt, :N], in0=pp[b][:, :N], scalar=cs[:, B_OUT + mt:B_OUT + mt + 1],
                                                                           in1=X32[:, mt, :N], op0=ALU.add, op1=ALU.add))(b, mt),
                     reads=[("pp", b), "X32"], writes=["X32"])
        P.op("sp", lambda e: e.dma_start(out=HS[:, :, t0:t0 + N].rearrange("k p n -> p k n"), in_=X32),
             reads=["X32"], writes=[("HS", t0)], dma=True, semkey="X32")
        rmsnorm_fm(P, "b", X32, "X32", N, cs[:, G_FFN:G_FFN + 16], NTb, "NTb", **nb)
        P.op("sp", lambda e: e.dma_start(out=N2S[:, :, 1 + t0:1 + t0 + N].rearrange("k p n -> p k n"), in_=NTb[:, :, :N]),
             reads=["NTb"], writes=[("N2S", t0)], dma=True, semkey="NTb")
        if t == 0:
            P.op("sp", lambda e: e.dma_start(out=NBin[:, 0:1].rearrange("(k p) n -> p k n", p=128), in_=NTb[:, :, 0:1], allow_slow_non_contiguous=True),
                 reads=["NTb"], writes=[("NB", 0)], dma=True, semkey="NTb")
            nbkeys.append(("NB", 0))
        if t == NTL - 1:
            P.op("sp", lambda e: e.dma_start(out=NBin[:, 1:2].rearrange("(k p) n -> p k n", p=128), in_=NTb[:, :, NT - 1:NT], allow_slow_non_contiguous=True),
                 reads=["NTb"], writes=[("NB", 1)], dma=True, semkey="NTb")
            nbkeys.append(("NB", 1))

    def p3b(t):
        t0 = t * NT
        rk = [("N2S", t0)]
        rk.append(("N2S", t0 - NT) if t > 0 else ("N2S", "halo"))
        rk.append(("N2S", t0 + NT) if t < NTL - 1 else ("N2S", "halo"))
        P.op("sp", lambda e: e.dma_start(out=NTb, in_=N2S[:, :, t0:t0 + NT + 2].rearrange("k p n -> p k n")),
             reads=rk, writes=["NTb"], dma=True)
        P.op("sp", lambda e: e.dma_start(out=X32, in_=HS[:, :, t0:t0 + NT].rearrange("k p n -> p k n")),
             reads=[("HS", t0)], writes=["X32"], dma=True)
        mL = cs[:, MK:MK + 1] if t == 0 else cs[:, MK + 2:MK + 3]
        mR = cs[:, MK + 1:MK + 2] if t == NTL - 1 else cs[:, MK + 2:MK + 3]
        for q in range(24):
            wv, wk = wload("wup", q, KC, 512)
            for ml in range(4):
                gv, jj = ml // 2, ml % 2
                jp = 2 * q + jj
                mt = jp + 48 * gv
                b = ppb()
                hbk = b % 2
                for kc in range(KC):
                    P.op("pe", (lambda kc, b, ml, wv: lambda e: e.matmul(pp[b][:], lhsT=wv[:, kc, ml * 128:(ml + 1) * 128], rhs=NTb[:, kc, 1:NT + 1],
                                                                       start=(kc == 0), stop=(kc == KC - 1)))(kc, b, ml, wv),
                         reads=["NTb", wk], writes=[("pp", b)])
                for kc in range(KC):
                    P.op("pe", (lambda kc, hbk, ml, wv: lambda e: e.matmul(ph[hbk][:, 0:2], lhsT=wv[:, kc, ml * 128:(ml + 1) * 128], rhs=NTb[:, kc, 0:NT + 2:NT + 1],
                                                                         start=(kc == 0), stop=(kc == KC - 1)))(kc, hbk, ml, wv),
                         reads=["NTb", wk], writes=[("ps", 5 + hbk)])
                i = ecnt[0] % 2
                ecnt[0] += 1
                bias = cs[:, B_UP + mt:B_UP + mt + 1]
                P.op("act", (lambda b, i, bias: lambda e: e.activation(out=upS[i][:, 1:NT + 1], in_=pp[b][:], func=AF.Identity, bias=bias, scale=1.0))(b, i, bias),
                     reads=[("pp", b)], writes=[("upS", i)])
                P.op("act", (lambda hbk, i, bias: lambda e: e.activation(out=upH[i], in_=ph[hbk][:, 0:2], func=AF.Identity, bias=bias, scale=1.0))(hbk, i, bias),
                     reads=[("ps", 5 + hbk)], writes=[("upH", i)])
                P.op("dve", (lambda i, mL: lambda e: e.tensor_scalar_mul(out=upS[i][:, 0:1], in0=upH[i][:, 0:1], scalar1=mL))(i, mL),
                     reads=[("upH", i)], writes=[("upS", i)])
                P.op("dve", (lambda i, mR: lambda e: e.tensor_scalar_mul(out=upS[i][:, NT + 1:NT + 2], in0=upH[i][:, 1:2], scalar1=mR))(i, mR),
                     reads=[("upH", i)], writes=[("upS", i)])
                w0 = cs[:, CFW + mt * 3:CFW + mt * 3 + 1]
                w1 = cs[:, CFW + mt * 3 + 1:CFW + mt * 3 + 2]
                w2 = cs[:, CFW + mt * 3 + 2:CFW + mt * 3 + 3]
                cb = cs[:, CFB + mt:CFB + mt + 1]
                P.op("dve", (lambda i, w0, cb: lambda e: e.tensor_scalar(out=cv[i], in0=upS[i][:, 0:NT], scalar1=w0, scalar2=cb, op0=ALU.mult, op1=ALU.add))(i, w0, cb),
                     reads=[("upS", i)], writes=[("e2", i)])
                P.op("dve", (lambda i, w1: lambda e: e.scalar_tensor_tensor(out=cv[i], in0=upS[i][:, 1:NT + 1], scalar=w1, in1=cv[i], op0=ALU.mult, op1=ALU.add))(i, w1),
                     reads=[("upS", i), ("e2", i)], writes=[("e2", i)])
                P.op("dve", (lambda i, w2: lambda e: e.scalar_tensor_tensor(out=cv[i], in0=upS[i][:, 2:NT + 2], scalar=w2, in1=cv[i], op0=ALU.mult, op1=ALU.add))(i, w2),
                     reads=[("upS", i), ("e2", i)], writes=[("e2", i)])
                if gv == 0:
                    P.op("act", (lambda i, jj: lambda e: e.activation(out=gg[jj], in_=cv[i], func=AF.Gelu_apprx_tanh))(i, jj),
                         reads=[("e2", i)], writes=[("gg", jj)])
                else:
                    P.op("pool", (lambda i, jj, jp: lambda e: e.tensor_tensor(out=BIG[:, jp, :], in0=gg[jj], in1=cv[i], op=ALU.mult))(i, jj, jp),
                         reads=[("gg", jj), ("e2", i)], writes=["BIG%d" % (jp // 16)])
        for mt in range(KC):
            wv, wk = wload("wdn", mt, 48, 128)
            b = linear(wv, wk, 0, 48, lambda kc: BIG[:, kc, :], ["BIG0", "BIG1", "BIG2"], NT)
            P.op("dve", (lambda b, mt: lambda e: e.scalar_tensor_tensor(out=X32[:, mt, :], in0=pp[b][:], scalar=cs[:, B_DN + mt:B_DN + mt + 1],
                                                                       in1=X32[:, mt, :], op0=ALU.add, op1=ALU.add))(b, mt),
                 reads=[("pp", b), "X32"], writes=["X32"])
        rmsnorm_fm(P, "c", X32, "X32", NT, cs[:, G_FIN:G_FIN + 16], X32, "X32", scratch=NTb, skey="NTb", **nb)
        k = ("YT", t)
        P.op("sp", lambda e: e.dma_start(out=YT[:, :, t0:t0 + NT].rearrange("k p n -> p k n"), in_=X32),
             reads=["X32"], writes=[k], dma=True)
        return k

    for t in range(NTL):
        p3a(t)
    P.op("pool", lambda e: e.collective_compute("AllGather", ALU.bypass, replica_groups=RG, ins=[NBin_h.ap().opt()], outs=[NBall_h.ap().opt()]),
         reads=nbkeys, writes=["NBall"], dma=True, inc=1, semkey="agNB")

    hl = P.sb("hl", [128, KC, 2], BF16)

    def ldhalo(e):
        j, g = ids(e)
        rl = ((j + 3) % 4) * 2048
        rr = ((j + 1) % 4) * 2048
        return [e.dma_start(out=hl[:, :, 0:1], in_=NBall[bass.ds(rl, 2048), 1:2].rearrange("(k p) n -> p k n", p=128), allow_slow_non_contiguous=True),
                e.dma_start(out=hl[:, :, 1:2], in_=NBall[bass.ds(rr, 2048), 0:1].rearrange("(k p) n -> p k n", p=128), allow_slow_non_contiguous=True)]
    P.op("pool", ldhalo, reads=["NBall"], writes=["hl"], dma=True, ndma=2, semkey="halo")
    P.op("sp", lambda e: [e.dma_start(out=N2S[:, :, 0:1].rearrange("k p n -> p k n"), in_=hl[:, :, 0:1], allow_slow_non_contiguous=True),
                          e.dma_start(out=N2S[:, :, TOK + 1:TOK + 2].rearrange("k p n -> p k n"), in_=hl[:, :, 1:2], allow_slow_non_contiguous=True)],
         reads=["hl"], writes=[("N2S", "halo")], dma=True, ndma=2, semkey="halo2")
    fin = [p3b(t) for t in range(NTL)]
    P.finish(fin)
    return P


def fm(a):
    a = np.asarray(a, np.float32)
    return np.ascontiguousarray(a.reshape(-1, 128).T)


def wfm(w):
    K, M = w.shape
    return np.ascontiguousarray(w.reshape(K // 128, 128, M).transpose(1, 0, 2))


def xfm(x):
    T, Fd = x.shape
    return np.ascontiguousarray(x.T.reshape(Fd // 128, 128, T).transpose(1, 0, 2))


def run(P, in_maps, names=None):
    if names is not None:
        in_maps = [{k: v for k, v in m.items() if k in names} for m in in_maps]
    res = run_bass_kernel_spmd(P.nc, in_maps, core_ids=list(range(8)))
    return res.results


def dft_tables(seq_len):
    nb = TS // seq_len
    n2n = seq_len // 128
    p = np.arange(128)
    bidx, n2 = p // n2n, p % n2n
    j = np.arange(128)
    bj, k2 = j // n2n, j % n2n
    ang = 2 * np.pi * np.outer(n2, k2) / n2n
    same = (bidx[:, None] == bj[None, :]).astype(np.float64)
    sc = 1.0 / np.sqrt(seq_len)
    T1 = np.concatenate([np.cos(ang) * same, -np.sin(ang) * same], 1) * sc
    n1 = np.arange(128)
    a2 = 2 * np.pi * np.outer(n1, n1) / 128.0
    Gr, Gi = np.cos(a2), -np.sin(a2)
    G1 = np.concatenate([Gr, Gi], 1)
    G2 = np.concatenate([-Gi, Gr], 1)
    at = 2 * np.pi * np.outer(n1, k2) / seq_len
    TW = np.concatenate([np.cos(at), -np.sin(at)], 1)
    return np.ascontiguousarray(np.stack([T1, G1, G2, TW], 1).astype(np.float32))


_PROGS = {}
_STOP = None


def kernel(x_prompt, x_sample, g_mix, w_in, b_in, conv_a_w, conv_a_b, lru_w_a, lru_b_a, lru_w_x, lru_b_x,
           lru_lam, w_out_a, w_out_b, w_out, b_out, g_ffn, w_up, b_up, conv_f_w, conv_f_b, w_down, b_down,
           g_final):
    f32 = np.float32
    xs = [np.asarray(x_prompt, f32).reshape(TS, D), np.asarray(x_sample, f32).reshape(TS, D)]
    seqlen = [16384, 8192]
    w_in = np.asarray(w_in, f32)[0]
    b_in = np.asarray(b_in, f32)[0]
    if "f" not in _PROGS:
        _PROGS["f"] = build_fused(_STOP)
    caw = np.asarray(conv_a_w, f32)[0]
    cab = np.asarray(conv_a_b, f32)[0]
    lwa = np.asarray(lru_w_a, f32)[0]
    lwx = np.asarray(lru_w_x, f32)[0]
    lba = np.asarray(lru_b_a, f32)[0]
    lbx = np.asarray(lru_b_x, f32)[0]
    lam = np.asarray(lru_lam, f32)[0]
    bgz = np.concatenate([fm(b_in[2048:4096]), fm(b_in[5120:7168]), fm(b_in[7168:9216])], 1)
    wgz_full = np.concatenate([w_in[:, 2048:4096], w_in[:, 5120:9216]], 1)
    wgz = np.stack([wfm(wgz_full[:, q * 512:(q + 1) * 512]) for q in range(12)], 0)
    woa_ = np.asarray(w_out_a, f32)[0]
    wob_ = np.asarray(w_out_b, f32)[0]
    wo_ = np.asarray(w_out, f32)[0]
    wup_ = np.asarray(w_up, f32)[0]
    wdn_ = np.asarray(w_down, f32)[0]
    woa = np.stack([wfm(woa_[:, q * 512:(q + 1) * 512]) for q in range(4)], 0)
    wob = np.stack([wfm(wob_[:, q * 512:(q + 1) * 512]) for q in range(4)], 0)
    wo = np.stack([wfm(wo_[:, q * 512:(q + 1) * 512]) for q in range(4)], 0)
    wupc = []
    for q in range(24):
        cols = np.concatenate([wup_[:, 256 * q:256 * q + 256], wup_[:, 6144 + 256 * q:6144 + 256 * q + 256]], 1)
        wupc.append(wfm(cols))
    wupc = np.stack(wupc, 0)
    wdn = np.stack([wfm(wdn_[:, m * 128:(m + 1) * 128]) for m in range(16)], 0)
    cc = np.arange(256)
    ang = 2 * np.pi * np.outer(cc, cc) / 256.0
    dm = np.stack([np.cos(ang), np.sin(ang)], 0) / 16.0
    dft = np.ascontiguousarray(dm.reshape(2, 2, 128, 256).transpose(2, 1, 0, 3).astype(f32)).reshape(128, 1024)
    cfw = np.asarray(conv_f_w, f32)[0]
    c3 = np.zeros((128, 640), f32)
    c3[:, 0:16] = fm(np.asarray(g_mix, f32)[0])
    c3[:, 16:32] = fm(np.asarray(g_ffn, f32)[0])
    c3[:, 32:48] = fm(np.asarray(g_final, f32))
    c3[:, 48:96] = bgz
    c3[:, 96:112] = fm(np.asarray(b_out, f32)[0])
    c3[:, 112:128] = fm(np.asarray(b_down, f32)[0])
    c3[:, 128:224] = fm(np.asarray(b_up, f32)[0])
    c3[:, 224:512] = cfw.reshape(3, 96, 128).transpose(2, 1, 0).reshape(128, 288)
    c3[:, 512:608] = fm(np.asarray(conv_f_b, f32)[0])
    c3[:, 610] = 1.0
    xgs = [xfm(xs[g]) for g in range(2)]
    ident = np.concatenate([np.eye(128, dtype=f32), np.zeros((128, 128), f32)], 1)
    ins = []
    for c in range(8):
        g, j = c // 4, c % 4
        L = seqlen[g]
        chs = slice(512 * j, 512 * j + 512)
        fch = slice(4096 + 256 * j, 4096 + 256 * j + 256)
        heads = slice(4 * j, 4 * j + 4)
        wg = np.stack([np.stack([lwa[d, heads], lwx[d, heads]], 0) for d in range(2)], 0)
        wg = np.ascontiguousarray(wg.transpose(3, 0, 1, 2, 4)).reshape(128, 2048)
        c2 = np.zeros((128, 64), f32)
        c2[:, 0:16] = caw[:, chs].reshape(4, 4, 128).transpose(2, 1, 0).reshape(128, 16)
        c2[:, 16:20] = fm(cab[chs])
        c2[:, 20:28] = np.concatenate([fm(lba[0, chs]), fm(lba[1, chs])], 1)
        c2[:, 28:36] = np.concatenate([fm(lbx[0, chs]), fm(lbx[1, chs])], 1)
        c2[:, 36:44] = np.concatenate([fm(lam[0, chs]), fm(lam[1, chs])], 1)
        c2[:, 44] = 1.0 if L == TS else 0.0
        c2[:, 48:52] = fm(b_in[chs])
        tb = np.concatenate([dft_tables(L), ident[:, None, :]], 1)
        lo, hi = j * TOK, (j + 1) * TOK
        cm = c3.copy()
        cm[:, 608] = 1.0 if (lo % L) != 0 else 0.0
        cm[:, 609] = 1.0 if (hi % L) != 0 else 0.0
        cm[:, 611] = 1.0 if L == TS else 0.0
        cm[:, 612] = 0.0 if L == TS else 1.0
        ins.append({"xg": xgs[g], "xo": np.ascontiguousarray(xgs[g][:, :, lo:hi]), "wx": wfm(w_in[:, chs]), "wf": wfm(w_in[:, fch]),
                    "bfT": np.ascontiguousarray(np.tile(b_in[fch][None, :], (128, 1))),
                    "wg": wg, "c2": c2, "tb": np.ascontiguousarray(tb), "wgz": wgz, "woa": woa, "wob": wob, "wo": wo,
                    "wup": wupc, "wdn": wdn, "dft": dft, "c3": cm})
    if _STOP is not None:
        return run(_PROGS["f"], ins, names={"xg", "xo", "wx", "wf", "bfT", "wg", "c2", "tb", "dft", "c3"})
    r = run(_PROGS["f"], ins)
    ys = []
    for g in range(2):
        yt = np.concatenate([r[g * 4 + j]["YT"] for j in range(4)], 2)
        ys.append(np.ascontiguousarray(yt.reshape(D, TS).T))
    return (ys[0].reshape(1, 16384, D).astype(f32), ys[1].reshape(2, 8192, D).astype(f32))
```

```python
import contextlib
import numpy as np
import ml_dtypes
import concourse.bass as bass
import concourse.mybir as mybir
from concourse.bass_utils import run_bass_kernel_spmd

F32 = mybir.dt.float32
BF16 = mybir.dt.bfloat16
AF = mybir.ActivationFunctionType
ALU = mybir.AluOpType
NPBF = ml_dtypes.bfloat16

D = 2048
KC = 16
TOK = 4096
TS = 16384
NT = 512
EPS = 1e-6


class Sched:
    def __init__(self, nc):
        self.nc = nc
        self.ops = []
        self.state = {}
        self.bar_from = 0

    def barrier(self):
        last = {}
        dmas = []
        for i, o in enumerate(self.ops[self.bar_from:], self.bar_from):
            if o["dma"]:
                dmas.append(i)
            elif o["fn"] is not None:
                last[o["eng"]] = i
        deps = sorted(set(dmas) | set(last.values()))
        for eng in ("pe", "act", "dve", "pool", "sp"):
            self.ops.append(dict(eng=eng, fn=None, deps=list(deps), dma=False, semkey=None, signal=False, ndma=1, inc=16,
                                 bar=True))
        self.bar_from = len(self.ops)
        self.state = {}

    def op(self, eng, fn, reads=(), writes=(), dma=False, semkey=None, ndma=1, inc=16):
        oid = len(self.ops)
        deps = set()
        for k in reads:
            st = self.state.setdefault(k, [None, []])
            if st[0] is not None:
                deps.add(st[0])
        for k in writes:
            st = self.state.setdefault(k, [None, []])
            if st[0] is not None:
                deps.add(st[0])
            last = {}
            for r in st[1]:
                o = self.ops[r]
                if o["dma"]:
                    deps.add(r)
                else:
                    last[o["eng"]] = max(last.get(o["eng"], -1), r)
            deps.update(last.values())
        for k in reads:
            self.state[k][1].append(oid)
        for k in writes:
            self.state[k] = [oid, []]
        deps.discard(oid)
        if dma and semkey is None:
            semkey = writes[0] if len(writes) else reads[0]
        self.ops.append(dict(eng=eng, fn=fn, deps=sorted(deps), dma=dma, semkey=semkey,
                             signal=dma, ndma=ndma, inc=inc))
        return oid

    def emit(self, final_keys=()):
        nc = self.nc
        ops = self.ops
        self.op("sp", None, reads=list(final_keys))
        for o in ops:
            for d in o["deps"]:
                od = ops[d]
                if od["dma"]:
                    continue
                if od["eng"] == "pe" and o["eng"] == "pe" and not o["dma"]:
                    continue
                od["signal"] = True
        cnt = {}
        for o in ops:
            if o["dma"]:
                k = ("d", o["semkey"])
                cnt[k] = cnt.get(k, 0) + o["inc"] * o["ndma"]
                o["done"] = (k, cnt[k])
            elif o["signal"]:
                k = ("e", o["eng"])
                cnt[k] = cnt.get(k, 0) + 1
                o["done"] = (k, cnt[k])
        semkeys = sorted(cnt.keys(), key=str)
        with contextlib.ExitStack() as es:
            sems = {}
            for i, k in enumerate(semkeys):
                sems[k] = es.enter_context(nc.semaphore("s%d" % i))
            block = es.enter_context(nc.Block())

            def run(engname):
                def body(e):
                    known = {}
                    for o in ops:
                        if o["eng"] != engname:
                            continue
                        for d in o["deps"]:
                            od = ops[d]
                            if "done" not in od:
                                continue
                            if (not od["dma"]) and od["eng"] == "pe" and engname == "pe" and not o["dma"]:
                                continue
                            k, v = od["done"]
                            if known.get(k, 0) < v:
                                e.wait_ge(sems[k], v)
                                known[k] = v
                        if o["fn"] is None:
                            continue
                        ins = o["fn"](e)
                        if o["dma"]:
                            if not isinstance(ins, (list, tuple)):
                                ins = [ins]
                            assert len(ins) == o["ndma"], (len(ins), o["ndma"])
                            for i_ in ins:
                                i_.then_inc(sems[o["done"][0]], o["inc"])
                        elif o["signal"]:
                            ins.then_inc(sems[o["done"][0]], 1)
                return body

            block.tensor(run("pe"))
            block.scalar(run("act"))
            block.vector(run("dve"))
            block.gpsimd(run("pool"))
            block.sync(run("sp"))


class Prog:
    def __init__(self):
        self.nc = bass.Bass("TRN2", target_bir_lowering=False)
        self.es = contextlib.ExitStack()
        self.S = Sched(self.nc)
        self.outs = []
        self._rr = 0

    def din(self, name, shape, dt=F32):
        return self.nc.dram_tensor(name, list(shape), dt, kind="ExternalInput").ap()

    def dout(self, name, shape, dt=F32):
        self.outs.append(name)
        return self.nc.dram_tensor(name, list(shape), dt, kind="ExternalOutput").ap()

    def dscr(self, name, shape, dt=F32):
        return self.nc.dram_tensor(name, list(shape), dt).ap()

    def sb(self, name, shape, dt=F32):
        return self.es.enter_context(self.nc.sbuf_tensor(name, list(shape), dt))

    def ps(self, name, shape, dt=F32):
        return self.es.enter_context(self.nc.psum_tensor(name, list(shape), dt))

    def op(self, *a, **k):
        return self.S.op(*a, **k)

    def ew(self):
        self._rr ^= 1
        return "dve" if self._rr else "pool"

    def load(self, dst_ap, src_ap, dkey, skey="in", eng="sp"):
        self.op(eng, lambda e: e.dma_start(out=dst_ap, in_=src_ap), reads=[skey], writes=[dkey], dma=True)

    def finish(self, final_keys):
        self.S.emit(final_keys=final_keys)
        self.es.close()


def rmsnorm_fm(P, tag, xT, xkey, N, gcol, outT, okey, ones32, epsT, ss_ps, sq, rt, rstd, scratch=None, skey=None):
    if scratch is None:
        scratch, skey = outT, okey
    P.op("act", lambda e: e.activation(out=scratch[:, :, :N], in_=xT[:, :, :N], func=AF.Square), reads=[xkey], writes=[skey])
    for kc in range(KC):
        P.op("pe", (lambda kc: lambda e: e.matmul(ss_ps[:, :N], lhsT=ones32[:, :], rhs=scratch[:, kc, :N],
                                                  start=(kc == 0), stop=(kc == KC - 1)))(kc),
             reads=[skey, "ones32"], writes=["ss_ps"])
    P.op("act", lambda e: e.activation(out=rt[:, :N], in_=ss_ps[:, :N], func=AF.Sqrt, bias=epsT[:, 0:1], scale=1.0 / D),
         reads=["ss_ps", "epsT"], writes=["rt"])
    P.op("dve", lambda e: e.reciprocal(out=rstd[:, :N], in_=rt[:, :N]), reads=["rt"], writes=["rstd"])
    for kc in range(KC):
        P.op("dve", (lambda kc: lambda e: e.scalar_tensor_tensor(out=outT[:, kc, :N], in0=xT[:, kc, :N],
                                                                scalar=gcol[:, kc:kc + 1], in1=rstd[:, :N],
                                                                op0=ALU.mult, op1=ALU.mult))(kc),
             reads=[xkey, "rstd", "consts"] + ([skey] if skey != okey else []), writes=[okey])


def norm_bufs(P):
    ones32 = P.sb("ones32", [128, 128])
    epsT = P.sb("epsT", [128, 1])
    P.op("dve", lambda e: e.memset(ones32[:], 1.0), writes=["ones32"])
    P.op("dve", lambda e: e.memset(epsT[:], EPS), writes=["epsT"])
    ss_ps = P.ps("ss_ps", [128, NT])
    sq = [P.sb("sq%d" % i, [128, NT]) for i in range(2)]
    rt = P.sb("rt", [128, NT])
    rstd = P.sb("rstd", [128, NT])
    return dict(ones32=ones32, epsT=epsT, ss_ps=ss_ps, sq=sq, rt=rt, rstd=rstd)


class Arena:
    def __init__(self, P, nbytes):
        self.t = P.sb("arena", [128, nbytes // 4])
        self.nbytes = nbytes
        self.off = 0

    def reset(self, off=0):
        self.off = off

    def _take(self, nb):
        nb = (nb + 63) // 64 * 64
        o = self.off
        self.off += nb
        assert self.off <= self.nbytes, (self.off, self.nbytes)
        return self.t[:, o // 4:(o + nb) // 4]

    @staticmethod
    def _shape(ap, shape):
        if len(shape) == 2:
            return ap
        if len(shape) == 3:
            return ap.rearrange("p (a b) -> p a b", a=shape[1])
        if len(shape) == 4:
            return ap.rearrange("p (a b c) -> p a b c", a=shape[1], b=shape[2])
        raise ValueError(shape)

    def f32(self, shape):
        n = int(np.prod(shape[1:]))
        return self._shape(self._take(4 * n)[:, :n], shape)

    def bf16(self, shape):
        n = int(np.prod(shape[1:]))
        return self._shape(self._take(2 * n).bitcast(BF16)[:, :n], shape)


RG = [[0, 1, 2, 3], [4, 5, 6, 7]]
G_MIX, G_FFN, G_FIN, B_GZ, B_OUT, B_DN, B_UP, CFW, CFB, MK = 0, 16, 32, 48, 96, 112, 128, 224, 512, 608


def build_fused(stop=None):
    P = Prog()
    nc = P.nc
    NTL = TOK // NT
    NTG = TS // NT
    HALF = TS // 2
    xg = P.din("xg", [128, KC, TS])
    xo = P.din("xo", [128, KC, TOK])
    wx = P.din("wx", [128, KC, 512])
    wf = P.din("wf", [128, KC, 256])
    bfT = P.din("bfT", [128, 256])
    wg = P.din("wg", [128, 2048])
    c2 = P.din("c2", [128, 64])
    tb = P.din("tb", [128, 5, 256])
    if stop is None:
        wgz = P.din("wgz", [12, 128, KC, 512])
        woa = P.din("woa", [4, 128, KC, 512])
        wob = P.din("wob", [4, 128, 8, 512])
        wo = P.din("wo", [4, 128, KC, 512])
        wup = P.din("wup", [24, 128, KC, 512])
        wdn = P.din("wdn", [16, 128, 48, 128])
        YT = P.dout("YT", [KC, 128, TOK])
        wsrc = dict(wgz=(wgz, 12, KC * 512), woa=(woa, 4, KC * 512), wob=(wob, 4, 8 * 512), wo=(wo, 4, KC * 512),
                    wup=(wup, 24, KC * 512), wdn=(wdn, 16, 48 * 128))
        wbf = {}
        for nm, (src, nchunk, ncol) in wsrc.items():
            wbf[nm] = P.dscr(nm + "_bf", [nchunk, 128, ncol], BF16)
    else:
        DBG = P.dout("DBG", [2048, NT], BF16)
        DBF = P.dout("DBF", [128, NT])
    dft = P.din("dft", [128, 1024])
    c3 = P.din("c3", [128, 640])
    Uscr = P.dscr("Uscr", [4, 128, TS + 3])
    HF = P.dscr("HF", [4, 128, TS])
    HS = P.dscr("HS", [KC, 128, TOK])
    N2S = P.dscr("N2S", [KC, 128, TOK + 2], BF16)
    HRloc = P.dscr("HRloc", [8 * 2048, NT], BF16)
    Qloc = P.dscr("Qloc", [2 * 8 * 2048, NT], BF16)
    HRin_h = nc.dram_tensor("HRin", [16 * 512, 2 * NT], BF16)
    HRall_h = nc.dram_tensor("HRall", [16 * 2048, 2 * NT], BF16)
    Qin_h = nc.dram_tensor("Qin", [2 * 8 * 256, 4 * NT], BF16)
    Qall_h = nc.dram_tensor("Qall", [2 * 8 * 1024, 4 * NT], BF16)
    NBin_h = nc.dram_tensor("NBin", [2048, 2], BF16)
    NBall_h = nc.dram_tensor("NBall", [8192, 2], BF16)
    HRin, HRall, Qin, Qall, NBin, NBall = (h.ap() for h in (HRin_h, HRall_h, Qin_h, Qall_h, NBin_h, NBall_h))

    cs2 = P.sb("cs2", [128, 64])
    cs = P.sb("cs3", [128, 640])
    P.load(cs2[:], c2[:, :], "consts")
    P.load(cs[:], c3[:, :], "consts")
    onesb = P.sb("onesb", [128, 128], BF16)
    epsT = P.sb("epsT", [128, 1])
    P.op("dve", lambda e: e.memset(onesb[:], 1.0), writes=["ones32"])
    P.op("dve", lambda e: e.memset(epsT[:], EPS), writes=["epsT"])
    PS = [P.ps("PS%d" % i, [128, NT]) for i in range(8)]
    sq = None
    rt = P.sb("rt", [128, NT])
    rstd = P.sb("rstd", [128, NT])
    nb = dict(ones32=onesb, epsT=epsT, ss_ps=PS[0], sq=sq, rt=rt, rstd=rstd)
    wgb_ = P.sb("wgb", [128, 2048], BF16)
    P.op("pool", lambda e: e.dma_start(out=wgb_[:], in_=wg[:, :]), reads=["in"], writes=["wgb"], dma=True)
    wgb = wgb_[:].rearrange("p (d a m j) -> p d a m j", d=2, a=2, m=4)
    tbb = P.sb("tbb", [128, 5, 256], BF16)
    P.op("pool", lambda e: e.dma_start(out=tbb[:], in_=tb[:, :, :]), reads=["in"], writes=["tbb"], dma=True)
    tw = P.sb("tw", [128, 256])
    P.load(tw[:], tb[:, 3, :], "tw")
    dftb_ = P.sb("dftb", [128, 1024], BF16)
    P.op("pool", lambda e: e.dma_start(out=dftb_[:], in_=dft[:, :]), reads=["in"], writes=["dftb"], dma=True)
    dftb = dftb_[:].rearrange("p (a b c) -> p a b c", a=2, b=2)
    bft = P.sb("bft", [128, 256])
    P.load(bft[:], bfT[:, :], "bft")
    kt = P.sb("kt", [128, 8]); kl = P.sb("kl", [128, 8]); kl2 = P.sb("kl2", [128, 8])
    st = P.sb("st", [128, 8])
    zt = P.sb("zt", [128, 4])
    AR = Arena(P, 188 * 1024)

    pcnt = [0]

    def ppb():
        b = pcnt[0] % 4
        pcnt[0] += 1
        return b

    pid_cache = {}

    def ids(e):
        k = id(e)
        if k not in pid_cache:
            pid = e.partition_id()
            pid_cache[k] = (pid % 4, pid // 4)
        return pid_cache[k]

    Xq = AR.bf16([128, 2, 128, 128])
    XQ_END = AR.off
    xs = [AR.f32([128, KC, NT]) for _ in range(2)]
    nTs = [AR.bf16([128, KC, NT]) for _ in range(2)]
    wxb = AR.bf16([128, KC, 512])
    wfb = AR.bf16([128, KC, 256])
    ob = [AR.f32([128, NT]) for _ in range(2)]
    P.op("pool", lambda e: e.dma_start(out=wxb, in_=wx[:, :, :]), reads=["in"], writes=["wxb"], dma=True)
    P.op("pool", lambda e: e.dma_start(out=wfb, in_=wf[:, :, :]), reads=["in"], writes=["wfb"], dma=True)
    P.op("dve", lambda e: e.memset(zt[:], 0.0), writes=["zt"])
    P.op("sp", lambda e: [e.dma_start(out=Uscr[mx, :, 0:2], in_=zt[:, 0:2], allow_slow_non_contiguous=True) for mx in range(4)] +
                         [e.dma_start(out=Uscr[mx, :, TS + 2:TS + 3], in_=zt[:, 0:1], allow_slow_non_contiguous=True) for mx in range(4)],
         reads=["zt"], writes=["Upad"], dma=True, ndma=8, semkey="zt")
    ocnt = [0]

    def normA(t):
        x = xs[t % 2]
        xk = ("x", t % 2)
        P.load(x, xg[:, :, t * NT:(t + 1) * NT], xk)
        rmsnorm_fm(P, "a", x, xk, NT, cs[:, G_MIX:G_MIX + 16], nTs[t % 2], ("nT", t % 2), **nb)

    def mmA(t):
        nT = nTs[t % 2]
        nk = ("nT", t % 2)
        for mx in range(4):
            b = ppb()
            o_ = ocnt[0] % 2
            ocnt[0] += 1
            for kc in range(KC):
                P.op("pe", (lambda mx, kc, b, nT: lambda e: e.matmul(PS[1 + b][:], lhsT=wxb[:, kc, mx * 128:(mx + 1) * 128], rhs=nT[:, kc, :],
                                                                   start=(kc == 0), stop=(kc == KC - 1)))(mx, kc, b, nT),
                     reads=[nk, "wxb"], writes=[("pp", b)])
            P.op("act", (lambda mx, b, o_: lambda e: e.activation(out=ob[o_], in_=PS[1 + b][:], func=AF.Identity,
                                                                 bias=cs2[:, 48 + mx:49 + mx], scale=1.0))(mx, b, o_),
                 reads=[("pp", b), "consts"], writes=[("ob", o_)])
            P.op("sp", (lambda mx, o_, t: lambda e: e.dma_start(out=Uscr[mx, :, 2 + t * NT:2 + (t + 1) * NT], in_=ob[o_]))(mx, o_, t),
                 reads=[("ob", o_)], writes=[("U", mx, t)], dma=True, semkey=("ob", o_))
        for blk in range(4):
            b = ppb()
            for kc in range(KC):
                P.op("pe", (lambda blk, kc, b, nT: lambda e: e.matmul(PS[1 + b][:, 0:256], lhsT=nT[:, kc, blk * 128:(blk + 1) * 128], rhs=wfb[:, kc, :],
                                                                    start=(kc == 0), stop=(kc == KC - 1)))(blk, kc, b, nT),
                     reads=[nk, "wfb"], writes=[("pp", b)])
            m = 4 * t + blk
            P.op("dve", (lambda m, b: lambda e: e.tensor_tensor(out=Xq[:, :, m, :], in0=PS[1 + b][:, 0:256].rearrange("p (h c) -> p h c", h=2),
                                                                in1=bft[:, :].rearrange("p (h c) -> p h c", h=2), op=ALU.add))(m, b),
                 reads=[("pp", b), "bft"], writes=["Xq"])

    normA(0)
    for t in range(NTG):
        if t + 1 < NTG:
            normA(t + 1)
        mmA(t)
    P.S.barrier()
    if stop == "A":
        P.op("sp", lambda e: [e.dma_start(out=DBF[:, :], in_=Uscr[1, :, 2 + 512:2 + 1024]),
                              e.dma_start(out=DBG[0:128, :], in_=Xq[:, 0, 5, :].rearrange("p c -> p c") if False else Xq[:, :, 5, :].rearrange("p h c -> p (h c)")[:, 0:256] if False else Xq[:, 0, 0:4, :].rearrange("p m c -> p (m c)"))],
             reads=[], writes=["dbg"], dma=True, ndma=2, semkey="dbg")
        P.finish(["dbg"])
        return P

    AR.reset(XQ_END)
    conv_list = []
    if stop is None:
        shp = "c p k m -> c p (k m)"
        for nm, (src, nchunk, ncol) in wsrc.items():
            for ci in range(nchunk):
                conv_list.append((nm, src, ci))

    def issue_conv(n):
        for _ in range(n):
            if conv_list:
                nm, src, ci = conv_list.pop(0)
                P.op("pool", (lambda nm, src, ci: lambda e: e.dma_start(out=wbf[nm][ci, :, :], in_=src.rearrange(shp)[ci, :, :]))(nm, src, ci),
                     reads=["in"], writes=[("wbf", nm, ci)], dma=True, semkey="wconv")
    P.op("act", lambda e: e.activation(out=kt[:], in_=cs2[:, 36:44], func=AF.Exp, scale=-1.0), reads=[], writes=["kt"])
    P.op("dve", lambda e: e.tensor_scalar_add(out=kt[:], in0=kt[:], scalar1=1.0), reads=["kt"], writes=["kt"])
    P.op("act", lambda e: e.activation(out=kt[:], in_=kt[:], func=AF.Ln), reads=["kt"], writes=["kt"])
    P.op("dve", lambda e: e.tensor_scalar_mul(out=kl[:], in0=kt[:], scalar1=-8.0), reads=["kt"], writes=["kl"])
    P.op("dve", lambda e: e.tensor_scalar_mul(out=kl2[:], in0=kt[:], scalar1=-16.0), reads=["kt"], writes=["kl2"])
    P.op("dve", lambda e: e.memset(st[:], 0.0), writes=[("st", c_) for c_ in range(8)])
    NB = 4
    ut = [[AR.f32([128, NT + 3]) for _ in range(2)] for _ in range(NB)]
    u = [[AR.f32([128, NT]) for _ in range(2)] for _ in range(NB)]
    ub = [[AR.bf16([128, NT])] * 2 for _ in range(NB)]
    r2 = [[AR.f32([128, NT]) for _ in range(2)] for _ in range(NB)]
    i2 = [[AR.f32([128, NT]) for _ in range(2)] for _ in range(NB)]
    a2 = [[AR.f32([128, NT]) for _ in range(2)] for _ in range(NB)]
    m2 = [[AR.f32([128, NT]) for _ in range(2)] for _ in range(NB)]
    h_ = [AR.f32([128, NT]) for _ in range(NB)]
    hf = [AR.f32([128, NT]) for _ in range(NB)]
    hb = [AR.bf16([128, NT]) for _ in range(NB)]
    pg = [PS[1], PS[2], PS[3], PS[4]]
    onesT = P.sb("onesT", [128, 1])
    P.op("dve", lambda e: e.memset(onesT[:], 1.0), writes=["onesT"])
    hrkeys = []

    def stage1(d, t, s_):
        issue_conv(2)
        for mx in range(4):
            b = mx
            utb, uu, ubb = ut[b][s_], u[b][s_], ub[b][s_]
            kut, ku, kub = ("ut", b, s_), ("u", b, s_), ("ub", b)
            P.load(utb, Uscr[mx, :, t * NT:t * NT + NT + 3], kut)
            if t == NTG // 2 - 1:
                P.op("dve", (lambda utb: lambda e: e.tensor_scalar_mul(out=utb[:, NT + 2:NT + 3], in0=utb[:, NT + 2:NT + 3], scalar1=cs2[:, 44:45]))(utb),
                     reads=[kut], writes=[kut])
            if t == NTG // 2:
                P.op("dve", (lambda utb: lambda e: e.tensor_scalar_mul(out=utb[:, 0:2], in0=utb[:, 0:2], scalar1=cs2[:, 44:45]))(utb),
                     reads=[kut], writes=[kut])
            P.op("dve", (lambda utb, uu, mx: lambda e: e.tensor_scalar(out=uu, in0=utb[:, 0:NT], scalar1=cs2[:, mx * 4:mx * 4 + 1],
                                                                       scalar2=cs2[:, 16 + mx:17 + mx], op0=ALU.mult, op1=ALU.add))(utb, uu, mx),
                 reads=[kut], writes=[ku])
            for k in range(1, 4):
                P.op("dve", (lambda utb, uu, mx, k: lambda e: e.scalar_tensor_tensor(out=uu, in0=utb[:, k:k + NT],
                                                                                     scalar=cs2[:, mx * 4 + k:mx * 4 + k + 1], in1=uu,
                                                                                     op0=ALU.mult, op1=ALU.add))(utb, uu, mx, k),
                     reads=[kut, ku], writes=[ku])
            P.op("act", (lambda uu, ubb: lambda e: e.copy(out=ubb, in_=uu))(uu, ubb), reads=[ku], writes=[kub])

    def stage23(d, t, s_):
        r_ = [r2[x][s_] for x in range(NB)]
        i_ = [i2[x][s_] for x in range(NB)]
        a_ = [a2[x][s_] for x in range(NB)]
        m_ = [m2[x][s_] for x in range(NB)]
        for mx in range(4):
            b = mx
            P.op("pe", (lambda b, d, mx, ubb: lambda e: e.matmul(pg[b][:], lhsT=wgb[:, d, 0, mx, :], rhs=ubb, start=True, stop=True))(b, d, mx, ub[b][s_]),
                 reads=[("ub", b), "wgb"], writes=[("pp", b)])
        for mx in range(4):
            b = mx
            col = d * 4 + mx
            P.op("act", (lambda b, col: lambda e, r_=r_: e.activation(out=r_[b], in_=pg[b][:], func=AF.Sigmoid, bias=cs2[:, 20 + col:21 + col], scale=1.0))(b, col),
                 reads=[("pp", b)], writes=[("r", b, s_)])
            P.op("pe", (lambda b, d, mx, ubb: lambda e: e.matmul(pg[b][:], lhsT=wgb[:, d, 1, mx, :], rhs=ubb, start=True, stop=True))(b, d, mx, ub[b][s_]),
                 reads=[("ub", b), "wgb"], writes=[("pp", b)])
        for mx in range(4):
            b = mx
            col = d * 4 + mx
            P.op("act", (lambda b, col: lambda e, i_=i_: e.activation(out=i_[b], in_=pg[b][:], func=AF.Sigmoid, bias=cs2[:, 28 + col:29 + col], scale=1.0))(b, col),
                 reads=[("pp", b)], writes=[("i", b, s_)])
        for mx in range(4):
            b = mx
            col = d * 4 + mx
            P.op("act", (lambda b, col: lambda e, a_=a_, r_=r_: e.activation(out=a_[b], in_=r_[b], func=AF.Exp, scale=kl[:, col:col + 1]))(b, col),
                 reads=[("r", b, s_), "kl"], writes=[("a", b, s_)])
            P.op("act", (lambda b, col: lambda e, m_=m_, r_=r_: e.activation(out=m_[b], in_=r_[b], func=AF.Exp, scale=kl2[:, col:col + 1]))(b, col),
                 reads=[("r", b, s_), "kl2"], writes=[("m", b, s_)])
        for mx in range(4):
            b = mx
            P.op("act", (lambda b: lambda e, m_=m_: e.activation(out=m_[b], in_=m_[b], func=AF.Sqrt, bias=onesT[:, 0:1], scale=-1.0))(b),
                 reads=[("m", b, s_), "onesT"], writes=[("m", b, s_)])

    def stage45(d, t, s_):
        r_ = [r2[x][s_] for x in range(NB)]
        i_ = [i2[x][s_] for x in range(NB)]
        a_ = [a2[x][s_] for x in range(NB)]
        m_ = [m2[x][s_] for x in range(NB)]
        for mx in range(4):
            b = mx
            uu = u[b][s_]
            P.op("pool", (lambda b, uu: lambda e, i_=i_: e.tensor_tensor(out=i_[b], in0=i_[b], in1=uu, op=ALU.mult))(b, uu),
                 reads=[("i", b, s_), ("u", b, s_)], writes=[("i", b, s_)])
            P.op("pool", (lambda b: lambda e, i_=i_, m_=m_: e.tensor_tensor(out=i_[b], in0=i_[b], in1=m_[b], op=ALU.mult))(b),
                 reads=[("i", b, s_), ("m", b, s_)], writes=[("i", b, s_)])
        for mx in range(4):
            b = mx
            col = d * 4 + mx
            if d == 0:
                P.op("dve", (lambda b, col: lambda e, a_=a_, i_=i_: e.tensor_tensor_scan(out=h_[b], data0=a_[b], data1=i_[b], initial=st[:, col:col + 1],
                                                                           op0=ALU.mult, op1=ALU.add))(b, col),
                     reads=[("a", b, s_), ("i", b, s_), ("st", col)], writes=[("h", b)])
                P.op("dve", (lambda b, col: lambda e: e.tensor_copy(out=st[:, col:col + 1], in_=h_[b][:, NT - 1:NT]))(b, col),
                     reads=[("h", b)], writes=[("st", col)])
                P.op("pool", (lambda b, mx, t: lambda e: e.dma_start(out=HF[mx, :, t * NT:(t + 1) * NT], in_=h_[b]))(b, mx, t),
                     reads=[("h", b)], writes=[("HF", mx, t)], dma=True, semkey=("h", b))
            else:
                P.load(hf[b], HF[mx, :, t * NT:(t + 1) * NT], ("hf", b), skey=("HF", mx, t))
                P.op("dve", (lambda b, col: lambda e, a_=a_, i_=i_: e.tensor_tensor_scan(out=h_[b][:, ::-1], data0=a_[b][:, ::-1], data1=i_[b][:, ::-1],
                                                                           initial=st[:, col:col + 1], op0=ALU.mult, op1=ALU.add))(b, col),
                     reads=[("a", b, s_), ("i", b, s_), ("st", col)], writes=[("h", b)])
                P.op("dve", (lambda b, col: lambda e: e.tensor_copy(out=st[:, col:col + 1], in_=h_[b][:, 0:1]))(b, col),
                     reads=[("h", b)], writes=[("st", col)])
                P.op("pool", (lambda b: lambda e: e.tensor_tensor(out=hb[b], in0=h_[b], in1=hf[b], op=ALU.add))(b),
                     reads=[("h", b), ("hf", b)], writes=[("hb", b)])
                k = ("HR", mx, t)
                P.op("pool", (lambda b, mx, t: lambda e: e.dma_start(out=HRin[(t // 2) * 512 + mx * 128:(t // 2) * 512 + (mx + 1) * 128, (t % 2) * NT:(t % 2 + 1) * NT], in_=hb[b]))(b, mx, t),
                     reads=[("hb", b)], writes=[k], dma=True, semkey=("hb", b))
                if mx == 3 and t % 2 == 0:
                    tp = t // 2
                    P.op("pool", (lambda tp: lambda e: e.collective_compute("AllGather", ALU.bypass, replica_groups=RG,
                                                                           ins=[HRin[tp * 512:(tp + 1) * 512, :].opt()],
                                                                           outs=[HRall[tp * 2048:(tp + 1) * 2048, :].opt()]))(tp),
                         reads=[("HR", m2, t_) for m2 in range(4) for t_ in (t, t + 1)], writes=[("HRall", tp)], dma=True, inc=1, semkey="agHR")
                    hrkeys.append(("HRall", tp))

    for d in range(2):
        order = list(range(NTG)) if d == 0 else list(range(NTG - 1, -1, -1))
        stage1(d, order[0], 0)
        for ti, t in enumerate(order):
            if ti == NTG // 2:
                P.op("dve", (lambda d: lambda e: e.tensor_scalar_mul(out=st[:, d * 4:d * 4 + 4], in0=st[:, d * 4:d * 4 + 4],
                                                                      scalar1=cs2[:, 44:45]))(d),
                     reads=[("st", d * 4 + c_) for c_ in range(4)], writes=[("st", d * 4 + c_) for c_ in range(4)])
            stage23(d, t, ti % 2)
            if ti + 1 < NTG:
                stage1(d, order[ti + 1], (ti + 1) % 2)
            stage45(d, t, ti % 2)
    P.S.barrier()
    if stop == "B":
        P.op("sp", lambda e: [e.dma_start(out=DBG[:, :], in_=HRall[3 * 2048:4 * 2048, 0:NT]), e.dma_start(out=DBF[:, :], in_=HF[2, :, 512:1024])],
             reads=[], writes=["dbg"], dma=True, ndma=2, semkey="dbg")
        P.finish(["dbg"])
        return P

    AR.reset(XQ_END)
    FB = AR.bf16([128, 128, 128])
    Bp = AR.bf16([128, 128, 2, 128])
    s1 = [AR.f32([128, 256]) for _ in range(2)]
    t1 = [AR.f32([128, 128]) for _ in range(2)]
    t2 = [AR.f32([128, 128]) for _ in range(2)]
    t3 = [AR.f32([128, 128]) for _ in range(2)]
    t4 = [AR.f32([128, 128]) for _ in range(2)]
    p1 = [PS[5], PS[6]]
    p2 = [PS[1], PS[2]]
    ptrs = [PS[7][:, :].bitcast(BF16)[:, 0:128], PS[0][:, :].bitcast(BF16)[:, 0:128]]
    ptrk = [("ps7", 0), "ss_ps"]
    Tr = tw[:, 0:128]
    Ti = tw[:, 128:256]
    ident = tbb[:, 4, 0:128]
    qkeys = []
    for ch in range(2):
        for c in range(128):
            b = c % 2
            P.op("pe", (lambda ch, c, b: lambda e: e.transpose(out=ptrs[b], in_=Xq[:, ch, :, c], identity=ident))(ch, c, b),
                 reads=[("SEL", ch), "tbb"], writes=[ptrk[b]])
            P.op("act" if b else "dve", (lambda c, b: lambda e: (e.copy if b else e.tensor_copy)(out=FB[:, :, c], in_=ptrs[b]))(c, b),
                 reads=[ptrk[b]], writes=["FB"])
        if stop == "C0":
            P.op("sp", lambda e: e.dma_start(out=DBG[0:128, :], in_=FB[:, 0:4, :].rearrange("p a b -> p (a b)")), reads=["FB"], writes=["dbg"], dma=True, semkey="dbg")
            P.op("sp", lambda e: e.dma_start(out=DBF[:, :], in_=HF[2, :, 512:1024]), reads=[], writes=["dbg2"], dma=True, semkey="dbg2")
            P.finish(["dbg", "dbg2"])
            return P
        for c in range(128):
            b = c % 2
            P.op("pe", (lambda c, b: lambda e: e.matmul(p1[b][:, 0:256], lhsT=FB[:, :, c], rhs=tbb[:, 0, :], start=True, stop=True))(c, b),
                 reads=["FB", "tbb"], writes=[("ps", 5 + b)])
            P.op("act", (lambda b: lambda e: e.copy(out=s1[b], in_=p1[b][:, 0:256]))(b), reads=[("ps", 5 + b)], writes=[("s1", b)])
            P.op("dve", (lambda b: lambda e: e.tensor_tensor(out=t1[b], in0=s1[b][:, 0:128], in1=Tr, op=ALU.mult))(b),
                 reads=[("s1", b), "tw"], writes=[("t1", b)])
            P.op("dve", (lambda b: lambda e: e.tensor_tensor(out=t2[b], in0=s1[b][:, 128:256], in1=Ti, op=ALU.mult))(b),
                 reads=[("s1", b), "tw"], writes=[("t2", b)])
            P.op("dve", (lambda b, c: lambda e: e.tensor_tensor(out=Bp[:, c, 0, :], in0=t1[b], in1=t2[b], op=ALU.subtract))(b, c),
                 reads=[("t1", b), ("t2", b)], writes=[("Bp", c)])
            P.op("pool", (lambda b: lambda e: e.tensor_tensor(out=t3[b], in0=s1[b][:, 0:128], in1=Ti, op=ALU.mult))(b),
                 reads=[("s1", b), "tw"], writes=[("t3", b)])
            P.op("pool", (lambda b: lambda e: e.tensor_tensor(out=t4[b], in0=s1[b][:, 128:256], in1=Tr, op=ALU.mult))(b),
                 reads=[("s1", b), "tw"], writes=[("t4", b)])
            P.op("pool", (lambda b, c: lambda e: e.tensor_tensor(out=Bp[:, c, 1, :], in0=t3[b], in1=t4[b], op=ALU.add))(b, c),
                 reads=[("t3", b), ("t4", b)], writes=[("Bp", c)])
        bpk = [("Bp", c) for c in range(128)]
        if stop == "C1":
            P.op("sp", lambda e: e.dma_start(out=DBG[0:128, :], in_=Bp[:, 0:2, :, :].rearrange("p a b c -> p (a b c)")), reads=bpk, writes=["dbg"], dma=True, semkey="dbg")
            P.op("sp", lambda e: e.dma_start(out=DBF[:, :], in_=HF[2, :, 512:1024]), reads=[], writes=["dbg2"], dma=True, semkey="dbg2")
            P.finish(["dbg", "dbg2"])
            return P
        for ri in range(2):
            for j in range(128):
                b = j % 2
                P.op("pe", (lambda j, b, ri: lambda e: e.matmul(p2[b][:, 0:128], lhsT=Bp[:, :, 0, j], rhs=tbb[:, 1, ri * 128:(ri + 1) * 128], start=True, stop=False))(j, b, ri),
                     reads=bpk + ["tbb"], writes=[("pp", b)])
                P.op("pe", (lambda j, b, ri: lambda e: e.matmul(p2[b][:, 0:128], lhsT=Bp[:, :, 1, j], rhs=tbb[:, 2, ri * 128:(ri + 1) * 128], start=False, stop=True))(j, b, ri),
                     reads=bpk + ["tbb"], writes=[("pp", b)])
                P.op("act" if b else "dve", (lambda j, b: lambda e: (e.copy if b else e.tensor_copy)(out=FB[:, :, j], in_=p2[b][:, 0:128]))(j, b),
                     reads=[("pp", b)], writes=["FB"])
            k = ("Q", ch, ri)
            SEL = Xq[:, ch, :, :].rearrange("p a b -> p (a b)")
            P.op("dve", (lambda SEL: lambda e: e.tensor_scalar_mul(out=SEL.rearrange("p (b k x) -> p b k x", b=2, x=64),
                                                                  in0=FB.rearrange("p k (b x) -> p b k x", b=2),
                                                                  scalar1=cs[:, MK + 4:MK + 5]))(SEL),
                 reads=["FB"], writes=[("SEL", ch)])
            P.op("dve", (lambda SEL: lambda e: e.scalar_tensor_tensor(out=SEL, in0=FB.rearrange("p a b -> p (a b)"), scalar=cs[:, MK + 3:MK + 4],
                                                                     in1=SEL, op0=ALU.mult, op1=ALU.add))(SEL),
                 reads=["FB", ("SEL", ch)], writes=[("SEL", ch)])
            P.op("sp", (lambda ch, ri, SEL: lambda e: e.dma_start(out=Qin[ch * 2048:(ch + 1) * 2048, :].rearrange("(s i c) n -> c s i n", i=2, c=128)[:, :, ri, :],
                                                                  in_=SEL.rearrange("p (s n) -> p s n", n=4 * NT)))(ch, ri, SEL),
                 reads=[("SEL", ch)], writes=[k], dma=True, semkey=("SEL", ch))
            qkeys.append(k)
        if stop != "C2":
            for sq_ in range(8):
                idx = ch * 8 + sq_
                P.op("pool", (lambda idx: lambda e: e.collective_compute("AllGather", ALU.bypass, replica_groups=RG,
                                                                         ins=[Qin[idx * 256:(idx + 1) * 256, :].opt()],
                                                                         outs=[Qall[idx * 1024:(idx + 1) * 1024, :].opt()]))(idx),
                     reads=[("Q", ch, 0), ("Q", ch, 1)], writes=[("Qall", idx)], dma=True, inc=1, semkey="agQ")
    if stop == "C2":
        P.op("sp", lambda e: e.dma_start(out=DBG[0:512, :], in_=Qin[512:1024, :]), reads=qkeys, writes=["dbg"], dma=True, semkey="dbg")
        P.op("sp", lambda e: e.dma_start(out=DBF[:, :], in_=HF[2, :, 512:1024]), reads=[], writes=["dbg2"], dma=True, semkey="dbg2")
        P.finish(["dbg", "dbg2"])
        return P
    P.S.barrier()
    if stop == "C":
        P.op("sp", lambda e: [e.dma_start(out=DBG[:, :], in_=Qall[2048:4096, 0:NT]), e.dma_start(out=DBF[:, :], in_=HF[2, :, 512:1024])],
             reads=[], writes=["dbg"], dma=True, ndma=2, semkey="dbg")
        P.finish(["dbg"])
        return P

    AR.reset(0)
    X32 = AR.f32([128, KC, NT])
    NTb = AR.bf16([128, KC, NT + 2])
    QTb = AR.bf16([128, 16, NT])
    MT = QTb
    FT = AR.bf16([128, 8, NT])
    BIG = AR.bf16([128, 48, NT])
    WB = [AR.bf16([128, KC * 512]) for _ in range(3)]
    e1 = [AR.f32([128, NT]) for _ in range(2)]
    e2 = [AR.f32([128, NT]) for _ in range(2)]
    upS = [AR.f32([128, NT + 2]) for _ in range(2)]
    upH = [AR.f32([128, 2]) for _ in range(2)]
    cv = e2
    gg = [AR.f32([128, NT]) for _ in range(2)]
    pp = [PS[1], PS[2], PS[3], PS[4]]
    ph = [PS[5], PS[6]]
    wcnt = [0]

    def wload(nm, ci, kcn, cols):
        i = wcnt[0] % 3
        wcnt[0] += 1
        view = WB[i][:, 0:kcn * cols].rearrange("p (k m) -> p k m", k=kcn)
        P.op("sp", lambda e: e.dma_start(out=WB[i][:, 0:kcn * cols], in_=wbf[nm][ci, :, :]), reads=["in"], writes=[("WB", i)], dma=True)
        return view, ("WB", i)

    def linear(wview, wkey, mloc, kcn, rhs_fn, rkeys, N):
        b = ppb()
        for kc in range(kcn):
            P.op("pe", (lambda kc, b: lambda e: e.matmul(pp[b][:, :N], lhsT=wview[:, kc, mloc * 128:(mloc + 1) * 128], rhs=rhs_fn(kc),
                                                       start=(kc == 0), stop=(kc == kcn - 1)))(kc, b),
                 reads=list(rkeys) + [wkey], writes=[("pp", b)])
        return b

    ecnt = [0]
    nbkeys = []

    def p3a(t):
        t0 = t * NT
        N = NT

        P.load(X32, xo[:, :, t0:t0 + NT], "X32", eng="act")
        def ldh(e):
            j, g = ids(e)
            return e.dma_start(out=BIG[:, 0:16, :], in_=HRall[bass.ds(j * 8192 + (t // 2) * 2048, 2048), (t % 2) * NT:(t % 2 + 1) * NT].rearrange("(k p) n -> p k n", p=128))
        P.op("sp", ldh, reads=[], writes=["BIG0"], dma=True)

        def ldq(e):
            j, g = ids(e)
            return [e.dma_start(out=QTb[:, ch_ * 8:(ch_ + 1) * 8, :],
                                in_=Qall[bass.ds(j * 2048 + (ch_ * 8192 + (t // 4) * 1024), 1024), (t % 4) * NT:(t % 4 + 1) * NT].rearrange("(k c) n -> c k n", c=128))
                    for ch_ in range(2)]
        P.op("act", ldq, reads=[], writes=["QTb"], dma=True, ndma=2)
        rmsnorm_fm(P, "a", X32, "X32", N, cs[:, G_MIX:G_MIX + 16], NTb, "NTb", **nb)
        for part in range(3):
            for cq in range(4):
                wv, wk = wload("wgz", part * 4 + cq, KC, 512)
                for ml in range(4):
                    mt = cq * 4 + ml
                    b = linear(wv, wk, ml, KC, lambda kc: NTb[:, kc, :N], ["NTb"], N)
                    bias = cs[:, B_GZ + part * 16 + mt:B_GZ + part * 16 + mt + 1]
                    if part == 0:
                        i = ecnt[0] % 2
                        ecnt[0] += 1
                        P.op("act", (lambda b, i, bias: lambda e: e.activation(out=e1[i][:, :N], in_=pp[b][:, :N], func=AF.Gelu_apprx_tanh, bias=bias, scale=1.0))(b, i, bias),
                             reads=[("pp", b)], writes=[("e1", i)])
                        P.op("dve", (lambda mt, i: lambda e: e.tensor_tensor(out=BIG[:, mt, :N], in0=BIG[:, mt, :N], in1=e1[i][:, :N], op=ALU.mult))(mt, i),
                             reads=[("e1", i), "BIG0"], writes=["BIG0"])
                    else:
                        P.op("act", (lambda b, mt, bias, part: lambda e: e.activation(out=BIG[:, part * 16 + mt, :N], in_=pp[b][:, :N], func=AF.Sigmoid, bias=bias, scale=1.0))(b, mt, bias, part),
                             reads=[("pp", b)], writes=["BIG%d" % part])
        for g4 in range(4):
            for ct in range(2):
                b = ppb()
                n = 0
                for chh in range(2):
                    for ri in range(2):
                        P.op("pe", (lambda g4, ct, chh, ri, b, n: lambda e: e.matmul(pp[b][:, :N], lhsT=dftb[:, chh, ri, ct * 128:(ct + 1) * 128],
                                                                                     rhs=QTb[:, chh * 8 + g4 * 2 + ri, :N], start=(n == 0), stop=(n == 3)))(g4, ct, chh, ri, b, n),
                             reads=["QTb", "dftb"], writes=[("pp", b)])
                        n += 1
                P.op("act", (lambda g4, ct, b: lambda e: e.copy(out=FT[:, g4 * 2 + ct, :N], in_=pp[b][:, :N]))(g4, ct, b),
                     reads=[("pp", b)], writes=["FT"])
        for cq in range(4):
            wva, wka = wload("woa", cq, KC, 512)
            wvb, wkb = wload("wob", cq, 8, 512)
            for ml in range(4):
                mt = cq * 4 + ml
                ba = linear(wva, wka, ml, KC, lambda kc: BIG[:, kc, :N], ["BIG0"], N)
                bb = linear(wvb, wkb, ml, 8, lambda kc: FT[:, kc, :N], ["FT"], N)
                i = ecnt[0] % 2
                ecnt[0] += 1
                P.op("dve", (lambda ba, mt, i: lambda e: e.tensor_tensor(out=e1[i][:, :N], in0=pp[ba][:, :N], in1=BIG[:, 16 + mt, :N], op=ALU.mult))(ba, mt, i),
                     reads=[("pp", ba), "BIG1"], writes=[("e1", i)])
                P.op("dve", (lambda bb, mt, i: lambda e: e.tensor_tensor(out=e2[i][:, :N], in0=pp[bb][:, :N], in1=BIG[:, 32 + mt, :N], op=ALU.mult))(bb, mt, i),
                     reads=[("pp", bb), "BIG2"], writes=[("e2", i)])
                P.op("pool", (lambda mt, i: lambda e: e.tensor_tensor(out=MT[:, mt, :N], in0=e1[i][:, :N], in1=e2[i][:, :N], op=ALU.add))(mt, i),
                     reads=[("e1", i), ("e2", i)], writes=["QTb"])
        for cq in range(4):
            wv, wk = wload("wo", cq, KC, 512)
            for ml in range(4):
                mt = cq * 4 + ml
                b = linear(wv, wk, ml, KC, lambda kc: MT[:, kc, :N], ["QTb"], N)
                P.op("dve", (lambda b, mt: lambda e: e.scalar_tensor_tensor(out=X32[:, mt, :N], in0=pp[b][:, :N], scalar=cs[:, B_OUT + mt:B_OUT + mt + 1],
                                                                           in1=X32[:, mt, :N], op0=ALU.add, op1=ALU.add))(b, mt),
                     reads=[("pp", b), "X32"], writes=["X32"])
        P.op("pool", lambda e: e.dma_start(out=HS[:, :, t0:t0 + N].rearrange("k p n -> p k n"), in_=X32),
             reads=["X32"], writes=[("HS", t0)], dma=True, semkey="X32")
        rmsnorm_fm(P, "b", X32, "X32", N, cs[:, G_FFN:G_FFN + 16], NTb, "NTb", **nb)
        P.op("pool", lambda e: e.dma_start(out=N2S[:, :, 1 + t0:1 + t0 + N].rearrange("k p n -> p k n"), in_=NTb[:, :, :N]),
             reads=["NTb"], writes=[("N2S", t0)], dma=True, semkey="NTb")
        if t == 0:
            P.op("pool", lambda e: e.dma_start(out=NBin[:, 0:1].rearrange("(k p) n -> p k n", p=128), in_=NTb[:, :, 0:1], allow_slow_non_contiguous=True),
                 reads=["NTb"], writes=[("NB", 0)], dma=True, semkey="NTb")
            nbkeys.append(("NB", 0))
        if t == NTL - 1:
            P.op("pool", lambda e: e.dma_start(out=NBin[:, 1:2].rearrange("(k p) n -> p k n", p=128), in_=NTb[:, :, NT - 1:NT], allow_slow_non_contiguous=True),
                 reads=["NTb"], writes=[("NB", 1)], dma=True, semkey="NTb")
            nbkeys.append(("NB", 1))

    def p3b(t):
        t0 = t * NT
        rk = [("N2S", t0)]
        rk.append(("N2S", t0 - NT) if t > 0 else ("N2S", "halo"))
        rk.append(("N2S", t0 + NT) if t < NTL - 1 else ("N2S", "halo"))
        P.op("act", lambda e: e.dma_start(out=NTb, in_=N2S[:, :, t0:t0 + NT + 2].rearrange("k p n -> p k n")),
             reads=rk, writes=["NTb"], dma=True)
        P.op("act", lambda e: e.dma_start(out=X32, in_=HS[:, :, t0:t0 + NT].rearrange("k p n -> p k n")),
             reads=[("HS", t0)], writes=["X32"], dma=True)
        mL = cs[:, MK:MK + 1] if t == 0 else cs[:, MK + 2:MK + 3]
        mR = cs[:, MK + 1:MK + 2] if t == NTL - 1 else cs[:, MK + 2:MK + 3]
        for q in range(24):
            wv, wk = wload("wup", q, KC, 512)
            for ml in range(4):
                gv, jj = ml // 2, ml % 2
                jp = 2 * q + jj
                mt = jp + 48 * gv
                b = ppb()
                hbk = b % 2
                for kc in range(KC):
                    P.op("pe", (lambda kc, b, ml, wv: lambda e: e.matmul(pp[b][:], lhsT=wv[:, kc, ml * 128:(ml + 1) * 128], rhs=NTb[:, kc, 1:NT + 1],
                                                                       start=(kc == 0), stop=(kc == KC - 1)))(kc, b, ml, wv),
                         reads=["NTb", wk], writes=[("pp", b)])
                for kc in range(KC):
                    P.op("pe", (lambda kc, hbk, ml, wv: lambda e: e.matmul(ph[hbk][:, 0:2], lhsT=wv[:, kc, ml * 128:(ml + 1) * 128], rhs=NTb[:, kc, 0:NT + 2:NT + 1],
                                                                         start=(kc == 0), stop=(kc == KC - 1)))(kc, hbk, ml, wv),
                         reads=["NTb", wk], writes=[("ps", 5 + hbk)])
                i = ecnt[0] % 2
                ecnt[0] += 1
                bias = cs[:, B_UP + mt:B_UP + mt + 1]
                P.op("act", (lambda b, i, bias: lambda e: e.activation(out=upS[i][:, 1:NT + 1], in_=pp[b][:], func=AF.Identity, bias=bias, scale=1.0))(b, i, bias),
                     reads=[("pp", b)], writes=[("upS", i)])
                P.op("act", (lambda hbk, i, bias: lambda e: e.activation(out=upH[i], in_=ph[hbk][:, 0:2], func=AF.Identity, bias=bias, scale=1.0))(hbk, i, bias),
                     reads=[("ps", 5 + hbk)], writes=[("upH", i)])
                P.op("dve", (lambda i, mL: lambda e: e.tensor_scalar_mul(out=upS[i][:, 0:1], in0=upH[i][:, 0:1], scalar1=mL))(i, mL),
                     reads=[("upH", i)], writes=[("upS", i)])
                P.op("dve", (lambda i, mR: lambda e: e.tensor_scalar_mul(out=upS[i][:, NT + 1:NT + 2], in0=upH[i][:, 1:2], scalar1=mR))(i, mR),
                     reads=[("upH", i)], writes=[("upS", i)])
                w0 = cs[:, CFW + mt * 3:CFW + mt * 3 + 1]
                w1 = cs[:, CFW + mt * 3 + 1:CFW + mt * 3 + 2]
                w2 = cs[:, CFW + mt * 3 + 2:CFW + mt * 3 + 3]
                cb = cs[:, CFB + mt:CFB + mt + 1]
                P.op("dve", (lambda i, w0, cb: lambda e: e.tensor_scalar(out=cv[i], in0=upS[i][:, 0:NT], scalar1=w0, scalar2=cb, op0=ALU.mult, op1=ALU.add))(i, w0, cb),
                     reads=[("upS", i)], writes=[("e2", i)])
                P.op("dve", (lambda i, w1: lambda e: e.scalar_tensor_tensor(out=cv[i], in0=upS[i][:, 1:NT + 1], scalar=w1, in1=cv[i], op0=ALU.mult, op1=ALU.add))(i, w1),
                     reads=[("upS", i), ("e2", i)], writes=[("e2", i)])
                P.op("dve", (lambda i, w2: lambda e: e.scalar_tensor_tensor(out=cv[i], in0=upS[i][:, 2:NT + 2], scalar=w2, in1=cv[i], op0=ALU.mult, op1=ALU.add))(i, w2),
                     reads=[("upS", i), ("e2", i)], writes=[("e2", i)])
                if gv == 0:
                    P.op("act", (lambda i, jj: lambda e: e.activation(out=gg[jj], in_=cv[i], func=AF.Gelu_apprx_tanh))(i, jj),
                         reads=[("e2", i)], writes=[("gg", jj)])
                else:
                    P.op("pool", (lambda i, jj, jp: lambda e: e.tensor_tensor(out=BIG[:, jp, :], in0=gg[jj], in1=cv[i], op=ALU.mult))(i, jj, jp),
                         reads=[("gg", jj), ("e2", i)], writes=["BIG%d" % (jp // 16)])
        for mt in range(KC):
            wv, wk = wload("wdn", mt, 48, 128)
            b = linear(wv, wk, 0, 48, lambda kc: BIG[:, kc, :], ["BIG0", "BIG1", "BIG2"], NT)
            P.op("dve", (lambda b, mt: lambda e: e.scalar_tensor_tensor(out=X32[:, mt, :], in0=pp[b][:], scalar=cs[:, B_DN + mt:B_DN + mt + 1],
                                                                       in1=X32[:, mt, :], op0=ALU.add, op1=ALU.add))(b, mt),
                 reads=[("pp", b), "X32"], writes=["X32"])
        rmsnorm_fm(P, "c", X32, "X32", NT, cs[:, G_FIN:G_FIN + 16], X32, "X32", scratch=NTb, skey="NTb", **nb)
        k = ("YT", t)
        P.op("pool", lambda e: e.dma_start(out=YT[:, :, t0:t0 + NT].rearrange("k p n -> p k n"), in_=X32),
             reads=["X32"], writes=[k], dma=True)
        return k

    for t in range(NTL):
        p3a(t)
    P.op("pool", lambda e: e.collective_compute("AllGather", ALU.bypass, replica_groups=RG, ins=[NBin_h.ap().opt()], outs=[NBall_h.ap().opt()]),
         reads=nbkeys, writes=["NBall"], dma=True, inc=1, semkey="agNB")

    hl = P.sb("hl", [128, KC, 2], BF16)

    def ldhalo(e):
        j, g = ids(e)
        rl = ((j + 3) % 4) * 2048
        rr = ((j + 1) % 4) * 2048
        return [e.dma_start(out=hl[:, :, 0:1], in_=NBall[bass.ds(rl, 2048), 1:2].rearrange("(k p) n -> p k n", p=128), allow_slow_non_contiguous=True),
                e.dma_start(out=hl[:, :, 1:2], in_=NBall[bass.ds(rr, 2048), 0:1].rearrange("(k p) n -> p k n", p=128), allow_slow_non_contiguous=True)]
    P.op("pool", ldhalo, reads=["NBall"], writes=["hl"], dma=True, ndma=2, semkey="halo")
    P.op("sp", lambda e: [e.dma_start(out=N2S[:, :, 0:1].rearrange("k p n -> p k n"), in_=hl[:, :, 0:1], allow_slow_non_contiguous=True),
                          e.dma_start(out=N2S[:, :, TOK + 1:TOK + 2].rearrange("k p n -> p k n"), in_=hl[:, :, 1:2], allow_slow_non_contiguous=True)],
         reads=["hl"], writes=[("N2S", "halo")], dma=True, ndma=2, semkey="halo2")
    fin = [p3b(t) for t in range(NTL)]
    P.finish(fin)
    return P


def fm(a):
    a = np.asarray(a, np.float32)
    return np.ascontiguousarray(a.reshape(-1, 128).T)


def wfm(w):
    K, M = w.shape
    return np.ascontiguousarray(w.reshape(K // 128, 128, M).transpose(1, 0, 2))


def xfm(x):
    T, Fd = x.shape
    return np.ascontiguousarray(x.T.reshape(Fd // 128, 128, T).transpose(1, 0, 2))


def run(P, in_maps, names=None):
    if names is not None:
        in_maps = [{k: v for k, v in m.items() if k in names} for m in in_maps]
    res = run_bass_kernel_spmd(P.nc, in_maps, core_ids=list(range(8)))
    return res.results


def dft_tables(seq_len):
    nb = TS // seq_len
    n2n = seq_len // 128
    p = np.arange(128)
    bidx, n2 = p // n2n, p % n2n
    j = np.arange(128)
    bj, k2 = j // n2n, j % n2n
    ang = 2 * np.pi * np.outer(n2, k2) / n2n
    same = (bidx[:, None] == bj[None, :]).astype(np.float64)
    sc = 1.0 / np.sqrt(seq_len)
    T1 = np.concatenate([np.cos(ang) * same, -np.sin(ang) * same], 1) * sc
    n1 = np.arange(128)
    a2 = 2 * np.pi * np.outer(n1, n1) / 128.0
    Gr, Gi = np.cos(a2), -np.sin(a2)
    G1 = np.concatenate([Gr, Gi], 1)
    G2 = np.concatenate([-Gi, Gr], 1)
    at = 2 * np.pi * np.outer(n1, k2) / seq_len
    TW = np.concatenate([np.cos(at), -np.sin(at)], 1)
    return np.ascontiguousarray(np.stack([T1, G1, G2, TW], 1).astype(np.float32))


_PROGS = {}
_STOP = None


def kernel(x_prompt, x_sample, g_mix, w_in, b_in, conv_a_w, conv_a_b, lru_w_a, lru_b_a, lru_w_x, lru_b_x,
           lru_lam, w_out_a, w_out_b, w_out, b_out, g_ffn, w_up, b_up, conv_f_w, conv_f_b, w_down, b_down,
           g_final):
    f32 = np.float32
    xs = [np.asarray(x_prompt, f32).reshape(TS, D), np.asarray(x_sample, f32).reshape(TS, D)]
    seqlen = [16384, 8192]
    w_in = np.asarray(w_in, f32)[0]
    b_in = np.asarray(b_in, f32)[0]
    if "f" not in _PROGS:
        _PROGS["f"] = build_fused(_STOP)
    caw = np.asarray(conv_a_w, f32)[0]
    cab = np.asarray(conv_a_b, f32)[0]
    lwa = np.asarray(lru_w_a, f32)[0]
    lwx = np.asarray(lru_w_x, f32)[0]
    lba = np.asarray(lru_b_a, f32)[0]
    lbx = np.asarray(lru_b_x, f32)[0]
    lam = np.asarray(lru_lam, f32)[0]
    bgz = np.concatenate([fm(b_in[2048:4096]), fm(b_in[5120:7168]), fm(b_in[7168:9216])], 1)
    wgz_full = np.concatenate([w_in[:, 2048:4096], w_in[:, 5120:9216]], 1)
    wgz = np.stack([wfm(wgz_full[:, q * 512:(q + 1) * 512]) for q in range(12)], 0)
    woa_ = np.asarray(w_out_a, f32)[0]
    wob_ = np.asarray(w_out_b, f32)[0]
    wo_ = np.asarray(w_out, f32)[0]
    wup_ = np.asarray(w_up, f32)[0]
    wdn_ = np.asarray(w_down, f32)[0]
    woa = np.stack([wfm(woa_[:, q * 512:(q + 1) * 512]) for q in range(4)], 0)
    wob = np.stack([wfm(wob_[:, q * 512:(q + 1) * 512]) for q in range(4)], 0)
    wo = np.stack([wfm(wo_[:, q * 512:(q + 1) * 512]) for q in range(4)], 0)
    wupc = []
    for q in range(24):
        cols = np.concatenate([wup_[:, 256 * q:256 * q + 256], wup_[:, 6144 + 256 * q:6144 + 256 * q + 256]], 1)
        wupc.append(wfm(cols))
    wupc = np.stack(wupc, 0)
    wdn = np.stack([wfm(wdn_[:, m * 128:(m + 1) * 128]) for m in range(16)], 0)
    cc = np.arange(256)
    ang = 2 * np.pi * np.outer(cc, cc) / 256.0
    dm = np.stack([np.cos(ang), np.sin(ang)], 0) / 16.0
    dft = np.ascontiguousarray(dm.reshape(2, 2, 128, 256).transpose(2, 1, 0, 3).astype(f32)).reshape(128, 1024)
    cfw = np.asarray(conv_f_w, f32)[0]
    c3 = np.zeros((128, 640), f32)
    c3[:, 0:16] = fm(np.asarray(g_mix, f32)[0])
    c3[:, 16:32] = fm(np.asarray(g_ffn, f32)[0])
    c3[:, 32:48] = fm(np.asarray(g_final, f32))
    c3[:, 48:96] = bgz
    c3[:, 96:112] = fm(np.asarray(b_out, f32)[0])
    c3[:, 112:128] = fm(np.asarray(b_down, f32)[0])
    c3[:, 128:224] = fm(np.asarray(b_up, f32)[0])
    c3[:, 224:512] = cfw.reshape(3, 96, 128).transpose(2, 1, 0).reshape(128, 288)
    c3[:, 512:608] = fm(np.asarray(conv_f_b, f32)[0])
    c3[:, 610] = 1.0
    xgs = [xfm(xs[g]) for g in range(2)]
    ident = np.concatenate([np.eye(128, dtype=f32), np.zeros((128, 128), f32)], 1)
    ins = []
    for c in range(8):
        g, j = c // 4, c % 4
        L = seqlen[g]
        chs = slice(512 * j, 512 * j + 512)
        fch = slice(4096 + 256 * j, 4096 + 256 * j + 256)
        heads = slice(4 * j, 4 * j + 4)
        wg = np.stack([np.stack([lwa[d, heads], lwx[d, heads]], 0) for d in range(2)], 0)
        wg = np.ascontiguousarray(wg.transpose(3, 0, 1, 2, 4)).reshape(128, 2048)
        c2 = np.zeros((128, 64), f32)
        c2[:, 0:16] = caw[:, chs].reshape(4, 4, 128).transpose(2, 1, 0).reshape(128, 16)
        c2[:, 16:20] = fm(cab[chs])
        c2[:, 20:28] = np.concatenate([fm(lba[0, chs]), fm(lba[1, chs])], 1)
        c2[:, 28:36] = np.concatenate([fm(lbx[0, chs]), fm(lbx[1, chs])], 1)
        c2[:, 36:44] = np.concatenate([fm(lam[0, chs]), fm(lam[1, chs])], 1)
        c2[:, 44] = 1.0 if L == TS else 0.0
        c2[:, 48:52] = fm(b_in[chs])
        tb = np.concatenate([dft_tables(L), ident[:, None, :]], 1)
        lo, hi = j * TOK, (j + 1) * TOK
        cm = c3.copy()
        cm[:, 608] = 1.0 if (lo % L) != 0 else 0.0
        cm[:, 609] = 1.0 if (hi % L) != 0 else 0.0
        cm[:, 611] = 1.0 if L == TS else 0.0
        cm[:, 612] = 0.0 if L == TS else 1.0
        ins.append({"xg": xgs[g], "xo": np.ascontiguousarray(xgs[g][:, :, lo:hi]), "wx": wfm(w_in[:, chs]), "wf": wfm(w_in[:, fch]),
                    "bfT": np.ascontiguousarray(np.tile(b_in[fch][None, :], (128, 1))),
                    "wg": wg, "c2": c2, "tb": np.ascontiguousarray(tb), "wgz": wgz, "woa": woa, "wob": wob, "wo": wo,
                    "wup": wupc, "wdn": wdn, "dft": dft, "c3": cm})
    if _STOP is not None:
        return run(_PROGS["f"], ins, names={"xg", "xo", "wx", "wf", "bfT", "wg", "c2", "tb", "dft", "c3"})
    r = run(_PROGS["f"], ins)
    ys = []
    for g in range(2):
        yt = np.concatenate([r[g * 4 + j]["YT"] for j in range(4)], 2)
        ys.append(np.ascontiguousarray(yt.reshape(D, TS).T))
    return (ys[0].reshape(1, 16384, D).astype(f32), ys[1].reshape(2, 8192, D).astype(f32))
```

```python
import contextlib
import numpy as np
import ml_dtypes
import concourse.bass as bass
import concourse.mybir as mybir
from concourse.bass_utils import run_bass_kernel_spmd

F32 = mybir.dt.float32
BF16 = mybir.dt.bfloat16
AF = mybir.ActivationFunctionType
ALU = mybir.AluOpType
NPBF = ml_dtypes.bfloat16

D = 2048
KC = 16
TOK = 4096
TS = 16384
NT = 512
EPS = 1e-6


class Sched:
    def __init__(self, nc):
        self.nc = nc
        self.ops = []
        self.state = {}
        self.bar_from = 0

    def barrier(self):
        last = {}
        dmas = []
        for i, o in enumerate(self.ops[self.bar_from:], self.bar_from):
            if o["dma"]:
                dmas.append(i)
            elif o["fn"] is not None:
                last[o["eng"]] = i
        deps = sorted(set(dmas) | set(last.values()))
        for eng in ("pe", "act", "dve", "pool", "sp"):
            self.ops.append(dict(eng=eng, fn=None, deps=list(deps), dma=False, semkey=None, signal=False, ndma=1, inc=16,
                                 bar=True))
        self.bar_from = len(self.ops)
        self.state = {}

    def op(self, eng, fn, reads=(), writes=(), dma=False, semkey=None, ndma=1, inc=16):
        oid = len(self.ops)
        deps = set()
        for k in reads:
            st = self.state.setdefault(k, [None, []])
            if st[0] is not None:
                deps.add(st[0])
        for k in writes:
            st = self.state.setdefault(k, [None, []])
            if st[0] is not None:
                deps.add(st[0])
            last = {}
            for r in st[1]:
                o = self.ops[r]
                if o["dma"]:
                    deps.add(r)
                else:
                    last[o["eng"]] = max(last.get(o["eng"], -1), r)
            deps.update(last.values())
        for k in reads:
            self.state[k][1].append(oid)
        for k in writes:
            self.state[k] = [oid, []]
        deps.discard(oid)
        if dma and semkey is None:
            semkey = writes[0] if len(writes) else reads[0]
        self.ops.append(dict(eng=eng, fn=fn, deps=sorted(deps), dma=dma, semkey=semkey,
                             signal=dma, ndma=ndma, inc=inc))
        return oid

    def emit(self, final_keys=()):
        nc = self.nc
        ops = self.ops
        self.op("sp", None, reads=list(final_keys))
        for o in ops:
            for d in o["deps"]:
                od = ops[d]
                if od["dma"]:
                    continue
                if od["eng"] == "pe" and o["eng"] == "pe" and not o["dma"]:
                    continue
                od["signal"] = True
        cnt = {}
        for o in ops:
            if o["dma"]:
                k = ("d", o["semkey"])
                cnt[k] = cnt.get(k, 0) + o["inc"] * o["ndma"]
                o["done"] = (k, cnt[k])
            elif o["signal"]:
                k = ("e", o["eng"])
                cnt[k] = cnt.get(k, 0) + 1
                o["done"] = (k, cnt[k])
        semkeys = sorted(cnt.keys(), key=str)
        with contextlib.ExitStack() as es:
            sems = {}
            for i, k in enumerate(semkeys):
                sems[k] = es.enter_context(nc.semaphore("s%d" % i))
            block = es.enter_context(nc.Block())

            def run(engname):
                def body(e):
                    known = {}
                    for o in ops:
                        if o["eng"] != engname:
                            continue
                        for d in o["deps"]:
                            od = ops[d]
                            if "done" not in od:
                                continue
                            if (not od["dma"]) and od["eng"] == "pe" and engname == "pe" and not o["dma"]:
                                continue
                            k, v = od["done"]
                            if known.get(k, 0) < v:
                                e.wait_ge(sems[k], v)
                                known[k] = v
                        if o["fn"] is None:
                            continue
                        ins = o["fn"](e)
                        if o["dma"]:
                            if not isinstance(ins, (list, tuple)):
                                ins = [ins]
                            assert len(ins) == o["ndma"], (len(ins), o["ndma"])
                            for i_ in ins:
                                i_.then_inc(sems[o["done"][0]], o["inc"])
                        elif o["signal"]:
                            ins.then_inc(sems[o["done"][0]], 1)
                return body

            block.tensor(run("pe"))
            block.scalar(run("act"))
            block.vector(run("dve"))
            block.gpsimd(run("pool"))
            block.sync(run("sp"))


class Prog:
    def __init__(self):
        self.nc = bass.Bass("TRN2", target_bir_lowering=False)
        self.es = contextlib.ExitStack()
        self.S = Sched(self.nc)
        self.outs = []
        self._rr = 0

    def din(self, name, shape, dt=F32):
        return self.nc.dram_tensor(name, list(shape), dt, kind="ExternalInput").ap()

    def dout(self, name, shape, dt=F32):
        self.outs.append(name)
        return self.nc.dram_tensor(name, list(shape), dt, kind="ExternalOutput").ap()

    def dscr(self, name, shape, dt=F32):
        return self.nc.dram_tensor(name, list(shape), dt).ap()

    def sb(self, name, shape, dt=F32):
        return self.es.enter_context(self.nc.sbuf_tensor(name, list(shape), dt))

    def ps(self, name, shape, dt=F32):
        return self.es.enter_context(self.nc.psum_tensor(name, list(shape), dt))

    def op(self, *a, **k):
        return self.S.op(*a, **k)

    def ew(self):
        self._rr ^= 1
        return "dve" if self._rr else "pool"

    def load(self, dst_ap, src_ap, dkey, skey="in", eng="sp"):
        self.op(eng, lambda e: e.dma_start(out=dst_ap, in_=src_ap), reads=[skey], writes=[dkey], dma=True)

    def finish(self, final_keys):
        self.S.emit(final_keys=final_keys)
        self.es.close()


def rmsnorm_fm(P, tag, xT, xkey, N, gcol, outT, okey, ones32, epsT, ss_ps, sq, rt, rstd, scratch=None, skey=None):
    if scratch is None:
        scratch, skey = outT, okey
    P.op("act", lambda e: e.activation(out=scratch[:, :, :N], in_=xT[:, :, :N], func=AF.Square), reads=[xkey], writes=[skey])
    for kc in range(KC):
        P.op("pe", (lambda kc: lambda e: e.matmul(ss_ps[:, :N], lhsT=ones32[:, :], rhs=scratch[:, kc, :N],
                                                  start=(kc == 0), stop=(kc == KC - 1)))(kc),
             reads=[skey, "ones32"], writes=["ss_ps"])
    P.op("act", lambda e: e.activation(out=rt[:, :N], in_=ss_ps[:, :N], func=AF.Sqrt, bias=epsT[:, 0:1], scale=1.0 / D),
         reads=["ss_ps", "epsT"], writes=["rt"])
    P.op("dve", lambda e: e.reciprocal(out=rstd[:, :N], in_=rt[:, :N]), reads=["rt"], writes=["rstd"])
    for kc in range(KC):
        P.op("dve", (lambda kc: lambda e: e.scalar_tensor_tensor(out=outT[:, kc, :N], in0=xT[:, kc, :N],
                                                                scalar=gcol[:, kc:kc + 1], in1=rstd[:, :N],
                                                                op0=ALU.mult, op1=ALU.mult))(kc),
             reads=[xkey, "rstd", "consts"] + ([skey] if skey != okey else []), writes=[okey])


def norm_bufs(P):
    ones32 = P.sb("ones32", [128, 128])
    epsT = P.sb("epsT", [128, 1])
    P.op("dve", lambda e: e.memset(ones32[:], 1.0), writes=["ones32"])
    P.op("dve", lambda e: e.memset(epsT[:], EPS), writes=["epsT"])
    ss_ps = P.ps("ss_ps", [128, NT])
    sq = [P.sb("sq%d" % i, [128, NT]) for i in range(2)]
    rt = P.sb("rt", [128, NT])
    rstd = P.sb("rstd", [128, NT])
    return dict(ones32=ones32, epsT=epsT, ss_ps=ss_ps, sq=sq, rt=rt, rstd=rstd)


class Arena:
    def __init__(self, P, nbytes):
        self.t = P.sb("arena", [128, nbytes // 4])
        self.nbytes = nbytes
        self.off = 0

    def reset(self, off=0):
        self.off = off

    def _take(self, nb):
        nb = (nb + 63) // 64 * 64
        o = self.off
        self.off += nb
        assert self.off <= self.nbytes, (self.off, self.nbytes)
        return self.t[:, o // 4:(o + nb) // 4]

    @staticmethod
    def _shape(ap, shape):
        if len(shape) == 2:
            return ap
        if len(shape) == 3:
            return ap.rearrange("p (a b) -> p a b", a=shape[1])
        if len(shape) == 4:
            return ap.rearrange("p (a b c) -> p a b c", a=shape[1], b=shape[2])
        raise ValueError(shape)

    def f32(self, shape):
        n = int(np.prod(shape[1:]))
        return self._shape(self._take(4 * n)[:, :n], shape)

    def bf16(self, shape):
        n = int(np.prod(shape[1:]))
        return self._shape(self._take(2 * n).bitcast(BF16)[:, :n], shape)


RG = [[0, 1, 2, 3], [4, 5, 6, 7]]
G_MIX, G_FFN, G_FIN, B_GZ, B_OUT, B_DN, B_UP, CFW, CFB, MK = 0, 16, 32, 48, 96, 112, 128, 224, 512, 608


def build_fused(stop=None):
    P = Prog()
    nc = P.nc
    NTL = TOK // NT
    NTG = TS // NT
    HALF = TS // 2
    xg = P.din("xg", [128, KC, TS])
    xo = P.din("xo", [128, KC, TOK])
    wx = P.din("wx", [128, KC, 512])
    wf = P.din("wf", [128, KC, 256])
    bfT = P.din("bfT", [128, 256])
    wg = P.din("wg", [128, 2048])
    c2 = P.din("c2", [128, 64])
    tb = P.din("tb", [128, 5, 256])
    if stop is None:
        wgz = P.din("wgz", [12, 128, KC, 512])
        woa = P.din("woa", [4, 128, KC, 512])
        wob = P.din("wob", [4, 128, 8, 512])
        wo = P.din("wo", [4, 128, KC, 512])
        wup = P.din("wup", [24, 128, KC, 512])
        wdn = P.din("wdn", [16, 128, 48, 128])
        YT = P.dout("YT", [KC, 128, TOK])
        wsrc = dict(wgz=(wgz, 12, KC * 512), woa=(woa, 4, KC * 512), wob=(wob, 4, 8 * 512), wo=(wo, 4, KC * 512),
                    wup=(wup, 24, KC * 512), wdn=(wdn, 16, 48 * 128))
        wbf = {}
        for nm, (src, nchunk, ncol) in wsrc.items():
            wbf[nm] = P.dscr(nm + "_bf", [nchunk, 128, ncol], BF16)
    else:
        DBG = P.dout("DBG", [2048, NT], BF16)
        DBF = P.dout("DBF", [128, NT])
    dft = P.din("dft", [128, 1024])
    c3 = P.din("c3", [128, 640])
    Uscr = P.dscr("Uscr", [4, 128, TS + 3])
    HF = P.dscr("HF", [4, 128, TS])
    HS = P.dscr("HS", [KC, 128, TOK])
    N2S = P.dscr("N2S", [KC, 128, TOK + 2], BF16)
    HRloc = P.dscr("HRloc", [8 * 2048, NT], BF16)
    Qloc = P.dscr("Qloc", [2 * 8 * 2048, NT], BF16)
    HRin_h = nc.dram_tensor("HRin", [16 * 512, 2 * NT], BF16)
    HRall_h = nc.dram_tensor("HRall", [16 * 2048, 2 * NT], BF16)
    Qin_h = nc.dram_tensor("Qin", [2 * 8 * 256, 4 * NT], BF16)
    Qall_h = nc.dram_tensor("Qall", [2 * 8 * 1024, 4 * NT], BF16)
    NBin_h = nc.dram_tensor("NBin", [2048, 2], BF16)
    NBall_h = nc.dram_tensor("NBall", [8192, 2], BF16)
    HRin, HRall, Qin, Qall, NBin, NBall = (h.ap() for h in (HRin_h, HRall_h, Qin_h, Qall_h, NBin_h, NBall_h))

    cs2 = P.sb("cs2", [128, 64])
    cs = P.sb("cs3", [128, 640])
    P.load(cs2[:], c2[:, :], "consts")
    P.load(cs[:], c3[:, :], "consts")
    onesb = P.sb("onesb", [128, 128], BF16)
    epsT = P.sb("epsT", [128, 1])
    P.op("dve", lambda e: e.memset(onesb[:], 1.0), writes=["ones32"])
    P.op("dve", lambda e: e.memset(epsT[:], EPS), writes=["epsT"])
    PS = [P.ps("PS%d" % i, [128, NT]) for i in range(8)]
    sq = None
    rt = P.sb("rt", [128, NT])
    rstd = P.sb("rstd", [128, NT])
    nb = dict(ones32=onesb, epsT=epsT, ss_ps=PS[0], sq=sq, rt=rt, rstd=rstd)
    wgb_ = P.sb("wgb", [128, 2048], BF16)
    P.op("pool", lambda e: e.dma_start(out=wgb_[:], in_=wg[:, :]), reads=["in"], writes=["wgb"], dma=True)
    wgb = wgb_[:].rearrange("p (d a m j) -> p d a m j", d=2, a=2, m=4)
    tbb = P.sb("tbb", [128, 5, 256], BF16)
    P.op("pool", lambda e: e.dma_start(out=tbb[:], in_=tb[:, :, :]), reads=["in"], writes=["tbb"], dma=True)
    tw = P.sb("tw", [128, 256])
    P.load(tw[:], tb[:, 3, :], "tw")
    dftb_ = P.sb("dftb", [128, 1024], BF16)
    P.op("pool", lambda e: e.dma_start(out=dftb_[:], in_=dft[:, :]), reads=["in"], writes=["dftb"], dma=True)
    dftb = dftb_[:].rearrange("p (a b c) -> p a b c", a=2, b=2)
    bft = P.sb("bft", [128, 256])
    P.load(bft[:], bfT[:, :], "bft")
    kt = P.sb("kt", [128, 8]); kl = P.sb("kl", [128, 8]); kl2 = P.sb("kl2", [128, 8])
    st = P.sb("st", [128, 8])
    zt = P.sb("zt", [128, 4])
    AR = Arena(P, 188 * 1024)

    pcnt = [0]

    def ppb():
        b = pcnt[0] % 4
        pcnt[0] += 1
        return b

    pid_cache = {}

    def ids(e):
        k = id(e)
        if k not in pid_cache:
            pid = e.partition_id()
            pid_cache[k] = (pid % 4, pid // 4)
        return pid_cache[k]

    Xq = AR.bf16([128, 2, 128, 128])
    XQ_END = AR.off
    xs = [AR.f32([128, KC, NT]) for _ in range(2)]
    nTs = [AR.bf16([128, KC, NT]) for _ in range(2)]
    wxb = AR.bf16([128, KC, 512])
    wfb = AR.bf16([128, KC, 256])
    ob = [AR.f32([128, NT]) for _ in range(2)]
    P.op("pool", lambda e: e.dma_start(out=wxb, in_=wx[:, :, :]), reads=["in"], writes=["wxb"], dma=True)
    P.op("pool", lambda e: e.dma_start(out=wfb, in_=wf[:, :, :]), reads=["in"], writes=["wfb"], dma=True)
    P.op("dve", lambda e: e.memset(zt[:], 0.0), writes=["zt"])
    P.op("sp", lambda e: [e.dma_start(out=Uscr[mx, :, 0:2], in_=zt[:, 0:2], allow_slow_non_contiguous=True) for mx in range(4)] +
                         [e.dma_start(out=Uscr[mx, :, TS + 2:TS + 3], in_=zt[:, 0:1], allow_slow_non_contiguous=True) for mx in range(4)],
         reads=["zt"], writes=["Upad"], dma=True, ndma=8, semkey="zt")
    ocnt = [0]

    def normA(t):
        x = xs[t % 2]
        xk = ("x", t % 2)
        P.load(x, xg[:, :, t * NT:(t + 1) * NT], xk)
        rmsnorm_fm(P, "a", x, xk, NT, cs[:, G_MIX:G_MIX + 16], nTs[t % 2], ("nT", t % 2), **nb)

    def mmA(t):
        nT = nTs[t % 2]
        nk = ("nT", t % 2)
        for mx in range(4):
            b = ppb()
            o_ = ocnt[0] % 2
            ocnt[0] += 1
            for kc in range(KC):
                P.op("pe", (lambda mx, kc, b, nT: lambda e: e.matmul(PS[1 + b][:], lhsT=wxb[:, kc, mx * 128:(mx + 1) * 128], rhs=nT[:, kc, :],
                                                                   start=(kc == 0), stop=(kc == KC - 1)))(mx, kc, b, nT),
                     reads=[nk, "wxb"], writes=[("pp", b)])
            P.op("act", (lambda mx, b, o_: lambda e: e.activation(out=ob[o_], in_=PS[1 + b][:], func=AF.Identity,
                                                                 bias=cs2[:, 48 + mx:49 + mx], scale=1.0))(mx, b, o_),
                 reads=[("pp", b), "consts"], writes=[("ob", o_)])
            P.op("pool", (lambda mx, o_, t: lambda e: e.dma_start(out=Uscr[mx, :, 2 + t * NT:2 + (t + 1) * NT], in_=ob[o_]))(mx, o_, t),
                 reads=[("ob", o_)], writes=[("U", mx, t)], dma=True, semkey=("ob", o_))
        for blk in range(4):
            b = ppb()
            for kc in range(KC):
                P.op("pe", (lambda blk, kc, b, nT: lambda e: e.matmul(PS[1 + b][:, 0:256], lhsT=nT[:, kc, blk * 128:(blk + 1) * 128], rhs=wfb[:, kc, :],
                                                                    start=(kc == 0), stop=(kc == KC - 1)))(blk, kc, b, nT),
                     reads=[nk, "wfb"], writes=[("pp", b)])
            m = 4 * t + blk
            P.op("dve", (lambda m, b: lambda e: e.tensor_tensor(out=Xq[:, :, m, :], in0=PS[1 + b][:, 0:256].rearrange("p (h c) -> p h c", h=2),
                                                                in1=bft[:, :].rearrange("p (h c) -> p h c", h=2), op=ALU.add))(m, b),
                 reads=[("pp", b), "bft"], writes=["Xq"])

    normA(0)
    for t in range(NTG):
        if t + 1 < NTG:
            normA(t + 1)
        mmA(t)
    P.S.barrier()
    if stop == "A":
        P.op("sp", lambda e: [e.dma_start(out=DBF[:, :], in_=Uscr[1, :, 2 + 512:2 + 1024]),
                              e.dma_start(out=DBG[0:128, :], in_=Xq[:, 0, 5, :].rearrange("p c -> p c") if False else Xq[:, :, 5, :].rearrange("p h c -> p (h c)")[:, 0:256] if False else Xq[:, 0, 0:4, :].rearrange("p m c -> p (m c)"))],
             reads=[], writes=["dbg"], dma=True, ndma=2, semkey="dbg")
        P.finish(["dbg"])
        return P

    AR.reset(XQ_END)
    conv_list = []
    if stop is None:
        shp = "c p k m -> c p (k m)"
        for nm, (src, nchunk, ncol) in wsrc.items():
            for ci in range(nchunk):
                conv_list.append((nm, src, ci))

    def issue_conv(n):
        for _ in range(n):
            if conv_list:
                nm, src, ci = conv_list.pop(0)
                P.op("pool", (lambda nm, src, ci: lambda e: e.dma_start(out=wbf[nm][ci, :, :], in_=src.rearrange(shp)[ci, :, :]))(nm, src, ci),
                     reads=["in"], writes=[("wbf", nm, ci)], dma=True, semkey="wconv")
    P.op("act", lambda e: e.activation(out=kt[:], in_=cs2[:, 36:44], func=AF.Exp, scale=-1.0), reads=[], writes=["kt"])
    P.op("dve", lambda e: e.tensor_scalar_add(out=kt[:], in0=kt[:], scalar1=1.0), reads=["kt"], writes=["kt"])
    P.op("act", lambda e: e.activation(out=kt[:], in_=kt[:], func=AF.Ln), reads=["kt"], writes=["kt"])
    P.op("dve", lambda e: e.tensor_scalar_mul(out=kl[:], in0=kt[:], scalar1=-8.0), reads=["kt"], writes=["kl"])
    P.op("dve", lambda e: e.tensor_scalar_mul(out=kl2[:], in0=kt[:], scalar1=-16.0), reads=["kt"], writes=["kl2"])
    P.op("dve", lambda e: e.memset(st[:], 0.0), writes=[("st", c_) for c_ in range(8)])
    NB = 4
    ut = [[AR.f32([128, NT + 3]) for _ in range(2)] for _ in range(NB)]
    u = [[AR.f32([128, NT]) for _ in range(2)] for _ in range(NB)]
    ub = [[AR.bf16([128, NT])] * 2 for _ in range(NB)]
    r2 = [[AR.f32([128, NT]) for _ in range(2)] for _ in range(NB)]
    i2 = [[AR.f32([128, NT]) for _ in range(2)] for _ in range(NB)]
    a2 = [[AR.f32([128, NT]) for _ in range(2)] for _ in range(NB)]
    m2 = [[AR.f32([128, NT]) for _ in range(2)] for _ in range(NB)]
    h_ = [AR.f32([128, NT]) for _ in range(NB)]
    hf = [AR.f32([128, NT]) for _ in range(NB)]
    hb = [AR.bf16([128, NT]) for _ in range(NB)]
    pg = [PS[1], PS[2], PS[3], PS[4]]
    onesT = P.sb("onesT", [128, 1])
    P.op("dve", lambda e: e.memset(onesT[:], 1.0), writes=["onesT"])
    hrkeys = []

    def stage1(d, t, s_):
        issue_conv(2)
        for mx in range(4):
            b = mx
            utb, uu, ubb = ut[b][s_], u[b][s_], ub[b][s_]
            kut, ku, kub = ("ut", b, s_), ("u", b, s_), ("ub", b)
            P.load(utb, Uscr[mx, :, t * NT:t * NT + NT + 3], kut)
            if t == NTG // 2 - 1:
                P.op("dve", (lambda utb: lambda e: e.tensor_scalar_mul(out=utb[:, NT + 2:NT + 3], in0=utb[:, NT + 2:NT + 3], scalar1=cs2[:, 44:45]))(utb),
                     reads=[kut], writes=[kut])
            if t == NTG // 2:
                P.op("dve", (lambda utb: lambda e: e.tensor_scalar_mul(out=utb[:, 0:2], in0=utb[:, 0:2], scalar1=cs2[:, 44:45]))(utb),
                     reads=[kut], writes=[kut])
            P.op("dve", (lambda utb, uu, mx: lambda e: e.tensor_scalar(out=uu, in0=utb[:, 0:NT], scalar1=cs2[:, mx * 4:mx * 4 + 1],
                                                                       scalar2=cs2[:, 16 + mx:17 + mx], op0=ALU.mult, op1=ALU.add))(utb, uu, mx),
                 reads=[kut], writes=[ku])
            for k in range(1, 4):
                P.op("dve", (lambda utb, uu, mx, k: lambda e: e.scalar_tensor_tensor(out=uu, in0=utb[:, k:k + NT],
                                                                                     scalar=cs2[:, mx * 4 + k:mx * 4 + k + 1], in1=uu,
                                                                                     op0=ALU.mult, op1=ALU.add))(utb, uu, mx, k),
                     reads=[kut, ku], writes=[ku])
            P.op("act", (lambda uu, ubb: lambda e: e.copy(out=ubb, in_=uu))(uu, ubb), reads=[ku], writes=[kub])

    def stage23(d, t, s_):
        r_ = [r2[x][s_] for x in range(NB)]
        i_ = [i2[x][s_] for x in range(NB)]
        a_ = [a2[x][s_] for x in range(NB)]
        m_ = [m2[x][s_] for x in range(NB)]
        for mx in range(4):
            b = mx
            P.op("pe", (lambda b, d, mx, ubb: lambda e: e.matmul(pg[b][:], lhsT=wgb[:, d, 0, mx, :], rhs=ubb, start=True, stop=True))(b, d, mx, ub[b][s_]),
                 reads=[("ub", b), "wgb"], writes=[("pp", b)])
        for mx in range(4):
            b = mx
            col = d * 4 + mx
            P.op("act", (lambda b, col: lambda e, r_=r_: e.activation(out=r_[b], in_=pg[b][:], func=AF.Sigmoid, bias=cs2[:, 20 + col:21 + col], scale=1.0))(b, col),
                 reads=[("pp", b)], writes=[("r", b, s_)])
            P.op("pe", (lambda b, d, mx, ubb: lambda e: e.matmul(pg[b][:], lhsT=wgb[:, d, 1, mx, :], rhs=ubb, start=True, stop=True))(b, d, mx, ub[b][s_]),
                 reads=[("ub", b), "wgb"], writes=[("pp", b)])
        for mx in range(4):
            b = mx
            col = d * 4 + mx
            P.op("act", (lambda b, col: lambda e, i_=i_: e.activation(out=i_[b], in_=pg[b][:], func=AF.Sigmoid, bias=cs2[:, 28 + col:29 + col], scale=1.0))(b, col),
                 reads=[("pp", b)], writes=[("i", b, s_)])
        for mx in range(4):
            b = mx
            col = d * 4 + mx
            P.op("act", (lambda b, col: lambda e, a_=a_, r_=r_: e.activation(out=a_[b], in_=r_[b], func=AF.Exp, scale=kl[:, col:col + 1]))(b, col),
                 reads=[("r", b, s_), "kl"], writes=[("a", b, s_)])
            P.op("act", (lambda b, col: lambda e, m_=m_, r_=r_: e.activation(out=m_[b], in_=r_[b], func=AF.Exp, scale=kl2[:, col:col + 1]))(b, col),
                 reads=[("r", b, s_), "kl2"], writes=[("m", b, s_)])
        for mx in range(4):
            b = mx
            P.op("act", (lambda b: lambda e, m_=m_: e.activation(out=m_[b], in_=m_[b], func=AF.Sqrt, bias=onesT[:, 0:1], scale=-1.0))(b),
                 reads=[("m", b, s_), "onesT"], writes=[("m", b, s_)])

    def stage45(d, t, s_):
        r_ = [r2[x][s_] for x in range(NB)]
        i_ = [i2[x][s_] for x in range(NB)]
        a_ = [a2[x][s_] for x in range(NB)]
        m_ = [m2[x][s_] for x in range(NB)]
        for mx in range(4):
            b = mx
            uu = u[b][s_]
            P.op("pool", (lambda b, uu: lambda e, i_=i_: e.tensor_tensor(out=i_[b], in0=i_[b], in1=uu, op=ALU.mult))(b, uu),
                 reads=[("i", b, s_), ("u", b, s_)], writes=[("i", b, s_)])
            P.op("pool", (lambda b: lambda e, i_=i_, m_=m_: e.tensor_tensor(out=i_[b], in0=i_[b], in1=m_[b], op=ALU.mult))(b),
                 reads=[("i", b, s_), ("m", b, s_)], writes=[("i", b, s_)])
        for mx in range(4):
            b = mx
            col = d * 4 + mx
            if d == 0:
                P.op("dve", (lambda b, col: lambda e, a_=a_, i_=i_: e.tensor_tensor_scan(out=h_[b], data0=a_[b], data1=i_[b], initial=st[:, col:col + 1],
                                                                           op0=ALU.mult, op1=ALU.add))(b, col),
                     reads=[("a", b, s_), ("i", b, s_), ("st", col)], writes=[("h", b)])
                P.op("dve", (lambda b, col: lambda e: e.tensor_copy(out=st[:, col:col + 1], in_=h_[b][:, NT - 1:NT]))(b, col),
                     reads=[("h", b)], writes=[("st", col)])
                P.op("pool", (lambda b, mx, t: lambda e: e.dma_start(out=HF[mx, :, t * NT:(t + 1) * NT], in_=h_[b]))(b, mx, t),
                     reads=[("h", b)], writes=[("HF", mx, t)], dma=True, semkey=("h", b))
            else:
                P.load(hf[b], HF[mx, :, t * NT:(t + 1) * NT], ("hf", b), skey=("HF", mx, t))
                P.op("dve", (lambda b, col: lambda e, a_=a_, i_=i_: e.tensor_tensor_scan(out=h_[b][:, ::-1], data0=a_[b][:, ::-1], data1=i_[b][:, ::-1],
                                                                           initial=st[:, col:col + 1], op0=ALU.mult, op1=ALU.add))(b, col),
                     reads=[("a", b, s_), ("i", b, s_), ("st", col)], writes=[("h", b)])
                P.op("dve", (lambda b, col: lambda e: e.tensor_copy(out=st[:, col:col + 1], in_=h_[b][:, 0:1]))(b, col),
                     reads=[("h", b)], writes=[("st", col)])
                P.op("pool", (lambda b: lambda e: e.tensor_tensor(out=hb[b], in0=h_[b], in1=hf[b], op=ALU.add))(b),
                     reads=[("h", b), ("hf", b)], writes=[("hb", b)])
                k = ("HR", mx, t)
                P.op("pool", (lambda b, mx, t: lambda e: e.dma_start(out=HRin[(t // 2) * 512 + mx * 128:(t // 2) * 512 + (mx + 1) * 128, (t % 2) * NT:(t % 2 + 1) * NT], in_=hb[b]))(b, mx, t),
                     reads=[("hb", b)], writes=[k], dma=True, semkey=("hb", b))
                if mx == 3 and t % 2 == 0:
                    tp = t // 2
                    P.op("pool", (lambda tp: lambda e: e.collective_compute("AllGather", ALU.bypass, replica_groups=RG,
                                                                           ins=[HRin[tp * 512:(tp + 1) * 512, :].opt()],
                                                                           outs=[HRall[tp * 2048:(tp + 1) * 2048, :].opt()]))(tp),
                         reads=[("HR", m2, t_) for m2 in range(4) for t_ in (t, t + 1)], writes=[("HRall", tp)], dma=True, inc=1, semkey="agHR")
                    hrkeys.append(("HRall", tp))

    for d in range(2):
        order = list(range(NTG)) if d == 0 else list(range(NTG - 1, -1, -1))
        stage1(d, order[0], 0)
        for ti, t in enumerate(order):
            if ti == NTG // 2:
                P.op("dve", (lambda d: lambda e: e.tensor_scalar_mul(out=st[:, d * 4:d * 4 + 4], in0=st[:, d * 4:d * 4 + 4],
                                                                      scalar1=cs2[:, 44:45]))(d),
                     reads=[("st", d * 4 + c_) for c_ in range(4)], writes=[("st", d * 4 + c_) for c_ in range(4)])
            stage23(d, t, ti % 2)
            if ti + 1 < NTG:
                stage1(d, order[ti + 1], (ti + 1) % 2)
            stage45(d, t, ti % 2)
    P.S.barrier()
    if stop == "B":
        P.op("sp", lambda e: [e.dma_start(out=DBG[:, :], in_=HRall[3 * 2048:4 * 2048, 0:NT]), e.dma_start(out=DBF[:, :], in_=HF[2, :, 512:1024])],
             reads=[], writes=["dbg"], dma=True, ndma=2, semkey="dbg")
        P.finish(["dbg"])
        return P

    AR.reset(XQ_END)
    FB = AR.bf16([128, 128, 128])
    Bp = AR.bf16([128, 128, 2, 128])
    s1 = [AR.f32([128, 256]) for _ in range(2)]
    t1 = [AR.f32([128, 128]) for _ in range(2)]
    t2 = [AR.f32([128, 128]) for _ in range(2)]
    t3 = [AR.f32([128, 128]) for _ in range(2)]
    t4 = [AR.f32([128, 128]) for _ in range(2)]
    p1 = [PS[5], PS[6]]
    p2 = [PS[1], PS[2]]
    ptrs = [PS[7][:, :].bitcast(BF16)[:, 0:128], PS[0][:, :].bitcast(BF16)[:, 0:128]]
    ptrk = [("ps7", 0), "ss_ps"]
    Tr = tw[:, 0:128]
    Ti = tw[:, 128:256]
    ident = tbb[:, 4, 0:128]
    qkeys = []
    for ch in range(2):
        for c in range(128):
            b = c % 2
            P.op("pe", (lambda ch, c, b: lambda e: e.transpose(out=ptrs[b], in_=Xq[:, ch, :, c], identity=ident))(ch, c, b),
                 reads=[("SEL", ch), "tbb"], writes=[ptrk[b]])
            P.op("act" if b else "dve", (lambda c, b: lambda e: (e.copy if b else e.tensor_copy)(out=FB[:, :, c], in_=ptrs[b]))(c, b),
                 reads=[ptrk[b]], writes=["FB"])
        if stop == "C0":
            P.op("sp", lambda e: e.dma_start(out=DBG[0:128, :], in_=FB[:, 0:4, :].rearrange("p a b -> p (a b)")), reads=["FB"], writes=["dbg"], dma=True, semkey="dbg")
            P.op("sp", lambda e: e.dma_start(out=DBF[:, :], in_=HF[2, :, 512:1024]), reads=[], writes=["dbg2"], dma=True, semkey="dbg2")
            P.finish(["dbg", "dbg2"])
            return P
        for c in range(128):
            b = c % 2
            P.op("pe", (lambda c, b: lambda e: e.matmul(p1[b][:, 0:256], lhsT=FB[:, :, c], rhs=tbb[:, 0, :], start=True, stop=True))(c, b),
                 reads=["FB", "tbb"], writes=[("ps", 5 + b)])
            P.op("act", (lambda b: lambda e: e.copy(out=s1[b], in_=p1[b][:, 0:256]))(b), reads=[("ps", 5 + b)], writes=[("s1", b)])
            P.op("dve", (lambda b: lambda e: e.tensor_tensor(out=t1[b], in0=s1[b][:, 0:128], in1=Tr, op=ALU.mult))(b),
                 reads=[("s1", b), "tw"], writes=[("t1", b)])
            P.op("dve", (lambda b: lambda e: e.tensor_tensor(out=t2[b], in0=s1[b][:, 128:256], in1=Ti, op=ALU.mult))(b),
                 reads=[("s1", b), "tw"], writes=[("t2", b)])
            P.op("dve", (lambda b, c: lambda e: e.tensor_tensor(out=Bp[:, c, 0, :], in0=t1[b], in1=t2[b], op=ALU.subtract))(b, c),
                 reads=[("t1", b), ("t2", b)], writes=[("Bp", c)])
            P.op("pool", (lambda b: lambda e: e.tensor_tensor(out=t3[b], in0=s1[b][:, 0:128], in1=Ti, op=ALU.mult))(b),
                 reads=[("s1", b), "tw"], writes=[("t3", b)])
            P.op("pool", (lambda b: lambda e: e.tensor_tensor(out=t4[b], in0=s1[b][:, 128:256], in1=Tr, op=ALU.mult))(b),
                 reads=[("s1", b), "tw"], writes=[("t4", b)])
            P.op("pool", (lambda b, c: lambda e: e.tensor_tensor(out=Bp[:, c, 1, :], in0=t3[b], in1=t4[b], op=ALU.add))(b, c),
                 reads=[("t3", b), ("t4", b)], writes=[("Bp", c)])
        bpk = [("Bp", c) for c in range(128)]
        if stop == "C1":
            P.op("sp", lambda e: e.dma_start(out=DBG[0:128, :], in_=Bp[:, 0:2, :, :].rearrange("p a b c -> p (a b c)")), reads=bpk, writes=["dbg"], dma=True, semkey="dbg")
            P.op("sp", lambda e: e.dma_start(out=DBF[:, :], in_=HF[2, :, 512:1024]), reads=[], writes=["dbg2"], dma=True, semkey="dbg2")
            P.finish(["dbg", "dbg2"])
            return P
        for ri in range(2):
            for j in range(128):
                b = j % 2
                P.op("pe", (lambda j, b, ri: lambda e: e.matmul(p2[b][:, 0:128], lhsT=Bp[:, :, 0, j], rhs=tbb[:, 1, ri * 128:(ri + 1) * 128], start=True, stop=False))(j, b, ri),
                     reads=bpk + ["tbb"], writes=[("pp", b)])
                P.op("pe", (lambda j, b, ri: lambda e: e.matmul(p2[b][:, 0:128], lhsT=Bp[:, :, 1, j], rhs=tbb[:, 2, ri * 128:(ri + 1) * 128], start=False, stop=True))(j, b, ri),
                     reads=bpk + ["tbb"], writes=[("pp", b)])
                P.op("act" if b else "dve", (lambda j, b: lambda e: (e.copy if b else e.tensor_copy)(out=FB[:, :, j], in_=p2[b][:, 0:128]))(j, b),
                     reads=[("pp", b)], writes=["FB"])
            k = ("Q", ch, ri)
            SEL = Xq[:, ch, :, :].rearrange("p a b -> p (a b)")
            P.op("dve", (lambda SEL: lambda e: e.tensor_scalar_mul(out=SEL.rearrange("p (b k x) -> p b k x", b=2, x=64),
                                                                  in0=FB.rearrange("p k (b x) -> p b k x", b=2),
                                                                  scalar1=cs[:, MK + 4:MK + 5]))(SEL),
                 reads=["FB"], writes=[("SEL", ch)])
            P.op("dve", (lambda SEL: lambda e: e.scalar_tensor_tensor(out=SEL, in0=FB.rearrange("p a b -> p (a b)"), scalar=cs[:, MK + 3:MK + 4],
                                                                     in1=SEL, op0=ALU.mult, op1=ALU.add))(SEL),
                 reads=["FB", ("SEL", ch)], writes=[("SEL", ch)])
            P.op("sp", (lambda ch, ri, SEL: lambda e: e.dma_start(out=Qin[ch * 2048:(ch + 1) * 2048, :].rearrange("(s i c) n -> c s i n", i=2, c=128)[:, :, ri, :],
                                                                  in_=SEL.rearrange("p (s n) -> p s n", n=4 * NT)))(ch, ri, SEL),
                 reads=[("SEL", ch)], writes=[k], dma=True, semkey=("SEL", ch))
            qkeys.append(k)
        if stop != "C2":
            for sq_ in range(8):
                idx = ch * 8 + sq_
                P.op("pool", (lambda idx: lambda e: e.collective_compute("AllGather", ALU.bypass, replica_groups=RG,
                                                                         ins=[Qin[idx * 256:(idx + 1) * 256, :].opt()],
                                                                         outs=[Qall[idx * 1024:(idx + 1) * 1024, :].opt()]))(idx),
                     reads=[("Q", ch, 0), ("Q", ch, 1)], writes=[("Qall", idx)], dma=True, inc=1, semkey="agQ")
    if stop == "C2":
        P.op("sp", lambda e: e.dma_start(out=DBG[0:512, :], in_=Qin[512:1024, :]), reads=qkeys, writes=["dbg"], dma=True, semkey="dbg")
        P.op("sp", lambda e: e.dma_start(out=DBF[:, :], in_=HF[2, :, 512:1024]), reads=[], writes=["dbg2"], dma=True, semkey="dbg2")
        P.finish(["dbg", "dbg2"])
        return P
    P.S.barrier()
    if stop == "C":
        P.op("sp", lambda e: [e.dma_start(out=DBG[:, :], in_=Qall[2048:4096, 0:NT]), e.dma_start(out=DBF[:, :], in_=HF[2, :, 512:1024])],
             reads=[], writes=["dbg"], dma=True, ndma=2, semkey="dbg")
        P.finish(["dbg"])
        return P

    AR.reset(0)
    X32 = AR.f32([128, KC, NT])
    NTb = AR.bf16([128, KC, NT + 2])
    QTb = AR.bf16([128, 16, NT])
    MT = QTb
    FT = AR.bf16([128, 8, NT])
    BIG = AR.bf16([128, 48, NT])
    WB = [AR.bf16([128, KC * 512]) for _ in range(3)]
    e1 = [AR.f32([128, NT]) for _ in range(2)]
    e2 = [AR.f32([128, NT]) for _ in range(2)]
    upS = [AR.f32([128, NT + 2]) for _ in range(2)]
    upH = [AR.f32([128, 2]) for _ in range(2)]
    cv = e2
    gg = [AR.f32([128, NT]) for _ in range(2)]
    pp = [PS[1], PS[2], PS[3], PS[4]]
    ph = [PS[5], PS[6]]
    wcnt = [0]

    def wload(nm, ci, kcn, cols):
        i = wcnt[0] % 3
        wcnt[0] += 1
        view = WB[i][:, 0:kcn * cols].rearrange("p (k m) -> p k m", k=kcn)
        P.op("sp", lambda e: e.dma_start(out=WB[i][:, 0:kcn * cols], in_=wbf[nm][ci, :, :]), reads=["in"], writes=[("WB", i)], dma=True)
        return view, ("WB", i)

    def linear(wview, wkey, mloc, kcn, rhs_fn, rkeys, N):
        b = ppb()
        for kc in range(kcn):
            P.op("pe", (lambda kc, b: lambda e: e.matmul(pp[b][:, :N], lhsT=wview[:, kc, mloc * 128:(mloc + 1) * 128], rhs=rhs_fn(kc),
                                                       start=(kc == 0), stop=(kc == kcn - 1)))(kc, b),
                 reads=list(rkeys) + [wkey], writes=[("pp", b)])
        return b

    ecnt = [0]
    nbkeys = []

    def p3a(t):
        t0 = t * NT
        N = NT

        P.load(X32, xo[:, :, t0:t0 + NT], "X32", eng="act")
        def ldh(e):
            j, g = ids(e)
            return e.dma_start(out=BIG[:, 0:16, :], in_=HRall[bass.ds(j * 8192 + (t // 2) * 2048, 2048), (t % 2) * NT:(t % 2 + 1) * NT].rearrange("(k p) n -> p k n", p=128))
        P.op("sp", ldh, reads=[], writes=["BIG0"], dma=True)

        def ldq(e):
            j, g = ids(e)
            return [e.dma_start(out=QTb[:, ch_ * 8:(ch_ + 1) * 8, :],
                                in_=Qall[bass.ds(j * 2048 + (ch_ * 8192 + (t // 4) * 1024), 1024), (t % 4) * NT:(t % 4 + 1) * NT].rearrange("(k c) n -> c k n", c=128))
                    for ch_ in range(2)]
        P.op("act", ldq, reads=[], writes=["QTb"], dma=True, ndma=2)
        rmsnorm_fm(P, "a", X32, "X32", N, cs[:, G_MIX:G_MIX + 16], NTb, "NTb", **nb)
        for part in range(3):
            for cq in range(4):
                wv, wk = wload("wgz", part * 4 + cq, KC, 512)
                for ml in range(4):
                    mt = cq * 4 + ml
                    b = linear(wv, wk, ml, KC, lambda kc: NTb[:, kc, :N], ["NTb"], N)
                    bias = cs[:, B_GZ + part * 16 + mt:B_GZ + part * 16 + mt + 1]
                    if part == 0:
                        i = ecnt[0] % 2
                        ecnt[0] += 1
                        P.op("act", (lambda b, i, bias: lambda e: e.activation(out=e1[i][:, :N], in_=pp[b][:, :N], func=AF.Gelu_apprx_tanh, bias=bias, scale=1.0))(b, i, bias),
                             reads=[("pp", b)], writes=[("e1", i)])
                        P.op("dve", (lambda mt, i: lambda e: e.tensor_tensor(out=BIG[:, mt, :N], in0=BIG[:, mt, :N], in1=e1[i][:, :N], op=ALU.mult))(mt, i),
                             reads=[("e1", i), "BIG0"], writes=["BIG0"])
                    else:
                        P.op("act", (lambda b, mt, bias, part: lambda e: e.activation(out=BIG[:, part * 16 + mt, :N], in_=pp[b][:, :N], func=AF.Sigmoid, bias=bias, scale=1.0))(b, mt, bias, part),
                             reads=[("pp", b)], writes=["BIG%d" % part])
        for g4 in range(4):
            for ct in range(2):
                b = ppb()
                n = 0
                for chh in range(2):
                    for ri in range(2):
                        P.op("pe", (lambda g4, ct, chh, ri, b, n: lambda e: e.matmul(pp[b][:, :N], lhsT=dftb[:, chh, ri, ct * 128:(ct + 1) * 128],
                                                                                     rhs=QTb[:, chh * 8 + g4 * 2 + ri, :N], start=(n == 0), stop=(n == 3)))(g4, ct, chh, ri, b, n),
                             reads=["QTb", "dftb"], writes=[("pp", b)])
                        n += 1
                P.op("act", (lambda g4, ct, b: lambda e: e.copy(out=FT[:, g4 * 2 + ct, :N], in_=pp[b][:, :N]))(g4, ct, b),
                     reads=[("pp", b)], writes=["FT"])
        for cq in range(4):
            wva, wka = wload("woa", cq, KC, 512)
            wvb, wkb = wload("wob", cq, 8, 512)
            for ml in range(4):
                mt = cq * 4 + ml
                ba = linear(wva, wka, ml, KC, lambda kc: BIG[:, kc, :N], ["BIG0"], N)
                bb = linear(wvb, wkb, ml, 8, lambda kc: FT[:, kc, :N], ["FT"], N)
                i = ecnt[0] % 2
                ecnt[0] += 1
                P.op("dve", (lambda ba, mt, i: lambda e: e.tensor_tensor(out=e1[i][:, :N], in0=pp[ba][:, :N], in1=BIG[:, 16 + mt, :N], op=ALU.mult))(ba, mt, i),
                     reads=[("pp", ba), "BIG1"], writes=[("e1", i)])
                P.op("dve", (lambda bb, mt, i: lambda e: e.tensor_tensor(out=e2[i][:, :N], in0=pp[bb][:, :N], in1=BIG[:, 32 + mt, :N], op=ALU.mult))(bb, mt, i),
                     reads=[("pp", bb), "BIG2"], writes=[("e2", i)])
                P.op("pool", (lambda mt, i: lambda e: e.tensor_tensor(out=MT[:, mt, :N], in0=e1[i][:, :N], in1=e2[i][:, :N], op=ALU.add))(mt, i),
                     reads=[("e1", i), ("e2", i)], writes=["QTb"])
        for cq in range(4):
            wv, wk = wload("wo", cq, KC, 512)
            for ml in range(4):
                mt = cq * 4 + ml
                b = linear(wv, wk, ml, KC, lambda kc: MT[:, kc, :N], ["QTb"], N)
                P.op("dve", (lambda b, mt: lambda e: e.scalar_tensor_tensor(out=X32[:, mt, :N], in0=pp[b][:, :N], scalar=cs[:, B_OUT + mt:B_OUT + mt + 1],
                                                                           in1=X32[:, mt, :N], op0=ALU.add, op1=ALU.add))(b, mt),
                     reads=[("pp", b), "X32"], writes=["X32"])
        P.op("pool", lambda e: e.dma_start(out=HS[:, :, t0:t0 + N].rearrange("k p n -> p k n"), in_=X32),
             reads=["X32"], writes=[("HS", t0)], dma=True, semkey="X32")
        rmsnorm_fm(P, "b", X32, "X32", N, cs[:, G_FFN:G_FFN + 16], NTb, "NTb", **nb)
        P.op("pool", lambda e: e.dma_start(out=N2S[:, :, 1 + t0:1 + t0 + N].rearrange("k p n -> p k n"), in_=NTb[:, :, :N]),
             reads=["NTb"], writes=[("N2S", t0)], dma=True, semkey="NTb")
        if t == 0:
            P.op("pool", lambda e: e.dma_start(out=NBin[:, 0:1].rearrange("(k p) n -> p k n", p=128), in_=NTb[:, :, 0:1], allow_slow_non_contiguous=True),
                 reads=["NTb"], writes=[("NB", 0)], dma=True, semkey="NTb")
            nbkeys.append(("NB", 0))
        if t == NTL - 1:
            P.op("pool", lambda e: e.dma_start(out=NBin[:, 1:2].rearrange("(k p) n -> p k n", p=128), in_=NTb[:, :, NT - 1:NT], allow_slow_non_contiguous=True),
                 reads=["NTb"], writes=[("NB", 1)], dma=True, semkey="NTb")
            nbkeys.append(("NB", 1))

    def p3b(t):
        t0 = t * NT
        rk = [("N2S", t0)]
        rk.append(("N2S", t0 - NT) if t > 0 else ("N2S", "halo"))
        rk.append(("N2S", t0 + NT) if t < NTL - 1 else ("N2S", "halo"))
        P.op("act", lambda e: e.dma_start(out=NTb, in_=N2S[:, :, t0:t0 + NT + 2].rearrange("k p n -> p k n")),
             reads=rk, writes=["NTb"], dma=True)
        P.op("act", lambda e: e.dma_start(out=X32, in_=HS[:, :, t0:t0 + NT].rearrange("k p n -> p k n")),
             reads=[("HS", t0)], writes=["X32"], dma=True)
        mL = cs[:, MK:MK + 1] if t == 0 else cs[:, MK + 2:MK + 3]
        mR = cs[:, MK + 1:MK + 2] if t == NTL - 1 else cs[:, MK + 2:MK + 3]
        for q in range(24):
            wv, wk = wload("wup", q, KC, 512)
            for ml in range(4):
                gv, jj = ml // 2, ml % 2
                jp = 2 * q + jj
                mt = jp + 48 * gv
                b = ppb()
                hbk = b % 2
                for kc in range(KC):
                    P.op("pe", (lambda kc, b, ml, wv: lambda e: e.matmul(pp[b][:], lhsT=wv[:, kc, ml * 128:(ml + 1) * 128], rhs=NTb[:, kc, 1:NT + 1],
                                                                       start=(kc == 0), stop=(kc == KC - 1)))(kc, b, ml, wv),
                         reads=["NTb", wk], writes=[("pp", b)])
                for kc in range(KC):
                    P.op("pe", (lambda kc, hbk, ml, wv: lambda e: e.matmul(ph[hbk][:, 0:2], lhsT=wv[:, kc, ml * 128:(ml + 1) * 128], rhs=NTb[:, kc, 0:NT + 2:NT + 1],
                                                                         start=(kc == 0), stop=(kc == KC - 1)))(kc, hbk, ml, wv),
                         reads=["NTb", wk], writes=[("ps", 5 + hbk)])
                i = ecnt[0] % 2
                ecnt[0] += 1
                bias = cs[:, B_UP + mt:B_UP + mt + 1]
                P.op("act", (lambda b, i, bias: lambda e: e.activation(out=upS[i][:, 1:NT + 1], in_=pp[b][:], func=AF.Identity, bias=bias, scale=1.0))(b, i, bias),
                     reads=[("pp", b)], writes=[("upS", i)])
                P.op("act", (lambda hbk, i, bias: lambda e: e.activation(out=upH[i], in_=ph[hbk][:, 0:2], func=AF.Identity, bias=bias, scale=1.0))(hbk, i, bias),
                     reads=[("ps", 5 + hbk)], writes=[("upH", i)])
                P.op("dve", (lambda i, mL: lambda e: e.tensor_scalar_mul(out=upS[i][:, 0:1], in0=upH[i][:, 0:1], scalar1=mL))(i, mL),
                     reads=[("upH", i)], writes=[("upS", i)])
                P.op("dve", (lambda i, mR: lambda e: e.tensor_scalar_mul(out=upS[i][:, NT + 1:NT + 2], in0=upH[i][:, 1:2], scalar1=mR))(i, mR),
                     reads=[("upH", i)], writes=[("upS", i)])
                w0 = cs[:, CFW + mt * 3:CFW + mt * 3 + 1]
                w1 = cs[:, CFW + mt * 3 + 1:CFW + mt * 3 + 2]
                w2 = cs[:, CFW + mt * 3 + 2:CFW + mt * 3 + 3]
                cb = cs[:, CFB + mt:CFB + mt + 1]
                P.op("dve", (lambda i, w0, cb: lambda e: e.tensor_scalar(out=cv[i], in0=upS[i][:, 0:NT], scalar1=w0, scalar2=cb, op0=ALU.mult, op1=ALU.add))(i, w0, cb),
                     reads=[("upS", i)], writes=[("e2", i)])
                P.op("dve", (lambda i, w1: lambda e: e.scalar_tensor_tensor(out=cv[i], in0=upS[i][:, 1:NT + 1], scalar=w1, in1=cv[i], op0=ALU.mult, op1=ALU.add))(i, w1),
                     reads=[("upS", i), ("e2", i)], writes=[("e2", i)])
                P.op("dve", (lambda i, w2: lambda e: e.scalar_tensor_tensor(out=cv[i], in0=upS[i][:, 2:NT + 2], scalar=w2, in1=cv[i], op0=ALU.mult, op1=ALU.add))(i, w2),
                     reads=[("upS", i), ("e2", i)], writes=[("e2", i)])
                if gv == 0:
                    P.op("act", (lambda i, jj: lambda e: e.activation(out=gg[jj], in_=cv[i], func=AF.Gelu_apprx_tanh))(i, jj),
                         reads=[("e2", i)], writes=[("gg", jj)])
                else:
                    P.op("pool", (lambda i, jj, jp: lambda e: e.tensor_tensor(out=BIG[:, jp, :], in0=gg[jj], in1=cv[i], op=ALU.mult))(i, jj, jp),
                         reads=[("gg", jj), ("e2", i)], writes=["BIG%d" % (jp // 16)])
        for mt in range(KC):
            wv, wk = wload("wdn", mt, 48, 128)
            b = linear(wv, wk, 0, 48, lambda kc: BIG[:, kc, :], ["BIG0", "BIG1", "BIG2"], NT)
            P.op("dve", (lambda b, mt: lambda e: e.scalar_tensor_tensor(out=X32[:, mt, :], in0=pp[b][:], scalar=cs[:, B_DN + mt:B_DN + mt + 1],
                                                                       in1=X32[:, mt, :], op0=ALU.add, op1=ALU.add))(b, mt),
                 reads=[("pp", b), "X32"], writes=["X32"])
        rmsnorm_fm(P, "c", X32, "X32", NT, cs[:, G_FIN:G_FIN + 16], X32, "X32", scratch=NTb, skey="NTb", **nb)
        k = ("YT", t)
        P.op("pool", lambda e: e.dma_start(out=YT[:, :, t0:t0 + NT].rearrange("k p n -> p k n"), in_=X32),
             reads=["X32"], writes=[k], dma=True)
        return k

    for t in range(NTL):
        p3a(t)
    P.op("pool", lambda e: e.collective_compute("AllGather", ALU.bypass, replica_groups=RG, ins=[NBin_h.ap().opt()], outs=[NBall_h.ap().opt()]),
         reads=nbkeys, writes=["NBall"], dma=True, inc=1, semkey="agNB")

    hl = P.sb("hl", [128, KC, 2], BF16)

    def ldhalo(e):
        j, g = ids(e)
        rl = ((j + 3) % 4) * 2048
        rr = ((j + 1) % 4) * 2048
        return [e.dma_start(out=hl[:, :, 0:1], in_=NBall[bass.ds(rl, 2048), 1:2].rearrange("(k p) n -> p k n", p=128), allow_slow_non_contiguous=True),
                e.dma_start(out=hl[:, :, 1:2], in_=NBall[bass.ds(rr, 2048), 0:1].rearrange("(k p) n -> p k n", p=128), allow_slow_non_contiguous=True)]
    P.op("pool", ldhalo, reads=["NBall"], writes=["hl"], dma=True, ndma=2, semkey="halo")
    P.op("sp", lambda e: [e.dma_start(out=N2S[:, :, 0:1].rearrange("k p n -> p k n"), in_=hl[:, :, 0:1], allow_slow_non_contiguous=True),
                          e.dma_start(out=N2S[:, :, TOK + 1:TOK + 2].rearrange("k p n -> p k n"), in_=hl[:, :, 1:2], allow_slow_non_contiguous=True)],
         reads=["hl"], writes=[("N2S", "halo")], dma=True, ndma=2, semkey="halo2")
    fin = [p3b(t) for t in range(NTL)]
    P.finish(fin)
    return P


def fm(a):
    a = np.asarray(a, np.float32)
    return np.ascontiguousarray(a.reshape(-1, 128).T)


def wfm(w):
    K, M = w.shape
    return np.ascontiguousarray(w.reshape(K // 128, 128, M).transpose(1, 0, 2))


def xfm(x):
    T, Fd = x.shape
    return np.ascontiguousarray(x.T.reshape(Fd // 128, 128, T).transpose(1, 0, 2))


def run(P, in_maps, names=None):
    if names is not None:
        in_maps = [{k: v for k, v in m.items() if k in names} for m in in_maps]
    res = run_bass_kernel_spmd(P.nc, in_maps, core_ids=list(range(8)))
    return res.results


def dft_tables(seq_len):
    nb = TS // seq_len
    n2n = seq_len // 128
    p = np.arange(128)
    bidx, n2 = p // n2n, p % n2n
    j = np.arange(128)
    bj, k2 = j // n2n, j % n2n
    ang = 2 * np.pi * np.outer(n2, k2) / n2n
    same = (bidx[:, None] == bj[None, :]).astype(np.float64)
    sc = 1.0 / np.sqrt(seq_len)
    T1 = np.concatenate([np.cos(ang) * same, -np.sin(ang) * same], 1) * sc
    n1 = np.arange(128)
    a2 = 2 * np.pi * np.outer(n1, n1) / 128.0
    Gr, Gi = np.cos(a2), -np.sin(a2)
    G1 = np.concatenate([Gr, Gi], 1)
    G2 = np.concatenate([-Gi, Gr], 1)
    at = 2 * np.pi * np.outer(n1, k2) / seq_len
    TW = np.concatenate([np.cos(at), -np.sin(at)], 1)
    return np.ascontiguousarray(np.stack([T1, G1, G2, TW], 1).astype(np.float32))


_PROGS = {}
_STOP = None


def kernel(x_prompt, x_sample, g_mix, w_in, b_in, conv_a_w, conv_a_b, lru_w_a, lru_b_a, lru_w_x, lru_b_x,
           lru_lam, w_out_a, w_out_b, w_out, b_out, g_ffn, w_up, b_up, conv_f_w, conv_f_b, w_down, b_down,
           g_final):
    f32 = np.float32
    xs = [np.asarray(x_prompt, f32).reshape(TS, D), np.asarray(x_sample, f32).reshape(TS, D)]
    seqlen = [16384, 8192]
    w_in = np.asarray(w_in, f32)[0]
    b_in = np.asarray(b_in, f32)[0]
    if "f" not in _PROGS:
        _PROGS["f"] = build_fused(_STOP)
    caw = np.asarray(conv_a_w, f32)[0]
    cab = np.asarray(conv_a_b, f32)[0]
    lwa = np.asarray(lru_w_a, f32)[0]
    lwx = np.asarray(lru_w_x, f32)[0]
    lba = np.asarray(lru_b_a, f32)[0]
    lbx = np.asarray(lru_b_x, f32)[0]
    lam = np.asarray(lru_lam, f32)[0]
    bgz = np.concatenate([fm(b_in[2048:4096]), fm(b_in[5120:7168]), fm(b_in[7168:9216])], 1)
    wgz_full = np.concatenate([w_in[:, 2048:4096], w_in[:, 5120:9216]], 1)
    wgz = np.stack([wfm(wgz_full[:, q * 512:(q + 1) * 512]) for q in range(12)], 0)
    woa_ = np.asarray(w_out_a, f32)[0]
    wob_ = np.asarray(w_out_b, f32)[0]
    wo_ = np.asarray(w_out, f32)[0]
    wup_ = np.asarray(w_up, f32)[0]
    wdn_ = np.asarray(w_down, f32)[0]
    woa = np.stack([wfm(woa_[:, q * 512:(q + 1) * 512]) for q in range(4)], 0)
    wob = np.stack([wfm(wob_[:, q * 512:(q + 1) * 512]) for q in range(4)], 0)
    wo = np.stack([wfm(wo_[:, q * 512:(q + 1) * 512]) for q in range(4)], 0)
    wupc = []
    for q in range(24):
        cols = np.concatenate([wup_[:, 256 * q:256 * q + 256], wup_[:, 6144 + 256 * q:6144 + 256 * q + 256]], 1)
        wupc.append(wfm(cols))
    wupc = np.stack(wupc, 0)
    wdn = np.stack([wfm(wdn_[:, m * 128:(m + 1) * 128]) for m in range(16)], 0)
    cc = np.arange(256)
    ang = 2 * np.pi * np.outer(cc, cc) / 256.0
    dm = np.stack([np.cos(ang), np.sin(ang)], 0) / 16.0
    dft = np.ascontiguousarray(dm.reshape(2, 2, 128, 256).transpose(2, 1, 0, 3).astype(f32)).reshape(128, 1024)
    cfw = np.asarray(conv_f_w, f32)[0]
    c3 = np.zeros((128, 640), f32)
    c3[:, 0:16] = fm(np.asarray(g_mix, f32)[0])
    c3[:, 16:32] = fm(np.asarray(g_ffn, f32)[0])
    c3[:, 32:48] = fm(np.asarray(g_final, f32))
    c3[:, 48:96] = bgz
    c3[:, 96:112] = fm(np.asarray(b_out, f32)[0])
    c3[:, 112:128] = fm(np.asarray(b_down, f32)[0])
    c3[:, 128:224] = fm(np.asarray(b_up, f32)[0])
    c3[:, 224:512] = cfw.reshape(3, 96, 128).transpose(2, 1, 0).reshape(128, 288)
    c3[:, 512:608] = fm(np.asarray(conv_f_b, f32)[0])
    c3[:, 610] = 1.0
    xgs = [xfm(xs[g]) for g in range(2)]
    ident = np.concatenate([np.eye(128, dtype=f32), np.zeros((128, 128), f32)], 1)
    ins = []
    for c in range(8):
        g, j = c // 4, c % 4
        L = seqlen[g]
        chs = slice(512 * j, 512 * j + 512)
        fch = slice(4096 + 256 * j, 4096 + 256 * j + 256)
        heads = slice(4 * j, 4 * j + 4)
        wg = np.stack([np.stack([lwa[d, heads], lwx[d, heads]], 0) for d in range(2)], 0)
        wg = np.ascontiguousarray(wg.transpose(3, 0, 1, 2, 4)).reshape(128, 2048)
        c2 = np.zeros((128, 64), f32)
        c2[:, 0:16] = caw[:, chs].reshape(4, 4, 128).transpose(2, 1, 0).reshape(128, 16)
        c2[:, 16:20] = fm(cab[chs])
        c2[:, 20:28] = np.concatenate([fm(lba[0, chs]), fm(lba[1, chs])], 1)
        c2[:, 28:36] = np.concatenate([fm(lbx[0, chs]), fm(lbx[1, chs])], 1)
        c2[:, 36:44] = np.concatenate([fm(lam[0, chs]), fm(lam[1, chs])], 1)
        c2[:, 44] = 1.0 if L == TS else 0.0
        c2[:, 48:52] = fm(b_in[chs])
        tb = np.concatenate([dft_tables(L), ident[:, None, :]], 1)
        lo, hi = j * TOK, (j + 1) * TOK
        cm = c3.copy()
        cm[:, 608] = 1.0 if (lo % L) != 0 else 0.0
        cm[:, 609] = 1.0 if (hi % L) != 0 else 0.0
        cm[:, 611] = 1.0 if L == TS else 0.0
        cm[:, 612] = 0.0 if L == TS else 1.0
        ins.append({"xg": xgs[g], "xo": np.ascontiguousarray(xgs[g][:, :, lo:hi]), "wx": wfm(w_in[:, chs]), "wf": wfm(w_in[:, fch]),
                    "bfT": np.ascontiguousarray(np.tile(b_in[fch][None, :], (128, 1))),
                    "wg": wg, "c2": c2, "tb": np.ascontiguousarray(tb), "wgz": wgz, "woa": woa, "wob": wob, "wo": wo,
                    "wup": wupc, "wdn": wdn, "dft": dft, "c3": cm})
    if _STOP is not None:
        return run(_PROGS["f"], ins, names={"xg", "xo", "wx", "wf", "bfT", "wg", "c2", "tb", "dft", "c3"})
    r = run(_PROGS["f"], ins)
    ys = []
    for g in range(2):
        yt = np.concatenate([r[g * 4 + j]["YT"] for j in range(4)], 2)
        ys.append(np.ascontiguousarray(yt.reshape(D, TS).T))
    return (ys[0].reshape(1, 16384, D).astype(f32), ys[1].reshape(2, 8192, D).astype(f32))
```
